# Optimizing a Trainium2 kernel written in Bass

```python
import jax
import jax.numpy as jnp
from jax import lax
import numpy as np

D_MODEL = 1024
BATCH = 8
SEQ = 4096
DEPTH = 2
DEC_BATCH = 128
DEC_SEQ = 4
PAST_LEN = 16384
PAGE_SIZE = 128

MIX_WIDTH = D_MODEL
GROUP_WIDTH = MIX_WIDTH // 4
HEAD_DIM = 64
A_HEADS = GROUP_WIDTH // HEAD_DIM
A_CONV = 4
A_CHUNK = 64
B_CONV = 3
C_HEADS = GROUP_WIDTH // HEAD_DIM
C_KV_HEADS = 2
WINDOW = 128
ROPE_DIM = HEAD_DIM // 4
ROPE_THETA = 500000.0
D_GROUPS = 4
D_CHUNK = 128
D_FF = 2816
FFN_CONV = 3
EPS = 1e-6
NEG_INF = -1e30
SPLIT_SIZES = (3 * GROUP_WIDTH, GROUP_WIDTH, A_HEADS, A_HEADS, GROUP_WIDTH, GROUP_WIDTH, GROUP_WIDTH, C_HEADS * HEAD_DIM, C_KV_HEADS * HEAD_DIM, C_KV_HEADS * HEAD_DIM, GROUP_WIDTH, GROUP_WIDTH)
P_TOTAL = sum(SPLIT_SIZES)

kernel_name = 'hybrid_parallel_groups_decode_step'


def rmsnorm(x, g):
    xf = x.astype(jnp.float32)
    y = xf * lax.rsqrt(jnp.mean(xf * xf, axis=-1, keepdims=True) + EPS)
    return (y * g.astype(jnp.float32)).astype(x.dtype)


def layernorm(x, g, b):
    xf = x.astype(jnp.float32)
    mu = jnp.mean(xf, axis=-1, keepdims=True)
    xc = xf - mu
    y = xc * lax.rsqrt(jnp.mean(xc * xc, axis=-1, keepdims=True) + EPS)
    return (y * g.astype(jnp.float32) + b.astype(jnp.float32)).astype(x.dtype)


def l2norm(x):
    xf = x.astype(jnp.float32)
    return xf * lax.rsqrt(jnp.sum(xf * xf, axis=-1, keepdims=True) + EPS)


def causal_dwconv(buf, x, w):
    K = w.shape[0]
    t = x.shape[1]
    xp = jnp.concatenate([buf.astype(x.dtype), x], axis=1)
    y = xp[:, 0:t] * w[0]
    for j in range(1, K):
        y = y + xp[:, j:j + t] * w[j]
    return y, xp[:, t:]


def partial_rope(x, pos):
    half = ROPE_DIM // 2
    inv = jnp.power(ROPE_THETA, -jnp.arange(half, dtype=jnp.float32) * (2.0 / ROPE_DIM))
    ang = pos.astype(jnp.float32)[:, None] * inv[None, :]
    cos = jnp.cos(ang)[None, :, None, :]
    sin = jnp.sin(ang)[None, :, None, :]
    xr = x[..., :ROPE_DIM].astype(jnp.float32)
    x1, x2 = xr[..., :half], xr[..., half:]
    rot = jnp.concatenate([x1 * cos - x2 * sin, x2 * cos + x1 * sin], axis=-1).astype(x.dtype)
    return jnp.concatenate([rot, x[..., ROPE_DIM:]], axis=-1)


def sink_softmax(s, sinks):
    sk = jnp.broadcast_to(sinks.astype(jnp.float32)[..., None, None], s.shape[:-1] + (1,))
    return jax.nn.softmax(jnp.concatenate([s, sk], axis=-1), axis=-1)[..., :-1]


def window_attn_banded(q, k, v, sinks):
    bsz, t, hq, hd = q.shape
    hkv = k.shape[2]
    nb = t // WINDOW
    qb = q.reshape(bsz, nb, WINDOW, hkv, hq // hkv, hd)

    def band(a):
        ab = a.reshape(bsz, nb, WINDOW, hkv, hd)
        prev = jnp.concatenate([jnp.zeros_like(ab[:, :1]), ab[:, :-1]], axis=1)
        return jnp.concatenate([prev, ab], axis=2)

    kb, vb = band(k), band(v)
    i = jnp.arange(WINDOW)[:, None]
    j = jnp.arange(2 * WINDOW)[None, :]
    blk = jnp.arange(nb)[:, None, None]
    mask = (j > i) & (j <= i + WINDOW) & (blk * WINDOW + j - WINDOW >= 0)
    s = jnp.einsum('bnqhgd,bnkhd->bnhgqk', qb, kb).astype(jnp.float32) * (hd ** -0.5)
    s = jnp.where(mask[None, :, None, None], s, NEG_INF)
    pr = sink_softmax(s, sinks.reshape(hkv, hq // hkv))
    o = jnp.einsum('bnhgqk,bnkhd->bnqhgd', pr.astype(vb.dtype), vb)
    return o.reshape(bsz, t, hq * hd)


def window_attn_cached(q, k, v, buf_k, buf_v, q_pos, k_pos, sinks):
    bsz, t, hq, hd = q.shape
    hkv = k.shape[2]
    kk = jnp.concatenate([buf_k.astype(k.dtype), k], axis=1)
    vv = jnp.concatenate([buf_v.astype(v.dtype), v], axis=1)
    qg = q.reshape(bsz, t, hkv, hq // hkv, hd)
    s = jnp.einsum('bqhgd,bkhd->bhgqk', qg, kk).astype(jnp.float32) * (hd ** -0.5)
    dpos = q_pos[:, None] - k_pos[None, :]
    s = jnp.where((dpos >= 0) & (dpos < WINDOW), s, NEG_INF)
    pr = sink_softmax(s, sinks.reshape(hkv, hq // hkv))
    o = jnp.einsum('bhgqk,bkhd->bqhgd', pr.astype(vv.dtype), vv).reshape(bsz, t, hq * hd)
    nbuf = buf_k.shape[1]
    return o, kk[:, -nbuf:], vv[:, -nbuf:]


def gated_delta_rule(q, k, v, g, beta, s0):
    f32 = jnp.float32
    bsz, t, h, dk = q.shape
    dv = v.shape[-1]
    c = A_CHUNK if t % A_CHUNK == 0 else t
    n = t // c

    def chunks(a):
        a = a.astype(f32).reshape((bsz, n, c) + a.shape[2:])
        return jnp.moveaxis(jnp.moveaxis(a, 1, 0), 2, 3)

    qc = chunks(q) * (dk ** -0.5)
    kc = chunks(k)
    vc = chunks(v)
    bc = chunks(beta)
    gc = jnp.cumsum(chunks(g), axis=-1)
    causal = jnp.tril(jnp.ones((c, c), bool))
    strict = jnp.tril(jnp.ones((c, c), bool), -1)
    diff = gc[..., :, None] - gc[..., None, :]
    decay = jnp.where(causal, jnp.exp(jnp.where(causal, diff, 0.0)), 0.0)
    kbeta = kc * bc[..., None]
    a_mat = jnp.where(strict, jnp.einsum('nbhcd,nbhsd->nbhcs', kbeta, kc) * decay, 0.0)
    eye = jnp.eye(c, dtype=f32)
    t_inv = lax.linalg.triangular_solve(a_mat + eye, jnp.broadcast_to(eye, a_mat.shape), left_side=True, lower=True)
    u = t_inv @ (vc * bc[..., None])
    w = t_inv @ (kbeta * jnp.exp(gc)[..., None])
    qk = jnp.einsum('nbhcd,nbhsd->nbhcs', qc, kc) * decay
    q_dec = qc * jnp.exp(gc)[..., None]
    g_last = gc[..., -1]
    k_dec = kc * jnp.exp(g_last[..., None] - gc)[..., None]

    def step(s, xs):
        q_i, k_i, u_i, w_i, qk_i, gl_i = xs
        v_new = u_i - w_i @ s
        o_i = q_i @ s + qk_i @ v_new
        s = s * jnp.exp(gl_i)[..., None, None] + jnp.swapaxes(k_i, -1, -2) @ v_new
        return s, o_i

    s_fin, o = lax.scan(step, s0.astype(f32), (q_dec, k_dec, u, w, qk, g_last))
    o = jnp.moveaxis(jnp.moveaxis(o, 3, 2), 0, 1).reshape(bsz, t, h, dv)
    return o.astype(v.dtype), s_fin


def chunk_token_mlp(u, v, ws, bias):
    bsz, t, width = v.shape
    n = min(t, D_CHUNK)
    nc = t // n
    wm = jnp.where(jnp.tril(jnp.ones((n, n), bool)), ws[:, :n, :n], 0.0).astype(v.dtype)
    vc = v.reshape(bsz, nc, n, D_GROUPS, width // D_GROUPS)
    mixed = jnp.einsum('gts,bcsgd->bctgd', wm, vc) + bias[:, :n].T[None, None, :, :, None].astype(v.dtype)
    return u * mixed.reshape(bsz, t, width)


def hybrid_layer(x, pos, k_pos, st, p):
    bsz, t, _ = x.shape
    f32 = jnp.float32
    h = rmsnorm(x, p['norm1_g'])
    proj = h @ p['w_in']
    (a_qkv, a_z, a_b, a_a, b_b, b_c, b_h, c_q, c_k, c_v, d_u, d_v) = jnp.split(proj, [int(s) for s in np.cumsum(SPLIT_SIZES)[:-1]], axis=-1)
    qkv, a_conv_new = causal_dwconv(st['a_conv'], a_qkv, p['a_conv_w'])
    qkv = jax.nn.silu(qkv).reshape(bsz, t, 3, A_HEADS, HEAD_DIM)
    q, k, v = l2norm(qkv[:, :, 0]), l2norm(qkv[:, :, 1]), qkv[:, :, 2]
    beta = jax.nn.sigmoid(a_b.astype(f32))
    g = -jnp.exp(p['a_log'].astype(f32)) * jax.nn.softplus(a_a.astype(f32) + p['a_dt_bias'].astype(f32))
    o, a_state_new = gated_delta_rule(q, k, v, g, beta, st['a_state'])
    o = rmsnorm(o, p['a_norm_g']) * jax.nn.silu(a_z.reshape(bsz, t, A_HEADS, HEAD_DIM))
    out_a = o.reshape(bsz, t, GROUP_WIDTH)
    bx, b_conv_new = causal_dwconv(st['b_conv'], b_c * b_h, p['b_conv_w'])
    out_b = b_b * bx
    cq = partial_rope(c_q.reshape(bsz, t, C_HEADS, HEAD_DIM), pos)
    ck = partial_rope(c_k.reshape(bsz, t, C_KV_HEADS, HEAD_DIM), pos)
    cv = c_v.reshape(bsz, t, C_KV_HEADS, HEAD_DIM)
    if k_pos is None:
        out_c = window_attn_banded(cq, ck, cv, p['c_sinks'])
        c_k_new, c_v_new = ck[:, -WINDOW:], cv[:, -WINDOW:]
    else:
        out_c, c_k_new, c_v_new = window_attn_cached(cq, ck, cv, st['c_k'], st['c_v'], pos, k_pos, p['c_sinks'])
    du = jax.nn.gelu(d_u)
    dvn = layernorm(jax.nn.gelu(d_v), p['d_ln_g'], p['d_ln_b'])
    out_d = chunk_token_mlp(du, dvn, p['d_ws'], p['d_bias'])
    x = x + jnp.concatenate([out_a, out_b, out_c, out_d], axis=-1) @ p['w_out']
    h2 = rmsnorm(x, p['norm2_g'])
    gate, f_conv_new = causal_dwconv(st['f_conv'], h2 @ p['ffn_w_gate'], p['ffn_conv_w'])
    x = x + (jax.nn.silu(gate) * (h2 @ p['ffn_w_up'])) @ p['ffn_w_down']
    new = {'a_state': a_state_new, 'a_conv': a_conv_new, 'b_conv': b_conv_new, 'c_k': c_k_new, 'c_v': c_v_new, 'f_conv': f_conv_new, 'd_v': dvn}
    return x, new


def setup_inputs(seed: int = 0) -> dict:
    key = jax.random.key(seed)
    ks = iter(jax.random.split(key, 40))
    f32 = jnp.float32

    def nrm(shape, scale):
        return jax.random.normal(next(ks), shape, f32) * scale

    win_buf = min(WINDOW, PAST_LEN)
    a_log = jnp.log(jax.random.uniform(next(ks), (DEPTH, A_HEADS), f32, minval=1.0, maxval=16.0))
    dt = jnp.exp(jax.random.uniform(next(ks), (DEPTH, A_HEADS), f32, minval=float(np.log(1e-3)), maxval=float(np.log(1e-1))))
    a_dt_bias = dt + jnp.log(-jnp.expm1(-dt))
    return {
        'x_prompt': nrm((BATCH, SEQ, D_MODEL), 1.0),
        'x_sample': nrm((DEC_BATCH, DEC_SEQ, D_MODEL), 1.0),
        'state_delta': nrm((DEPTH, DEC_BATCH, A_HEADS, HEAD_DIM, HEAD_DIM), 0.3),
        'state_delta_conv': nrm((DEPTH, DEC_BATCH, A_CONV - 1, 3 * GROUP_WIDTH), 1.0),
        'state_shortconv': nrm((DEPTH, DEC_BATCH, B_CONV - 1, GROUP_WIDTH), 1.0),
        'cache_win_k': nrm((DEPTH, DEC_BATCH, win_buf, C_KV_HEADS, HEAD_DIM), 1.0),
        'cache_win_v': nrm((DEPTH, DEC_BATCH, win_buf, C_KV_HEADS, HEAD_DIM), 1.0),
        'state_ffn_conv': nrm((DEPTH, DEC_BATCH, FFN_CONV - 1, D_FF), 1.0),
        'norm1_g': 1.0 + nrm((DEPTH, D_MODEL), 0.1),
        'w_in': nrm((DEPTH, D_MODEL, P_TOTAL), D_MODEL ** -0.5),
        'a_conv_w': nrm((DEPTH, A_CONV, 3 * GROUP_WIDTH), A_CONV ** -0.5),
        'a_log': a_log,
        'a_dt_bias': a_dt_bias,
        'a_norm_g': 1.0 + nrm((DEPTH, HEAD_DIM), 0.1),
        'b_conv_w': nrm((DEPTH, B_CONV, GROUP_WIDTH), B_CONV ** -0.5),
        'c_sinks': nrm((DEPTH, C_HEADS), 1.0),
        'd_ln_g': 1.0 + nrm((DEPTH, GROUP_WIDTH), 0.1),
        'd_ln_b': nrm((DEPTH, GROUP_WIDTH), 0.1),
        'd_ws': nrm((DEPTH, D_GROUPS, D_CHUNK, D_CHUNK), D_CHUNK ** -0.5),
        'd_bias': 1.0 + nrm((DEPTH, D_GROUPS, D_CHUNK), 0.1),
        'w_out': nrm((DEPTH, MIX_WIDTH, D_MODEL), MIX_WIDTH ** -0.5),
        'norm2_g': 1.0 + nrm((DEPTH, D_MODEL), 0.1),
        'ffn_w_gate': nrm((DEPTH, D_MODEL, D_FF), D_MODEL ** -0.5),
        'ffn_w_up': nrm((DEPTH, D_MODEL, D_FF), D_MODEL ** -0.5),
        'ffn_conv_w': nrm((DEPTH, FFN_CONV, D_FF), FFN_CONV ** -0.5),
        'ffn_w_down': nrm((DEPTH, D_FF, D_MODEL), D_FF ** -0.5),
        'final_norm_g': 1.0 + nrm((D_MODEL,), 0.1),
    }


def reference(x_prompt, x_sample, state_delta, state_delta_conv, state_shortconv, cache_win_k, cache_win_v, state_ffn_conv, norm1_g, w_in, a_conv_w, a_log, a_dt_bias, a_norm_g, b_conv_w, c_sinks, d_ln_g, d_ln_b, d_ws, d_bias, w_out, norm2_g, ffn_w_gate, ffn_w_up, ffn_conv_w, ffn_w_down, final_norm_g):
    bp, tp, _ = x_prompt.shape
    bs, ts, _ = x_sample.shape
    dt = x_prompt.dtype
    win_buf = cache_win_k.shape[2]
    pos_p = jnp.arange(tp, dtype=jnp.int32)
    pos_s = PAST_LEN + jnp.arange(ts, dtype=jnp.int32)
    k_pos_s = PAST_LEN - win_buf + jnp.arange(win_buf + ts, dtype=jnp.int32)
    names = ('a_state', 'a_conv', 'b_conv', 'c_k', 'c_v', 'f_conv')
    new_p = {n: [] for n in names}
    new_s = {n: [] for n in names}
    chunk_v_rows = []
    hp, hs = x_prompt, x_sample
    for l in range(DEPTH):
        p = {'norm1_g': norm1_g[l], 'w_in': w_in[l], 'a_conv_w': a_conv_w[l], 'a_log': a_log[l], 'a_dt_bias': a_dt_bias[l], 'a_norm_g': a_norm_g[l], 'b_conv_w': b_conv_w[l], 'c_sinks': c_sinks[l], 'd_ln_g': d_ln_g[l], 'd_ln_b': d_ln_b[l], 'd_ws': d_ws[l], 'd_bias': d_bias[l], 'w_out': w_out[l], 'norm2_g': norm2_g[l], 'ffn_w_gate': ffn_w_gate[l], 'ffn_w_up': ffn_w_up[l], 'ffn_conv_w': ffn_conv_w[l], 'ffn_w_down': ffn_w_down[l]}
        st_p = {'a_state': jnp.zeros((bp, A_HEADS, HEAD_DIM, HEAD_DIM), jnp.float32), 'a_conv': jnp.zeros((bp, A_CONV - 1, 3 * GROUP_WIDTH), dt), 'b_conv': jnp.zeros((bp, B_CONV - 1, GROUP_WIDTH), dt), 'f_conv': jnp.zeros((bp, FFN_CONV - 1, D_FF), dt)}
        hp, np_ = hybrid_layer(hp, pos_p, None, st_p, p)
        st_s = {'a_state': state_delta[l], 'a_conv': state_delta_conv[l], 'b_conv': state_shortconv[l], 'c_k': cache_win_k[l], 'c_v': cache_win_v[l], 'f_conv': state_ffn_conv[l]}
        hs, ns_ = hybrid_layer(hs, pos_s, k_pos_s, st_s, p)
        for n in names:
            new_p[n].append(np_[n])
            new_s[n].append(ns_[n])
        chunk_v_rows.append(ns_['d_v'])
    y_prompt = rmsnorm(hp, final_norm_g)
    y_sample = rmsnorm(hs, final_norm_g)
    sp = {n: jnp.stack(new_p[n]) for n in names}
    ss = {n: jnp.stack(new_s[n]) for n in names}
    chunk_v_s = jnp.stack(chunk_v_rows)
    return (y_prompt, y_sample, sp['a_state'], ss['a_state'], sp['a_conv'], ss['a_conv'], sp['b_conv'], ss['b_conv'], sp['c_k'], ss['c_k'], sp['c_v'], ss['c_v'], sp['f_conv'], ss['f_conv'], chunk_v_s)
```

```python
import numpy as np
import concourse.bass as bass
import concourse.mybir as mybir

F32 = mybir.dt.float32
BF16 = mybir.dt.bfloat16
I32 = mybir.dt.int32
ALU = mybir.AluOpType
AF = mybir.ActivationFunctionType
AX = mybir.AxisListType

ENGINES = ("pe", "act", "dve", "pool", "sp")


class _Buf:
    __slots__ = ("last_w", "readers")

    def __init__(self):
        self.last_w = None
        self.readers = []


class Prog:
    def __init__(self, nc, ring=12):
        self.nc = nc
        self.ops = {e: [] for e in ENGINES}
        self.cnt = {e: 0 for e in ENGINES}
        self.known = {e: {} for e in ENGINES}
        self.bufs = {}
        self.stack = []
        self.sems = {}
        self.nsem = 0
        self.ring_n = ring
        self.ring = {}
        self.ring_pos = {}
        self.untracked = set()
        self.nops = 0

    def _sem(self, key):
        if key not in self.sems:
            cm = self.nc.semaphore("s_" + key)
            self.sems[key] = cm.__enter__()
            self.stack.append(cm)
            self.nsem += 1
        return self.sems[key]

    def sb(self, name, shape, dtype=F32):
        cm = self.nc.sbuf_tensor(name, list(shape), dtype)
        t = cm.__enter__()
        self.stack.append(cm)
        return t

    def ps(self, name, shape, dtype=F32):
        cm = self.nc.psum_tensor(name, list(shape), dtype)
        t = cm.__enter__()
        self.stack.append(cm)
        return t

    def dram(self, name, shape, dtype=F32, kind="Internal"):
        return self.nc.dram_tensor(name, list(shape), dtype, kind=kind)

    def _key(self, ap):
        if isinstance(ap, str):
            return ap
        return ap.tensor.name

    def _deps(self, eng, reads, writes):
        waits = []
        known = self.known[eng]

        def need(ev):
            if ev is None:
                return
            sk, val, kn = ev
            if eng == "pe" and sk == "pe":
                return
            if known.get(sk, 0) >= val:
                return
            waits.append((sk, val))
            known[sk] = val
            for k, v in kn.items():
                if known.get(k, 0) < v:
                    known[k] = v

        rk = [self._key(a) for a in reads]
        wk = [self._key(a) for a in writes]
        rk = [k for k in rk if k not in self.untracked]
        wk = [k for k in wk if k not in self.untracked]
        for k in rk:
            b = self.bufs.setdefault(k, _Buf())
            need(b.last_w)
        for k in wk:
            b = self.bufs.setdefault(k, _Buf())
            need(b.last_w)
            for r in b.readers:
                need(r)
        return waits, rk, wk

    def _commit(self, ev, rk, wk):
        for k in rk:
            if k not in wk:
                self.bufs[k].readers.append(ev)
        for k in wk:
            b = self.bufs[k]
            b.last_w = ev
            b.readers = []

    def op(self, eng, fn, reads, writes, inc=True):
        waits, rk, wk = self._deps(eng, reads, writes)
        ev = (eng, self.cnt[eng] + 1, dict(self.known[eng]))
        if inc:
            self.cnt[eng] += 1
        self.ops[eng].append((waits, fn, (eng, 1) if inc else None))
        self._commit(ev, rk, wk)
        self.nops += 1

    def dma(self, eng, out, in_, extra_reads=(), extra_writes=(), **kw):
        if eng not in self.ring:
            self.ring[eng] = [["d%s%d" % (eng, i), 0] for i in range(self.ring_n)]
            self.ring_pos[eng] = 0
        slot = self.ring[eng][self.ring_pos[eng]]
        self.ring_pos[eng] = (self.ring_pos[eng] + 1) % self.ring_n
        waits, rk, wk = self._deps(eng, [in_] + list(extra_reads), [out] + list(extra_writes))
        known = self.known[eng]
        if slot[1] > 0 and known.get(slot[0], 0) < 16 * slot[1]:
            waits.append((slot[0], 16 * slot[1]))
            known[slot[0]] = 16 * slot[1]
        slot[1] += 1
        ev = (slot[0], 16 * slot[1], dict(known))

        def fn(e, out=out, in_=in_, kw=kw):
            return e.dma_start(out=out, in_=in_, **kw)

        self.ops[eng].append((waits, fn, (slot[0], 16)))
        self._commit(ev, rk, wk)
        self.nops += 1

    def mm(self, out, lhsT, rhs, start=True, stop=True, inc=True, **kw):
        self.op("pe", lambda e: e.matmul(out, lhsT, rhs, start=start, stop=stop, **kw),
                [lhsT, rhs] + ([] if start else [out]), [out], inc=inc)

    def tr(self, out, in_, ident, inc=True):
        self.op("pe", lambda e: e.transpose(out, in_, ident), [in_, ident], [out], inc=inc)

    def act(self, out, in_, func, bias=None, scale=None, accum_out=None, eng="act"):
        kw = {}
        reads = [in_]
        writes = [out]
        if bias is not None:
            kw["bias"] = bias
            if not isinstance(bias, (int, float)):
                reads.append(bias)
        if scale is not None:
            kw["scale"] = scale
            if not isinstance(scale, (int, float)):
                reads.append(scale)
        if accum_out is not None:
            kw["accum_out"] = accum_out
            writes.append(accum_out)
        self.op("act", lambda e: e.activation(out, in_, func, **kw), reads, writes)

    def tt(self, eng, out, in0, in1, op):
        self.op(eng, lambda e: e.tensor_tensor(out, in0, in1, op), [in0, in1], [out])

    def ts(self, eng, out, in0, s1, op0, s2=None, op1=None, accum_out=None):
        reads = [in0]
        if not isinstance(s1, (int, float)):
            reads.append(s1)
        if s2 is not None and not isinstance(s2, (int, float)):
            reads.append(s2)
        writes = [out]
        kw = {}
        if op1 is not None:
            kw["op1"] = op1
        if accum_out is not None:
            kw["accum_out"] = accum_out
            writes.append(accum_out)
        self.op(eng, lambda e: e.tensor_scalar(out, in0, s1, s2, op0, **kw), reads, writes)

    def stt(self, out, in0, scalar, in1, op0, op1, eng="dve"):
        reads = [in0, in1]
        if not isinstance(scalar, (int, float)):
            reads.append(scalar)
        self.op(eng, lambda e: e.scalar_tensor_tensor(out, in0, scalar, in1, op0, op1), reads, [out])

    def copy(self, eng, out, in_):
        if eng == "act":
            self.op("act", lambda e: e.copy(out, in_), [in_], [out])
        else:
            self.op(eng, lambda e: e.tensor_copy(out, in_), [in_], [out])

    def memset(self, eng, out, val):
        self.op(eng, lambda e: e.memset(out, val), [], [out])

    def reduce(self, out, in_, op, axis=AX.X, eng="dve"):
        self.op(eng, lambda e: e.tensor_reduce(out, in_, axis, op), [in_], [out])

    def recip(self, out, in_):
        self.op("dve", lambda e: e.reciprocal(out, in_), [in_], [out])

    def emit(self):
        nc = self.nc
        fin = []
        for eng, ring in self.ring.items():
            for key, uses in ring:
                if uses > 0:
                    fin.append((key, 16 * uses))
        for e in ("pe", "act", "dve", "pool"):
            if self.cnt[e] > 0:
                fin.append((e, self.cnt[e]))
        for k in list(self.sems) + [f[0] for f in fin]:
            self._sem(k)
        for eng in ENGINES:
            for waits, fn, inc in self.ops[eng]:
                for sk, v in waits:
                    self._sem(sk)
                if inc is not None:
                    self._sem(inc[0])
        sems = self.sems
        ops = self.ops

        def replay(e, lst, final=False):
            for waits, fn, inc in lst:
                for sk, v in waits:
                    e.wait_ge(sems[sk], v)
                ins = fn(e)
                if inc is not None:
                    ins.then_inc(sems[inc[0]], inc[1])
            if final:
                for sk, v in fin:
                    e.wait_ge(sems[sk], v)

        with nc.Block() as block:
            @block.sync
            def _(e):
                replay(e, ops["sp"], final=True)

            @block.tensor
            def _(e):
                replay(e, ops["pe"])

            @block.scalar
            def _(e):
                replay(e, ops["act"])

            @block.vector
            def _(e):
                replay(e, ops["dve"])

            @block.gpsimd
            def _(e):
                replay(e, ops["pool"])

    def close(self):
        while self.stack:
            self.stack.pop().__exit__(None, None, None)


from concourse.bass_utils import run_bass_kernel_spmd

D = 1024
KC = 8
PTOT = 2824
DFF = 2816
FC = 22
EPS = 1e-6
PAST = 16384
NEG = -30000.0
TP = 512
NTP = 8
TS = 64


def _consts():
    f = np.float32
    c = {}
    c["ident"] = np.eye(128, dtype=f)
    s = np.arange(128)
    c["U"] = (s[:, None] <= s[None, :]).astype(f)
    c["Us"] = (s[:, None] < s[None, :]).astype(f)
    c["ones"] = np.ones((128, 128), f)
    bo = np.zeros((128, 128), f)
    bo[:64, :64] = 1
    bo[64:, 64:] = 1
    c["bones"] = bo
    sel = np.zeros((8, 4, 128), f)
    for h in range(4):
        sel[4 + h, h, :] = 1
    c["selg"] = sel.reshape(8, 512)
    selb = np.zeros((8, 4, 128), f)
    for h in range(4):
        selb[h, h, :] = 1
    c["selb"] = selb.reshape(8, 512)
    sel2 = np.zeros((8, 2, 128), f)
    for j in range(2):
        sel2[4 + 2 * j, j, :64] = 1
        sel2[4 + 2 * j + 1, j, 64:] = 1
    c["sel2"] = sel2.reshape(8, 256)
    perm = np.zeros((128, 128), f)
    for half in range(2):
        for d in range(8):
            r = half * 64 + d
            perm[r + 8, r] = -1.0
            perm[r, r + 8] = 1.0
    c["perm"] = perm
    inv = np.power(np.float32(500000.0), -np.arange(8, dtype=f) * f(2.0 / 16)).astype(f)

    def cs(pos):
        ang = pos.astype(f)[None, :] * inv[:, None]
        C = np.ones((128, pos.shape[0]), f)
        S = np.zeros((128, pos.shape[0]), f)
        for half in range(2):
            for q in range(2):
                r0 = half * 64 + q * 8
                C[r0:r0 + 8] = np.cos(ang)
                S[r0:r0 + 8] = np.sin(ang)
        return C, S

    c["cosp"], c["sinp"] = cs(np.arange(4096))
    c["coss"], c["sins"] = cs(np.tile(PAST + np.arange(4), 16))
    i = np.arange(128)[:, None]
    j = np.arange(256)[None, :]
    c["amask"] = np.where((j > i) & (j <= i + 128), 0.0, NEG).astype(f)
    rp = np.ones((8, TP), f)
    rp[:, ::128] = 0
    c["rstp"] = rp
    rs = np.ones((8, TS), f)
    rs[:, ::4] = 0
    c["rsts"] = rs
    return c


CONST_SHAPES = {"ident": (128, 128), "U": (128, 128), "Us": (128, 128), "ones": (128, 128), "bones": (128, 128),
                "selg": (8, 512), "selb": (8, 512), "sel2": (8, 256), "perm": (128, 128),
                "cosp": (128, 4096), "sinp": (128, 4096), "coss": (128, 64), "sins": (128, 64),
                "amask": (128, 256), "rstp": (8, TP), "rsts": (8, TS)}

IN_SHAPES = {
    "xp": (4096, D), "xs": (64, D),
    "state_delta": (2, 16, 4, 64, 64), "state_delta_conv": (2, 48, 768), "state_shortconv": (2, 32, 256),
    "cache_win_k": (2, 16, 128, 128), "cache_win_v": (2, 16, 128, 128), "state_ffn_conv": (2, 32, DFF),
    "norm1_g": (2, D), "w_in": (2, D, PTOT), "a_conv_w": (2, 4, 768), "a_log": (2, 4), "a_dt_bias": (2, 4),
    "a_norm_g": (2, 64), "b_conv_w": (2, 3, 256), "c_sinks": (2, 4), "d_ln_g": (2, 256), "d_ln_b": (2, 256),
    "d_ws": (2, 4, 128, 128), "d_bias": (2, 4, 128), "w_out": (2, D, D), "norm2_g": (2, D),
    "ffn_w_gate": (2, D, DFF), "ffn_w_up": (2, D, DFF), "ffn_conv_w": (2, 3, DFF), "ffn_w_down": (2, DFF, D),
    "final_norm_g": (D,),
}
OUT_SHAPES = {
    "y_p": (4096, D), "y_s": (64, D), "sd_p": (2, 4, 64, 64), "sd_s": (2, 16, 4, 64, 64),
    "ac_p": (2, 3, 768), "ac_s": (2, 48, 768), "bc_p": (2, 2, 256), "bc_s": (2, 32, 256),
    "ck_p": (2, 128, 128), "ck_s": (2, 16, 128, 128), "cv_p": (2, 128, 128), "cv_s": (2, 16, 128, 128),
    "fc_p": (2, 2, DFF), "fc_s": (2, 32, DFF), "dv_s": (2, 64, 256),
}


def build_program():
    nc = bass.Bass("TRN2", target_bir_lowering=False)
    P = Prog(nc, ring=12)
    I = {k: nc.dram_tensor(k, list(s), F32, kind="ExternalInput").ap() for k, s in IN_SHAPES.items()}
    C = {k: nc.dram_tensor("c_" + k, list(s), F32, kind="ExternalInput").ap() for k, s in CONST_SHAPES.items()}
    O = {k: nc.dram_tensor(k, list(s), F32, kind="ExternalOutput").ap() for k, s in OUT_SHAPES.items()}
    for k in list(IN_SHAPES) + ["c_" + k for k in CONST_SHAPES] + list(OUT_SHAPES):
        P.untracked.add(k)
    NC_ = {"ns": 0}

    def small(out, in_):
        P.dma("sp", out, in_, allow_slow_non_contiguous=True)

    def cload(name, shape, src, bf=False):
        t = P.sb("k_" + name, shape)
        P.dma("sp", t[:], src)
        if bf:
            tb = P.sb("kb_" + name, shape, BF16)
            P.copy("dve", tb[:], t[:])
            return t, tb
        return t

    ident, identb = cload("ident", [128, 128], C["ident"], True)
    U = cload("U", [128, 128], C["U"])
    Us = cload("Us", [128, 128], C["Us"])
    ones, onesb = cload("ones", [128, 128], C["ones"], True)
    bones, bonesb = cload("bones", [128, 128], C["bones"], True)
    selg = cload("selg", [8, 512], C["selg"])
    selb = cload("selb", [8, 512], C["selb"])
    sel2 = cload("sel2", [8, 256], C["sel2"])
    perm = cload("perm", [128, 128], C["perm"])
    amask = cload("amask", [128, 256], C["amask"])

    pb = [P.ps("pb%d" % i, [128, 512]) for i in range(7)]
    pT16 = P.ps("pT16", [128, 1024], BF16)

    S2flat = P.sb("S2", [128, 11 * TP])

    def ptrans(dst3, src_rows, R, Cn):
        P.dma("sp", S2flat[0:R, 0:Cn * 128], src_rows)
        for c in range(Cn):
            P.tr(pb[6][:, c * R:(c + 1) * R], S2flat[0:R, c * 128:(c + 1) * 128], ident[0:R, 0:R], inc=(c == Cn - 1))
        P.copy("dve", dst3, pb[6][:, 0:Cn * R].rearrange("p (c r) -> p c r", c=Cn))

    class _Stop(Exception):
        pass

    import os
    stop_at = float(os.environ.get("MK_STOP", "99"))

    lpos = {"i": 0}

    def stage(k):
        if lpos["i"] * 10 + k >= stop_at:
            raise _Stop()

    prm = []
    wtmp_sh = S2flat[:, 0:512].rearrange("p (g s) -> p g s", g=4)
    brow_sh = S2flat[0:1, 512:1024]
    for l in range(2):
        d = {}
        d["g1"] = P.sb("g1_%d" % l, [128, 8])
        ptrans(d["g1"][:].unsqueeze(1), I["norm1_g"][l].rearrange("(k p) -> k p", p=128), 8, 1)
        d["g2"] = P.sb("g2_%d" % l, [128, 8])
        ptrans(d["g2"][:].unsqueeze(1), I["norm2_g"][l].rearrange("(k p) -> k p", p=128), 8, 1)
        d["wA"] = P.sb("wA_%d" % l, [128, 6, 4])
        ptrans(d["wA"][:], I["a_conv_w"][l], 4, 6)
        d["wB"] = P.sb("wB_%d" % l, [128, 2, 3])
        ptrans(d["wB"][:], I["b_conv_w"][l], 3, 2)
        d["wF"] = P.sb("wF_%d" % l, [128, FC, 3])
        ptrans(d["wF"][:], I["ffn_conv_w"][l], 3, FC)
        d["dtb"] = P.sb("dtb_%d" % l, [8, 1])
        P.memset("dve", d["dtb"][:], 0.0)
        small(d["dtb"][4:8, :], I["a_dt_bias"][l].rearrange("(h o) -> h o", o=1))
        al = P.sb("al_%d" % l, [8, 1])
        P.memset("dve", al[:], 0.0)
        small(al[4:8, :], I["a_log"][l].rearrange("(h o) -> h o", o=1))
        d["negA"] = P.sb("negA_%d" % l, [8, 1])
        P.act(d["negA"][:], al[:], AF.Exp)
        P.ts("dve", d["negA"][:], d["negA"][:], -1.0, ALU.mult)
        d["gn"] = P.sb("gn_%d" % l, [128, 1])
        small(d["gn"][0:64, :], I["a_norm_g"][l].rearrange("(h o) -> h o", o=1))
        small(d["gn"][64:128, :], I["a_norm_g"][l].rearrange("(h o) -> h o", o=1))
        d["sink"] = P.sb("sink_%d" % l, [128, 4])
        small(d["sink"][:], I["c_sinks"][l:l + 1, :].partition_broadcast(128).rearrange("p o h -> p (o h)"))
        d["lng"] = P.sb("lng_%d" % l, [128, 2])
        ptrans(d["lng"][:].unsqueeze(1), I["d_ln_g"][l].rearrange("(c p) -> c p", p=128), 2, 1)
        d["lnb"] = P.sb("lnb_%d" % l, [128, 2])
        ptrans(d["lnb"][:].unsqueeze(1), I["d_ln_b"][l].rearrange("(c p) -> c p", p=128), 2, 1)
        d["WT"] = P.sb("WT_%d" % l, [128, 4, 128], BF16)
        wtmp = wtmp_sh
        P.dma("sp", wtmp[:], I["d_ws"][l].rearrange("g t s -> t g s"))
        for g in range(4):
            P.tr(pb[6][:, g * 128:(g + 1) * 128], wtmp[:, g, :], ident[:], inc=(g == 3))
        P.tt("dve", d["WT"][:], pb[6][:].rearrange("p (g t) -> p g t", g=4),
             U[:].unsqueeze(1).to_broadcast([128, 4, 128]), ALU.mult)
        brow = brow_sh
        P.dma("sp", brow[:], I["d_bias"][l:l + 1].rearrange("o g t -> o (g t)"))
        d["brow"] = P.sb("browb_%d" % l, [1, 512], BF16)
        P.copy("dve", d["brow"][:], brow[:])
        d["S"] = [P.sb("S_%d_%d" % (l, jj), [128, 64]) for jj in range(2)]
        d["cA"] = P.sb("cA_%d" % l, [128, 6, 3])
        d["cB"] = P.sb("cB_%d" % l, [128, 2, 2])
        d["cF"] = P.sb("cF_%d" % l, [128, FC, 2])
        d["kprev"] = P.sb("kprev_%d" % l, [128, 128], BF16)
        d["vprev"] = P.sb("vprev_%d" % l, [128, 128], BF16)
        prm.append(d)
    gfin = P.sb("gfin", [128, 8])
    ptrans(gfin[:].unsqueeze(1), I["final_norm_g"].rearrange("(k p) -> k p", p=128), 8, 1)
    onesrow = onesb

    xs_t = P.sb("xs_t", [128, 8, TP])
    hT = P.sb("hT", [128, 8, TP], BF16)
    mixT = P.sb("mixT", [128, 8, TP], BF16)
    sqb = P.sb("sqb", [128, 4, TP], BF16)
    rstd = P.sb("rstd", [128, TP])
    S1 = P.sb("S1", [128, 6 * (TP + 3 * 16)])
    S2 = S2flat[:, 0:8 * TP].rearrange("p (c t) -> p c t", c=8)
    p8 = P.sb("p8", [8, TP])
    sig8 = P.sb("sig8", [8, TP])
    g8 = P.sb("g8", [8, TP])
    gc8 = P.sb("gc8", [8, TP])
    actb = S2flat[:].bitcast(BF16).rearrange("p (j t) -> p j t", j=FC)
    sgb = P.sb("sgb", [128, TP])
    gpre = P.sb("gpre", [128, TP + 2 * 16])
    gcv = P.sb("gcv", [128, TP])
    xin = P.sb("xio", [128, D])
    yout = xin
    NRING = 5
    wring = [P.sb("wr%d" % i, [128, 8, 512], BF16) for i in range(NRING)]
    wpos = {"w": 0, "d": 0}
    tA = P.sb("tA", [128, TP])
    tB = P.sb("tB", [128, TP])
    tC = gcv
    cosT = sgb
    sinT = gpre
    qTb = P.sb("qTb", [128, 2, TP], BF16)
    kTb = P.sb("kTb", [128, TP], BF16)
    krf = P.sb("krf", [128, TP])
    def T3(name, dt=F32):
        return P.sb(name, [128, 4, 128], dt)
    TH = []
    for jj in range(2):
        t_ = {}
        for nm in ("Eu", "QKd", "Mm0", "Mm1", "Ma0", "Ma1", "Xt"):
            t_[nm] = P.sb("%s_%d" % (nm, jj), [128, 2, 128])
        for nm in ("vb", "kbd", "kdec", "vnew"):
            t_[nm] = P.sb("%s_%d" % (nm, jj), [128, 2, 64])
        t_["nwT"] = P.sb("nwT_%d" % jj, [128, 128], BF16)
        t_["qdec"] = P.sb("qdec_%d" % jj, [128, 128], BF16)
        t_["eG2"] = P.sb("eG2_%d" % jj, [128, 128])
        t_["egl2"] = P.sb("egl2_%d" % jj, [128, 1])
        t_["x8"] = P.sb("x8_%d" % jj, [128, 16])
        t_["bege"] = P.sb("bege_%d" % jj, [128, 2])
        t_["egc"] = P.sb("egc_%d" % jj, [128, 2])
        t_["osq"] = P.sb("osq_%d" % jj, [128, 128], BF16)
        t_["orn"] = P.sb("orn_%d" % jj, [128, 128])
        t_["on_"] = P.sb("on__%d" % jj, [128, 128])
        t_["Sb"] = P.sb("Sb_%d" % jj, [128, 64], BF16)
        TH.append(t_)
    pbv = [pb[i][:, :] for i in range(7)] + [pT16[:, :].bitcast(F32)]
    x8 = P.sb("x8", [128, 16])
    zs = P.sb("zs", [128, 2, TP])
    sc = P.sb("sc", [128, 4, 256]); Pf = sc; Pn = P.sb("Pn", [128, 4, 256], BF16)
    PT = P.sb("PT", [128, 4, 2, 128], BF16)
    mx = P.sb("mx", [128, 4]); nmx = P.sb("nmx", [128, 4]); sm = P.sb("sm", [128, 4]); es = P.sb("es", [128, 4])
    vtokb = P.sb("vtokb", [128, 128], BF16); vtokf = P.sb("vtokf", [128, 128])
    kcache = P.sb("kcache", [128, 128]); vcache = P.sb("vcache", [128, 128])
    dvT = P.sb("dvT", [128, 256], BF16)
    stT = P.sb("stT", [64, 512])
    stF = P.sb("stF", [128, FC, 48])
    ktok = P.sb("ktok", [128, 128])

    evt = {"i": 0}

    def evac(out, in_, scale=None):
        evt["i"] ^= 1
        if scale is not None:
            if evt["i"]:
                P.act(out, in_, AF.Copy, scale=float(scale))
            else:
                P.ts("dve", out, in_, float(scale), ALU.mult)
        elif evt["i"]:
            P.copy("act", out, in_)
        else:
            P.copy("dve", out, in_)

    def wslab(src):
        t = wring[wpos["w"] % NRING]
        wpos["w"] += 1
        ncol = src.shape[1]
        P.dma("pool", t[:, :, 0:ncol], src.rearrange("(k p) n -> p k n", p=128))
        return t

    def bigmm(ps, slab, c0, ncols, rhs, T, m0=0):
        for k in range(KC):
            P.mm(ps[m0:m0 + ncols, 0:T], slab[:, k, c0:c0 + ncols], rhs[:, k, 0:T],
                 start=(k == 0), stop=(k == KC - 1), inc=(k == KC - 1))

    bigi = {"i": 0}

    def bigps():
        bigi["i"] ^= 1
        return pb[bigi["i"]]

    def rmsnorm(g, T, out_bf=True, out=None):
        ps = pb[2]
        for hf in range(2):
            P.act(sqb[:, :, 0:T], xs_t[:, hf * 4:hf * 4 + 4, 0:T], AF.Square)
            for k in range(4):
                P.mm(ps[:, 0:T], onesb[:], sqb[:, k, 0:T], start=(hf == 0 and k == 0), stop=(hf == 1 and k == 3),
                     inc=(k == 3))
        P.act(rstd[:, 0:T], ps[:, 0:T], AF.Sqrt, bias=EPS, scale=1.0 / D)
        P.recip(rstd[:, 0:T], rstd[:, 0:T])
        dst = hT if out is None else out
        for k in range(KC):
            P.stt(dst[:, k, 0:T], xs_t[:, k, 0:T], g[:, k:k + 1], rstd[:, 0:T], ALU.mult, ALU.mult)

    def conv(out3, buf3, w, c, K, seglen, nseg):
        P.ts("dve", out3, buf3[:, :, 0:seglen], w[:, c, 0:1], ALU.mult)
        for j in range(1, K):
            P.stt(out3, buf3[:, :, j:j + seglen], w[:, c, j:j + 1], out3, ALU.mult, ALU.add)

    def store_rows(dst_rows, srcF, nrows, ncols_list):
        nchunk = len(ncols_list)
        for c0 in range(0, nchunk, 4):
            cn = min(4, nchunk - c0)
            for c in range(c0, c0 + cn):
                P.tr(pb[6][0:nrows, (c - c0) * 128:(c - c0 + 1) * 128], srcF(c), ident[:], inc=(c == c0 + cn - 1))
            evac(stT[0:nrows, 0:cn * 128], pb[6][0:nrows, 0:cn * 128])
            P.dma("sp", dst_rows[:, c0 * 128:(c0 + cn) * 128], stT[0:nrows, 0:cn * 128])

    def load_rows(src_rows, nrows, nchunk):
        for c0 in range(0, nchunk, 4):
            cn = min(4, nchunk - c0)
            P.dma("sp", stT[0:nrows, 0:cn * 128], src_rows[:, c0 * 128:(c0 + cn) * 128])
            for c in range(c0, c0 + cn):
                P.tr(pb[6][:, (c - c0) * nrows:(c - c0 + 1) * nrows], stT[0:nrows, (c - c0) * 128:(c - c0 + 1) * 128],
                     ident[0:nrows, 0:nrows], inc=(c == c0 + cn - 1))
            evac(stF[:, c0:c0 + cn, 0:nrows], pb[6][:, 0:cn * nrows].rearrange("p (c r) -> p c r", c=cn))

    def run_layer(l, T, n, nblk, nseg, seglen, prompt, first, last, tile_i):
        d = prm[l]
        hb = lambda ap2, m: ap2.unsqueeze(2).to_broadcast([ap2.shape[0], ap2.shape[1], m])

        def seg3(t2d, K1):
            return t2d[:, 0:nseg * (K1 + seglen)].rearrange("p (s t) -> p s t", s=nseg)

        def tok3(ap2d):
            return ap2d.rearrange("p (s t) -> p s t", s=nseg)

        stage(0)
        rmsnorm(d["g1"], T)
        w_in = I["w_in"][l]
        stage(1)

        pre = [seg3(S1[:, c * (TP + 48):(c + 1) * (TP + 48)], 3) for c in range(6)]
        sl0 = wslab(w_in[:, 0:512])
        sl1 = wslab(w_in[:, 512:1024])
        sl2 = wslab(w_in[:, 1024:1288])
        if prompt:
            for c in range(6):
                if first:
                    P.memset("dve", pre[c][:, :, 0:3], 0.0)
                else:
                    P.copy("dve", pre[c][:, 0, 0:3], d["cA"][:, c, :])
        else:
            load_rows(I["state_delta_conv"][l], 48, 6)
            for c in range(6):
                P.copy("dve", pre[c][:, :, 0:3], stF[:, c, 0:48].rearrange("p (b r) -> p b r", r=3))
        stage(1.1)
        for c in range(4):
            ps = bigps()
            bigmm(ps, sl0, c * 128, 128, hT, T)
            evac(pre[c][:, :, 3:3 + seglen], tok3(ps[:, 0:T]))
        stage(1.2)
        for c in range(2):
            ps = bigps()
            bigmm(ps, sl1, c * 128, 128, hT, T)
            evac(pre[4 + c][:, :, 3:3 + seglen], tok3(ps[:, 0:T]))
        for c in range(2):
            ps = bigps()
            bigmm(ps, sl1, 256 + c * 128, 128, hT, T)
            P.act(zs[:, c, 0:T], ps[:, 0:T], AF.Silu)
        stage(1.3)
        ps = bigps()
        bigmm(ps, sl2, 0, 8, hT, T)
        P.copy("dve", p8[:, 0:T], ps[0:8, 0:T])
        stage(1.4)
        if prompt:
            for c in range(6):
                P.copy("dve", d["cA"][:, c, :], pre[c][:, 0, seglen:seglen + 3])
            if last:
                store_rows(O["ac_p"][l], lambda c: d["cA"][:, c, :], 3, [128] * 6)
        else:
            for c in range(6):
                P.copy("dve", stF[:, c, 0:48].rearrange("p (b r) -> p b r", r=3), pre[c][:, :, seglen:seglen + 3])
            store_rows(O["ac_s"][l], lambda c: stF[:, c, 0:48], 48, [128] * 6)
        stage(1.5)
        for c in range(6):
            conv(tok3(S2[:, c, 0:T]), pre[c], d["wA"], c, 4, seglen, nseg)
        P.act(S2[:, 0:6, 0:T], S2[:, 0:6, 0:T], AF.Silu)
        P.act(sqb[:, 0:4, 0:T], S2[:, 0:4, 0:T], AF.Square)
        for c in range(4):
            ps = pb[2]
            P.mm(ps[:, 0:T], bonesb[:], sqb[:, c, 0:T])
            P.act(tA[:, 0:T], ps[:, 0:T], AF.Sqrt, bias=EPS)
            P.recip(tA[:, 0:T], tA[:, 0:T])
            if c < 2:
                P.stt(S2[:, c, 0:T], S2[:, c, 0:T], 0.125, tA[:, 0:T], ALU.mult, ALU.mult)
            else:
                P.tt("dve", S2[:, c, 0:T], S2[:, c, 0:T], tA[:, 0:T], ALU.mult)
        stage(1.7)
        P.act(sig8[:, 0:T], p8[:, 0:T], AF.Sigmoid)
        P.act(g8[:, 0:T], p8[:, 0:T], AF.Exp, bias=d["dtb"][:])
        P.act(g8[:, 0:T], g8[:, 0:T], AF.Ln, bias=1.0)
        P.ts("dve", g8[:, 0:T], g8[:, 0:T], d["negA"][:], ALU.mult)
        stage(1.8)
        for b in range(nblk):
            cs = slice(b * n, (b + 1) * n)
            P.mm(pb[5][0:n, 0:8], g8[:, cs], ident[0:8, 0:8])
            P.copy("dve", x8[0:n, 0:8], pb[5][0:n, 0:8])
            P.mm(pb[5][0:8, 16:16 + n], x8[0:n, 0:8], U[0:n, 0:n])
            P.copy("dve", gc8[:, cs], pb[5][0:8, 16:16 + n])
        nlev = int(np.log2(n)) - 1
        stage(2)
        bk = [(pbv[0], pbv[1], pbv[2], pbv[3]), (pbv[4], pbv[5], pbv[6], pbv[7])]

        def gdn_thread(j):
            qa, qb, qc, qd = bk[j]
            t = TH[j]
            Sj = d["S"][j]
            Eu, QKd, Xt = t["Eu"], t["QKd"], t["Xt"]
            Mm = [t["Mm0"], t["Mm1"]]
            Ma = [t["Ma0"], t["Ma1"]]
            tmpE, Eus = Ma[1], Mm[1]
            vb_, kbd, kdec, vnew = t["vb"], t["kbd"], t["kdec"], t["vnew"]
            nwT, qdec, eG2, egl2, x8t = t["nwT"], t["qdec"], t["eG2"], t["egl2"], t["x8"]
            bege, egc, osq, orn, on_, Sb = t["bege"], t["egc"], t["osq"], t["orn"], t["on_"], t["Sb"]
            q2 = lambda t3: t3[0:n, :, 0:n]
            v3 = lambda ap, w: ap.rearrange("p (e t) -> p e t", e=2)
            Ub = U[0:n, 0:n].unsqueeze(1).to_broadcast([n, 2, n])
            Usb = Us[0:n, 0:n].unsqueeze(1).to_broadcast([n, 2, n])
            Ib = ident[0:n, 0:n].unsqueeze(1).to_broadcast([n, 2, n])
            for b in range(nblk):
                cs = slice(b * n, (b + 1) * n)
                if not prompt:
                    for e in range(2):
                        P.dma("sp", Sj[e * 64:(e + 1) * 64, :], I["state_delta"][l, b, 2 * j + e])
                elif first and b == 0:
                    P.memset("dve", Sj[:], 0.0)
                P.copy("act", Sb[:], Sj[:])
                P.tr(qc[0:n, 0:128], S2[:, 2 + j, cs], ident[:], inc=False)
                P.tr(qc[0:n, 128:256], S2[:, 4 + j, cs], ident[:])
                kv_ps = qc[0:n, 0:256].rearrange("p (a e d) -> p a e d", a=2, e=2)
                P.mm(qb[0:n, 256:264], sig8[:, cs], ident[0:8, 0:8], inc=False)
                P.mm(qb[0:n, 264:272], gc8[:, cs], ident[0:8, 0:8])
                G_ps = v3(qd[:, 0:2 * n], n)
                for e in range(2):
                    h = 2 * j + e
                    P.mm(G_ps[:, e, :], selg[:, h * 128:(h + 1) * 128], gc8[:, cs], inc=(e == 1))
                G2_ps = qb[:, 272:272 + n]
                P.mm(G2_ps, sel2[:, j * 128:(j + 1) * 128], gc8[:, cs])
                yield
                P.copy("dve", x8t[0:n, :], qb[0:n, 256:272])
                bt = x8t[0:n, 2 * j:2 * j + 2]
                gct = x8t[0:n, 12 + 2 * j:14 + 2 * j]
                P.act(egc[0:n, :], gct, AF.Exp)
                P.tt("dve", bege[0:n, :], bt, egc[0:n, :], ALU.mult)
                P.act(eG2[:, 0:n], G2_ps, AF.Exp)
                P.copy("dve", egl2[:], eG2[:, n - 1:n])
                P.tt("dve", qdec[:, 0:n], S2[:, j, cs], eG2[:, 0:n], ALU.mult)
                P.tt("dve", q2(tmpE), G_ps[0:n], hb(gct, n), ALU.subtract)
                P.ts("dve", q2(tmpE), q2(tmpE), 0.0, ALU.min)
                P.act(q2(tmpE), q2(tmpE), AF.Exp)
                Bb_ps = v3(qd[0:n, 0:2 * n], n)
                for e in range(2):
                    h = 2 * j + e
                    P.mm(Bb_ps[:, e, :], selb[:, h * 128:h * 128 + n], sig8[:, cs], inc=(e == 1))
                km = [orn[:, 0:n], on_[:, 0:n]]
                for e in range(2):
                    P.ts("dve", km[e], S2[:, 2 + j, cs], bones[:, e * 64:e * 64 + 1], ALU.mult)
                KK_ps = v3(qa[0:n, 0:2 * n], n)
                QK_ps = v3(qb[0:n, 0:2 * n], n)
                for e in range(2):
                    P.mm(KK_ps[:, e, :], km[e], S2[:, 2 + j, cs], inc=(e == 1))
                for e in range(2):
                    P.mm(QK_ps[:, e, :], km[e], S2[:, j, cs], inc=(e == 1))
                yield
                P.tt("dve", q2(Eu), q2(tmpE), Ub, ALU.mult)
                P.tt("dve", q2(Eus), q2(tmpE), Usb, ALU.mult)
                P.tt("dve", q2(Eus), q2(Eus), Bb_ps, ALU.mult)
                P.tt("dve", q2(Mm[0]), KK_ps, q2(Eus), ALU.mult)
                P.tt("dve", q2(QKd), QK_ps, q2(Eu), ALU.mult)
                P.tt("dve", vb_[0:n], kv_ps[:, 1], hb(bt, 64), ALU.mult)
                P.tt("dve", kbd[0:n], kv_ps[:, 0], hb(bege[0:n, :], 64), ALU.mult)
                P.tt("dve", kdec[0:n], kv_ps[:, 0], Eu[0:n, :, n - 1:n].to_broadcast([n, 2, 64]), ALU.mult)
                At_ps = v3(qd[0:n, 0:2 * n], n)
                for e in range(2):
                    P.tr(At_ps[:, e, :], Mm[0][0:n, e, 0:n], ident[0:n, 0:n], inc=(e == 1))
                yield
                P.copy("act", q2(Ma[0]), At_ps)
                P.tt("dve", q2(Xt), Ib, q2(Mm[0]), ALU.subtract)
                for k in range(1, nlev + 1):
                    cur, prv = k % 2, (k - 1) % 2
                    Pa_ps = v3(qa[0:n, 0:2 * n], n)
                    for e in range(2):
                        P.mm(Pa_ps[:, e, :], Mm[prv][0:n, e, 0:n], Ma[prv][0:n, e, 0:n], inc=(e == 1))
                    if k < nlev:
                        Pm_ps = v3(qb[0:n, 0:2 * n], n)
                        for e in range(2):
                            P.mm(Pm_ps[:, e, :], Ma[prv][0:n, e, 0:n], Mm[prv][0:n, e, 0:n], inc=(e == 1))
                    yield
                    P.copy("act", q2(Ma[cur]), Pa_ps)
                    if k < nlev:
                        P.copy("dve", q2(Mm[cur]), Pm_ps)
                    X_ps = v3(qd[0:n, 0:2 * n], n)
                    for e in range(2):
                        P.mm(X_ps[:, e, :], Ma[cur][0:n, e, 0:n], Xt[0:n, e, 0:n], inc=(e == 1))
                    yield
                    P.tt("dve", q2(Xt), q2(Xt), X_ps, ALU.add)
                wT_ps = qc[:, 0:n]
                for e in range(2):
                    P.mm(wT_ps[e * 64:(e + 1) * 64, :], kbd[0:n, e, :], Xt[0:n, e, 0:n], inc=(e == 1))
                yield
                P.ts("dve", nwT[:, 0:n], wT_ps, -1.0, ALU.mult)
                vn_ps = qd[0:n, 0:128].rearrange("p (e d) -> p e d", e=2)
                for e in range(2):
                    pr = slice(e * 64, (e + 1) * 64)
                    P.mm(vn_ps[:, e, :], Xt[0:n, e, 0:n], vb_[0:n, e, :], start=True, stop=False, inc=False)
                    P.mm(vn_ps[:, e, :], nwT[pr, 0:n], Sb[pr, :], start=False, stop=True, inc=(e == 1))
                yield
                P.copy("act", vnew[0:n], vn_ps)
                o_ps = qc[:, 128:128 + n]
                for e in range(2):
                    pr = slice(e * 64, (e + 1) * 64)
                    P.mm(o_ps[pr, :], Sb[pr, :], qdec[pr, 0:n], start=True, stop=False, inc=False)
                    P.mm(o_ps[pr, :], vnew[0:n, e, :], QKd[0:n, e, 0:n], start=False, stop=True, inc=(e == 1))
                Su_ps = qb[:, 384:448]
                for e in range(2):
                    pr = slice(e * 64, (e + 1) * 64)
                    P.mm(Su_ps[pr, :], kdec[0:n, e, :], vnew[0:n, e, :], inc=(e == 1))
                yield
                P.stt(Sj[:], Sj[:], egl2[:, 0:1], Su_ps, ALU.mult, ALU.add)
                P.act(osq[:, 0:n], o_ps, AF.Square)
                ss_ps = qa[:, 256:256 + n]
                P.mm(ss_ps, bonesb[:], osq[:, 0:n])
                yield
                P.act(orn[:, 0:n], ss_ps, AF.Sqrt, bias=EPS, scale=1.0 / 64)
                P.recip(orn[:, 0:n], orn[:, 0:n])
                P.tt("dve", on_[:, 0:n], o_ps, orn[:, 0:n], ALU.mult)
                P.stt(mixT[:, j, cs], on_[:, 0:n], d["gn"][:], zs[:, j, cs], ALU.mult, ALU.mult)
                if not prompt:
                    for e in range(2):
                        P.dma("sp", O["sd_s"][l, b, 2 * j + e], Sj[e * 64:(e + 1) * 64, :])
            if prompt and last:
                for e in range(2):
                    P.dma("sp", O["sd_p"][l, 2 * j + e], Sj[e * 64:(e + 1) * 64, :])

        gens = [gdn_thread(0), gdn_thread(1)]
        alive = [True, True]
        while any(alive):
            for gi, g in enumerate(gens):
                if alive[gi]:
                    try:
                        next(g)
                    except StopIteration:
                        alive[gi] = False


        stage(3)
        sl3 = wslab(w_in[:, 1288:1800])
        preB = [seg3(S1[:, c * (TP + 48):(c + 1) * (TP + 48)], 2) for c in range(2)]
        if prompt:
            for c in range(2):
                if first:
                    P.memset("dve", preB[c][:, :, 0:2], 0.0)
                else:
                    P.copy("dve", preB[c][:, 0, 0:2], d["cB"][:, c, :])
        else:
            load_rows(I["state_shortconv"][l], 32, 2)
            for c in range(2):
                P.copy("dve", preB[c][:, :, 0:2], stF[:, c, 0:32].rearrange("p (b r) -> p b r", r=2))
        for c in range(2):
            ps = bigps()
            bigmm(ps, sl2, 8 + c * 128, 128, hT, T)
            evac(S2[:, c, 0:T], ps[:, 0:T])
        for c in range(2):
            ps = bigps()
            bigmm(ps, sl3, c * 128, 128, hT, T)
            evac(tA[:, 0:T], ps[:, 0:T])
            ps = bigps()
            bigmm(ps, sl3, 256 + c * 128, 128, hT, T)
            P.tt("dve", preB[c][:, :, 2:2 + seglen], tok3(tA[:, 0:T]), tok3(ps[:, 0:T]), ALU.mult)
        if prompt:
            for c in range(2):
                P.copy("dve", d["cB"][:, c, :], preB[c][:, 0, seglen:seglen + 2])
            if last:
                store_rows(O["bc_p"][l], lambda c: d["cB"][:, c, :], 2, [128] * 2)
        else:
            for c in range(2):
                P.copy("dve", stF[:, c, 0:32].rearrange("p (b r) -> p b r", r=2), preB[c][:, :, seglen:seglen + 2])
            store_rows(O["bc_s"][l], lambda c: stF[:, c, 0:32], 32, [128] * 2)
        for c in range(2):
            conv(tok3(tB[:, 0:T]), preB[c], d["wB"], c, 3, seglen, nseg)
            P.tt("dve", mixT[:, 2 + c, 0:T], S2[:, c, 0:T], tB[:, 0:T], ALU.mult)

        stage(4)
        sl4 = wslab(w_in[:, 1800:2312])
        if prompt:
            P.dma("sp", cosT[:, 0:T], C["cosp"][:, tile_i * TP:(tile_i + 1) * TP])
            P.dma("sp", sinT[:, 0:T], C["sinp"][:, tile_i * TP:(tile_i + 1) * TP])
        else:
            P.dma("sp", cosT[:, 0:T], C["coss"])
            P.dma("sp", sinT[:, 0:T], C["sins"])

        def rope(dst_bf, dst_f, src):
            P.mm(pb[2][:, 0:T], perm[:], src)
            P.tt("dve", tB[:, 0:T], src, cosT[:, 0:T], ALU.mult)
            P.tt("dve", tC[:, 0:T], pb[2][:, 0:T], sinT[:, 0:T], ALU.mult)
            if dst_f is not None:
                P.tt("dve", dst_f, tB[:, 0:T], tC[:, 0:T], ALU.add)
                P.copy("act", dst_bf, dst_f)
            else:
                P.tt("dve", dst_bf, tB[:, 0:T], tC[:, 0:T], ALU.add)

        for j in range(2):
            ps = bigps()
            for half in range(2):
                h = half * 2 + j
                bigmm(ps, sl4, h * 64, 64, hT, T, m0=half * 64)
            evac(tA[:, 0:T], ps[:, 0:T])
            rope(qTb[:, j, 0:T], None, tA[:, 0:T])
        ps = bigps()
        bigmm(ps, sl4, 256, 128, hT, T)
        evac(tA[:, 0:T], ps[:, 0:T])
        rope(kTb[:, 0:T], krf[:, 0:T], tA[:, 0:T])
        ps = bigps()
        bigmm(ps, sl4, 384, 128, hT, T)
        evac(S2[:, 7, 0:T], ps[:, 0:T])
        if not prompt:
            P.tr(pb[6][0:64, 0:128], krf[:, 0:64], ident[:])
            evac(ktok[0:64, :], pb[6][0:64, 0:128])
            P.tr(pb[6][0:64, 128:256], S2[:, 7, 0:64], ident[:])
            evac(vtokf[0:64, :], pb[6][0:64, 128:256])
            for b in range(16):
                P.dma("sp", O["ck_s"][l, b, 124:128, :], ktok[b * 4:(b + 1) * 4, :])
                P.dma("sp", O["cv_s"][l, b, 124:128, :], vtokf[b * 4:(b + 1) * 4, :])
            P.dma("sp", O["ck_s"][l, :, 0:124, :], I["cache_win_k"][l, :, 4:128, :])
            P.dma("sp", O["cv_s"][l, :, 0:124, :], I["cache_win_v"][l, :, 4:128, :])
        for b in range(nblk):
            cs = slice(b * n, (b + 1) * n)
            hasprev = (not prompt) or (not (first and b == 0))
            off = 0 if hasprev else 128
            W = 256 - off if prompt else 128 + n
            if prompt:
                kpv = d["kprev"]
                vpv = d["vprev"]
            else:
                P.dma("sp", kcache[:], I["cache_win_k"][l, b])
                P.dma("sp", vcache[:], I["cache_win_v"][l, b])
                P.tr(pb[6][:, 0:128], kcache[:], ident[:])
                evac(d["kprev"][:], pb[6][:, 0:128])
                P.copy("dve", d["vprev"][:], vcache[:])
                kpv = d["kprev"]
                vpv = d["vprev"]
            P.tr(pb[6][0:n, 256:384], S2[:, 7, cs], ident[:])
            P.copy("dve", vtokb[0:n, :], pb[6][0:n, 256:384])
            if prompt and last and b == nblk - 1:
                P.copy("act", vtokf[0:n, :], pb[6][0:n, 256:384])
                P.dma("sp", O["cv_p"][l], vtokf[:])
                P.tr(pb[6][:, 384:512], krf[:, cs], ident[:])
                evac(ktok[:], pb[6][:, 384:512])
                P.dma("sp", O["ck_p"][l], ktok[:])
            S_ps = [pb[0][0:n, :].rearrange("p (h k) -> p h k", h=2), pb[1][0:n, :].rearrange("p (h k) -> p h k", h=2)]
            for h in range(4):
                pr = slice((h // 2) * 64, (h // 2) * 64 + 64)
                dst = S_ps[h // 2][:, h % 2, :]
                if hasprev:
                    P.mm(dst[:, 0:128], qTb[pr, h % 2, cs], kpv[pr, :], inc=False)
                P.mm(dst[:, 128:128 + n], qTb[pr, h % 2, cs], kTb[pr, cs], inc=(h % 2 == 1))
            mk = amask[0:n, off:off + W].unsqueeze(1).to_broadcast([n, 2, W])
            for hh in range(2):
                P.tt("dve", sc[0:n, 2 * hh:2 * hh + 2, 0:W], S_ps[hh][:, :, off:off + W], mk, ALU.add)
            P.reduce(mx[0:n, :], sc[0:n, :, 0:W], ALU.max)
            P.ts("dve", nmx[0:n, :], mx[0:n, :], -0.125, ALU.mult)
            for h in range(4):
                P.act(Pf[0:n, h, 0:W], sc[0:n, h, 0:W], AF.Exp, bias=nmx[0:n, h:h + 1], scale=0.125,
                      accum_out=sm[0:n, h:h + 1])
            P.tt("dve", es[0:n, :], nmx[0:n, :], d["sink"][0:n, :], ALU.add)
            P.act(es[0:n, :], es[0:n, :], AF.Exp)
            P.tt("dve", es[0:n, :], es[0:n, :], sm[0:n, :], ALU.add)
            P.recip(es[0:n, :], es[0:n, :])
            P.tt("dve", Pn[0:n, :, 0:W], Pf[0:n, :, 0:W], hb(es[0:n, :], W), ALU.mult)
            PT_ps = pT16[:, :].rearrange("p (h s q) -> p h s q", h=4, s=2)
            for h in range(4):
                if hasprev:
                    P.tr(PT_ps[:, h, 0, 0:n], Pn[0:n, h, 0:128], identb[0:n, 0:n], inc=False)
                P.tr(PT_ps[0:n, h, 1, 0:n], Pn[0:n, h, 128 - off:128 - off + n], identb[0:n, 0:n], inc=(h == 3))
            if hasprev:
                P.copy("act", PT[:, :, 0, 0:n], PT_ps[:, :, 0, 0:n])
            P.copy("dve", PT[0:n, :, 1, 0:n], PT_ps[0:n, :, 1, 0:n])
            o_ps = pb[6][:, 0:2 * n].rearrange("p (j t) -> p j t", j=2)
            for h in range(4):
                kvs = slice((h // 2) * 64, (h // 2) * 64 + 64)
                dst = o_ps[(h % 2) * 64:(h % 2) * 64 + 64, h // 2, :]
                if hasprev:
                    P.mm(dst, vpv[:, kvs], PT[:, h, 0, 0:n], start=True, stop=False, inc=False)
                P.mm(dst, vtokb[0:n, kvs], PT[0:n, h, 1, 0:n], start=(not hasprev), stop=True, inc=(h == 3))
            evac(mixT[:, 4:6, cs], o_ps)
            if prompt:
                P.copy("dve", d["kprev"][:], kTb[:, cs])
                P.copy("act", d["vprev"][:], vtokb[:])

        stage(5)
        sl5 = wslab(w_in[:, 2312:2824])

        def gelu(dst, ps):
            P.act(tB[:, 0:T], ps, AF.Square)
            P.ts("dve", tB[:, 0:T], tB[:, 0:T], 0.044715, ALU.mult, 1.0, ALU.add)
            P.tt("dve", tB[:, 0:T], tB[:, 0:T], ps, ALU.mult)
            P.act(tB[:, 0:T], tB[:, 0:T], AF.Sigmoid, scale=1.5957691216057308)
            P.tt("dve", dst, tB[:, 0:T], ps, ALU.mult)

        for c in range(2):
            ps = bigps()
            bigmm(ps, sl5, c * 128, 128, hT, T)
            gelu(S2[:, c, 0:T], ps[:, 0:T])
        for c in range(2):
            ps = bigps()
            bigmm(ps, sl5, 256 + c * 128, 128, hT, T)
            gelu(S2[:, 2 + c, 0:T], ps[:, 0:T])
        P.copy("act", sqb[:, 0:2, 0:T], S2[:, 2:4, 0:T])
        P.act(sqb[:, 2:4, 0:T], S2[:, 2:4, 0:T], AF.Square)
        for c in range(2):
            P.mm(pb[2][:, 0:T], onesb[:], sqb[:, c, 0:T], start=(c == 0), stop=(c == 1), inc=(c == 1))
        for c in range(2):
            P.mm(pb[3][:, 0:T], onesb[:], sqb[:, 2 + c, 0:T], start=(c == 0), stop=(c == 1), inc=(c == 1))
        P.act(tA[:, 0:T], pb[2][:, 0:T], AF.Copy, scale=1.0 / 256)
        P.tt("dve", tB[:, 0:T], tA[:, 0:T], tA[:, 0:T], ALU.mult)
        P.stt(tB[:, 0:T], pb[3][:, 0:T], 1.0 / 256, tB[:, 0:T], ALU.mult, ALU.subtract)
        P.act(tB[:, 0:T], tB[:, 0:T], AF.Sqrt, bias=EPS)
        P.recip(tB[:, 0:T], tB[:, 0:T])
        for c in range(2):
            P.tt("dve", S2[:, 2 + c, 0:T], S2[:, 2 + c, 0:T], tA[:, 0:T], ALU.subtract)
            P.tt("dve", S2[:, 2 + c, 0:T], S2[:, 2 + c, 0:T], tB[:, 0:T], ALU.mult)
            P.ts("dve", S2[:, 2 + c, 0:T], S2[:, 2 + c, 0:T], d["lng"][:, c:c + 1], ALU.mult, d["lnb"][:, c:c + 1], ALU.add)
        if not prompt:
            store_rows(O["dv_s"][l], lambda c: S2[:, 2 + c, 0:T], 64, [128] * 2)
        for b in range(nblk):
            cs = slice(b * n, (b + 1) * n)
            for c in range(2):
                P.tr(pb[2][0:n, c * 128:(c + 1) * 128], S2[:, 2 + c, cs], ident[:], inc=(c == 1))
            evac(dvT[0:n, :], pb[2][0:n, 0:256])
            m_ps = pb[3][:, 0:2 * n].rearrange("p (j t) -> p j t", j=2)
            for g in range(4):
                dst = m_ps[(g % 2) * 64:(g % 2) * 64 + 64, g // 2, :]
                P.mm(dst, dvT[0:n, g * 64:(g + 1) * 64], d["WT"][0:n, g, 0:n], start=True, stop=False, inc=False)
                P.mm(dst, onesrow[0:1, 0:64], d["brow"][0:1, g * 128:g * 128 + n], start=False, stop=True, inc=(g == 3))
            P.tt("dve", mixT[:, 6:8, cs], S2[:, 0:2, cs], m_ps, ALU.mult)

        stage(6)
        for s in range(2):
            sl = wslab(I["w_out"][l][:, s * 512:(s + 1) * 512])
            for c in range(4):
                ps = bigps()
                bigmm(ps, sl, c * 128, 128, mixT, T)
                P.tt("dve", xs_t[:, s * 4 + c, 0:T], xs_t[:, s * 4 + c, 0:T], ps[:, 0:T], ALU.add)

        stage(7)
        rmsnorm(d["g2"], T)
        gp3 = seg3(gpre[:, :], 2)
        if not prompt:
            load_rows(I["state_ffn_conv"][l], 32, FC)
        for s in range(6):
            ncol = 512 if s < 5 else 256
            slg = wslab(I["ffn_w_gate"][l][:, s * 512:s * 512 + ncol])
            slu = wslab(I["ffn_w_up"][l][:, s * 512:s * 512 + ncol])
            for c in range(ncol // 128):
                j = s * 4 + c
                ps = bigps()
                bigmm(ps, slg, c * 128, 128, hT, T)
                if prompt:
                    if first:
                        P.memset("dve", gp3[:, :, 0:2], 0.0)
                    else:
                        P.copy("dve", gp3[:, 0, 0:2], d["cF"][:, j, :])
                else:
                    P.copy("dve", gp3[:, :, 0:2], stF[:, j, 0:32].rearrange("p (b r) -> p b r", r=2))
                P.copy("act", gp3[:, :, 2:2 + seglen], tok3(ps[:, 0:T]))
                if prompt:
                    P.copy("dve", d["cF"][:, j, :], gp3[:, 0, seglen:seglen + 2])
                else:
                    P.copy("dve", stF[:, j, 0:32].rearrange("p (b r) -> p b r", r=2), gp3[:, :, seglen:seglen + 2])
                conv(tok3(gcv[:, 0:T]), gp3, d["wF"], j, 3, seglen, nseg)
                P.act(sgb[:, 0:T], gcv[:, 0:T], AF.Silu)
                ps = bigps()
                bigmm(ps, slu, c * 128, 128, hT, T)
                P.tt("dve", actb[:, j, 0:T], sgb[:, 0:T], ps[:, 0:T], ALU.mult)
        stage(7.5)
        if prompt and last:
            store_rows(O["fc_p"][l], lambda c: d["cF"][:, c, :], 2, [128] * FC)
        if not prompt:
            store_rows(O["fc_s"][l], lambda c: stF[:, c, 0:32], 32, [128] * FC)
        wd = I["ffn_w_down"][l].rearrange("(j p) n -> p j n", p=128)
        acc = [pb[i][:, :] for i in range(7)] + [pT16[:, :].bitcast(F32)]
        for jg in range((FC + 3) // 4):
            nj = min(4, FC - 4 * jg)
            tw = wring[wpos["w"] % NRING]
            wpos["w"] += 1
            t = tw[:, :, :].rearrange("p k n -> p (k n)").rearrange("p (j n) -> p j n", j=4)
            P.dma("pool", t[:, 0:nj, :], wd[:, 4 * jg:4 * jg + nj, :])
            for jj in range(nj):
                j = 4 * jg + jj
                for oc in range(8):
                    P.mm(acc[oc][:, 0:T], t[:, jj, oc * 128:(oc + 1) * 128], actb[:, j, 0:T],
                         start=(j == 0), stop=(j == FC - 1), inc=(oc == 7 or j == FC - 1))
        for oc in range(8):
            P.tt("dve", xs_t[:, oc, 0:T], xs_t[:, oc, 0:T], acc[oc][:, 0:T], ALU.add)
        stage(8)

    def run_tile(prompt, ti):
        if prompt:
            T, n, nblk, nseg, seglen = TP, 128, 4, 1, TP
            src = I["xp"][ti * TP:(ti + 1) * TP, :]
            dst = O["y_p"][ti * TP:(ti + 1) * TP, :]
        else:
            T, n, nblk, nseg, seglen = TS, 4, 16, 16, 4
            src = I["xs"]
            dst = O["y_s"]
        nrb = (T + 127) // 128
        for rb in range(nrb):
            r = min(128, T - rb * 128)
            P.dma("sp", xin[0:r, :], src[rb * 128:rb * 128 + r, :])
            for g in range(2):
                for k in range(4):
                    P.tr(pb[6][:, k * r:(k + 1) * r], xin[0:r, (g * 4 + k) * 128:(g * 4 + k + 1) * 128],
                         ident[0:r, 0:r], inc=(k == 3))
                evac(xs_t[:, g * 4:g * 4 + 4, rb * 128:rb * 128 + r],
                     pb[6][:, 0:4 * r].rearrange("p (k t) -> p k t", k=4))
        for li, l in enumerate([int(c) for c in os.environ.get("MK_LAYERS", "01")]):
            lpos["i"] = li
            run_layer(l, T, n, nblk, nseg, seglen, prompt, first=(ti == 0), last=(ti == NTP - 1), tile_i=ti)
        rmsnorm(gfin, T, out=S2)
        for rb in range(nrb):
            r = min(128, T - rb * 128)
            for g in range(2):
                for k in range(4):
                    P.tr(pb[6][0:r, k * 128:(k + 1) * 128], S2[:, g * 4 + k, rb * 128:rb * 128 + r], ident[:],
                         inc=(k == 3))
                evac(yout[0:r, g * 512:(g + 1) * 512], pb[6][0:r, :])
            P.dma("sp", dst[rb * 128:rb * 128 + r, :], yout[0:r, :])

    ntiles = int(os.environ.get("MK_NTILES", NTP))
    try:
        for ti in range(ntiles):
            run_tile(True, ti)
        if os.environ.get("MK_NOSAMPLE") is None:
            run_tile(False, 0)
    except _Stop:
        P.dma("sp", O["y_s"][0:64, :], xs_t[0:64, 0, 0:TP].rearrange("p t -> p t") if False else xin[0:64, :])
    P.emit()
    P.close()
    return nc, P


_CACHE = {}


def kernel(**inp):
    f = np.float32
    if "nc" not in _CACHE:
        _CACHE["nc"], _ = build_program()
        _CACHE["consts"] = _consts()
    nc = _CACHE["nc"]
    consts = _CACHE["consts"]
    shared = {}
    for k in IN_SHAPES:
        if k in ("xp", "xs", "state_delta", "state_delta_conv", "state_shortconv", "cache_win_k", "cache_win_v",
                 "state_ffn_conv"):
            continue
        shared[k] = np.ascontiguousarray(np.asarray(inp[k], f))
    for k, v in consts.items():
        shared["c_" + k] = np.ascontiguousarray(v.reshape(CONST_SHAPES[k]).astype(f))
    in_maps = []
    for c in range(8):
        m = dict(shared)
        bs = slice(16 * c, 16 * c + 16)
        m["xp"] = np.ascontiguousarray(np.asarray(inp["x_prompt"][c], f))
        m["xs"] = np.ascontiguousarray(np.asarray(inp["x_sample"][bs], f).reshape(64, D))
        m["state_delta"] = np.ascontiguousarray(np.asarray(inp["state_delta"][:, bs], f))
        m["state_delta_conv"] = np.ascontiguousarray(np.asarray(inp["state_delta_conv"][:, bs], f).reshape(2, 48, 768))
        m["state_shortconv"] = np.ascontiguousarray(np.asarray(inp["state_shortconv"][:, bs], f).reshape(2, 32, 256))
        m["cache_win_k"] = np.ascontiguousarray(np.asarray(inp["cache_win_k"][:, bs], f).reshape(2, 16, 128, 128))
        m["cache_win_v"] = np.ascontiguousarray(np.asarray(inp["cache_win_v"][:, bs], f).reshape(2, 16, 128, 128))
        m["state_ffn_conv"] = np.ascontiguousarray(np.asarray(inp["state_ffn_conv"][:, bs], f).reshape(2, 32, DFF))
        in_maps.append(m)
    res = run_bass_kernel_spmd(nc, in_maps, core_ids=list(range(8)))
    R = res.results

    def cat_p(k, shape):
        return np.stack([np.asarray(R[c][k], f).reshape(shape) for c in range(8)], axis=1)

    def cat_s(k, shape):
        return np.concatenate([np.asarray(R[c][k], f).reshape(shape) for c in range(8)], axis=1)

    y_p = np.stack([np.asarray(R[c]["y_p"], f) for c in range(8)], axis=0)
    y_s = np.concatenate([np.asarray(R[c]["y_s"], f).reshape(16, 4, D) for c in range(8)], axis=0)
    return (
        y_p, y_s,
        cat_p("sd_p", (2, 4, 64, 64)), cat_s("sd_s", (2, 16, 4, 64, 64)),
        cat_p("ac_p", (2, 3, 768)), cat_s("ac_s", (2, 16, 3, 768)),
        cat_p("bc_p", (2, 2, 256)), cat_s("bc_s", (2, 16, 2, 256)),
        cat_p("ck_p", (2, 128, 2, 64)), cat_s("ck_s", (2, 16, 128, 2, 64)),
        cat_p("cv_p", (2, 128, 2, 64)), cat_s("cv_s", (2, 16, 128, 2, 64)),
        cat_p("fc_p", (2, 2, DFF)), cat_s("fc_s", (2, 16, 2, DFF)),
        cat_s("dv_s", (2, 16, 4, 256)),
    )
```

```python
import numpy as np
import concourse.bass as bass
import concourse.mybir as mybir

F32 = mybir.dt.float32
BF16 = mybir.dt.bfloat16
I32 = mybir.dt.int32
ALU = mybir.AluOpType
AF = mybir.ActivationFunctionType
AX = mybir.AxisListType

ENGINES = ("pe", "act", "dve", "pool", "sp")


class _Buf:
    __slots__ = ("last_w", "readers")

    def __init__(self):
        self.last_w = None
        self.readers = []


class Prog:
    def __init__(self, nc, ring=12):
        self.nc = nc
        self.ops = {e: [] for e in ENGINES}
        self.cnt = {e: 0 for e in ENGINES}
        self.known = {e: {} for e in ENGINES}
        self.bufs = {}
        self.stack = []
        self.sems = {}
        self.nsem = 0
        self.ring_n = ring
        self.ring = {}
        self.ring_pos = {}
        self.untracked = set()
        self.nops = 0

    def _sem(self, key):
        if key not in self.sems:
            cm = self.nc.semaphore("s_" + key)
            self.sems[key] = cm.__enter__()
            self.stack.append(cm)
            self.nsem += 1
        return self.sems[key]

    def sb(self, name, shape, dtype=F32):
        cm = self.nc.sbuf_tensor(name, list(shape), dtype)
        t = cm.__enter__()
        self.stack.append(cm)
        return t

    def ps(self, name, shape, dtype=F32):
        cm = self.nc.psum_tensor(name, list(shape), dtype)
        t = cm.__enter__()
        self.stack.append(cm)
        return t

    def dram(self, name, shape, dtype=F32, kind="Internal"):
        return self.nc.dram_tensor(name, list(shape), dtype, kind=kind)

    def _key(self, ap):
        if isinstance(ap, str):
            return ap
        return ap.tensor.name

    def _deps(self, eng, reads, writes):
        waits = []
        known = self.known[eng]

        def need(ev):
            if ev is None:
                return
            sk, val, kn = ev
            if eng == "pe" and sk == "pe":
                return
            if known.get(sk, 0) >= val:
                return
            waits.append((sk, val))
            known[sk] = val
            for k, v in kn.items():
                if known.get(k, 0) < v:
                    known[k] = v

        rk = [self._key(a) for a in reads]
        wk = [self._key(a) for a in writes]
        rk = [k for k in rk if k not in self.untracked]
        wk = [k for k in wk if k not in self.untracked]
        for k in rk:
            b = self.bufs.setdefault(k, _Buf())
            need(b.last_w)
        for k in wk:
            b = self.bufs.setdefault(k, _Buf())
            need(b.last_w)
            for r in b.readers:
                need(r)
        return waits, rk, wk

    def _commit(self, ev, rk, wk):
        for k in rk:
            if k not in wk:
                self.bufs[k].readers.append(ev)
        for k in wk:
            b = self.bufs[k]
            b.last_w = ev
            b.readers = []

    def op(self, eng, fn, reads, writes, inc=True):
        waits, rk, wk = self._deps(eng, reads, writes)
        ev = (eng, self.cnt[eng] + 1, dict(self.known[eng]))
        if inc:
            self.cnt[eng] += 1
        self.ops[eng].append((waits, fn, (eng, 1) if inc else None))
        self._commit(ev, rk, wk)
        self.nops += 1

    def dma(self, eng, out, in_, extra_reads=(), extra_writes=(), **kw):
        if eng not in self.ring:
            self.ring[eng] = [["d%s%d" % (eng, i), 0] for i in range(self.ring_n)]
            self.ring_pos[eng] = 0
        slot = self.ring[eng][self.ring_pos[eng]]
        self.ring_pos[eng] = (self.ring_pos[eng] + 1) % self.ring_n
        waits, rk, wk = self._deps(eng, [in_] + list(extra_reads), [out] + list(extra_writes))
        known = self.known[eng]
        if slot[1] > 0 and known.get(slot[0], 0) < 16 * slot[1]:
            waits.append((slot[0], 16 * slot[1]))
            known[slot[0]] = 16 * slot[1]
        slot[1] += 1
        ev = (slot[0], 16 * slot[1], dict(known))

        def fn(e, out=out, in_=in_, kw=kw):
            return e.dma_start(out=out, in_=in_, **kw)

        self.ops[eng].append((waits, fn, (slot[0], 16)))
        self._commit(ev, rk, wk)
        self.nops += 1

    def mm(self, out, lhsT, rhs, start=True, stop=True, inc=True, **kw):
        self.op("pe", lambda e: e.matmul(out, lhsT, rhs, start=start, stop=stop, **kw),
                [lhsT, rhs] + ([] if start else [out]), [out], inc=inc)

    def tr(self, out, in_, ident, inc=True):
        self.op("pe", lambda e: e.transpose(out, in_, ident), [in_, ident], [out], inc=inc)

    def act(self, out, in_, func, bias=None, scale=None, accum_out=None, eng="act"):
        kw = {}
        reads = [in_]
        writes = [out]
        if bias is not None:
            kw["bias"] = bias
            if not isinstance(bias, (int, float)):
                reads.append(bias)
        if scale is not None:
            kw["scale"] = scale
            if not isinstance(scale, (int, float)):
                reads.append(scale)
        if accum_out is not None:
            kw["accum_out"] = accum_out
            writes.append(accum_out)
        self.op("act", lambda e: e.activation(out, in_, func, **kw), reads, writes)

    def tt(self, eng, out, in0, in1, op):
        self.op(eng, lambda e: e.tensor_tensor(out, in0, in1, op), [in0, in1], [out])

    def ts(self, eng, out, in0, s1, op0, s2=None, op1=None, accum_out=None):
        reads = [in0]
        if not isinstance(s1, (int, float)):
            reads.append(s1)
        if s2 is not None and not isinstance(s2, (int, float)):
            reads.append(s2)
        writes = [out]
        kw = {}
        if op1 is not None:
            kw["op1"] = op1
        if accum_out is not None:
            kw["accum_out"] = accum_out
            writes.append(accum_out)
        self.op(eng, lambda e: e.tensor_scalar(out, in0, s1, s2, op0, **kw), reads, writes)

    def stt(self, out, in0, scalar, in1, op0, op1, eng="dve"):
        reads = [in0, in1]
        if not isinstance(scalar, (int, float)):
            reads.append(scalar)
        self.op(eng, lambda e: e.scalar_tensor_tensor(out, in0, scalar, in1, op0, op1), reads, [out])

    def copy(self, eng, out, in_):
        if eng == "act":
            self.op("act", lambda e: e.copy(out, in_), [in_], [out])
        else:
            self.op(eng, lambda e: e.tensor_copy(out, in_), [in_], [out])

    def memset(self, eng, out, val):
        self.op(eng, lambda e: e.memset(out, val), [], [out])

    def reduce(self, out, in_, op, axis=AX.X, eng="dve"):
        self.op(eng, lambda e: e.tensor_reduce(out, in_, axis, op), [in_], [out])

    def recip(self, out, in_):
        self.op("dve", lambda e: e.reciprocal(out, in_), [in_], [out])

    def emit(self):
        nc = self.nc
        fin = []
        for eng, ring in self.ring.items():
            for key, uses in ring:
                if uses > 0:
                    fin.append((key, 16 * uses))
        for e in ("pe", "act", "dve", "pool"):
            if self.cnt[e] > 0:
                fin.append((e, self.cnt[e]))
        for k in list(self.sems) + [f[0] for f in fin]:
            self._sem(k)
        for eng in ENGINES:
            for waits, fn, inc in self.ops[eng]:
                for sk, v in waits:
                    self._sem(sk)
                if inc is not None:
                    self._sem(inc[0])
        sems = self.sems
        ops = self.ops

        def replay(e, lst, final=False):
            for waits, fn, inc in lst:
                for sk, v in waits:
                    e.wait_ge(sems[sk], v)
                ins = fn(e)
                if inc is not None:
                    ins.then_inc(sems[inc[0]], inc[1])
            if final:
                for sk, v in fin:
                    e.wait_ge(sems[sk], v)

        with nc.Block() as block:
            @block.sync
            def _(e):
                replay(e, ops["sp"], final=True)

            @block.tensor
            def _(e):
                replay(e, ops["pe"])

            @block.scalar
            def _(e):
                replay(e, ops["act"])

            @block.vector
            def _(e):
                replay(e, ops["dve"])

            @block.gpsimd
            def _(e):
                replay(e, ops["pool"])

    def close(self):
        while self.stack:
            self.stack.pop().__exit__(None, None, None)


from concourse.bass_utils import run_bass_kernel_spmd

D = 1024
KC = 8
PTOT = 2824
DFF = 2816
FC = 22
EPS = 1e-6
PAST = 16384
NEG = -30000.0
TP = 512
NTP = 8
TS = 64


def _consts():
    f = np.float32
    c = {}
    c["ident"] = np.eye(128, dtype=f)
    s = np.arange(128)
    c["U"] = (s[:, None] <= s[None, :]).astype(f)
    c["Us"] = (s[:, None] < s[None, :]).astype(f)
    c["ones"] = np.ones((128, 128), f)
    bo = np.zeros((128, 128), f)
    bo[:64, :64] = 1
    bo[64:, 64:] = 1
    c["bones"] = bo
    sel = np.zeros((8, 4, 128), f)
    for h in range(4):
        sel[4 + h, h, :] = 1
    c["selg"] = sel.reshape(8, 512)
    selb = np.zeros((8, 4, 128), f)
    for h in range(4):
        selb[h, h, :] = 1
    c["selb"] = selb.reshape(8, 512)
    sel2 = np.zeros((8, 2, 128), f)
    for j in range(2):
        sel2[4 + 2 * j, j, :64] = 1
        sel2[4 + 2 * j + 1, j, 64:] = 1
    c["sel2"] = sel2.reshape(8, 256)
    perm = np.zeros((128, 128), f)
    for half in range(2):
        for d in range(8):
            r = half * 64 + d
            perm[r + 8, r] = -1.0
            perm[r, r + 8] = 1.0
    c["perm"] = perm
    inv = np.power(np.float32(500000.0), -np.arange(8, dtype=f) * f(2.0 / 16)).astype(f)

    def cs(pos):
        ang = pos.astype(f)[None, :] * inv[:, None]
        C = np.ones((128, pos.shape[0]), f)
        S = np.zeros((128, pos.shape[0]), f)
        for half in range(2):
            for q in range(2):
                r0 = half * 64 + q * 8
                C[r0:r0 + 8] = np.cos(ang)
                S[r0:r0 + 8] = np.sin(ang)
        return C, S

    c["cosp"], c["sinp"] = cs(np.arange(4096))
    c["coss"], c["sins"] = cs(np.tile(PAST + np.arange(4), 16))
    i = np.arange(128)[:, None]
    j = np.arange(256)[None, :]
    c["amask"] = np.where((j > i) & (j <= i + 128), 0.0, NEG).astype(f)
    rp = np.ones((8, TP), f)
    rp[:, ::128] = 0
    c["rstp"] = rp
    rs = np.ones((8, TS), f)
    rs[:, ::4] = 0
    c["rsts"] = rs
    return c


CONST_SHAPES = {"ident": (128, 128), "U": (128, 128), "Us": (128, 128), "ones": (128, 128), "bones": (128, 128),
                "selg": (8, 512), "selb": (8, 512), "sel2": (8, 256), "perm": (128, 128),
                "cosp": (128, 4096), "sinp": (128, 4096), "coss": (128, 64), "sins": (128, 64),
                "amask": (128, 256), "rstp": (8, TP), "rsts": (8, TS)}

IN_SHAPES = {
    "xp": (4096, D), "xs": (64, D),
    "state_delta": (2, 16, 4, 64, 64), "state_delta_conv": (2, 48, 768), "state_shortconv": (2, 32, 256),
    "cache_win_k": (2, 16, 128, 128), "cache_win_v": (2, 16, 128, 128), "state_ffn_conv": (2, 32, DFF),
    "norm1_g": (2, D), "w_in": (2, D, PTOT), "a_conv_w": (2, 4, 768), "a_log": (2, 4), "a_dt_bias": (2, 4),
    "a_norm_g": (2, 64), "b_conv_w": (2, 3, 256), "c_sinks": (2, 4), "d_ln_g": (2, 256), "d_ln_b": (2, 256),
    "d_ws": (2, 4, 128, 128), "d_bias": (2, 4, 128), "w_out": (2, D, D), "norm2_g": (2, D),
    "ffn_w_gate": (2, D, DFF), "ffn_w_up": (2, D, DFF), "ffn_conv_w": (2, 3, DFF), "ffn_w_down": (2, DFF, D),
    "final_norm_g": (D,),
}
OUT_SHAPES = {
    "y_p": (4096, D), "y_s": (64, D), "sd_p": (2, 4, 64, 64), "sd_s": (2, 16, 4, 64, 64),
    "ac_p": (2, 3, 768), "ac_s": (2, 48, 768), "bc_p": (2, 2, 256), "bc_s": (2, 32, 256),
    "ck_p": (2, 128, 128), "ck_s": (2, 16, 128, 128), "cv_p": (2, 128, 128), "cv_s": (2, 16, 128, 128),
    "fc_p": (2, 2, DFF), "fc_s": (2, 32, DFF), "dv_s": (2, 64, 256),
}


def build_program():
    nc = bass.Bass("TRN2", target_bir_lowering=False)
    P = Prog(nc, ring=12)
    I = {k: nc.dram_tensor(k, list(s), F32, kind="ExternalInput").ap() for k, s in IN_SHAPES.items()}
    C = {k: nc.dram_tensor("c_" + k, list(s), F32, kind="ExternalInput").ap() for k, s in CONST_SHAPES.items()}
    O = {k: nc.dram_tensor(k, list(s), F32, kind="ExternalOutput").ap() for k, s in OUT_SHAPES.items()}
    for k in list(IN_SHAPES) + ["c_" + k for k in CONST_SHAPES] + list(OUT_SHAPES):
        P.untracked.add(k)
    NC_ = {"ns": 0}

    def small(out, in_):
        P.dma("sp", out, in_, allow_slow_non_contiguous=True)

    def cload(name, shape, src, bf=False):
        t = P.sb("k_" + name, shape)
        P.dma("sp", t[:], src)
        if bf:
            tb = P.sb("kb_" + name, shape, BF16)
            P.copy("dve", tb[:], t[:])
            return t, tb
        return t

    ident, identb = cload("ident", [128, 128], C["ident"], True)
    U = cload("U", [128, 128], C["U"])
    Us = cload("Us", [128, 128], C["Us"])
    ones, onesb = cload("ones", [128, 128], C["ones"], True)
    bones, bonesb = cload("bones", [128, 128], C["bones"], True)
    selg = cload("selg", [8, 512], C["selg"])
    selb = cload("selb", [8, 512], C["selb"])
    sel2 = cload("sel2", [8, 256], C["sel2"])
    perm = cload("perm", [128, 128], C["perm"])
    amask = cload("amask", [128, 256], C["amask"])

    pb = [P.ps("pb%d" % i, [128, 512]) for i in range(7)]
    pT16 = P.ps("pT16", [128, 1024], BF16)

    S2flat = P.sb("S2", [128, 11 * TP])

    def ptrans(dst3, src_rows, R, Cn):
        P.dma("sp", S2flat[0:R, 0:Cn * 128], src_rows)
        for c in range(Cn):
            P.tr(pb[6][:, c * R:(c + 1) * R], S2flat[0:R, c * 128:(c + 1) * 128], ident[0:R, 0:R], inc=(c == Cn - 1))
        P.copy("dve", dst3, pb[6][:, 0:Cn * R].rearrange("p (c r) -> p c r", c=Cn))

    class _Stop(Exception):
        pass

    import os
    stop_at = float(os.environ.get("MK_STOP", "99"))

    lpos = {"i": 0}

    def stage(k):
        if lpos["i"] * 10 + k >= stop_at:
            raise _Stop()

    prm = []
    wtmp_sh = S2flat[:, 0:512].rearrange("p (g s) -> p g s", g=4)
    brow_sh = S2flat[0:1, 512:1024]
    for l in range(2):
        d = {}
        d["g1"] = P.sb("g1_%d" % l, [128, 8])
        ptrans(d["g1"][:].unsqueeze(1), I["norm1_g"][l].rearrange("(k p) -> k p", p=128), 8, 1)
        d["g2"] = P.sb("g2_%d" % l, [128, 8])
        ptrans(d["g2"][:].unsqueeze(1), I["norm2_g"][l].rearrange("(k p) -> k p", p=128), 8, 1)
        d["wA"] = P.sb("wA_%d" % l, [128, 6, 4])
        ptrans(d["wA"][:], I["a_conv_w"][l], 4, 6)
        d["wB"] = P.sb("wB_%d" % l, [128, 2, 3])
        ptrans(d["wB"][:], I["b_conv_w"][l], 3, 2)
        d["wF"] = P.sb("wF_%d" % l, [128, FC, 3])
        ptrans(d["wF"][:], I["ffn_conv_w"][l], 3, FC)
        d["dtb"] = P.sb("dtb_%d" % l, [8, 1])
        P.memset("dve", d["dtb"][:], 0.0)
        small(d["dtb"][4:8, :], I["a_dt_bias"][l].rearrange("(h o) -> h o", o=1))
        al = P.sb("al_%d" % l, [8, 1])
        P.memset("dve", al[:], 0.0)
        small(al[4:8, :], I["a_log"][l].rearrange("(h o) -> h o", o=1))
        d["negA"] = P.sb("negA_%d" % l, [8, 1])
        P.act(d["negA"][:], al[:], AF.Exp)
        P.ts("dve", d["negA"][:], d["negA"][:], -1.0, ALU.mult)
        d["gn"] = P.sb("gn_%d" % l, [128, 1])
        small(d["gn"][0:64, :], I["a_norm_g"][l].rearrange("(h o) -> h o", o=1))
        small(d["gn"][64:128, :], I["a_norm_g"][l].rearrange("(h o) -> h o", o=1))
        d["sink"] = P.sb("sink_%d" % l, [128, 4])
        small(d["sink"][:], I["c_sinks"][l:l + 1, :].partition_broadcast(128).rearrange("p o h -> p (o h)"))
        d["lng"] = P.sb("lng_%d" % l, [128, 2])
        ptrans(d["lng"][:].unsqueeze(1), I["d_ln_g"][l].rearrange("(c p) -> c p", p=128), 2, 1)
        d["lnb"] = P.sb("lnb_%d" % l, [128, 2])
        ptrans(d["lnb"][:].unsqueeze(1), I["d_ln_b"][l].rearrange("(c p) -> c p", p=128), 2, 1)
        d["WT"] = P.sb("WT_%d" % l, [128, 4, 128], BF16)
        wtmp = wtmp_sh
        P.dma("sp", wtmp[:], I["d_ws"][l].rearrange("g t s -> t g s"))
        for g in range(4):
            P.tr(pb[6][:, g * 128:(g + 1) * 128], wtmp[:, g, :], ident[:], inc=(g == 3))
        P.tt("dve", d["WT"][:], pb[6][:].rearrange("p (g t) -> p g t", g=4),
             U[:].unsqueeze(1).to_broadcast([128, 4, 128]), ALU.mult)
        brow = brow_sh
        P.dma("sp", brow[:], I["d_bias"][l:l + 1].rearrange("o g t -> o (g t)"))
        d["brow"] = P.sb("browb_%d" % l, [1, 512], BF16)
        P.copy("dve", d["brow"][:], brow[:])
        d["S"] = [P.sb("S_%d_%d" % (l, jj), [128, 64]) for jj in range(2)]
        d["cA"] = P.sb("cA_%d" % l, [128, 6, 3])
        d["cB"] = P.sb("cB_%d" % l, [128, 2, 2])
        d["cF"] = P.sb("cF_%d" % l, [128, FC, 2])
        d["kprev"] = P.sb("kprev_%d" % l, [128, 128], BF16)
        d["vprev"] = P.sb("vprev_%d" % l, [128, 128], BF16)
        prm.append(d)
    gfin = P.sb("gfin", [128, 8])
    ptrans(gfin[:].unsqueeze(1), I["final_norm_g"].rearrange("(k p) -> k p", p=128), 8, 1)
    onesrow = onesb

    xs_t = P.sb("xs_t", [128, 8, TP])
    hT = P.sb("hT", [128, 8, TP], BF16)
    mixT = P.sb("mixT", [128, 8, TP], BF16)
    sqb = P.sb("sqb", [128, 4, TP], BF16)
    rstd = P.sb("rstd", [128, TP])
    S1 = P.sb("S1", [128, 6 * (TP + 3 * 16)])
    S2 = S2flat[:, 0:8 * TP].rearrange("p (c t) -> p c t", c=8)
    p8 = P.sb("p8", [8, TP])
    sig8 = P.sb("sig8", [8, TP])
    g8 = P.sb("g8", [8, TP])
    gc8 = P.sb("gc8", [8, TP])
    actb = S2flat[:].bitcast(BF16).rearrange("p (j t) -> p j t", j=FC)
    sgb = P.sb("sgb", [128, TP])
    gpre = P.sb("gpre", [128, TP + 2 * 16])
    gcv = P.sb("gcv", [128, TP])
    gpre2 = P.sb("gpre2", [128, TP + 2 * 16])
    xin = P.sb("xio", [128, D])
    yout = xin
    NRING = 5
    wring = [P.sb("wr%d" % i, [128, 8, 512], BF16) for i in range(NRING)]
    wpos = {"w": 0, "d": 0}
    tA = P.sb("tA", [128, TP])
    tB = P.sb("tB", [128, TP])
    tC = gcv
    cosT = sgb
    sinT = gpre
    qTb = P.sb("qTb", [128, 2, TP], BF16)
    kTb = P.sb("kTb", [128, TP], BF16)
    krf = P.sb("krf", [128, TP])
    def T3(name, dt=F32):
        return P.sb(name, [128, 4, 128], dt)
    TH = []
    for jj in range(2):
        t_ = {}
        for nm in ("Eu", "QKd", "Mm0", "Mm1", "Ma0", "Ma1", "Xt"):
            t_[nm] = P.sb("%s_%d" % (nm, jj), [128, 2, 128])
        for nm in ("vb", "kbd", "kdec", "vnew"):
            t_[nm] = P.sb("%s_%d" % (nm, jj), [128, 2, 64])
        t_["nwT"] = P.sb("nwT_%d" % jj, [128, 128], BF16)
        t_["qdec"] = P.sb("qdec_%d" % jj, [128, 128], BF16)
        t_["eG2"] = P.sb("eG2_%d" % jj, [128, 128])
        t_["egl2"] = P.sb("egl2_%d" % jj, [128, 1])
        t_["x8"] = P.sb("x8_%d" % jj, [128, 16])
        t_["bege"] = P.sb("bege_%d" % jj, [128, 2])
        t_["egc"] = P.sb("egc_%d" % jj, [128, 2])
        t_["osq"] = P.sb("osq_%d" % jj, [128, 128], BF16)
        t_["orn"] = P.sb("orn_%d" % jj, [128, 128])
        t_["on_"] = P.sb("on__%d" % jj, [128, 128])
        t_["Sb"] = P.sb("Sb_%d" % jj, [128, 64], BF16)
        TH.append(t_)
    pbv = [pb[i][:, :] for i in range(7)] + [pT16[:, :].bitcast(F32)]
    x8 = P.sb("x8", [128, 16])
    zs = P.sb("zs", [128, 2, TP])
    sc = P.sb("sc", [128, 4, 256]); Pf = sc; Pn = P.sb("Pn", [128, 4, 256], BF16)
    PT = P.sb("PT", [128, 4, 2, 128], BF16)
    mx = P.sb("mx", [128, 4]); nmx = P.sb("nmx", [128, 4]); sm = P.sb("sm", [128, 4]); es = P.sb("es", [128, 4])
    vtokb = P.sb("vtokb", [128, 128], BF16); vtokf = P.sb("vtokf", [128, 128])
    kcache = P.sb("kcache", [128, 128]); vcache = P.sb("vcache", [128, 128])
    dvT = P.sb("dvT", [128, 256], BF16)
    stT = P.sb("stT", [64, 512])
    stF = P.sb("stF", [128, FC, 48])
    ktok = P.sb("ktok", [128, 128])

    evt = {"i": 0}

    def evac(out, in_, scale=None):
        evt["i"] ^= 1
        if scale is not None:
            if evt["i"]:
                P.act(out, in_, AF.Copy, scale=float(scale))
            else:
                P.ts("dve", out, in_, float(scale), ALU.mult)
        elif evt["i"]:
            P.copy("act", out, in_)
        else:
            P.copy("dve", out, in_)

    def wslab(src):
        t = wring[wpos["w"] % NRING]
        wpos["w"] += 1
        ncol = src.shape[1]
        P.dma("pool", t[:, :, 0:ncol], src.rearrange("(k p) n -> p k n", p=128))
        return t

    def bigmm(ps, slab, c0, ncols, rhs, T, m0=0):
        for k in range(KC):
            P.mm(ps[m0:m0 + ncols, 0:T], slab[:, k, c0:c0 + ncols], rhs[:, k, 0:T],
                 start=(k == 0), stop=(k == KC - 1), inc=(k == KC - 1))

    bigi = {"i": 0}

    def bigps():
        bigi["i"] ^= 1
        return pb[bigi["i"]]

    def rmsnorm(g, T, out_bf=True, out=None):
        ps = pb[2]
        for hf in range(2):
            P.act(sqb[:, :, 0:T], xs_t[:, hf * 4:hf * 4 + 4, 0:T], AF.Square)
            for k in range(4):
                P.mm(ps[:, 0:T], onesb[:], sqb[:, k, 0:T], start=(hf == 0 and k == 0), stop=(hf == 1 and k == 3),
                     inc=(k == 3))
        P.act(rstd[:, 0:T], ps[:, 0:T], AF.Sqrt, bias=EPS, scale=1.0 / D)
        P.recip(rstd[:, 0:T], rstd[:, 0:T])
        dst = hT if out is None else out
        for k in range(KC):
            P.stt(dst[:, k, 0:T], xs_t[:, k, 0:T], g[:, k:k + 1], rstd[:, 0:T], ALU.mult, ALU.mult)

    def conv(out3, buf3, w, c, K, seglen, nseg):
        P.ts("dve", out3, buf3[:, :, 0:seglen], w[:, c, 0:1], ALU.mult)
        for j in range(1, K):
            P.stt(out3, buf3[:, :, j:j + seglen], w[:, c, j:j + 1], out3, ALU.mult, ALU.add)

    def store_rows(dst_rows, srcF, nrows, ncols_list):
        nchunk = len(ncols_list)
        for c0 in range(0, nchunk, 4):
            cn = min(4, nchunk - c0)
            for c in range(c0, c0 + cn):
                P.tr(pb[6][0:nrows, (c - c0) * 128:(c - c0 + 1) * 128], srcF(c), ident[:], inc=(c == c0 + cn - 1))
            evac(stT[0:nrows, 0:cn * 128], pb[6][0:nrows, 0:cn * 128])
            P.dma("sp", dst_rows[:, c0 * 128:(c0 + cn) * 128], stT[0:nrows, 0:cn * 128])

    def load_rows(src_rows, nrows, nchunk):
        for c0 in range(0, nchunk, 4):
            cn = min(4, nchunk - c0)
            P.dma("sp", stT[0:nrows, 0:cn * 128], src_rows[:, c0 * 128:(c0 + cn) * 128])
            for c in range(c0, c0 + cn):
                P.tr(pb[6][:, (c - c0) * nrows:(c - c0 + 1) * nrows], stT[0:nrows, (c - c0) * 128:(c - c0 + 1) * 128],
                     ident[0:nrows, 0:nrows], inc=(c == c0 + cn - 1))
            evac(stF[:, c0:c0 + cn, 0:nrows], pb[6][:, 0:cn * nrows].rearrange("p (c r) -> p c r", c=cn))

    def run_layer(l, T, n, nblk, nseg, seglen, prompt, first, last, tile_i):
        d = prm[l]
        hb = lambda ap2, m: ap2.unsqueeze(2).to_broadcast([ap2.shape[0], ap2.shape[1], m])

        def seg3(t2d, K1):
            return t2d[:, 0:nseg * (K1 + seglen)].rearrange("p (s t) -> p s t", s=nseg)

        def tok3(ap2d):
            return ap2d.rearrange("p (s t) -> p s t", s=nseg)

        stage(0)
        rmsnorm(d["g1"], T)
        w_in = I["w_in"][l]
        stage(1)

        pre = [seg3(S1[:, c * (TP + 48):(c + 1) * (TP + 48)], 3) for c in range(6)]
        sl0 = wslab(w_in[:, 0:512])
        sl1 = wslab(w_in[:, 512:1024])
        sl2 = wslab(w_in[:, 1024:1288])
        if prompt:
            for c in range(6):
                if first:
                    P.memset("dve", pre[c][:, :, 0:3], 0.0)
                else:
                    P.copy("dve", pre[c][:, 0, 0:3], d["cA"][:, c, :])
        else:
            load_rows(I["state_delta_conv"][l], 48, 6)
            for c in range(6):
                P.copy("dve", pre[c][:, :, 0:3], stF[:, c, 0:48].rearrange("p (b r) -> p b r", r=3))
        stage(1.1)
        for c in range(4):
            ps = bigps()
            bigmm(ps, sl0, c * 128, 128, hT, T)
            evac(pre[c][:, :, 3:3 + seglen], tok3(ps[:, 0:T]))
        stage(1.2)
        for c in range(2):
            ps = bigps()
            bigmm(ps, sl1, c * 128, 128, hT, T)
            evac(pre[4 + c][:, :, 3:3 + seglen], tok3(ps[:, 0:T]))
        for c in range(2):
            ps = bigps()
            bigmm(ps, sl1, 256 + c * 128, 128, hT, T)
            P.act(zs[:, c, 0:T], ps[:, 0:T], AF.Silu)
        stage(1.3)
        ps = bigps()
        bigmm(ps, sl2, 0, 8, hT, T)
        P.copy("dve", p8[:, 0:T], ps[0:8, 0:T])
        stage(1.4)
        if prompt:
            for c in range(6):
                P.copy("dve", d["cA"][:, c, :], pre[c][:, 0, seglen:seglen + 3])
            if last:
                store_rows(O["ac_p"][l], lambda c: d["cA"][:, c, :], 3, [128] * 6)
        else:
            for c in range(6):
                P.copy("dve", stF[:, c, 0:48].rearrange("p (b r) -> p b r", r=3), pre[c][:, :, seglen:seglen + 3])
            store_rows(O["ac_s"][l], lambda c: stF[:, c, 0:48], 48, [128] * 6)
        stage(1.5)
        for c in range(6):
            conv(tok3(S2[:, c, 0:T]), pre[c], d["wA"], c, 4, seglen, nseg)
        P.act(S2[:, 0:6, 0:T], S2[:, 0:6, 0:T], AF.Silu)
        P.act(sqb[:, 0:4, 0:T], S2[:, 0:4, 0:T], AF.Square)
        for c in range(4):
            ps = pb[2]
            P.mm(ps[:, 0:T], bonesb[:], sqb[:, c, 0:T])
            P.act(tA[:, 0:T], ps[:, 0:T], AF.Sqrt, bias=EPS)
            P.recip(tA[:, 0:T], tA[:, 0:T])
            if c < 2:
                P.stt(S2[:, c, 0:T], S2[:, c, 0:T], 0.125, tA[:, 0:T], ALU.mult, ALU.mult)
            else:
                P.tt("dve", S2[:, c, 0:T], S2[:, c, 0:T], tA[:, 0:T], ALU.mult)
        stage(1.7)
        P.act(sig8[:, 0:T], p8[:, 0:T], AF.Sigmoid)
        P.act(g8[:, 0:T], p8[:, 0:T], AF.Exp, bias=d["dtb"][:])
        P.act(g8[:, 0:T], g8[:, 0:T], AF.Ln, bias=1.0)
        P.ts("dve", g8[:, 0:T], g8[:, 0:T], d["negA"][:], ALU.mult)
        stage(1.8)
        for b in range(nblk):
            cs = slice(b * n, (b + 1) * n)
            P.mm(pb[5][0:n, 0:8], g8[:, cs], ident[0:8, 0:8])
            P.copy("dve", x8[0:n, 0:8], pb[5][0:n, 0:8])
            P.mm(pb[5][0:8, 16:16 + n], x8[0:n, 0:8], U[0:n, 0:n])
            P.copy("dve", gc8[:, cs], pb[5][0:8, 16:16 + n])
        nlev = int(np.log2(n)) - 1
        stage(2)
        bk = [(pbv[0], pbv[1], pbv[2], pbv[3]), (pbv[4], pbv[5], pbv[6], pbv[7])]

        def gdn_thread(j):
            qa, qb, qc, qd = bk[j]
            t = TH[j]
            Sj = d["S"][j]
            Eu, QKd, Xt = t["Eu"], t["QKd"], t["Xt"]
            Mm = [t["Mm0"], t["Mm1"]]
            Ma = [t["Ma0"], t["Ma1"]]
            tmpE, Eus = Ma[1], Mm[1]
            vb_, kbd, kdec, vnew = t["vb"], t["kbd"], t["kdec"], t["vnew"]
            nwT, qdec, eG2, egl2, x8t = t["nwT"], t["qdec"], t["eG2"], t["egl2"], t["x8"]
            bege, egc, osq, orn, on_, Sb = t["bege"], t["egc"], t["osq"], t["orn"], t["on_"], t["Sb"]
            q2 = lambda t3: t3[0:n, :, 0:n]
            v3 = lambda ap, w: ap.rearrange("p (e t) -> p e t", e=2)
            Ub = U[0:n, 0:n].unsqueeze(1).to_broadcast([n, 2, n])
            Usb = Us[0:n, 0:n].unsqueeze(1).to_broadcast([n, 2, n])
            Ib = ident[0:n, 0:n].unsqueeze(1).to_broadcast([n, 2, n])
            for b in range(nblk):
                cs = slice(b * n, (b + 1) * n)
                if not prompt:
                    for e in range(2):
                        P.dma("sp", Sj[e * 64:(e + 1) * 64, :], I["state_delta"][l, b, 2 * j + e])
                elif first and b == 0:
                    P.memset("dve", Sj[:], 0.0)
                P.copy("act", Sb[:], Sj[:])
                P.tr(qc[0:n, 0:128], S2[:, 2 + j, cs], ident[:], inc=False)
                P.tr(qc[0:n, 128:256], S2[:, 4 + j, cs], ident[:])
                kv_ps = qc[0:n, 0:256].rearrange("p (a e d) -> p a e d", a=2, e=2)
                P.mm(qb[0:n, 256:264], sig8[:, cs], ident[0:8, 0:8], inc=False)
                P.mm(qb[0:n, 264:272], gc8[:, cs], ident[0:8, 0:8])
                G_ps = v3(qd[:, 0:2 * n], n)
                for e in range(2):
                    h = 2 * j + e
                    P.mm(G_ps[:, e, :], selg[:, h * 128:(h + 1) * 128], gc8[:, cs], inc=(e == 1))
                G2_ps = qb[:, 272:272 + n]
                P.mm(G2_ps, sel2[:, j * 128:(j + 1) * 128], gc8[:, cs])
                yield
                P.copy("dve", x8t[0:n, :], qb[0:n, 256:272])
                bt = x8t[0:n, 2 * j:2 * j + 2]
                gct = x8t[0:n, 12 + 2 * j:14 + 2 * j]
                P.act(egc[0:n, :], gct, AF.Exp)
                P.tt("dve", bege[0:n, :], bt, egc[0:n, :], ALU.mult)
                P.act(eG2[:, 0:n], G2_ps, AF.Exp)
                P.copy("dve", egl2[:], eG2[:, n - 1:n])
                P.tt("dve", qdec[:, 0:n], S2[:, j, cs], eG2[:, 0:n], ALU.mult)
                P.tt("dve", q2(tmpE), G_ps[0:n], hb(gct, n), ALU.subtract)
                P.ts("dve", q2(tmpE), q2(tmpE), 0.0, ALU.min)
                P.act(q2(tmpE), q2(tmpE), AF.Exp)
                Bb_ps = v3(qd[0:n, 0:2 * n], n)
                for e in range(2):
                    h = 2 * j + e
                    P.mm(Bb_ps[:, e, :], selb[:, h * 128:h * 128 + n], sig8[:, cs], inc=(e == 1))
                km = [orn[:, 0:n], on_[:, 0:n]]
                for e in range(2):
                    P.ts("dve", km[e], S2[:, 2 + j, cs], bones[:, e * 64:e * 64 + 1], ALU.mult)
                KK_ps = v3(qa[0:n, 0:2 * n], n)
                QK_ps = v3(qb[0:n, 0:2 * n], n)
                for e in range(2):
                    P.mm(KK_ps[:, e, :], km[e], S2[:, 2 + j, cs], inc=(e == 1))
                for e in range(2):
                    P.mm(QK_ps[:, e, :], km[e], S2[:, j, cs], inc=(e == 1))
                yield
                P.tt("dve", q2(Eu), q2(tmpE), Ub, ALU.mult)
                P.tt("dve", q2(Eus), q2(tmpE), Usb, ALU.mult)
                P.tt("dve", q2(Eus), q2(Eus), Bb_ps, ALU.mult)
                P.tt("dve", q2(Mm[0]), KK_ps, q2(Eus), ALU.mult)
                P.tt("dve", q2(QKd), QK_ps, q2(Eu), ALU.mult)
                P.tt("dve", vb_[0:n], kv_ps[:, 1], hb(bt, 64), ALU.mult)
                P.tt("dve", kbd[0:n], kv_ps[:, 0], hb(bege[0:n, :], 64), ALU.mult)
                P.tt("dve", kdec[0:n], kv_ps[:, 0], Eu[0:n, :, n - 1:n].to_broadcast([n, 2, 64]), ALU.mult)
                At_ps = v3(qd[0:n, 0:2 * n], n)
                for e in range(2):
                    P.tr(At_ps[:, e, :], Mm[0][0:n, e, 0:n], ident[0:n, 0:n], inc=(e == 1))
                yield
                P.copy("act", q2(Ma[0]), At_ps)
                P.tt("dve", q2(Xt), Ib, q2(Mm[0]), ALU.subtract)
                for k in range(1, nlev + 1):
                    cur, prv = k % 2, (k - 1) % 2
                    Pa_ps = v3(qa[0:n, 0:2 * n], n)
                    for e in range(2):
                        P.mm(Pa_ps[:, e, :], Mm[prv][0:n, e, 0:n], Ma[prv][0:n, e, 0:n], inc=(e == 1))
                    if k < nlev:
                        Pm_ps = v3(qb[0:n, 0:2 * n], n)
                        for e in range(2):
                            P.mm(Pm_ps[:, e, :], Ma[prv][0:n, e, 0:n], Mm[prv][0:n, e, 0:n], inc=(e == 1))
                    yield
                    P.copy("act", q2(Ma[cur]), Pa_ps)
                    if k < nlev:
                        P.copy("dve", q2(Mm[cur]), Pm_ps)
                    X_ps = v3(qd[0:n, 0:2 * n], n)
                    for e in range(2):
                        P.mm(X_ps[:, e, :], Ma[cur][0:n, e, 0:n], Xt[0:n, e, 0:n], inc=(e == 1))
                    yield
                    P.tt("dve", q2(Xt), q2(Xt), X_ps, ALU.add)
                wT_ps = qc[:, 0:n]
                for e in range(2):
                    P.mm(wT_ps[e * 64:(e + 1) * 64, :], kbd[0:n, e, :], Xt[0:n, e, 0:n], inc=(e == 1))
                yield
                P.ts("dve", nwT[:, 0:n], wT_ps, -1.0, ALU.mult)
                vn_ps = qd[0:n, 0:128].rearrange("p (e d) -> p e d", e=2)
                for e in range(2):
                    pr = slice(e * 64, (e + 1) * 64)
                    P.mm(vn_ps[:, e, :], Xt[0:n, e, 0:n], vb_[0:n, e, :], start=True, stop=False, inc=False)
                    P.mm(vn_ps[:, e, :], nwT[pr, 0:n], Sb[pr, :], start=False, stop=True, inc=(e == 1))
                yield
                P.copy("act", vnew[0:n], vn_ps)
                o_ps = qc[:, 128:128 + n]
                for e in range(2):
                    pr = slice(e * 64, (e + 1) * 64)
                    P.mm(o_ps[pr, :], Sb[pr, :], qdec[pr, 0:n], start=True, stop=False, inc=False)
                    P.mm(o_ps[pr, :], vnew[0:n, e, :], QKd[0:n, e, 0:n], start=False, stop=True, inc=(e == 1))
                Su_ps = qb[:, 384:448]
                for e in range(2):
                    pr = slice(e * 64, (e + 1) * 64)
                    P.mm(Su_ps[pr, :], kdec[0:n, e, :], vnew[0:n, e, :], inc=(e == 1))
                yield
                P.stt(Sj[:], Sj[:], egl2[:, 0:1], Su_ps, ALU.mult, ALU.add)
                P.act(osq[:, 0:n], o_ps, AF.Square)
                ss_ps = qa[:, 256:256 + n]
                P.mm(ss_ps, bonesb[:], osq[:, 0:n])
                yield
                P.act(orn[:, 0:n], ss_ps, AF.Sqrt, bias=EPS, scale=1.0 / 64)
                P.recip(orn[:, 0:n], orn[:, 0:n])
                P.tt("dve", on_[:, 0:n], o_ps, orn[:, 0:n], ALU.mult)
                P.stt(mixT[:, j, cs], on_[:, 0:n], d["gn"][:], zs[:, j, cs], ALU.mult, ALU.mult)
                if not prompt:
                    for e in range(2):
                        P.dma("sp", O["sd_s"][l, b, 2 * j + e], Sj[e * 64:(e + 1) * 64, :])
            if prompt and last:
                for e in range(2):
                    P.dma("sp", O["sd_p"][l, 2 * j + e], Sj[e * 64:(e + 1) * 64, :])

        gens = [gdn_thread(0), gdn_thread(1)]
        alive = [True, True]
        while any(alive):
            for gi, g in enumerate(gens):
                if alive[gi]:
                    try:
                        next(g)
                    except StopIteration:
                        alive[gi] = False


        stage(3)
        sl3 = wslab(w_in[:, 1288:1800])
        preB = [seg3(S1[:, c * (TP + 48):(c + 1) * (TP + 48)], 2) for c in range(2)]
        if prompt:
            for c in range(2):
                if first:
                    P.memset("dve", preB[c][:, :, 0:2], 0.0)
                else:
                    P.copy("dve", preB[c][:, 0, 0:2], d["cB"][:, c, :])
        else:
            load_rows(I["state_shortconv"][l], 32, 2)
            for c in range(2):
                P.copy("dve", preB[c][:, :, 0:2], stF[:, c, 0:32].rearrange("p (b r) -> p b r", r=2))
        for c in range(2):
            ps = bigps()
            bigmm(ps, sl2, 8 + c * 128, 128, hT, T)
            evac(S2[:, c, 0:T], ps[:, 0:T])
        for c in range(2):
            ps = bigps()
            bigmm(ps, sl3, c * 128, 128, hT, T)
            evac(tA[:, 0:T], ps[:, 0:T])
            ps = bigps()
            bigmm(ps, sl3, 256 + c * 128, 128, hT, T)
            P.tt("dve", preB[c][:, :, 2:2 + seglen], tok3(tA[:, 0:T]), tok3(ps[:, 0:T]), ALU.mult)
        if prompt:
            for c in range(2):
                P.copy("dve", d["cB"][:, c, :], preB[c][:, 0, seglen:seglen + 2])
            if last:
                store_rows(O["bc_p"][l], lambda c: d["cB"][:, c, :], 2, [128] * 2)
        else:
            for c in range(2):
                P.copy("dve", stF[:, c, 0:32].rearrange("p (b r) -> p b r", r=2), preB[c][:, :, seglen:seglen + 2])
            store_rows(O["bc_s"][l], lambda c: stF[:, c, 0:32], 32, [128] * 2)
        for c in range(2):
            conv(tok3(tB[:, 0:T]), preB[c], d["wB"], c, 3, seglen, nseg)
            P.tt("dve", mixT[:, 2 + c, 0:T], S2[:, c, 0:T], tB[:, 0:T], ALU.mult)

        stage(4)
        sl4 = wslab(w_in[:, 1800:2312])
        if prompt:
            P.dma("sp", cosT[:, 0:T], C["cosp"][:, tile_i * TP:(tile_i + 1) * TP])
            P.dma("sp", sinT[:, 0:T], C["sinp"][:, tile_i * TP:(tile_i + 1) * TP])
        else:
            P.dma("sp", cosT[:, 0:T], C["coss"])
            P.dma("sp", sinT[:, 0:T], C["sins"])

        def rope(dst_bf, dst_f, src):
            P.mm(pb[2][:, 0:T], perm[:], src)
            P.tt("dve", tB[:, 0:T], src, cosT[:, 0:T], ALU.mult)
            P.tt("dve", tC[:, 0:T], pb[2][:, 0:T], sinT[:, 0:T], ALU.mult)
            if dst_f is not None:
                P.tt("dve", dst_f, tB[:, 0:T], tC[:, 0:T], ALU.add)
                P.copy("act", dst_bf, dst_f)
            else:
                P.tt("dve", dst_bf, tB[:, 0:T], tC[:, 0:T], ALU.add)

        for j in range(2):
            ps = bigps()
            for half in range(2):
                h = half * 2 + j
                bigmm(ps, sl4, h * 64, 64, hT, T, m0=half * 64)
            evac(tA[:, 0:T], ps[:, 0:T])
            rope(qTb[:, j, 0:T], None, tA[:, 0:T])
        ps = bigps()
        bigmm(ps, sl4, 256, 128, hT, T)
        evac(tA[:, 0:T], ps[:, 0:T])
        rope(kTb[:, 0:T], krf[:, 0:T], tA[:, 0:T])
        ps = bigps()
        bigmm(ps, sl4, 384, 128, hT, T)
        evac(S2[:, 7, 0:T], ps[:, 0:T])
        if not prompt:
            P.tr(pb[6][0:64, 0:128], krf[:, 0:64], ident[:])
            evac(ktok[0:64, :], pb[6][0:64, 0:128])
            P.tr(pb[6][0:64, 128:256], S2[:, 7, 0:64], ident[:])
            evac(vtokf[0:64, :], pb[6][0:64, 128:256])
            for b in range(16):
                P.dma("sp", O["ck_s"][l, b, 124:128, :], ktok[b * 4:(b + 1) * 4, :])
                P.dma("sp", O["cv_s"][l, b, 124:128, :], vtokf[b * 4:(b + 1) * 4, :])
            P.dma("sp", O["ck_s"][l, :, 0:124, :], I["cache_win_k"][l, :, 4:128, :])
            P.dma("sp", O["cv_s"][l, :, 0:124, :], I["cache_win_v"][l, :, 4:128, :])
        for b in range(nblk):
            cs = slice(b * n, (b + 1) * n)
            hasprev = (not prompt) or (not (first and b == 0))
            off = 0 if hasprev else 128
            W = 256 - off if prompt else 128 + n
            if prompt:
                kpv = d["kprev"]
                vpv = d["vprev"]
            else:
                P.dma("sp", kcache[:], I["cache_win_k"][l, b])
                P.dma("sp", vcache[:], I["cache_win_v"][l, b])
                P.tr(pb[6][:, 0:128], kcache[:], ident[:])
                evac(d["kprev"][:], pb[6][:, 0:128])
                P.copy("dve", d["vprev"][:], vcache[:])
                kpv = d["kprev"]
                vpv = d["vprev"]
            P.tr(pb[6][0:n, 256:384], S2[:, 7, cs], ident[:])
            P.copy("dve", vtokb[0:n, :], pb[6][0:n, 256:384])
            if prompt and last and b == nblk - 1:
                P.copy("act", vtokf[0:n, :], pb[6][0:n, 256:384])
                P.dma("sp", O["cv_p"][l], vtokf[:])
                P.tr(pb[6][:, 384:512], krf[:, cs], ident[:])
                evac(ktok[:], pb[6][:, 384:512])
                P.dma("sp", O["ck_p"][l], ktok[:])
            S_ps = [pb[0][0:n, :].rearrange("p (h k) -> p h k", h=2), pb[1][0:n, :].rearrange("p (h k) -> p h k", h=2)]
            for h in range(4):
                pr = slice((h // 2) * 64, (h // 2) * 64 + 64)
                dst = S_ps[h // 2][:, h % 2, :]
                if hasprev:
                    P.mm(dst[:, 0:128], qTb[pr, h % 2, cs], kpv[pr, :], inc=False)
                P.mm(dst[:, 128:128 + n], qTb[pr, h % 2, cs], kTb[pr, cs], inc=(h % 2 == 1))
            mk = amask[0:n, off:off + W].unsqueeze(1).to_broadcast([n, 2, W])
            for hh in range(2):
                P.tt("dve", sc[0:n, 2 * hh:2 * hh + 2, 0:W], S_ps[hh][:, :, off:off + W], mk, ALU.add)
            P.reduce(mx[0:n, :], sc[0:n, :, 0:W], ALU.max)
            P.ts("dve", nmx[0:n, :], mx[0:n, :], -0.125, ALU.mult)
            for h in range(4):
                P.act(Pf[0:n, h, 0:W], sc[0:n, h, 0:W], AF.Exp, bias=nmx[0:n, h:h + 1], scale=0.125,
                      accum_out=sm[0:n, h:h + 1])
            P.tt("dve", es[0:n, :], nmx[0:n, :], d["sink"][0:n, :], ALU.add)
            P.act(es[0:n, :], es[0:n, :], AF.Exp)
            P.tt("dve", es[0:n, :], es[0:n, :], sm[0:n, :], ALU.add)
            P.recip(es[0:n, :], es[0:n, :])
            P.tt("dve", Pn[0:n, :, 0:W], Pf[0:n, :, 0:W], hb(es[0:n, :], W), ALU.mult)
            PT_ps = pT16[:, :].rearrange("p (h s q) -> p h s q", h=4, s=2)
            for h in range(4):
                if hasprev:
                    P.tr(PT_ps[:, h, 0, 0:n], Pn[0:n, h, 0:128], identb[0:n, 0:n], inc=False)
                P.tr(PT_ps[0:n, h, 1, 0:n], Pn[0:n, h, 128 - off:128 - off + n], identb[0:n, 0:n], inc=(h == 3))
            if hasprev:
                P.copy("act", PT[:, :, 0, 0:n], PT_ps[:, :, 0, 0:n])
            P.copy("dve", PT[0:n, :, 1, 0:n], PT_ps[0:n, :, 1, 0:n])
            o_ps = pb[6][:, 0:2 * n].rearrange("p (j t) -> p j t", j=2)
            for h in range(4):
                kvs = slice((h // 2) * 64, (h // 2) * 64 + 64)
                dst = o_ps[(h % 2) * 64:(h % 2) * 64 + 64, h // 2, :]
                if hasprev:
                    P.mm(dst, vpv[:, kvs], PT[:, h, 0, 0:n], start=True, stop=False, inc=False)
                P.mm(dst, vtokb[0:n, kvs], PT[0:n, h, 1, 0:n], start=(not hasprev), stop=True, inc=(h == 3))
            evac(mixT[:, 4:6, cs], o_ps)
            if prompt:
                P.copy("dve", d["kprev"][:], kTb[:, cs])
                P.copy("act", d["vprev"][:], vtokb[:])

        stage(5)
        sl5 = wslab(w_in[:, 2312:2824])

        def gelu(dst, ps):
            P.act(tB[:, 0:T], ps, AF.Square)
            P.ts("dve", tB[:, 0:T], tB[:, 0:T], 0.044715, ALU.mult, 1.0, ALU.add)
            P.tt("dve", tB[:, 0:T], tB[:, 0:T], ps, ALU.mult)
            P.act(tB[:, 0:T], tB[:, 0:T], AF.Sigmoid, scale=1.5957691216057308)
            P.tt("dve", dst, tB[:, 0:T], ps, ALU.mult)

        for c in range(2):
            ps = bigps()
            bigmm(ps, sl5, c * 128, 128, hT, T)
            gelu(S2[:, c, 0:T], ps[:, 0:T])
        for c in range(2):
            ps = bigps()
            bigmm(ps, sl5, 256 + c * 128, 128, hT, T)
            gelu(S2[:, 2 + c, 0:T], ps[:, 0:T])
        P.copy("act", sqb[:, 0:2, 0:T], S2[:, 2:4, 0:T])
        P.act(sqb[:, 2:4, 0:T], S2[:, 2:4, 0:T], AF.Square)
        for c in range(2):
            P.mm(pb[2][:, 0:T], onesb[:], sqb[:, c, 0:T], start=(c == 0), stop=(c == 1), inc=(c == 1))
        for c in range(2):
            P.mm(pb[3][:, 0:T], onesb[:], sqb[:, 2 + c, 0:T], start=(c == 0), stop=(c == 1), inc=(c == 1))
        P.act(tA[:, 0:T], pb[2][:, 0:T], AF.Copy, scale=1.0 / 256)
        P.tt("dve", tB[:, 0:T], tA[:, 0:T], tA[:, 0:T], ALU.mult)
        P.stt(tB[:, 0:T], pb[3][:, 0:T], 1.0 / 256, tB[:, 0:T], ALU.mult, ALU.subtract)
        P.act(tB[:, 0:T], tB[:, 0:T], AF.Sqrt, bias=EPS)
        P.recip(tB[:, 0:T], tB[:, 0:T])
        for c in range(2):
            P.tt("dve", S2[:, 2 + c, 0:T], S2[:, 2 + c, 0:T], tA[:, 0:T], ALU.subtract)
            P.tt("dve", S2[:, 2 + c, 0:T], S2[:, 2 + c, 0:T], tB[:, 0:T], ALU.mult)
            P.ts("dve", S2[:, 2 + c, 0:T], S2[:, 2 + c, 0:T], d["lng"][:, c:c + 1], ALU.mult, d["lnb"][:, c:c + 1], ALU.add)
        if not prompt:
            store_rows(O["dv_s"][l], lambda c: S2[:, 2 + c, 0:T], 64, [128] * 2)
        for b in range(nblk):
            cs = slice(b * n, (b + 1) * n)
            for c in range(2):
                P.tr(pb[2][0:n, c * 128:(c + 1) * 128], S2[:, 2 + c, cs], ident[:], inc=(c == 1))
            evac(dvT[0:n, :], pb[2][0:n, 0:256])
            m_ps = pb[3][:, 0:2 * n].rearrange("p (j t) -> p j t", j=2)
            for g in range(4):
                dst = m_ps[(g % 2) * 64:(g % 2) * 64 + 64, g // 2, :]
                P.mm(dst, dvT[0:n, g * 64:(g + 1) * 64], d["WT"][0:n, g, 0:n], start=True, stop=False, inc=False)
                P.mm(dst, onesrow[0:1, 0:64], d["brow"][0:1, g * 128:g * 128 + n], start=False, stop=True, inc=(g == 3))
            P.tt("dve", mixT[:, 6:8, cs], S2[:, 0:2, cs], m_ps, ALU.mult)

        stage(6)
        for s in range(2):
            sl = wslab(I["w_out"][l][:, s * 512:(s + 1) * 512])
            for c in range(4):
                ps = bigps()
                bigmm(ps, sl, c * 128, 128, mixT, T)
                P.tt("dve", xs_t[:, s * 4 + c, 0:T], xs_t[:, s * 4 + c, 0:T], ps[:, 0:T], ALU.add)

        stage(7)
        rmsnorm(d["g2"], T)
        gp3s = [seg3(gpre[:, :], 2), seg3(gpre2[:, :], 2)]
        gcvs = [gcv, sgb]
        if not prompt:
            load_rows(I["state_ffn_conv"][l], 32, FC)
        slabs = {}

        def fslab(kind, s_):
            if (kind, s_) not in slabs:
                ncol = 512 if s_ < 5 else 256
                w_ = I["ffn_w_gate" if kind == "g" else "ffn_w_up"][l]
                slabs[(kind, s_)] = wslab(w_[:, s_ * 512:s_ * 512 + ncol])
            return slabs[(kind, s_)]

        fpi = {"i": 0}

        def fbank():
            fpi["i"] = (fpi["i"] + 1) % 4
            return pb[fpi["i"]]

        def stageA(j):
            s_, c = divmod(j, 4)
            gp3, gcv_ = gp3s[j % 2], gcvs[j % 2]
            ps = fbank()
            bigmm(ps, fslab("g", s_), c * 128, 128, hT, T)
            if prompt:
                if first:
                    P.memset("dve", gp3[:, :, 0:2], 0.0)
                else:
                    P.copy("dve", gp3[:, 0, 0:2], d["cF"][:, j, :])
            else:
                P.copy("dve", gp3[:, :, 0:2], stF[:, j, 0:32].rearrange("p (b r) -> p b r", r=2))
            P.copy("act", gp3[:, :, 2:2 + seglen], tok3(ps[:, 0:T]))
            if prompt:
                P.copy("dve", d["cF"][:, j, :], gp3[:, 0, seglen:seglen + 2])
            else:
                P.copy("dve", stF[:, j, 0:32].rearrange("p (b r) -> p b r", r=2), gp3[:, :, seglen:seglen + 2])
            conv(tok3(gcv_[:, 0:T]), gp3, d["wF"], j, 3, seglen, nseg)

        def stageB(j):
            s_, c = divmod(j, 4)
            gcv_ = gcvs[j % 2]
            P.act(gcv_[:, 0:T], gcv_[:, 0:T], AF.Silu)
            ps = fbank()
            bigmm(ps, fslab("u", s_), c * 128, 128, hT, T)
            P.tt("dve", actb[:, j, 0:T], gcv_[:, 0:T], ps[:, 0:T], ALU.mult)

        stageA(0)
        for j in range(FC):
            if j + 1 < FC:
                stageA(j + 1)
            stageB(j)
        stage(7.5)
        if prompt and last:
            store_rows(O["fc_p"][l], lambda c: d["cF"][:, c, :], 2, [128] * FC)
        if not prompt:
            store_rows(O["fc_s"][l], lambda c: stF[:, c, 0:32], 32, [128] * FC)
        wd = I["ffn_w_down"][l].rearrange("(j p) n -> p j n", p=128)
        acc = [pb[i][:, :] for i in range(7)] + [pT16[:, :].bitcast(F32)]
        for jg in range((FC + 3) // 4):
            nj = min(4, FC - 4 * jg)
            tw = wring[wpos["w"] % NRING]
            wpos["w"] += 1
            t = tw[:, :, :].rearrange("p k n -> p (k n)").rearrange("p (j n) -> p j n", j=4)
            P.dma("pool", t[:, 0:nj, :], wd[:, 4 * jg:4 * jg + nj, :])
            for jj in range(nj):
                j = 4 * jg + jj
                for oc in range(8):
                    P.mm(acc[oc][:, 0:T], t[:, jj, oc * 128:(oc + 1) * 128], actb[:, j, 0:T],
                         start=(j == 0), stop=(j == FC - 1), inc=(oc == 7 or j == FC - 1))
        for oc in range(8):
            P.tt("dve", xs_t[:, oc, 0:T], xs_t[:, oc, 0:T], acc[oc][:, 0:T], ALU.add)
        stage(8)

    def run_tile(prompt, ti):
        if prompt:
            T, n, nblk, nseg, seglen = TP, 128, 4, 1, TP
            src = I["xp"][ti * TP:(ti + 1) * TP, :]
            dst = O["y_p"][ti * TP:(ti + 1) * TP, :]
        else:
            T, n, nblk, nseg, seglen = TS, 4, 16, 16, 4
            src = I["xs"]
            dst = O["y_s"]
        nrb = (T + 127) // 128
        for rb in range(nrb):
            r = min(128, T - rb * 128)
            P.dma("sp", xin[0:r, :], src[rb * 128:rb * 128 + r, :])
            for g in range(2):
                for k in range(4):
                    P.tr(pb[6][:, k * r:(k + 1) * r], xin[0:r, (g * 4 + k) * 128:(g * 4 + k + 1) * 128],
                         ident[0:r, 0:r], inc=(k == 3))
                evac(xs_t[:, g * 4:g * 4 + 4, rb * 128:rb * 128 + r],
                     pb[6][:, 0:4 * r].rearrange("p (k t) -> p k t", k=4))
        for li, l in enumerate([int(c) for c in os.environ.get("MK_LAYERS", "01")]):
            lpos["i"] = li
            run_layer(l, T, n, nblk, nseg, seglen, prompt, first=(ti == 0), last=(ti == NTP - 1), tile_i=ti)
        rmsnorm(gfin, T, out=S2)
        for rb in range(nrb):
            r = min(128, T - rb * 128)
            for g in range(2):
                for k in range(4):
                    P.tr(pb[6][0:r, k * 128:(k + 1) * 128], S2[:, g * 4 + k, rb * 128:rb * 128 + r], ident[:],
                         inc=(k == 3))
                evac(yout[0:r, g * 512:(g + 1) * 512], pb[6][0:r, :])
            P.dma("sp", dst[rb * 128:rb * 128 + r, :], yout[0:r, :])

    ntiles = int(os.environ.get("MK_NTILES", NTP))
    try:
        for ti in range(ntiles):
            run_tile(True, ti)
        if os.environ.get("MK_NOSAMPLE") is None:
            run_tile(False, 0)
    except _Stop:
        P.dma("sp", O["y_s"][0:64, :], xs_t[0:64, 0, 0:TP].rearrange("p t -> p t") if False else xin[0:64, :])
    P.emit()
    P.close()
    return nc, P


_CACHE = {}


def kernel(**inp):
    f = np.float32
    if "nc" not in _CACHE:
        _CACHE["nc"], _ = build_program()
        _CACHE["consts"] = _consts()
    nc = _CACHE["nc"]
    consts = _CACHE["consts"]
    shared = {}
    for k in IN_SHAPES:
        if k in ("xp", "xs", "state_delta", "state_delta_conv", "state_shortconv", "cache_win_k", "cache_win_v",
                 "state_ffn_conv"):
            continue
        shared[k] = np.ascontiguousarray(np.asarray(inp[k], f))
    for k, v in consts.items():
        shared["c_" + k] = np.ascontiguousarray(v.reshape(CONST_SHAPES[k]).astype(f))
    in_maps = []
    for c in range(8):
        m = dict(shared)
        bs = slice(16 * c, 16 * c + 16)
        m["xp"] = np.ascontiguousarray(np.asarray(inp["x_prompt"][c], f))
        m["xs"] = np.ascontiguousarray(np.asarray(inp["x_sample"][bs], f).reshape(64, D))
        m["state_delta"] = np.ascontiguousarray(np.asarray(inp["state_delta"][:, bs], f))
        m["state_delta_conv"] = np.ascontiguousarray(np.asarray(inp["state_delta_conv"][:, bs], f).reshape(2, 48, 768))
        m["state_shortconv"] = np.ascontiguousarray(np.asarray(inp["state_shortconv"][:, bs], f).reshape(2, 32, 256))
        m["cache_win_k"] = np.ascontiguousarray(np.asarray(inp["cache_win_k"][:, bs], f).reshape(2, 16, 128, 128))
        m["cache_win_v"] = np.ascontiguousarray(np.asarray(inp["cache_win_v"][:, bs], f).reshape(2, 16, 128, 128))
        m["state_ffn_conv"] = np.ascontiguousarray(np.asarray(inp["state_ffn_conv"][:, bs], f).reshape(2, 32, DFF))
        in_maps.append(m)
    res = run_bass_kernel_spmd(nc, in_maps, core_ids=list(range(8)))
    R = res.results

    def cat_p(k, shape):
        return np.stack([np.asarray(R[c][k], f).reshape(shape) for c in range(8)], axis=1)

    def cat_s(k, shape):
        return np.concatenate([np.asarray(R[c][k], f).reshape(shape) for c in range(8)], axis=1)

    y_p = np.stack([np.asarray(R[c]["y_p"], f) for c in range(8)], axis=0)
    y_s = np.concatenate([np.asarray(R[c]["y_s"], f).reshape(16, 4, D) for c in range(8)], axis=0)
    return (
        y_p, y_s,
        cat_p("sd_p", (2, 4, 64, 64)), cat_s("sd_s", (2, 16, 4, 64, 64)),
        cat_p("ac_p", (2, 3, 768)), cat_s("ac_s", (2, 16, 3, 768)),
        cat_p("bc_p", (2, 2, 256)), cat_s("bc_s", (2, 16, 2, 256)),
        cat_p("ck_p", (2, 128, 2, 64)), cat_s("ck_s", (2, 16, 128, 2, 64)),
        cat_p("cv_p", (2, 128, 2, 64)), cat_s("cv_s", (2, 16, 128, 2, 64)),
        cat_p("fc_p", (2, 2, DFF)), cat_s("fc_s", (2, 16, 2, DFF)),
        cat_s("dv_s", (2, 16, 4, 256)),
    )
```

```python
import numpy as np
import concourse.bass as bass
import concourse.mybir as mybir

F32 = mybir.dt.float32
BF16 = mybir.dt.bfloat16
I32 = mybir.dt.int32
ALU = mybir.AluOpType
AF = mybir.ActivationFunctionType
AX = mybir.AxisListType

ENGINES = ("pe", "act", "dve", "pool", "sp")


class _Buf:
    __slots__ = ("last_w", "readers")

    def __init__(self):
        self.last_w = None
        self.readers = []


class Prog:
    def __init__(self, nc, ring=12):
        self.nc = nc
        self.ops = {e: [] for e in ENGINES}
        self.cnt = {e: 0 for e in ENGINES}
        self.known = {e: {} for e in ENGINES}
        self.bufs = {}
        self.stack = []
        self.sems = {}
        self.nsem = 0
        self.ring_n = ring
        self.ring = {}
        self.ring_pos = {}
        self.untracked = set()
        self.nops = 0

    def _sem(self, key):
        if key not in self.sems:
            cm = self.nc.semaphore("s_" + key)
            self.sems[key] = cm.__enter__()
            self.stack.append(cm)
            self.nsem += 1
        return self.sems[key]

    def sb(self, name, shape, dtype=F32):
        cm = self.nc.sbuf_tensor(name, list(shape), dtype)
        t = cm.__enter__()
        self.stack.append(cm)
        return t

    def ps(self, name, shape, dtype=F32):
        cm = self.nc.psum_tensor(name, list(shape), dtype)
        t = cm.__enter__()
        self.stack.append(cm)
        return t

    def dram(self, name, shape, dtype=F32, kind="Internal"):
        return self.nc.dram_tensor(name, list(shape), dtype, kind=kind)

    def _key(self, ap):
        if isinstance(ap, str):
            return ap
        return ap.tensor.name

    def _deps(self, eng, reads, writes):
        waits = []
        known = self.known[eng]

        def need(ev):
            if ev is None:
                return
            sk, val, kn = ev
            if eng == "pe" and sk == "pe":
                return
            if known.get(sk, 0) >= val:
                return
            waits.append((sk, val))
            known[sk] = val
            for k, v in kn.items():
                if known.get(k, 0) < v:
                    known[k] = v

        rk = [self._key(a) for a in reads]
        wk = [self._key(a) for a in writes]
        rk = [k for k in rk if k not in self.untracked]
        wk = [k for k in wk if k not in self.untracked]
        for k in rk:
            b = self.bufs.setdefault(k, _Buf())
            need(b.last_w)
        for k in wk:
            b = self.bufs.setdefault(k, _Buf())
            need(b.last_w)
            for r in b.readers:
                need(r)
        return waits, rk, wk

    def _commit(self, ev, rk, wk):
        for k in rk:
            if k not in wk:
                self.bufs[k].readers.append(ev)
        for k in wk:
            b = self.bufs[k]
            b.last_w = ev
            b.readers = []

    def op(self, eng, fn, reads, writes, inc=True):
        waits, rk, wk = self._deps(eng, reads, writes)
        ev = (eng, self.cnt[eng] + 1, dict(self.known[eng]))
        if inc:
            self.cnt[eng] += 1
        self.ops[eng].append((waits, fn, (eng, 1) if inc else None))
        self._commit(ev, rk, wk)
        self.nops += 1

    def dma(self, eng, out, in_, extra_reads=(), extra_writes=(), **kw):
        if eng not in self.ring:
            self.ring[eng] = [["d%s%d" % (eng, i), 0] for i in range(self.ring_n)]
            self.ring_pos[eng] = 0
        slot = self.ring[eng][self.ring_pos[eng]]
        self.ring_pos[eng] = (self.ring_pos[eng] + 1) % self.ring_n
        waits, rk, wk = self._deps(eng, [in_] + list(extra_reads), [out] + list(extra_writes))
        known = self.known[eng]
        if slot[1] > 0 and known.get(slot[0], 0) < 16 * slot[1]:
            waits.append((slot[0], 16 * slot[1]))
            known[slot[0]] = 16 * slot[1]
        slot[1] += 1
        ev = (slot[0], 16 * slot[1], dict(known))

        def fn(e, out=out, in_=in_, kw=kw):
            return e.dma_start(out=out, in_=in_, **kw)

        self.ops[eng].append((waits, fn, (slot[0], 16)))
        self._commit(ev, rk, wk)
        self.nops += 1

    def mm(self, out, lhsT, rhs, start=True, stop=True, inc=True, **kw):
        self.op("pe", lambda e: e.matmul(out, lhsT, rhs, start=start, stop=stop, **kw),
                [lhsT, rhs] + ([] if start else [out]), [out], inc=inc)

    def tr(self, out, in_, ident, inc=True):
        self.op("pe", lambda e: e.transpose(out, in_, ident), [in_, ident], [out], inc=inc)

    def act(self, out, in_, func, bias=None, scale=None, accum_out=None, eng="act"):
        kw = {}
        reads = [in_]
        writes = [out]
        if bias is not None:
            kw["bias"] = bias
            if not isinstance(bias, (int, float)):
                reads.append(bias)
        if scale is not None:
            kw["scale"] = scale
            if not isinstance(scale, (int, float)):
                reads.append(scale)
        if accum_out is not None:
            kw["accum_out"] = accum_out
            writes.append(accum_out)
        self.op("act", lambda e: e.activation(out, in_, func, **kw), reads, writes)

    def tt(self, eng, out, in0, in1, op):
        self.op(eng, lambda e: e.tensor_tensor(out, in0, in1, op), [in0, in1], [out])

    def ts(self, eng, out, in0, s1, op0, s2=None, op1=None, accum_out=None):
        reads = [in0]
        if not isinstance(s1, (int, float)):
            reads.append(s1)
        if s2 is not None and not isinstance(s2, (int, float)):
            reads.append(s2)
        writes = [out]
        kw = {}
        if op1 is not None:
            kw["op1"] = op1
        if accum_out is not None:
            kw["accum_out"] = accum_out
            writes.append(accum_out)
        self.op(eng, lambda e: e.tensor_scalar(out, in0, s1, s2, op0, **kw), reads, writes)

    def stt(self, out, in0, scalar, in1, op0, op1, eng="dve"):
        reads = [in0, in1]
        if not isinstance(scalar, (int, float)):
            reads.append(scalar)
        self.op(eng, lambda e: e.scalar_tensor_tensor(out, in0, scalar, in1, op0, op1), reads, [out])

    def copy(self, eng, out, in_):
        if eng == "act":
            self.op("act", lambda e: e.copy(out, in_), [in_], [out])
        else:
            self.op(eng, lambda e: e.tensor_copy(out, in_), [in_], [out])

    def memset(self, eng, out, val):
        self.op(eng, lambda e: e.memset(out, val), [], [out])

    def reduce(self, out, in_, op, axis=AX.X, eng="dve"):
        self.op(eng, lambda e: e.tensor_reduce(out, in_, axis, op), [in_], [out])

    def recip(self, out, in_):
        self.op("dve", lambda e: e.reciprocal(out, in_), [in_], [out])

    def emit(self):
        nc = self.nc
        fin = []
        for eng, ring in self.ring.items():
            for key, uses in ring:
                if uses > 0:
                    fin.append((key, 16 * uses))
        for e in ("pe", "act", "dve", "pool"):
            if self.cnt[e] > 0:
                fin.append((e, self.cnt[e]))
        for k in list(self.sems) + [f[0] for f in fin]:
            self._sem(k)
        for eng in ENGINES:
            for waits, fn, inc in self.ops[eng]:
                for sk, v in waits:
                    self._sem(sk)
                if inc is not None:
                    self._sem(inc[0])
        sems = self.sems
        ops = self.ops

        def replay(e, lst, final=False):
            for waits, fn, inc in lst:
                for sk, v in waits:
                    e.wait_ge(sems[sk], v)
                ins = fn(e)
                if inc is not None:
                    ins.then_inc(sems[inc[0]], inc[1])
            if final:
                for sk, v in fin:
                    e.wait_ge(sems[sk], v)

        with nc.Block() as block:
            @block.sync
            def _(e):
                replay(e, ops["sp"], final=True)

            @block.tensor
            def _(e):
                replay(e, ops["pe"])

            @block.scalar
            def _(e):
                replay(e, ops["act"])

            @block.vector
            def _(e):
                replay(e, ops["dve"])

            @block.gpsimd
            def _(e):
                replay(e, ops["pool"])

    def close(self):
        while self.stack:
            self.stack.pop().__exit__(None, None, None)


from concourse.bass_utils import run_bass_kernel_spmd

D = 1024
KC = 8
PTOT = 2824
DFF = 2816
FC = 22
EPS = 1e-6
PAST = 16384
NEG = -30000.0
TP = 512
NTP = 8
TS = 64


def _consts():
    f = np.float32
    c = {}
    c["ident"] = np.eye(128, dtype=f)
    s = np.arange(128)
    c["U"] = (s[:, None] <= s[None, :]).astype(f)
    c["Us"] = (s[:, None] < s[None, :]).astype(f)
    c["ones"] = np.ones((128, 128), f)
    bo = np.zeros((128, 128), f)
    bo[:64, :64] = 1
    bo[64:, 64:] = 1
    c["bones"] = bo
    sel = np.zeros((8, 4, 128), f)
    for h in range(4):
        sel[4 + h, h, :] = 1
    c["selg"] = sel.reshape(8, 512)
    selb = np.zeros((8, 4, 128), f)
    for h in range(4):
        selb[h, h, :] = 1
    c["selb"] = selb.reshape(8, 512)
    sel2 = np.zeros((8, 2, 128), f)
    for j in range(2):
        sel2[4 + 2 * j, j, :64] = 1
        sel2[4 + 2 * j + 1, j, 64:] = 1
    c["sel2"] = sel2.reshape(8, 256)
    perm = np.zeros((128, 128), f)
    for half in range(2):
        for d in range(8):
            r = half * 64 + d
            perm[r + 8, r] = -1.0
            perm[r, r + 8] = 1.0
    c["perm"] = perm
    inv = np.power(np.float32(500000.0), -np.arange(8, dtype=f) * f(2.0 / 16)).astype(f)

    def cs(pos):
        ang = pos.astype(f)[None, :] * inv[:, None]
        C = np.ones((128, pos.shape[0]), f)
        S = np.zeros((128, pos.shape[0]), f)
        for half in range(2):
            for q in range(2):
                r0 = half * 64 + q * 8
                C[r0:r0 + 8] = np.cos(ang)
                S[r0:r0 + 8] = np.sin(ang)
        return C, S

    c["cosp"], c["sinp"] = cs(np.arange(4096))
    c["coss"], c["sins"] = cs(np.tile(PAST + np.arange(4), 16))
    i = np.arange(128)[:, None]
    j = np.arange(256)[None, :]
    c["amask"] = np.where((j > i) & (j <= i + 128), 0.0, NEG).astype(f)
    rp = np.ones((8, TP), f)
    rp[:, ::128] = 0
    c["rstp"] = rp
    rs = np.ones((8, TS), f)
    rs[:, ::4] = 0
    c["rsts"] = rs
    return c


CONST_SHAPES = {"ident": (128, 128), "U": (128, 128), "Us": (128, 128), "ones": (128, 128), "bones": (128, 128),
                "selg": (8, 512), "selb": (8, 512), "sel2": (8, 256), "perm": (128, 128),
                "cosp": (128, 4096), "sinp": (128, 4096), "coss": (128, 64), "sins": (128, 64),
                "amask": (128, 256), "rstp": (8, TP), "rsts": (8, TS)}

IN_SHAPES = {
    "xp": (4096, D), "xs": (64, D),
    "state_delta": (2, 16, 4, 64, 64), "state_delta_conv": (2, 48, 768), "state_shortconv": (2, 32, 256),
    "cache_win_k": (2, 16, 128, 128), "cache_win_v": (2, 16, 128, 128), "state_ffn_conv": (2, 32, DFF),
    "norm1_g": (2, D), "w_in": (2, D, PTOT), "a_conv_w": (2, 4, 768), "a_log": (2, 4), "a_dt_bias": (2, 4),
    "a_norm_g": (2, 64), "b_conv_w": (2, 3, 256), "c_sinks": (2, 4), "d_ln_g": (2, 256), "d_ln_b": (2, 256),
    "d_ws": (2, 4, 128, 128), "d_bias": (2, 4, 128), "w_out": (2, D, D), "norm2_g": (2, D),
    "ffn_w_gate": (2, D, DFF), "ffn_w_up": (2, D, DFF), "ffn_conv_w": (2, 3, DFF), "ffn_w_down": (2, DFF, D),
    "final_norm_g": (D,),
}
OUT_SHAPES = {
    "y_p": (4096, D), "y_s": (64, D), "sd_p": (2, 4, 64, 64), "sd_s": (2, 16, 4, 64, 64),
    "ac_p": (2, 3, 768), "ac_s": (2, 48, 768), "bc_p": (2, 2, 256), "bc_s": (2, 32, 256),
    "ck_p": (2, 128, 128), "ck_s": (2, 16, 128, 128), "cv_p": (2, 128, 128), "cv_s": (2, 16, 128, 128),
    "fc_p": (2, 2, DFF), "fc_s": (2, 32, DFF), "dv_s": (2, 64, 256),
}


def build_program():
    nc = bass.Bass("TRN2", target_bir_lowering=False)
    P = Prog(nc, ring=12)
    I = {k: nc.dram_tensor(k, list(s), F32, kind="ExternalInput").ap() for k, s in IN_SHAPES.items()}
    C = {k: nc.dram_tensor("c_" + k, list(s), F32, kind="ExternalInput").ap() for k, s in CONST_SHAPES.items()}
    O = {k: nc.dram_tensor(k, list(s), F32, kind="ExternalOutput").ap() for k, s in OUT_SHAPES.items()}
    for k in list(IN_SHAPES) + ["c_" + k for k in CONST_SHAPES] + list(OUT_SHAPES):
        P.untracked.add(k)
    NC_ = {"ns": 0}

    def small(out, in_):
        P.dma("sp", out, in_, allow_slow_non_contiguous=True)

    def cload(name, shape, src, bf=False):
        t = P.sb("k_" + name, shape)
        P.dma("sp", t[:], src)
        if bf:
            tb = P.sb("kb_" + name, shape, BF16)
            P.copy("dve", tb[:], t[:])
            return t, tb
        return t

    ident, identb = cload("ident", [128, 128], C["ident"], True)
    U = cload("U", [128, 128], C["U"])
    Us = cload("Us", [128, 128], C["Us"])
    ones, onesb = cload("ones", [128, 128], C["ones"], True)
    bones, bonesb = cload("bones", [128, 128], C["bones"], True)
    selg = cload("selg", [8, 512], C["selg"])
    selb = cload("selb", [8, 512], C["selb"])
    sel2 = cload("sel2", [8, 256], C["sel2"])
    perm = cload("perm", [128, 128], C["perm"])
    amask = cload("amask", [128, 256], C["amask"])

    pb = [P.ps("pb%d" % i, [128, 512]) for i in range(7)]
    pT16 = P.ps("pT16", [128, 1024], BF16)

    S2flat = P.sb("S2", [128, 11 * TP])

    def ptrans(dst3, src_rows, R, Cn):
        P.dma("sp", S2flat[0:R, 0:Cn * 128], src_rows)
        for c in range(Cn):
            P.tr(pb[6][:, c * R:(c + 1) * R], S2flat[0:R, c * 128:(c + 1) * 128], ident[0:R, 0:R], inc=(c == Cn - 1))
        P.copy("dve", dst3, pb[6][:, 0:Cn * R].rearrange("p (c r) -> p c r", c=Cn))

    class _Stop(Exception):
        pass

    import os
    stop_at = float(os.environ.get("MK_STOP", "99"))

    lpos = {"i": 0}

    def stage(k):
        if lpos["i"] * 10 + k >= stop_at:
            raise _Stop()

    prm = []
    wtmp_sh = S2flat[:, 0:512].rearrange("p (g s) -> p g s", g=4)
    brow_sh = S2flat[0:1, 512:1024]
    for l in range(2):
        d = {}
        d["g1"] = P.sb("g1_%d" % l, [128, 8])
        ptrans(d["g1"][:].unsqueeze(1), I["norm1_g"][l].rearrange("(k p) -> k p", p=128), 8, 1)
        d["g2"] = P.sb("g2_%d" % l, [128, 8])
        ptrans(d["g2"][:].unsqueeze(1), I["norm2_g"][l].rearrange("(k p) -> k p", p=128), 8, 1)
        d["wA"] = P.sb("wA_%d" % l, [128, 6, 4])
        ptrans(d["wA"][:], I["a_conv_w"][l], 4, 6)
        d["wB"] = P.sb("wB_%d" % l, [128, 2, 3])
        ptrans(d["wB"][:], I["b_conv_w"][l], 3, 2)
        d["wF"] = P.sb("wF_%d" % l, [128, FC, 3])
        ptrans(d["wF"][:], I["ffn_conv_w"][l], 3, FC)
        d["dtb"] = P.sb("dtb_%d" % l, [8, 1])
        P.memset("dve", d["dtb"][:], 0.0)
        small(d["dtb"][4:8, :], I["a_dt_bias"][l].rearrange("(h o) -> h o", o=1))
        al = P.sb("al_%d" % l, [8, 1])
        P.memset("dve", al[:], 0.0)
        small(al[4:8, :], I["a_log"][l].rearrange("(h o) -> h o", o=1))
        d["negA"] = P.sb("negA_%d" % l, [8, 1])
        P.act(d["negA"][:], al[:], AF.Exp)
        P.ts("dve", d["negA"][:], d["negA"][:], -1.0, ALU.mult)
        d["gn"] = P.sb("gn_%d" % l, [128, 1])
        small(d["gn"][0:64, :], I["a_norm_g"][l].rearrange("(h o) -> h o", o=1))
        small(d["gn"][64:128, :], I["a_norm_g"][l].rearrange("(h o) -> h o", o=1))
        d["sink"] = P.sb("sink_%d" % l, [128, 4])
        small(d["sink"][:], I["c_sinks"][l:l + 1, :].partition_broadcast(128).rearrange("p o h -> p (o h)"))
        d["lng"] = P.sb("lng_%d" % l, [128, 2])
        ptrans(d["lng"][:].unsqueeze(1), I["d_ln_g"][l].rearrange("(c p) -> c p", p=128), 2, 1)
        d["lnb"] = P.sb("lnb_%d" % l, [128, 2])
        ptrans(d["lnb"][:].unsqueeze(1), I["d_ln_b"][l].rearrange("(c p) -> c p", p=128), 2, 1)
        d["WT"] = P.sb("WT_%d" % l, [128, 4, 128], BF16)
        wtmp = wtmp_sh
        P.dma("sp", wtmp[:], I["d_ws"][l].rearrange("g t s -> t g s"))
        for g in range(4):
            P.tr(pb[6][:, g * 128:(g + 1) * 128], wtmp[:, g, :], ident[:], inc=(g == 3))
        P.tt("dve", d["WT"][:], pb[6][:].rearrange("p (g t) -> p g t", g=4),
             U[:].unsqueeze(1).to_broadcast([128, 4, 128]), ALU.mult)
        brow = brow_sh
        P.dma("sp", brow[:], I["d_bias"][l:l + 1].rearrange("o g t -> o (g t)"))
        d["brow"] = P.sb("browb_%d" % l, [1, 512], BF16)
        P.copy("dve", d["brow"][:], brow[:])
        d["S"] = [P.sb("S_%d_%d" % (l, jj), [128, 64]) for jj in range(2)]
        d["cA"] = P.sb("cA_%d" % l, [128, 6, 3])
        d["cB"] = P.sb("cB_%d" % l, [128, 2, 2])
        d["cF"] = P.sb("cF_%d" % l, [128, FC, 2])
        d["kprev"] = P.sb("kprev_%d" % l, [128, 128], BF16)
        d["vprev"] = P.sb("vprev_%d" % l, [128, 128], BF16)
        prm.append(d)
    gfin = P.sb("gfin", [128, 8])
    ptrans(gfin[:].unsqueeze(1), I["final_norm_g"].rearrange("(k p) -> k p", p=128), 8, 1)
    onesrow = onesb

    xs_t = P.sb("xs_t", [128, 8, TP])
    hT = P.sb("hT", [128, 8, TP], BF16)
    mixT = P.sb("mixT", [128, 8, TP], BF16)
    sqb = P.sb("sqb", [128, 4, TP], BF16)
    rstd = P.sb("rstd", [128, TP])
    S1 = P.sb("S1", [128, 6 * (TP + 3 * 16)])
    S2 = S2flat[:, 0:8 * TP].rearrange("p (c t) -> p c t", c=8)
    p8 = P.sb("p8", [8, TP])
    sig8 = P.sb("sig8", [8, TP])
    g8 = P.sb("g8", [8, TP])
    gc8 = P.sb("gc8", [8, TP])
    actb = S2flat[:].bitcast(BF16).rearrange("p (j t) -> p j t", j=FC)
    sgb = P.sb("sgb", [128, TP])
    gpre = P.sb("gpre", [128, TP + 2 * 16])
    gcv = P.sb("gcv", [128, TP])
    gpre2 = P.sb("gpre2", [128, TP + 2 * 16])
    xin = P.sb("xio", [128, D])
    yout = xin
    NRING = 5
    wring = [P.sb("wr%d" % i, [128, 8, 512], BF16) for i in range(NRING)]
    wpos = {"w": 0, "d": 0}
    tA = P.sb("tA", [128, TP])
    tB = P.sb("tB", [128, TP])
    tC = gcv
    cosT = sgb
    sinT = gpre
    qTb = P.sb("qTb", [128, 2, TP], BF16)
    kTb = P.sb("kTb", [128, TP], BF16)
    krf = P.sb("krf", [128, TP])
    def T3(name, dt=F32):
        return P.sb(name, [128, 4, 128], dt)
    TH = []
    for jj in range(2):
        t_ = {}
        for nm in ("Eu", "QKd", "Mm0", "Mm1", "Ma0", "Ma1", "Xt"):
            t_[nm] = P.sb("%s_%d" % (nm, jj), [128, 2, 128])
        for nm in ("vb", "kbd", "kdec", "vnew"):
            t_[nm] = P.sb("%s_%d" % (nm, jj), [128, 2, 64])
        t_["nwT"] = P.sb("nwT_%d" % jj, [128, 128], BF16)
        t_["qdec"] = P.sb("qdec_%d" % jj, [128, 128], BF16)
        t_["eG2"] = P.sb("eG2_%d" % jj, [128, 128])
        t_["egl2"] = P.sb("egl2_%d" % jj, [128, 1])
        t_["x8"] = P.sb("x8_%d" % jj, [128, 16])
        t_["bege"] = P.sb("bege_%d" % jj, [128, 2])
        t_["egc"] = P.sb("egc_%d" % jj, [128, 2])
        t_["osq"] = P.sb("osq_%d" % jj, [128, 128], BF16)
        t_["orn"] = P.sb("orn_%d" % jj, [128, 128])
        t_["on_"] = P.sb("on__%d" % jj, [128, 128])
        t_["Sb"] = P.sb("Sb_%d" % jj, [128, 64], BF16)
        TH.append(t_)
    pbv = [pb[i][:, :] for i in range(7)] + [pT16[:, :].bitcast(F32)]
    x8 = P.sb("x8", [128, 16])
    zs = P.sb("zs", [128, 2, TP])
    sc = P.sb("sc", [128, 4, 256]); Pf = sc; Pn = P.sb("Pn", [128, 4, 256], BF16)
    PT = P.sb("PT", [128, 4, 2, 128], BF16)
    mx = P.sb("mx", [128, 4]); nmx = P.sb("nmx", [128, 4]); sm = P.sb("sm", [128, 4]); es = P.sb("es", [128, 4])
    vtokb = P.sb("vtokb", [128, 128], BF16); vtokf = P.sb("vtokf", [128, 128])
    kcache = P.sb("kcache", [128, 128]); vcache = P.sb("vcache", [128, 128])
    dvT = P.sb("dvT", [128, 256], BF16)
    stT = P.sb("stT", [64, 512])
    stF = P.sb("stF", [128, FC, 48])
    ktok = P.sb("ktok", [128, 128])

    evt = {"i": 0}

    def evac(out, in_, scale=None):
        evt["i"] ^= 1
        if scale is not None:
            if evt["i"]:
                P.act(out, in_, AF.Copy, scale=float(scale))
            else:
                P.ts("dve", out, in_, float(scale), ALU.mult)
        elif evt["i"]:
            P.copy("act", out, in_)
        else:
            P.copy("dve", out, in_)

    def wslab(src):
        t = wring[wpos["w"] % NRING]
        wpos["w"] += 1
        ncol = src.shape[1]
        P.dma("pool", t[:, :, 0:ncol], src.rearrange("(k p) n -> p k n", p=128))
        return t

    def bigmm(ps, slab, c0, ncols, rhs, T, m0=0):
        for k in range(KC):
            P.mm(ps[m0:m0 + ncols, 0:T], slab[:, k, c0:c0 + ncols], rhs[:, k, 0:T],
                 start=(k == 0), stop=(k == KC - 1), inc=(k == KC - 1))

    bigi = {"i": 0}

    def bigps():
        bigi["i"] ^= 1
        return pb[bigi["i"]]

    def rmsnorm(g, T, out_bf=True, out=None):
        ps = pb[2]
        for hf in range(2):
            P.act(sqb[:, :, 0:T], xs_t[:, hf * 4:hf * 4 + 4, 0:T], AF.Square)
            for k in range(4):
                P.mm(ps[:, 0:T], onesb[:], sqb[:, k, 0:T], start=(hf == 0 and k == 0), stop=(hf == 1 and k == 3),
                     inc=(k == 3))
        P.act(rstd[:, 0:T], ps[:, 0:T], AF.Sqrt, bias=EPS, scale=1.0 / D)
        P.recip(rstd[:, 0:T], rstd[:, 0:T])
        dst = hT if out is None else out
        for k in range(KC):
            P.stt(dst[:, k, 0:T], xs_t[:, k, 0:T], g[:, k:k + 1], rstd[:, 0:T], ALU.mult, ALU.mult)

    def conv(out3, buf3, w, c, K, seglen, nseg):
        P.ts("dve", out3, buf3[:, :, 0:seglen], w[:, c, 0:1], ALU.mult)
        for j in range(1, K):
            P.stt(out3, buf3[:, :, j:j + seglen], w[:, c, j:j + 1], out3, ALU.mult, ALU.add)

    def store_rows(dst_rows, srcF, nrows, ncols_list):
        nchunk = len(ncols_list)
        for c0 in range(0, nchunk, 4):
            cn = min(4, nchunk - c0)
            for c in range(c0, c0 + cn):
                P.tr(pb[6][0:nrows, (c - c0) * 128:(c - c0 + 1) * 128], srcF(c), ident[:], inc=(c == c0 + cn - 1))
            evac(stT[0:nrows, 0:cn * 128], pb[6][0:nrows, 0:cn * 128])
            P.dma("sp", dst_rows[:, c0 * 128:(c0 + cn) * 128], stT[0:nrows, 0:cn * 128])

    def load_rows(src_rows, nrows, nchunk):
        for c0 in range(0, nchunk, 4):
            cn = min(4, nchunk - c0)
            P.dma("sp", stT[0:nrows, 0:cn * 128], src_rows[:, c0 * 128:(c0 + cn) * 128])
            for c in range(c0, c0 + cn):
                P.tr(pb[6][:, (c - c0) * nrows:(c - c0 + 1) * nrows], stT[0:nrows, (c - c0) * 128:(c - c0 + 1) * 128],
                     ident[0:nrows, 0:nrows], inc=(c == c0 + cn - 1))
            evac(stF[:, c0:c0 + cn, 0:nrows], pb[6][:, 0:cn * nrows].rearrange("p (c r) -> p c r", c=cn))

    def run_layer(l, T, n, nblk, nseg, seglen, prompt, first, last, tile_i):
        d = prm[l]
        hb = lambda ap2, m: ap2.unsqueeze(2).to_broadcast([ap2.shape[0], ap2.shape[1], m])

        def seg3(t2d, K1):
            return t2d[:, 0:nseg * (K1 + seglen)].rearrange("p (s t) -> p s t", s=nseg)

        def tok3(ap2d):
            return ap2d.rearrange("p (s t) -> p s t", s=nseg)

        stage(0)
        rmsnorm(d["g1"], T)
        w_in = I["w_in"][l]
        stage(1)

        pre = [seg3(S1[:, c * (TP + 48):(c + 1) * (TP + 48)], 3) for c in range(6)]
        sl0 = wslab(w_in[:, 0:512])
        sl1 = wslab(w_in[:, 512:1024])
        sl2 = wslab(w_in[:, 1024:1288])
        if prompt:
            for c in range(6):
                if first:
                    P.memset("dve", pre[c][:, :, 0:3], 0.0)
                else:
                    P.copy("dve", pre[c][:, 0, 0:3], d["cA"][:, c, :])
        else:
            load_rows(I["state_delta_conv"][l], 48, 6)
            for c in range(6):
                P.copy("dve", pre[c][:, :, 0:3], stF[:, c, 0:48].rearrange("p (b r) -> p b r", r=3))
        stage(1.1)
        for c in range(4):
            ps = bigps()
            bigmm(ps, sl0, c * 128, 128, hT, T)
            evac(pre[c][:, :, 3:3 + seglen], tok3(ps[:, 0:T]))
        stage(1.2)
        for c in range(2):
            ps = bigps()
            bigmm(ps, sl1, c * 128, 128, hT, T)
            evac(pre[4 + c][:, :, 3:3 + seglen], tok3(ps[:, 0:T]))
        for c in range(2):
            ps = bigps()
            bigmm(ps, sl1, 256 + c * 128, 128, hT, T)
            P.act(zs[:, c, 0:T], ps[:, 0:T], AF.Silu)
        stage(1.3)
        ps = bigps()
        bigmm(ps, sl2, 0, 8, hT, T)
        P.copy("dve", p8[:, 0:T], ps[0:8, 0:T])
        stage(1.4)
        if prompt:
            for c in range(6):
                P.copy("dve", d["cA"][:, c, :], pre[c][:, 0, seglen:seglen + 3])
            if last:
                store_rows(O["ac_p"][l], lambda c: d["cA"][:, c, :], 3, [128] * 6)
        else:
            for c in range(6):
                P.copy("dve", stF[:, c, 0:48].rearrange("p (b r) -> p b r", r=3), pre[c][:, :, seglen:seglen + 3])
            store_rows(O["ac_s"][l], lambda c: stF[:, c, 0:48], 48, [128] * 6)
        stage(1.5)
        for c in range(6):
            conv(tok3(S2[:, c, 0:T]), pre[c], d["wA"], c, 4, seglen, nseg)
        P.act(S2[:, 0:6, 0:T], S2[:, 0:6, 0:T], AF.Silu)
        P.act(sqb[:, 0:4, 0:T], S2[:, 0:4, 0:T], AF.Square)
        sc4 = [tA, tB, gcv, sgb]
        for c in range(4):
            P.mm(pb[c][:, 0:T], bonesb[:], sqb[:, c, 0:T])
        for c in range(4):
            P.act(sc4[c][:, 0:T], pb[c][:, 0:T], AF.Sqrt, bias=EPS)
        for c in range(4):
            P.recip(sc4[c][:, 0:T], sc4[c][:, 0:T])
        for c in range(4):
            if c < 2:
                P.stt(S2[:, c, 0:T], S2[:, c, 0:T], 0.125, sc4[c][:, 0:T], ALU.mult, ALU.mult)
            else:
                P.tt("dve", S2[:, c, 0:T], S2[:, c, 0:T], sc4[c][:, 0:T], ALU.mult)
        stage(1.7)
        P.act(sig8[:, 0:T], p8[:, 0:T], AF.Sigmoid)
        P.act(g8[:, 0:T], p8[:, 0:T], AF.Exp, bias=d["dtb"][:])
        P.act(g8[:, 0:T], g8[:, 0:T], AF.Ln, bias=1.0)
        P.ts("dve", g8[:, 0:T], g8[:, 0:T], d["negA"][:], ALU.mult)
        stage(1.8)
        for b in range(nblk):
            cs = slice(b * n, (b + 1) * n)
            P.mm(pb[5][0:n, 0:8], g8[:, cs], ident[0:8, 0:8])
            P.copy("dve", x8[0:n, 0:8], pb[5][0:n, 0:8])
            P.mm(pb[5][0:8, 16:16 + n], x8[0:n, 0:8], U[0:n, 0:n])
            P.copy("dve", gc8[:, cs], pb[5][0:8, 16:16 + n])
        nlev = int(np.log2(n)) - 1
        stage(2)
        bk = [(pbv[0], pbv[1], pbv[2], pbv[3]), (pbv[4], pbv[5], pbv[6], pbv[7])]

        def gdn_thread(j):
            qa, qb, qc, qd = bk[j]
            t = TH[j]
            Sj = d["S"][j]
            Eu, QKd, Xt = t["Eu"], t["QKd"], t["Xt"]
            Mm = [t["Mm0"], t["Mm1"]]
            Ma = [t["Ma0"], t["Ma1"]]
            tmpE, Eus = Ma[1], Mm[1]
            vb_, kbd, kdec, vnew = t["vb"], t["kbd"], t["kdec"], t["vnew"]
            nwT, qdec, eG2, egl2, x8t = t["nwT"], t["qdec"], t["eG2"], t["egl2"], t["x8"]
            bege, egc, osq, orn, on_, Sb = t["bege"], t["egc"], t["osq"], t["orn"], t["on_"], t["Sb"]
            q2 = lambda t3: t3[0:n, :, 0:n]
            v3 = lambda ap, w: ap.rearrange("p (e t) -> p e t", e=2)
            Ub = U[0:n, 0:n].unsqueeze(1).to_broadcast([n, 2, n])
            Usb = Us[0:n, 0:n].unsqueeze(1).to_broadcast([n, 2, n])
            Ib = ident[0:n, 0:n].unsqueeze(1).to_broadcast([n, 2, n])
            for b in range(nblk):
                cs = slice(b * n, (b + 1) * n)
                if not prompt:
                    for e in range(2):
                        P.dma("sp", Sj[e * 64:(e + 1) * 64, :], I["state_delta"][l, b, 2 * j + e])
                elif first and b == 0:
                    P.memset("dve", Sj[:], 0.0)
                P.copy("act", Sb[:], Sj[:])
                P.tr(qc[0:n, 0:128], S2[:, 2 + j, cs], ident[:], inc=False)
                P.tr(qc[0:n, 128:256], S2[:, 4 + j, cs], ident[:])
                kv_ps = qc[0:n, 0:256].rearrange("p (a e d) -> p a e d", a=2, e=2)
                P.mm(qb[0:n, 256:264], sig8[:, cs], ident[0:8, 0:8], inc=False)
                P.mm(qb[0:n, 264:272], gc8[:, cs], ident[0:8, 0:8])
                G_ps = v3(qd[:, 0:2 * n], n)
                for e in range(2):
                    h = 2 * j + e
                    P.mm(G_ps[:, e, :], selg[:, h * 128:(h + 1) * 128], gc8[:, cs], inc=(e == 1))
                G2_ps = qb[:, 272:272 + n]
                P.mm(G2_ps, sel2[:, j * 128:(j + 1) * 128], gc8[:, cs])
                yield
                P.copy("dve", x8t[0:n, :], qb[0:n, 256:272])
                bt = x8t[0:n, 2 * j:2 * j + 2]
                gct = x8t[0:n, 12 + 2 * j:14 + 2 * j]
                P.act(egc[0:n, :], gct, AF.Exp)
                P.tt("dve", bege[0:n, :], bt, egc[0:n, :], ALU.mult)
                P.act(eG2[:, 0:n], G2_ps, AF.Exp)
                P.copy("dve", egl2[:], eG2[:, n - 1:n])
                P.tt("dve", qdec[:, 0:n], S2[:, j, cs], eG2[:, 0:n], ALU.mult)
                P.tt("dve", q2(tmpE), G_ps[0:n], hb(gct, n), ALU.subtract)
                P.ts("dve", q2(tmpE), q2(tmpE), 0.0, ALU.min)
                P.act(q2(tmpE), q2(tmpE), AF.Exp)
                Bb_ps = v3(qd[0:n, 0:2 * n], n)
                for e in range(2):
                    h = 2 * j + e
                    P.mm(Bb_ps[:, e, :], selb[:, h * 128:h * 128 + n], sig8[:, cs], inc=(e == 1))
                km = [orn[:, 0:n], on_[:, 0:n]]
                for e in range(2):
                    P.ts("dve", km[e], S2[:, 2 + j, cs], bones[:, e * 64:e * 64 + 1], ALU.mult)
                KK_ps = v3(qa[0:n, 0:2 * n], n)
                QK_ps = v3(qb[0:n, 0:2 * n], n)
                for e in range(2):
                    P.mm(KK_ps[:, e, :], km[e], S2[:, 2 + j, cs], inc=(e == 1))
                for e in range(2):
                    P.mm(QK_ps[:, e, :], km[e], S2[:, j, cs], inc=(e == 1))
                yield
                P.tt("dve", q2(Eu), q2(tmpE), Ub, ALU.mult)
                P.tt("dve", q2(Eus), q2(tmpE), Usb, ALU.mult)
                P.tt("dve", q2(Eus), q2(Eus), Bb_ps, ALU.mult)
                P.tt("dve", q2(Mm[0]), KK_ps, q2(Eus), ALU.mult)
                P.tt("dve", q2(QKd), QK_ps, q2(Eu), ALU.mult)
                P.tt("dve", vb_[0:n], kv_ps[:, 1], hb(bt, 64), ALU.mult)
                P.tt("dve", kbd[0:n], kv_ps[:, 0], hb(bege[0:n, :], 64), ALU.mult)
                P.tt("dve", kdec[0:n], kv_ps[:, 0], Eu[0:n, :, n - 1:n].to_broadcast([n, 2, 64]), ALU.mult)
                At_ps = v3(qd[0:n, 0:2 * n], n)
                for e in range(2):
                    P.tr(At_ps[:, e, :], Mm[0][0:n, e, 0:n], ident[0:n, 0:n], inc=(e == 1))
                yield
                P.copy("act", q2(Ma[0]), At_ps)
                P.tt("dve", q2(Xt), Ib, q2(Mm[0]), ALU.subtract)
                for k in range(1, nlev + 1):
                    cur, prv = k % 2, (k - 1) % 2
                    Pa_ps = v3(qa[0:n, 0:2 * n], n)
                    for e in range(2):
                        P.mm(Pa_ps[:, e, :], Mm[prv][0:n, e, 0:n], Ma[prv][0:n, e, 0:n], inc=(e == 1))
                    if k < nlev:
                        Pm_ps = v3(qb[0:n, 0:2 * n], n)
                        for e in range(2):
                            P.mm(Pm_ps[:, e, :], Ma[prv][0:n, e, 0:n], Mm[prv][0:n, e, 0:n], inc=(e == 1))
                    yield
                    P.copy("act", q2(Ma[cur]), Pa_ps)
                    if k < nlev:
                        P.copy("dve", q2(Mm[cur]), Pm_ps)
                    X_ps = v3(qd[0:n, 0:2 * n], n)
                    for e in range(2):
                        P.mm(X_ps[:, e, :], Ma[cur][0:n, e, 0:n], Xt[0:n, e, 0:n], inc=(e == 1))
                    yield
                    P.tt("dve", q2(Xt), q2(Xt), X_ps, ALU.add)
                wT_ps = qc[:, 0:n]
                for e in range(2):
                    P.mm(wT_ps[e * 64:(e + 1) * 64, :], kbd[0:n, e, :], Xt[0:n, e, 0:n], inc=(e == 1))
                yield
                P.ts("dve", nwT[:, 0:n], wT_ps, -1.0, ALU.mult)
                vn_ps = qd[0:n, 0:128].rearrange("p (e d) -> p e d", e=2)
                for e in range(2):
                    pr = slice(e * 64, (e + 1) * 64)
                    P.mm(vn_ps[:, e, :], Xt[0:n, e, 0:n], vb_[0:n, e, :], start=True, stop=False, inc=False)
                    P.mm(vn_ps[:, e, :], nwT[pr, 0:n], Sb[pr, :], start=False, stop=True, inc=(e == 1))
                yield
                P.copy("act", vnew[0:n], vn_ps)
                o_ps = qc[:, 128:128 + n]
                for e in range(2):
                    pr = slice(e * 64, (e + 1) * 64)
                    P.mm(o_ps[pr, :], Sb[pr, :], qdec[pr, 0:n], start=True, stop=False, inc=False)
                    P.mm(o_ps[pr, :], vnew[0:n, e, :], QKd[0:n, e, 0:n], start=False, stop=True, inc=(e == 1))
                Su_ps = qb[:, 384:448]
                for e in range(2):
                    pr = slice(e * 64, (e + 1) * 64)
                    P.mm(Su_ps[pr, :], kdec[0:n, e, :], vnew[0:n, e, :], inc=(e == 1))
                yield
                P.stt(Sj[:], Sj[:], egl2[:, 0:1], Su_ps, ALU.mult, ALU.add)
                P.act(osq[:, 0:n], o_ps, AF.Square)
                ss_ps = qa[:, 256:256 + n]
                P.mm(ss_ps, bonesb[:], osq[:, 0:n])
                yield
                P.act(orn[:, 0:n], ss_ps, AF.Sqrt, bias=EPS, scale=1.0 / 64)
                P.recip(orn[:, 0:n], orn[:, 0:n])
                P.tt("dve", on_[:, 0:n], o_ps, orn[:, 0:n], ALU.mult)
                P.stt(mixT[:, j, cs], on_[:, 0:n], d["gn"][:], zs[:, j, cs], ALU.mult, ALU.mult)
                if not prompt:
                    for e in range(2):
                        P.dma("sp", O["sd_s"][l, b, 2 * j + e], Sj[e * 64:(e + 1) * 64, :])
            if prompt and last:
                for e in range(2):
                    P.dma("sp", O["sd_p"][l, 2 * j + e], Sj[e * 64:(e + 1) * 64, :])

        gens = [gdn_thread(0), gdn_thread(1)]
        alive = [True, True]
        while any(alive):
            for gi, g in enumerate(gens):
                if alive[gi]:
                    try:
                        next(g)
                    except StopIteration:
                        alive[gi] = False


        stage(3)
        sl3 = wslab(w_in[:, 1288:1800])
        preB = [seg3(S1[:, c * (TP + 48):(c + 1) * (TP + 48)], 2) for c in range(2)]
        if prompt:
            for c in range(2):
                if first:
                    P.memset("dve", preB[c][:, :, 0:2], 0.0)
                else:
                    P.copy("dve", preB[c][:, 0, 0:2], d["cB"][:, c, :])
        else:
            load_rows(I["state_shortconv"][l], 32, 2)
            for c in range(2):
                P.copy("dve", preB[c][:, :, 0:2], stF[:, c, 0:32].rearrange("p (b r) -> p b r", r=2))
        for c in range(2):
            ps = bigps()
            bigmm(ps, sl2, 8 + c * 128, 128, hT, T)
            evac(S2[:, c, 0:T], ps[:, 0:T])
        for c in range(2):
            ps = bigps()
            bigmm(ps, sl3, c * 128, 128, hT, T)
            evac(tA[:, 0:T], ps[:, 0:T])
            ps = bigps()
            bigmm(ps, sl3, 256 + c * 128, 128, hT, T)
            P.tt("dve", preB[c][:, :, 2:2 + seglen], tok3(tA[:, 0:T]), tok3(ps[:, 0:T]), ALU.mult)
        if prompt:
            for c in range(2):
                P.copy("dve", d["cB"][:, c, :], preB[c][:, 0, seglen:seglen + 2])
            if last:
                store_rows(O["bc_p"][l], lambda c: d["cB"][:, c, :], 2, [128] * 2)
        else:
            for c in range(2):
                P.copy("dve", stF[:, c, 0:32].rearrange("p (b r) -> p b r", r=2), preB[c][:, :, seglen:seglen + 2])
            store_rows(O["bc_s"][l], lambda c: stF[:, c, 0:32], 32, [128] * 2)
        for c in range(2):
            conv(tok3(tB[:, 0:T]), preB[c], d["wB"], c, 3, seglen, nseg)
            P.tt("dve", mixT[:, 2 + c, 0:T], S2[:, c, 0:T], tB[:, 0:T], ALU.mult)

        stage(4)
        sl4 = wslab(w_in[:, 1800:2312])
        if prompt:
            P.dma("sp", cosT[:, 0:T], C["cosp"][:, tile_i * TP:(tile_i + 1) * TP])
            P.dma("sp", sinT[:, 0:T], C["sinp"][:, tile_i * TP:(tile_i + 1) * TP])
        else:
            P.dma("sp", cosT[:, 0:T], C["coss"])
            P.dma("sp", sinT[:, 0:T], C["sins"])

        def rope(dst_bf, dst_f, src):
            P.mm(pb[2][:, 0:T], perm[:], src)
            P.tt("dve", tB[:, 0:T], src, cosT[:, 0:T], ALU.mult)
            P.tt("dve", tC[:, 0:T], pb[2][:, 0:T], sinT[:, 0:T], ALU.mult)
            if dst_f is not None:
                P.tt("dve", dst_f, tB[:, 0:T], tC[:, 0:T], ALU.add)
                P.copy("act", dst_bf, dst_f)
            else:
                P.tt("dve", dst_bf, tB[:, 0:T], tC[:, 0:T], ALU.add)

        for j in range(2):
            ps = bigps()
            for half in range(2):
                h = half * 2 + j
                bigmm(ps, sl4, h * 64, 64, hT, T, m0=half * 64)
            evac(tA[:, 0:T], ps[:, 0:T])
            rope(qTb[:, j, 0:T], None, tA[:, 0:T])
        ps = bigps()
        bigmm(ps, sl4, 256, 128, hT, T)
        evac(tA[:, 0:T], ps[:, 0:T])
        rope(kTb[:, 0:T], krf[:, 0:T], tA[:, 0:T])
        ps = bigps()
        bigmm(ps, sl4, 384, 128, hT, T)
        evac(S2[:, 7, 0:T], ps[:, 0:T])
        if not prompt:
            P.tr(pb[6][0:64, 0:128], krf[:, 0:64], ident[:])
            evac(ktok[0:64, :], pb[6][0:64, 0:128])
            P.tr(pb[6][0:64, 128:256], S2[:, 7, 0:64], ident[:])
            evac(vtokf[0:64, :], pb[6][0:64, 128:256])
            for b in range(16):
                P.dma("sp", O["ck_s"][l, b, 124:128, :], ktok[b * 4:(b + 1) * 4, :])
                P.dma("sp", O["cv_s"][l, b, 124:128, :], vtokf[b * 4:(b + 1) * 4, :])
            P.dma("sp", O["ck_s"][l, :, 0:124, :], I["cache_win_k"][l, :, 4:128, :])
            P.dma("sp", O["cv_s"][l, :, 0:124, :], I["cache_win_v"][l, :, 4:128, :])
        for b in range(nblk):
            cs = slice(b * n, (b + 1) * n)
            hasprev = (not prompt) or (not (first and b == 0))
            off = 0 if hasprev else 128
            W = 256 - off if prompt else 128 + n
            if prompt:
                kpv = d["kprev"]
                vpv = d["vprev"]
            else:
                P.dma("sp", kcache[:], I["cache_win_k"][l, b])
                P.dma("sp", vcache[:], I["cache_win_v"][l, b])
                P.tr(pb[6][:, 0:128], kcache[:], ident[:])
                evac(d["kprev"][:], pb[6][:, 0:128])
                P.copy("dve", d["vprev"][:], vcache[:])
                kpv = d["kprev"]
                vpv = d["vprev"]
            P.tr(pb[6][0:n, 256:384], S2[:, 7, cs], ident[:])
            P.copy("dve", vtokb[0:n, :], pb[6][0:n, 256:384])
            if prompt and last and b == nblk - 1:
                P.copy("act", vtokf[0:n, :], pb[6][0:n, 256:384])
                P.dma("sp", O["cv_p"][l], vtokf[:])
                P.tr(pb[6][:, 384:512], krf[:, cs], ident[:])
                evac(ktok[:], pb[6][:, 384:512])
                P.dma("sp", O["ck_p"][l], ktok[:])
            S_ps = [pb[0][0:n, :].rearrange("p (h k) -> p h k", h=2), pb[1][0:n, :].rearrange("p (h k) -> p h k", h=2)]
            for h in range(4):
                pr = slice((h // 2) * 64, (h // 2) * 64 + 64)
                dst = S_ps[h // 2][:, h % 2, :]
                if hasprev:
                    P.mm(dst[:, 0:128], qTb[pr, h % 2, cs], kpv[pr, :], inc=False)
                P.mm(dst[:, 128:128 + n], qTb[pr, h % 2, cs], kTb[pr, cs], inc=(h % 2 == 1))
            mk = amask[0:n, off:off + W].unsqueeze(1).to_broadcast([n, 2, W])
            for hh in range(2):
                P.tt("dve", sc[0:n, 2 * hh:2 * hh + 2, 0:W], S_ps[hh][:, :, off:off + W], mk, ALU.add)
            P.reduce(mx[0:n, :], sc[0:n, :, 0:W], ALU.max)
            P.ts("dve", nmx[0:n, :], mx[0:n, :], -0.125, ALU.mult)
            for h in range(4):
                P.act(Pf[0:n, h, 0:W], sc[0:n, h, 0:W], AF.Exp, bias=nmx[0:n, h:h + 1], scale=0.125,
                      accum_out=sm[0:n, h:h + 1])
            P.tt("dve", es[0:n, :], nmx[0:n, :], d["sink"][0:n, :], ALU.add)
            P.act(es[0:n, :], es[0:n, :], AF.Exp)
            P.tt("dve", es[0:n, :], es[0:n, :], sm[0:n, :], ALU.add)
            P.recip(es[0:n, :], es[0:n, :])
            P.tt("dve", Pn[0:n, :, 0:W], Pf[0:n, :, 0:W], hb(es[0:n, :], W), ALU.mult)
            PT_ps = pT16[:, :].rearrange("p (h s q) -> p h s q", h=4, s=2)
            for h in range(4):
                if hasprev:
                    P.tr(PT_ps[:, h, 0, 0:n], Pn[0:n, h, 0:128], identb[0:n, 0:n], inc=False)
                P.tr(PT_ps[0:n, h, 1, 0:n], Pn[0:n, h, 128 - off:128 - off + n], identb[0:n, 0:n], inc=(h == 3))
            if hasprev:
                P.copy("act", PT[:, :, 0, 0:n], PT_ps[:, :, 0, 0:n])
            P.copy("dve", PT[0:n, :, 1, 0:n], PT_ps[0:n, :, 1, 0:n])
            o_ps = pb[6][:, 0:2 * n].rearrange("p (j t) -> p j t", j=2)
            for h in range(4):
                kvs = slice((h // 2) * 64, (h // 2) * 64 + 64)
                dst = o_ps[(h % 2) * 64:(h % 2) * 64 + 64, h // 2, :]
                if hasprev:
                    P.mm(dst, vpv[:, kvs], PT[:, h, 0, 0:n], start=True, stop=False, inc=False)
                P.mm(dst, vtokb[0:n, kvs], PT[0:n, h, 1, 0:n], start=(not hasprev), stop=True, inc=(h == 3))
            evac(mixT[:, 4:6, cs], o_ps)
            if prompt:
                P.copy("dve", d["kprev"][:], kTb[:, cs])
                P.copy("act", d["vprev"][:], vtokb[:])

        stage(5)
        sl5 = wslab(w_in[:, 2312:2824])

        def gelu(dst, ps):
            P.act(tB[:, 0:T], ps, AF.Square)
            P.ts("dve", tB[:, 0:T], tB[:, 0:T], 0.044715, ALU.mult, 1.0, ALU.add)
            P.tt("dve", tB[:, 0:T], tB[:, 0:T], ps, ALU.mult)
            P.act(tB[:, 0:T], tB[:, 0:T], AF.Sigmoid, scale=1.5957691216057308)
            P.tt("dve", dst, tB[:, 0:T], ps, ALU.mult)

        gsc = [tB, gcv, sgb, tA]
        for g_ in range(4):
            bigmm(pb[g_], sl5, g_ * 128, 128, hT, T)
        for g_ in range(4):
            P.act(gsc[g_][:, 0:T], pb[g_][:, 0:T], AF.Square)
        for g_ in range(4):
            P.ts("dve", gsc[g_][:, 0:T], gsc[g_][:, 0:T], 0.044715, ALU.mult, 1.0, ALU.add)
        for g_ in range(4):
            P.tt("dve", gsc[g_][:, 0:T], gsc[g_][:, 0:T], pb[g_][:, 0:T], ALU.mult)
        for g_ in range(4):
            P.act(gsc[g_][:, 0:T], gsc[g_][:, 0:T], AF.Sigmoid, scale=1.5957691216057308)
        for g_ in range(4):
            P.tt("dve", S2[:, g_, 0:T], gsc[g_][:, 0:T], pb[g_][:, 0:T], ALU.mult)
        P.copy("act", sqb[:, 0:2, 0:T], S2[:, 2:4, 0:T])
        P.act(sqb[:, 2:4, 0:T], S2[:, 2:4, 0:T], AF.Square)
        for c in range(2):
            P.mm(pb[2][:, 0:T], onesb[:], sqb[:, c, 0:T], start=(c == 0), stop=(c == 1), inc=(c == 1))
        for c in range(2):
            P.mm(pb[3][:, 0:T], onesb[:], sqb[:, 2 + c, 0:T], start=(c == 0), stop=(c == 1), inc=(c == 1))
        P.act(tA[:, 0:T], pb[2][:, 0:T], AF.Copy, scale=1.0 / 256)
        P.tt("dve", tB[:, 0:T], tA[:, 0:T], tA[:, 0:T], ALU.mult)
        P.stt(tB[:, 0:T], pb[3][:, 0:T], 1.0 / 256, tB[:, 0:T], ALU.mult, ALU.subtract)
        P.act(tB[:, 0:T], tB[:, 0:T], AF.Sqrt, bias=EPS)
        P.recip(tB[:, 0:T], tB[:, 0:T])
        for c in range(2):
            P.tt("dve", S2[:, 2 + c, 0:T], S2[:, 2 + c, 0:T], tA[:, 0:T], ALU.subtract)
            P.tt("dve", S2[:, 2 + c, 0:T], S2[:, 2 + c, 0:T], tB[:, 0:T], ALU.mult)
            P.ts("dve", S2[:, 2 + c, 0:T], S2[:, 2 + c, 0:T], d["lng"][:, c:c + 1], ALU.mult, d["lnb"][:, c:c + 1], ALU.add)
        if not prompt:
            store_rows(O["dv_s"][l], lambda c: S2[:, 2 + c, 0:T], 64, [128] * 2)
        for b in range(nblk):
            cs = slice(b * n, (b + 1) * n)
            for c in range(2):
                P.tr(pb[2][0:n, c * 128:(c + 1) * 128], S2[:, 2 + c, cs], ident[:], inc=(c == 1))
            evac(dvT[0:n, :], pb[2][0:n, 0:256])
            m_ps = pb[3][:, 0:2 * n].rearrange("p (j t) -> p j t", j=2)
            for g in range(4):
                dst = m_ps[(g % 2) * 64:(g % 2) * 64 + 64, g // 2, :]
                P.mm(dst, dvT[0:n, g * 64:(g + 1) * 64], d["WT"][0:n, g, 0:n], start=True, stop=False, inc=False)
                P.mm(dst, onesrow[0:1, 0:64], d["brow"][0:1, g * 128:g * 128 + n], start=False, stop=True, inc=(g == 3))
            P.tt("dve", mixT[:, 6:8, cs], S2[:, 0:2, cs], m_ps, ALU.mult)

        stage(6)
        for s in range(2):
            sl = wslab(I["w_out"][l][:, s * 512:(s + 1) * 512])
            for c in range(4):
                ps = bigps()
                bigmm(ps, sl, c * 128, 128, mixT, T)
                P.tt("dve", xs_t[:, s * 4 + c, 0:T], xs_t[:, s * 4 + c, 0:T], ps[:, 0:T], ALU.add)

        stage(7)
        rmsnorm(d["g2"], T)
        gp3s = [seg3(gpre[:, :], 2), seg3(gpre2[:, :], 2)]
        gcvs = [gcv, sgb]
        if not prompt:
            load_rows(I["state_ffn_conv"][l], 32, FC)
        slabs = {}

        def fslab(kind, s_):
            if (kind, s_) not in slabs:
                ncol = 512 if s_ < 5 else 256
                w_ = I["ffn_w_gate" if kind == "g" else "ffn_w_up"][l]
                slabs[(kind, s_)] = wslab(w_[:, s_ * 512:s_ * 512 + ncol])
            return slabs[(kind, s_)]

        fpi = {"i": 0}

        def fbank():
            fpi["i"] = (fpi["i"] + 1) % 4
            return pb[fpi["i"]]

        def stageA(j):
            s_, c = divmod(j, 4)
            gp3, gcv_ = gp3s[j % 2], gcvs[j % 2]
            ps = fbank()
            bigmm(ps, fslab("g", s_), c * 128, 128, hT, T)
            if prompt:
                if first:
                    P.memset("dve", gp3[:, :, 0:2], 0.0)
                else:
                    P.copy("dve", gp3[:, 0, 0:2], d["cF"][:, j, :])
            else:
                P.copy("dve", gp3[:, :, 0:2], stF[:, j, 0:32].rearrange("p (b r) -> p b r", r=2))
            P.copy("act", gp3[:, :, 2:2 + seglen], tok3(ps[:, 0:T]))
            if prompt:
                P.copy("dve", d["cF"][:, j, :], gp3[:, 0, seglen:seglen + 2])
            else:
                P.copy("dve", stF[:, j, 0:32].rearrange("p (b r) -> p b r", r=2), gp3[:, :, seglen:seglen + 2])
            conv(tok3(gcv_[:, 0:T]), gp3, d["wF"], j, 3, seglen, nseg)

        def stageB(j):
            s_, c = divmod(j, 4)
            gcv_ = gcvs[j % 2]
            P.act(gcv_[:, 0:T], gcv_[:, 0:T], AF.Silu)
            ps = fbank()
            bigmm(ps, fslab("u", s_), c * 128, 128, hT, T)
            P.tt("dve", actb[:, j, 0:T], gcv_[:, 0:T], ps[:, 0:T], ALU.mult)

        stageA(0)
        for j in range(FC):
            if j + 1 < FC:
                stageA(j + 1)
            stageB(j)
        stage(7.5)
        if prompt and last:
            store_rows(O["fc_p"][l], lambda c: d["cF"][:, c, :], 2, [128] * FC)
        if not prompt:
            store_rows(O["fc_s"][l], lambda c: stF[:, c, 0:32], 32, [128] * FC)
        wd = I["ffn_w_down"][l].rearrange("(j p) n -> p j n", p=128)
        acc = [pb[i][:, :] for i in range(7)] + [pT16[:, :].bitcast(F32)]
        for jg in range((FC + 3) // 4):
            nj = min(4, FC - 4 * jg)
            tw = wring[wpos["w"] % NRING]
            wpos["w"] += 1
            t = tw[:, :, :].rearrange("p k n -> p (k n)").rearrange("p (j n) -> p j n", j=4)
            P.dma("pool", t[:, 0:nj, :], wd[:, 4 * jg:4 * jg + nj, :])
            for jj in range(nj):
                j = 4 * jg + jj
                for oc in range(8):
                    P.mm(acc[oc][:, 0:T], t[:, jj, oc * 128:(oc + 1) * 128], actb[:, j, 0:T],
                         start=(j == 0), stop=(j == FC - 1), inc=(oc == 7 or j == FC - 1))
        for oc in range(8):
            P.tt("dve", xs_t[:, oc, 0:T], xs_t[:, oc, 0:T], acc[oc][:, 0:T], ALU.add)
        stage(8)

    def run_tile(prompt, ti):
        if prompt:
            T, n, nblk, nseg, seglen = TP, 128, 4, 1, TP
            src = I["xp"][ti * TP:(ti + 1) * TP, :]
            dst = O["y_p"][ti * TP:(ti + 1) * TP, :]
        else:
            T, n, nblk, nseg, seglen = TS, 4, 16, 16, 4
            src = I["xs"]
            dst = O["y_s"]
        nrb = (T + 127) // 128
        for rb in range(nrb):
            r = min(128, T - rb * 128)
            P.dma("sp", xin[0:r, :], src[rb * 128:rb * 128 + r, :])
            for g in range(2):
                for k in range(4):
                    P.tr(pb[6][:, k * r:(k + 1) * r], xin[0:r, (g * 4 + k) * 128:(g * 4 + k + 1) * 128],
                         ident[0:r, 0:r], inc=(k == 3))
                evac(xs_t[:, g * 4:g * 4 + 4, rb * 128:rb * 128 + r],
                     pb[6][:, 0:4 * r].rearrange("p (k t) -> p k t", k=4))
        for li, l in enumerate([int(c) for c in os.environ.get("MK_LAYERS", "01")]):
            lpos["i"] = li
            run_layer(l, T, n, nblk, nseg, seglen, prompt, first=(ti == 0), last=(ti == NTP - 1), tile_i=ti)
        rmsnorm(gfin, T, out=S2)
        for rb in range(nrb):
            r = min(128, T - rb * 128)
            for g in range(2):
                for k in range(4):
                    P.tr(pb[6][0:r, k * 128:(k + 1) * 128], S2[:, g * 4 + k, rb * 128:rb * 128 + r], ident[:],
                         inc=(k == 3))
                evac(yout[0:r, g * 512:(g + 1) * 512], pb[6][0:r, :])
            P.dma("sp", dst[rb * 128:rb * 128 + r, :], yout[0:r, :])

    ntiles = int(os.environ.get("MK_NTILES", NTP))
    try:
        for ti in range(ntiles):
            run_tile(True, ti)
        if os.environ.get("MK_NOSAMPLE") is None:
            run_tile(False, 0)
    except _Stop:
        P.dma("sp", O["y_s"][0:64, :], xs_t[0:64, 0, 0:TP].rearrange("p t -> p t") if False else xin[0:64, :])
    P.emit()
    P.close()
    return nc, P


_CACHE = {}


def kernel(**inp):
    f = np.float32
    if "nc" not in _CACHE:
        _CACHE["nc"], _ = build_program()
        _CACHE["consts"] = _consts()
    nc = _CACHE["nc"]
    consts = _CACHE["consts"]
    shared = {}
    for k in IN_SHAPES:
        if k in ("xp", "xs", "state_delta", "state_delta_conv", "state_shortconv", "cache_win_k", "cache_win_v",
                 "state_ffn_conv"):
            continue
        shared[k] = np.ascontiguousarray(np.asarray(inp[k], f))
    for k, v in consts.items():
        shared["c_" + k] = np.ascontiguousarray(v.reshape(CONST_SHAPES[k]).astype(f))
    in_maps = []
    for c in range(8):
        m = dict(shared)
        bs = slice(16 * c, 16 * c + 16)
        m["xp"] = np.ascontiguousarray(np.asarray(inp["x_prompt"][c], f))
        m["xs"] = np.ascontiguousarray(np.asarray(inp["x_sample"][bs], f).reshape(64, D))
        m["state_delta"] = np.ascontiguousarray(np.asarray(inp["state_delta"][:, bs], f))
        m["state_delta_conv"] = np.ascontiguousarray(np.asarray(inp["state_delta_conv"][:, bs], f).reshape(2, 48, 768))
        m["state_shortconv"] = np.ascontiguousarray(np.asarray(inp["state_shortconv"][:, bs], f).reshape(2, 32, 256))
        m["cache_win_k"] = np.ascontiguousarray(np.asarray(inp["cache_win_k"][:, bs], f).reshape(2, 16, 128, 128))
        m["cache_win_v"] = np.ascontiguousarray(np.asarray(inp["cache_win_v"][:, bs], f).reshape(2, 16, 128, 128))
        m["state_ffn_conv"] = np.ascontiguousarray(np.asarray(inp["state_ffn_conv"][:, bs], f).reshape(2, 32, DFF))
        in_maps.append(m)
    res = run_bass_kernel_spmd(nc, in_maps, core_ids=list(range(8)))
    R = res.results

    def cat_p(k, shape):
        return np.stack([np.asarray(R[c][k], f).reshape(shape) for c in range(8)], axis=1)

    def cat_s(k, shape):
        return np.concatenate([np.asarray(R[c][k], f).reshape(shape) for c in range(8)], axis=1)

    y_p = np.stack([np.asarray(R[c]["y_p"], f) for c in range(8)], axis=0)
    y_s = np.concatenate([np.asarray(R[c]["y_s"], f).reshape(16, 4, D) for c in range(8)], axis=0)
    return (
        y_p, y_s,
        cat_p("sd_p", (2, 4, 64, 64)), cat_s("sd_s", (2, 16, 4, 64, 64)),
        cat_p("ac_p", (2, 3, 768)), cat_s("ac_s", (2, 16, 3, 768)),
        cat_p("bc_p", (2, 2, 256)), cat_s("bc_s", (2, 16, 2, 256)),
        cat_p("ck_p", (2, 128, 2, 64)), cat_s("ck_s", (2, 16, 128, 2, 64)),
        cat_p("cv_p", (2, 128, 2, 64)), cat_s("cv_s", (2, 16, 128, 2, 64)),
        cat_p("fc_p", (2, 2, DFF)), cat_s("fc_s", (2, 16, 2, DFF)),
        cat_s("dv_s", (2, 16, 4, 256)),
    )
```

```python
import numpy as np
import concourse.bass as bass
import concourse.mybir as mybir

F32 = mybir.dt.float32
BF16 = mybir.dt.bfloat16
I32 = mybir.dt.int32
ALU = mybir.AluOpType
AF = mybir.ActivationFunctionType
AX = mybir.AxisListType

ENGINES = ("pe", "act", "dve", "pool", "sp")


class _Buf:
    __slots__ = ("last_w", "readers")

    def __init__(self):
        self.last_w = None
        self.readers = []


class Prog:
    def __init__(self, nc, ring=12):
        self.nc = nc
        self.ops = {e: [] for e in ENGINES}
        self.cnt = {e: 0 for e in ENGINES}
        self.known = {e: {} for e in ENGINES}
        self.bufs = {}
        self.stack = []
        self.sems = {}
        self.nsem = 0
        self.ring_n = ring
        self.ring = {}
        self.ring_pos = {}
        self.untracked = set()
        self.nops = 0

    def _sem(self, key):
        if key not in self.sems:
            cm = self.nc.semaphore("s_" + key)
            self.sems[key] = cm.__enter__()
            self.stack.append(cm)
            self.nsem += 1
        return self.sems[key]

    def sb(self, name, shape, dtype=F32):
        cm = self.nc.sbuf_tensor(name, list(shape), dtype)
        t = cm.__enter__()
        self.stack.append(cm)
        return t

    def ps(self, name, shape, dtype=F32):
        cm = self.nc.psum_tensor(name, list(shape), dtype)
        t = cm.__enter__()
        self.stack.append(cm)
        return t

    def dram(self, name, shape, dtype=F32, kind="Internal"):
        return self.nc.dram_tensor(name, list(shape), dtype, kind=kind)

    def _key(self, ap):
        if isinstance(ap, str):
            return ap
        return ap.tensor.name

    def _deps(self, eng, reads, writes):
        waits = []
        known = self.known[eng]

        def need(ev):
            if ev is None:
                return
            sk, val, kn = ev
            if eng == "pe" and sk == "pe":
                return
            if known.get(sk, 0) >= val:
                return
            waits.append((sk, val))
            known[sk] = val
            for k, v in kn.items():
                if known.get(k, 0) < v:
                    known[k] = v

        rk = [self._key(a) for a in reads]
        wk = [self._key(a) for a in writes]
        rk = [k for k in rk if k not in self.untracked]
        wk = [k for k in wk if k not in self.untracked]
        for k in rk:
            b = self.bufs.setdefault(k, _Buf())
            need(b.last_w)
        for k in wk:
            b = self.bufs.setdefault(k, _Buf())
            need(b.last_w)
            for r in b.readers:
                need(r)
        return waits, rk, wk

    def _commit(self, ev, rk, wk):
        for k in rk:
            if k not in wk:
                self.bufs[k].readers.append(ev)
        for k in wk:
            b = self.bufs[k]
            b.last_w = ev
            b.readers = []

    def op(self, eng, fn, reads, writes, inc=True):
        waits, rk, wk = self._deps(eng, reads, writes)
        ev = (eng, self.cnt[eng] + 1, dict(self.known[eng]))
        if inc:
            self.cnt[eng] += 1
        self.ops[eng].append((waits, fn, (eng, 1) if inc else None))
        self._commit(ev, rk, wk)
        self.nops += 1

    def dma(self, eng, out, in_, extra_reads=(), extra_writes=(), **kw):
        if eng not in self.ring:
            self.ring[eng] = [["d%s%d" % (eng, i), 0] for i in range(self.ring_n)]
            self.ring_pos[eng] = 0
        slot = self.ring[eng][self.ring_pos[eng]]
        self.ring_pos[eng] = (self.ring_pos[eng] + 1) % self.ring_n
        waits, rk, wk = self._deps(eng, [in_] + list(extra_reads), [out] + list(extra_writes))
        known = self.known[eng]
        if slot[1] > 0 and known.get(slot[0], 0) < 16 * slot[1]:
            waits.append((slot[0], 16 * slot[1]))
            known[slot[0]] = 16 * slot[1]
        slot[1] += 1
        ev = (slot[0], 16 * slot[1], dict(known))

        def fn(e, out=out, in_=in_, kw=kw):
            return e.dma_start(out=out, in_=in_, **kw)

        self.ops[eng].append((waits, fn, (slot[0], 16)))
        self._commit(ev, rk, wk)
        self.nops += 1

    def mm(self, out, lhsT, rhs, start=True, stop=True, inc=True, **kw):
        self.op("pe", lambda e: e.matmul(out, lhsT, rhs, start=start, stop=stop, **kw),
                [lhsT, rhs] + ([] if start else [out]), [out], inc=inc)

    def tr(self, out, in_, ident, inc=True):
        self.op("pe", lambda e: e.transpose(out, in_, ident), [in_, ident], [out], inc=inc)

    def act(self, out, in_, func, bias=None, scale=None, accum_out=None, eng="act"):
        kw = {}
        reads = [in_]
        writes = [out]
        if bias is not None:
            kw["bias"] = bias
            if not isinstance(bias, (int, float)):
                reads.append(bias)
        if scale is not None:
            kw["scale"] = scale
            if not isinstance(scale, (int, float)):
                reads.append(scale)
        if accum_out is not None:
            kw["accum_out"] = accum_out
            writes.append(accum_out)
        self.op("act", lambda e: e.activation(out, in_, func, **kw), reads, writes)

    def tt(self, eng, out, in0, in1, op):
        self.op(eng, lambda e: e.tensor_tensor(out, in0, in1, op), [in0, in1], [out])

    def ts(self, eng, out, in0, s1, op0, s2=None, op1=None, accum_out=None):
        reads = [in0]
        if not isinstance(s1, (int, float)):
            reads.append(s1)
        if s2 is not None and not isinstance(s2, (int, float)):
            reads.append(s2)
        writes = [out]
        kw = {}
        if op1 is not None:
            kw["op1"] = op1
        if accum_out is not None:
            kw["accum_out"] = accum_out
            writes.append(accum_out)
        self.op(eng, lambda e: e.tensor_scalar(out, in0, s1, s2, op0, **kw), reads, writes)

    def stt(self, out, in0, scalar, in1, op0, op1, eng="dve"):
        reads = [in0, in1]
        if not isinstance(scalar, (int, float)):
            reads.append(scalar)
        self.op(eng, lambda e: e.scalar_tensor_tensor(out, in0, scalar, in1, op0, op1), reads, [out])

    def copy(self, eng, out, in_):
        if eng == "act":
            self.op("act", lambda e: e.copy(out, in_), [in_], [out])
        else:
            self.op(eng, lambda e: e.tensor_copy(out, in_), [in_], [out])

    def memset(self, eng, out, val):
        self.op(eng, lambda e: e.memset(out, val), [], [out])

    def reduce(self, out, in_, op, axis=AX.X, eng="dve"):
        self.op(eng, lambda e: e.tensor_reduce(out, in_, axis, op), [in_], [out])

    def recip(self, out, in_):
        self.op("dve", lambda e: e.reciprocal(out, in_), [in_], [out])

    def emit(self):
        nc = self.nc
        fin = []
        for eng, ring in self.ring.items():
            for key, uses in ring:
                if uses > 0:
                    fin.append((key, 16 * uses))
        for e in ("pe", "act", "dve", "pool"):
            if self.cnt[e] > 0:
                fin.append((e, self.cnt[e]))
        for k in list(self.sems) + [f[0] for f in fin]:
            self._sem(k)
        for eng in ENGINES:
            for waits, fn, inc in self.ops[eng]:
                for sk, v in waits:
                    self._sem(sk)
                if inc is not None:
                    self._sem(inc[0])
        sems = self.sems
        ops = self.ops

        def replay(e, lst, final=False):
            for waits, fn, inc in lst:
                for sk, v in waits:
                    e.wait_ge(sems[sk], v)
                ins = fn(e)
                if inc is not None:
                    ins.then_inc(sems[inc[0]], inc[1])
            if final:
                for sk, v in fin:
                    e.wait_ge(sems[sk], v)

        with nc.Block() as block:
            @block.sync
            def _(e):
                replay(e, ops["sp"], final=True)

            @block.tensor
            def _(e):
                replay(e, ops["pe"])

            @block.scalar
            def _(e):
                replay(e, ops["act"])

            @block.vector
            def _(e):
                replay(e, ops["dve"])

            @block.gpsimd
            def _(e):
                replay(e, ops["pool"])

    def close(self):
        while self.stack:
            self.stack.pop().__exit__(None, None, None)


from concourse.bass_utils import run_bass_kernel_spmd

D = 1024
KC = 8
PTOT = 2824
DFF = 2816
FC = 22
EPS = 1e-6
PAST = 16384
NEG = -30000.0
TP = 512
NTP = 8
TS = 64


def _consts():
    f = np.float32
    c = {}
    c["ident"] = np.eye(128, dtype=f)
    s = np.arange(128)
    c["U"] = (s[:, None] <= s[None, :]).astype(f)
    c["Us"] = (s[:, None] < s[None, :]).astype(f)
    c["ones"] = np.ones((128, 128), f)
    bo = np.zeros((128, 128), f)
    bo[:64, :64] = 1
    bo[64:, 64:] = 1
    c["bones"] = bo
    sel = np.zeros((8, 4, 128), f)
    for h in range(4):
        sel[4 + h, h, :] = 1
    c["selg"] = sel.reshape(8, 512)
    selb = np.zeros((8, 4, 128), f)
    for h in range(4):
        selb[h, h, :] = 1
    c["selb"] = selb.reshape(8, 512)
    sel2 = np.zeros((8, 2, 128), f)
    for j in range(2):
        sel2[4 + 2 * j, j, :64] = 1
        sel2[4 + 2 * j + 1, j, 64:] = 1
    c["sel2"] = sel2.reshape(8, 256)
    perm = np.zeros((128, 128), f)
    for half in range(2):
        for d in range(8):
            r = half * 64 + d
            perm[r + 8, r] = -1.0
            perm[r, r + 8] = 1.0
    c["perm"] = perm
    inv = np.power(np.float32(500000.0), -np.arange(8, dtype=f) * f(2.0 / 16)).astype(f)

    def cs(pos):
        ang = pos.astype(f)[None, :] * inv[:, None]
        C = np.ones((128, pos.shape[0]), f)
        S = np.zeros((128, pos.shape[0]), f)
        for half in range(2):
            for q in range(2):
                r0 = half * 64 + q * 8
                C[r0:r0 + 8] = np.cos(ang)
                S[r0:r0 + 8] = np.sin(ang)
        return C, S

    c["cosp"], c["sinp"] = cs(np.arange(4096))
    c["coss"], c["sins"] = cs(np.tile(PAST + np.arange(4), 16))
    i = np.arange(128)[:, None]
    j = np.arange(256)[None, :]
    c["amask"] = np.where((j > i) & (j <= i + 128), 0.0, NEG).astype(f)
    rp = np.ones((8, TP), f)
    rp[:, ::128] = 0
    c["rstp"] = rp
    rs = np.ones((8, TS), f)
    rs[:, ::4] = 0
    c["rsts"] = rs
    return c


CONST_SHAPES = {"ident": (128, 128), "U": (128, 128), "Us": (128, 128), "ones": (128, 128), "bones": (128, 128),
                "selg": (8, 512), "selb": (8, 512), "sel2": (8, 256), "perm": (128, 128),
                "cosp": (128, 4096), "sinp": (128, 4096), "coss": (128, 64), "sins": (128, 64),
                "amask": (128, 256), "rstp": (8, TP), "rsts": (8, TS)}

IN_SHAPES = {
    "xp": (4096, D), "xs": (64, D),
    "state_delta": (2, 16, 4, 64, 64), "state_delta_conv": (2, 48, 768), "state_shortconv": (2, 32, 256),
    "cache_win_k": (2, 16, 128, 128), "cache_win_v": (2, 16, 128, 128), "state_ffn_conv": (2, 32, DFF),
    "norm1_g": (2, D), "w_in": (2, D, PTOT), "a_conv_w": (2, 4, 768), "a_log": (2, 4), "a_dt_bias": (2, 4),
    "a_norm_g": (2, 64), "b_conv_w": (2, 3, 256), "c_sinks": (2, 4), "d_ln_g": (2, 256), "d_ln_b": (2, 256),
    "d_ws": (2, 4, 128, 128), "d_bias": (2, 4, 128), "w_out": (2, D, D), "norm2_g": (2, D),
    "ffn_w_gate": (2, D, DFF), "ffn_w_up": (2, D, DFF), "ffn_conv_w": (2, 3, DFF), "ffn_w_down": (2, DFF, D),
    "final_norm_g": (D,),
}
OUT_SHAPES = {
    "y_p": (4096, D), "y_s": (64, D), "sd_p": (2, 4, 64, 64), "sd_s": (2, 16, 4, 64, 64),
    "ac_p": (2, 3, 768), "ac_s": (2, 48, 768), "bc_p": (2, 2, 256), "bc_s": (2, 32, 256),
    "ck_p": (2, 128, 128), "ck_s": (2, 16, 128, 128), "cv_p": (2, 128, 128), "cv_s": (2, 16, 128, 128),
    "fc_p": (2, 2, DFF), "fc_s": (2, 32, DFF), "dv_s": (2, 64, 256),
}


def build_program():
    nc = bass.Bass("TRN2", target_bir_lowering=False)
    P = Prog(nc, ring=12)
    I = {k: nc.dram_tensor(k, list(s), F32, kind="ExternalInput").ap() for k, s in IN_SHAPES.items()}
    C = {k: nc.dram_tensor("c_" + k, list(s), F32, kind="ExternalInput").ap() for k, s in CONST_SHAPES.items()}
    O = {k: nc.dram_tensor(k, list(s), F32, kind="ExternalOutput").ap() for k, s in OUT_SHAPES.items()}
    for k in list(IN_SHAPES) + ["c_" + k for k in CONST_SHAPES] + list(OUT_SHAPES):
        P.untracked.add(k)
    NC_ = {"ns": 0}

    def small(out, in_):
        P.dma("sp", out, in_, allow_slow_non_contiguous=True)

    def cload(name, shape, src, bf=False):
        t = P.sb("k_" + name, shape)
        P.dma("sp", t[:], src)
        if bf:
            tb = P.sb("kb_" + name, shape, BF16)
            P.copy("dve", tb[:], t[:])
            return t, tb
        return t

    ident, identb = cload("ident", [128, 128], C["ident"], True)
    U = cload("U", [128, 128], C["U"])
    Us = cload("Us", [128, 128], C["Us"])
    ones, onesb = cload("ones", [128, 128], C["ones"], True)
    bones, bonesb = cload("bones", [128, 128], C["bones"], True)
    selg = cload("selg", [8, 512], C["selg"])
    selb = cload("selb", [8, 512], C["selb"])
    sel2 = cload("sel2", [8, 256], C["sel2"])
    perm = cload("perm", [128, 128], C["perm"])
    amask = cload("amask", [128, 256], C["amask"])

    pb = [P.ps("pb%d" % i, [128, 512]) for i in range(7)]
    pT16 = P.ps("pT16", [128, 1024], BF16)

    S2flat = P.sb("S2", [128, 11 * TP])

    def ptrans(dst3, src_rows, R, Cn):
        P.dma("sp", S2flat[0:R, 0:Cn * 128], src_rows)
        for c in range(Cn):
            P.tr(pb[6][:, c * R:(c + 1) * R], S2flat[0:R, c * 128:(c + 1) * 128], ident[0:R, 0:R], inc=(c == Cn - 1))
        P.copy("dve", dst3, pb[6][:, 0:Cn * R].rearrange("p (c r) -> p c r", c=Cn))

    class _Stop(Exception):
        pass

    import os
    stop_at = float(os.environ.get("MK_STOP", "99"))

    lpos = {"i": 0}

    def stage(k):
        if lpos["i"] * 10 + k >= stop_at:
            raise _Stop()

    prm = []
    wtmp_sh = S2flat[:, 0:512].rearrange("p (g s) -> p g s", g=4)
    brow_sh = S2flat[0:1, 512:1024]
    for l in range(2):
        d = {}
        d["g1"] = P.sb("g1_%d" % l, [128, 8])
        ptrans(d["g1"][:].unsqueeze(1), I["norm1_g"][l].rearrange("(k p) -> k p", p=128), 8, 1)
        d["g2"] = P.sb("g2_%d" % l, [128, 8])
        ptrans(d["g2"][:].unsqueeze(1), I["norm2_g"][l].rearrange("(k p) -> k p", p=128), 8, 1)
        d["wA"] = P.sb("wA_%d" % l, [128, 6, 4])
        ptrans(d["wA"][:], I["a_conv_w"][l], 4, 6)
        d["wB"] = P.sb("wB_%d" % l, [128, 2, 3])
        ptrans(d["wB"][:], I["b_conv_w"][l], 3, 2)
        d["wF"] = P.sb("wF_%d" % l, [128, FC, 3])
        ptrans(d["wF"][:], I["ffn_conv_w"][l], 3, FC)
        d["dtb"] = P.sb("dtb_%d" % l, [8, 1])
        P.memset("dve", d["dtb"][:], 0.0)
        small(d["dtb"][4:8, :], I["a_dt_bias"][l].rearrange("(h o) -> h o", o=1))
        al = P.sb("al_%d" % l, [8, 1])
        P.memset("dve", al[:], 0.0)
        small(al[4:8, :], I["a_log"][l].rearrange("(h o) -> h o", o=1))
        d["negA"] = P.sb("negA_%d" % l, [8, 1])
        P.act(d["negA"][:], al[:], AF.Exp)
        P.ts("dve", d["negA"][:], d["negA"][:], -1.0, ALU.mult)
        d["gn"] = P.sb("gn_%d" % l, [128, 1])
        small(d["gn"][0:64, :], I["a_norm_g"][l].rearrange("(h o) -> h o", o=1))
        small(d["gn"][64:128, :], I["a_norm_g"][l].rearrange("(h o) -> h o", o=1))
        d["sink"] = P.sb("sink_%d" % l, [128, 4])
        small(d["sink"][:], I["c_sinks"][l:l + 1, :].partition_broadcast(128).rearrange("p o h -> p (o h)"))
        d["lng"] = P.sb("lng_%d" % l, [128, 2])
        ptrans(d["lng"][:].unsqueeze(1), I["d_ln_g"][l].rearrange("(c p) -> c p", p=128), 2, 1)
        d["lnb"] = P.sb("lnb_%d" % l, [128, 2])
        ptrans(d["lnb"][:].unsqueeze(1), I["d_ln_b"][l].rearrange("(c p) -> c p", p=128), 2, 1)
        d["WT"] = P.sb("WT_%d" % l, [128, 4, 128], BF16)
        wtmp = wtmp_sh
        P.dma("sp", wtmp[:], I["d_ws"][l].rearrange("g t s -> t g s"))
        for g in range(4):
            P.tr(pb[6][:, g * 128:(g + 1) * 128], wtmp[:, g, :], ident[:], inc=(g == 3))
        P.tt("dve", d["WT"][:], pb[6][:].rearrange("p (g t) -> p g t", g=4),
             U[:].unsqueeze(1).to_broadcast([128, 4, 128]), ALU.mult)
        brow = brow_sh
        P.dma("sp", brow[:], I["d_bias"][l:l + 1].rearrange("o g t -> o (g t)"))
        d["brow"] = P.sb("browb_%d" % l, [1, 512], BF16)
        P.copy("dve", d["brow"][:], brow[:])
        d["S"] = [P.sb("S_%d_%d" % (l, jj), [128, 64]) for jj in range(2)]
        d["cA"] = P.sb("cA_%d" % l, [128, 6, 3])
        d["cB"] = P.sb("cB_%d" % l, [128, 2, 2])
        d["cF"] = P.sb("cF_%d" % l, [128, FC, 2])
        d["kprev"] = P.sb("kprev_%d" % l, [128, 128], BF16)
        d["vprev"] = P.sb("vprev_%d" % l, [128, 128], BF16)
        prm.append(d)
    gfin = P.sb("gfin", [128, 8])
    ptrans(gfin[:].unsqueeze(1), I["final_norm_g"].rearrange("(k p) -> k p", p=128), 8, 1)
    onesrow = onesb

    xs_t = P.sb("xs_t", [128, 8, TP])
    hT = P.sb("hT", [128, 8, TP], BF16)
    mixT = P.sb("mixT", [128, 8, TP], BF16)
    sqb = P.sb("sqb", [128, 4, TP], BF16)
    rstd = P.sb("rstd", [128, TP])
    S1 = P.sb("S1", [128, 6 * (TP + 3 * 16)])
    S2 = S2flat[:, 0:8 * TP].rearrange("p (c t) -> p c t", c=8)
    p8 = P.sb("p8", [8, TP])
    sig8 = P.sb("sig8", [8, TP])
    g8 = P.sb("g8", [8, TP])
    gc8 = P.sb("gc8", [8, TP])
    actb = S2flat[:].bitcast(BF16).rearrange("p (j t) -> p j t", j=FC)
    sgb = P.sb("sgb", [128, TP])
    gpre = P.sb("gpre", [128, TP + 2 * 16])
    gcv = P.sb("gcv", [128, TP])
    gpre2 = P.sb("gpre2", [128, TP + 2 * 16])
    xin = P.sb("xio", [128, D])
    yout = xin
    NRING = 5
    wring = [P.sb("wr%d" % i, [128, 8, 512], BF16) for i in range(NRING)]
    wpos = {"w": 0, "d": 0}
    tA = P.sb("tA", [128, TP])
    tB = P.sb("tB", [128, TP])
    tC = gcv
    cosT = sgb
    sinT = gpre
    qTb = P.sb("qTb", [128, 2, TP], BF16)
    kTb = P.sb("kTb", [128, TP], BF16)
    krf = P.sb("krf", [128, TP])
    def T3(name, dt=F32):
        return P.sb(name, [128, 4, 128], dt)
    TH = []
    for jj in range(2):
        t_ = {}
        for nm in ("Eu", "QKd", "Mm0", "Mm1", "Ma0", "Ma1", "Xt"):
            t_[nm] = P.sb("%s_%d" % (nm, jj), [128, 2, 128])
        for nm in ("vb", "kbd", "kdec", "vnew"):
            t_[nm] = P.sb("%s_%d" % (nm, jj), [128, 2, 64])
        t_["nwT"] = P.sb("nwT_%d" % jj, [128, 128], BF16)
        t_["qdec"] = P.sb("qdec_%d" % jj, [128, 128], BF16)
        t_["eG2"] = P.sb("eG2_%d" % jj, [128, 128])
        t_["egl2"] = P.sb("egl2_%d" % jj, [128, 1])
        t_["x8"] = P.sb("x8_%d" % jj, [128, 16])
        t_["bege"] = P.sb("bege_%d" % jj, [128, 2])
        t_["egc"] = P.sb("egc_%d" % jj, [128, 2])
        t_["osq"] = P.sb("osq_%d" % jj, [128, 128], BF16)
        t_["orn"] = P.sb("orn_%d" % jj, [128, 128])
        t_["on_"] = P.sb("on__%d" % jj, [128, 128])
        t_["Sb"] = P.sb("Sb_%d" % jj, [128, 64], BF16)
        TH.append(t_)
    pbv = [pb[i][:, :] for i in range(7)] + [pT16[:, :].bitcast(F32)]
    x8 = P.sb("x8", [128, 16])
    zs = P.sb("zs", [128, 2, TP])
    sc = P.sb("sc", [128, 4, 256]); Pf = sc; Pn = P.sb("Pn", [128, 4, 256], BF16)
    PT = P.sb("PT", [128, 4, 2, 128], BF16)
    mx = P.sb("mx", [128, 4]); nmx = P.sb("nmx", [128, 4]); sm = P.sb("sm", [128, 4]); es = P.sb("es", [128, 4])
    vtokb = P.sb("vtokb", [128, 128], BF16); vtokf = P.sb("vtokf", [128, 128])
    kcache = P.sb("kcache", [128, 128]); vcache = P.sb("vcache", [128, 128])
    dvT = P.sb("dvT", [128, 256], BF16)
    stT = P.sb("stT", [64, 512])
    stF = P.sb("stF", [128, FC, 48])
    ktok = P.sb("ktok", [128, 128])

    evt = {"i": 0}

    def evac(out, in_, scale=None):
        evt["i"] ^= 1
        if scale is not None:
            if evt["i"]:
                P.act(out, in_, AF.Copy, scale=float(scale))
            else:
                P.ts("dve", out, in_, float(scale), ALU.mult)
        elif evt["i"]:
            P.copy("act", out, in_)
        else:
            P.copy("dve", out, in_)

    def wslab(src):
        t = wring[wpos["w"] % NRING]
        wpos["w"] += 1
        ncol = src.shape[1]
        P.dma("pool", t[:, :, 0:ncol], src.rearrange("(k p) n -> p k n", p=128))
        return t

    def bigmm(ps, slab, c0, ncols, rhs, T, m0=0):
        for k in range(KC):
            P.mm(ps[m0:m0 + ncols, 0:T], slab[:, k, c0:c0 + ncols], rhs[:, k, 0:T],
                 start=(k == 0), stop=(k == KC - 1), inc=(k == KC - 1))

    bigi = {"i": 0}

    def bigps():
        bigi["i"] ^= 1
        return pb[bigi["i"]]

    def rmsnorm(g, T, out_bf=True, out=None):
        ps = pb[2]
        for hf in range(2):
            P.act(sqb[:, :, 0:T], xs_t[:, hf * 4:hf * 4 + 4, 0:T], AF.Square)
            for k in range(4):
                P.mm(ps[:, 0:T], onesb[:], sqb[:, k, 0:T], start=(hf == 0 and k == 0), stop=(hf == 1 and k == 3),
                     inc=(k == 3))
        P.act(rstd[:, 0:T], ps[:, 0:T], AF.Sqrt, bias=EPS, scale=1.0 / D)
        P.recip(rstd[:, 0:T], rstd[:, 0:T])
        dst = hT if out is None else out
        for k in range(KC):
            P.stt(dst[:, k, 0:T], xs_t[:, k, 0:T], g[:, k:k + 1], rstd[:, 0:T], ALU.mult, ALU.mult)

    def conv(out3, buf3, w, c, K, seglen, nseg):
        P.ts("dve", out3, buf3[:, :, 0:seglen], w[:, c, 0:1], ALU.mult)
        for j in range(1, K):
            P.stt(out3, buf3[:, :, j:j + seglen], w[:, c, j:j + 1], out3, ALU.mult, ALU.add)

    def store_rows(dst_rows, srcF, nrows, ncols_list):
        nchunk = len(ncols_list)
        for c0 in range(0, nchunk, 4):
            cn = min(4, nchunk - c0)
            for c in range(c0, c0 + cn):
                P.tr(pb[6][0:nrows, (c - c0) * 128:(c - c0 + 1) * 128], srcF(c), ident[:], inc=(c == c0 + cn - 1))
            evac(stT[0:nrows, 0:cn * 128], pb[6][0:nrows, 0:cn * 128])
            P.dma("sp", dst_rows[:, c0 * 128:(c0 + cn) * 128], stT[0:nrows, 0:cn * 128])

    def load_rows(src_rows, nrows, nchunk):
        for c0 in range(0, nchunk, 4):
            cn = min(4, nchunk - c0)
            P.dma("sp", stT[0:nrows, 0:cn * 128], src_rows[:, c0 * 128:(c0 + cn) * 128])
            for c in range(c0, c0 + cn):
                P.tr(pb[6][:, (c - c0) * nrows:(c - c0 + 1) * nrows], stT[0:nrows, (c - c0) * 128:(c - c0 + 1) * 128],
                     ident[0:nrows, 0:nrows], inc=(c == c0 + cn - 1))
            evac(stF[:, c0:c0 + cn, 0:nrows], pb[6][:, 0:cn * nrows].rearrange("p (c r) -> p c r", c=cn))

    def run_layer(l, T, n, nblk, nseg, seglen, prompt, first, last, tile_i):
        d = prm[l]
        hb = lambda ap2, m: ap2.unsqueeze(2).to_broadcast([ap2.shape[0], ap2.shape[1], m])

        def seg3(t2d, K1):
            return t2d[:, 0:nseg * (K1 + seglen)].rearrange("p (s t) -> p s t", s=nseg)

        def tok3(ap2d):
            return ap2d.rearrange("p (s t) -> p s t", s=nseg)

        stage(0)
        rmsnorm(d["g1"], T)
        w_in = I["w_in"][l]
        stage(1)

        pre = [seg3(S1[:, c * (TP + 48):(c + 1) * (TP + 48)], 3) for c in range(6)]
        sl0 = wslab(w_in[:, 0:512])
        sl1 = wslab(w_in[:, 512:1024])
        sl2 = wslab(w_in[:, 1024:1288])
        if prompt:
            for c in range(6):
                if first:
                    P.memset("dve", pre[c][:, :, 0:3], 0.0)
                else:
                    P.copy("dve", pre[c][:, 0, 0:3], d["cA"][:, c, :])
        else:
            load_rows(I["state_delta_conv"][l], 48, 6)
            for c in range(6):
                P.copy("dve", pre[c][:, :, 0:3], stF[:, c, 0:48].rearrange("p (b r) -> p b r", r=3))
        stage(1.1)
        for c in range(4):
            ps = bigps()
            bigmm(ps, sl0, c * 128, 128, hT, T)
            evac(pre[c][:, :, 3:3 + seglen], tok3(ps[:, 0:T]))
        stage(1.2)
        for c in range(2):
            ps = bigps()
            bigmm(ps, sl1, c * 128, 128, hT, T)
            evac(pre[4 + c][:, :, 3:3 + seglen], tok3(ps[:, 0:T]))
        for c in range(2):
            ps = bigps()
            bigmm(ps, sl1, 256 + c * 128, 128, hT, T)
            P.act(zs[:, c, 0:T], ps[:, 0:T], AF.Silu)
        stage(1.3)
        ps = bigps()
        bigmm(ps, sl2, 0, 8, hT, T)
        P.copy("dve", p8[:, 0:T], ps[0:8, 0:T])
        stage(1.4)
        if prompt:
            for c in range(6):
                P.copy("dve", d["cA"][:, c, :], pre[c][:, 0, seglen:seglen + 3])
            if last:
                store_rows(O["ac_p"][l], lambda c: d["cA"][:, c, :], 3, [128] * 6)
        else:
            for c in range(6):
                P.copy("dve", stF[:, c, 0:48].rearrange("p (b r) -> p b r", r=3), pre[c][:, :, seglen:seglen + 3])
            store_rows(O["ac_s"][l], lambda c: stF[:, c, 0:48], 48, [128] * 6)
        stage(1.5)
        for c in range(6):
            conv(tok3(S2[:, c, 0:T]), pre[c], d["wA"], c, 4, seglen, nseg)
        P.act(S2[:, 0:6, 0:T], S2[:, 0:6, 0:T], AF.Silu)
        P.act(sqb[:, 0:4, 0:T], S2[:, 0:4, 0:T], AF.Square)
        sc4 = [tA, tB, gcv, sgb]
        for c in range(4):
            P.mm(pb[c][:, 0:T], bonesb[:], sqb[:, c, 0:T])
        for c in range(4):
            P.act(sc4[c][:, 0:T], pb[c][:, 0:T], AF.Sqrt, bias=EPS)
        for c in range(4):
            P.recip(sc4[c][:, 0:T], sc4[c][:, 0:T])
        for c in range(4):
            if c < 2:
                P.stt(S2[:, c, 0:T], S2[:, c, 0:T], 0.125, sc4[c][:, 0:T], ALU.mult, ALU.mult)
            else:
                P.tt("dve", S2[:, c, 0:T], S2[:, c, 0:T], sc4[c][:, 0:T], ALU.mult)
        stage(1.7)
        P.act(sig8[:, 0:T], p8[:, 0:T], AF.Sigmoid)
        P.act(g8[:, 0:T], p8[:, 0:T], AF.Exp, bias=d["dtb"][:])
        P.act(g8[:, 0:T], g8[:, 0:T], AF.Ln, bias=1.0)
        P.ts("dve", g8[:, 0:T], g8[:, 0:T], d["negA"][:], ALU.mult)
        stage(1.8)
        for b in range(nblk):
            cs = slice(b * n, (b + 1) * n)
            P.mm(pb[5][0:n, 0:8], g8[:, cs], ident[0:8, 0:8])
            P.copy("dve", x8[0:n, 0:8], pb[5][0:n, 0:8])
            P.mm(pb[5][0:8, 16:16 + n], x8[0:n, 0:8], U[0:n, 0:n])
            P.copy("dve", gc8[:, cs], pb[5][0:8, 16:16 + n])
        nlev = int(np.log2(n)) - 1
        stage(2)
        bk = [(pbv[0], pbv[1], pbv[2], pbv[3]), (pbv[4], pbv[5], pbv[6], pbv[7])]

        def gdn_thread(j):
            qa, qb, qc, qd = bk[j]
            t = TH[j]
            Sj = d["S"][j]
            Eu, QKd, Xt = t["Eu"], t["QKd"], t["Xt"]
            Mm = [t["Mm0"], t["Mm1"]]
            Ma = [t["Ma0"], t["Ma1"]]
            tmpE, Eus = Ma[1], Mm[1]
            vb_, kbd, kdec, vnew = t["vb"], t["kbd"], t["kdec"], t["vnew"]
            nwT, qdec, eG2, egl2, x8t = t["nwT"], t["qdec"], t["eG2"], t["egl2"], t["x8"]
            bege, egc, osq, orn, on_, Sb = t["bege"], t["egc"], t["osq"], t["orn"], t["on_"], t["Sb"]
            q2 = lambda t3: t3[0:n, :, 0:n]
            v3 = lambda ap, w: ap.rearrange("p (e t) -> p e t", e=2)
            Ub = U[0:n, 0:n].unsqueeze(1).to_broadcast([n, 2, n])
            Usb = Us[0:n, 0:n].unsqueeze(1).to_broadcast([n, 2, n])
            Ib = ident[0:n, 0:n].unsqueeze(1).to_broadcast([n, 2, n])
            for b in range(nblk):
                cs = slice(b * n, (b + 1) * n)
                if not prompt:
                    for e in range(2):
                        P.dma("sp", Sj[e * 64:(e + 1) * 64, :], I["state_delta"][l, b, 2 * j + e])
                elif first and b == 0:
                    P.memset("dve", Sj[:], 0.0)
                P.tr(qc[0:n, 0:128], S2[:, 2 + j, cs], ident[:], inc=False)
                P.tr(qc[0:n, 128:256], S2[:, 4 + j, cs], ident[:])
                kv_ps = qc[0:n, 0:256].rearrange("p (a e d) -> p a e d", a=2, e=2)
                P.mm(qb[0:n, 256:264], sig8[:, cs], ident[0:8, 0:8], inc=False)
                P.mm(qb[0:n, 264:272], gc8[:, cs], ident[0:8, 0:8])
                G_ps = v3(qd[:, 0:2 * n], n)
                for e in range(2):
                    h = 2 * j + e
                    P.mm(G_ps[:, e, :], selg[:, h * 128:(h + 1) * 128], gc8[:, cs], inc=(e == 1))
                G2_ps = qb[:, 272:272 + n]
                P.mm(G2_ps, sel2[:, j * 128:(j + 1) * 128], gc8[:, cs])
                yield
                P.copy("dve", x8t[0:n, :], qb[0:n, 256:272])
                bt = x8t[0:n, 2 * j:2 * j + 2]
                gct = x8t[0:n, 12 + 2 * j:14 + 2 * j]
                P.act(egc[0:n, :], gct, AF.Exp)
                P.tt("dve", bege[0:n, :], bt, egc[0:n, :], ALU.mult)
                P.act(eG2[:, 0:n], G2_ps, AF.Exp)
                P.copy("dve", egl2[:], eG2[:, n - 1:n])
                P.tt("dve", qdec[:, 0:n], S2[:, j, cs], eG2[:, 0:n], ALU.mult)
                P.tt("dve", q2(tmpE), G_ps[0:n], hb(gct, n), ALU.subtract)
                P.ts("dve", q2(tmpE), q2(tmpE), 0.0, ALU.min)
                P.act(q2(tmpE), q2(tmpE), AF.Exp)
                Bb_ps = v3(qd[0:n, 0:2 * n], n)
                for e in range(2):
                    h = 2 * j + e
                    P.mm(Bb_ps[:, e, :], selb[:, h * 128:h * 128 + n], sig8[:, cs], inc=(e == 1))
                km = [orn[:, 0:n], on_[:, 0:n]]
                for e in range(2):
                    P.ts("dve", km[e], S2[:, 2 + j, cs], bones[:, e * 64:e * 64 + 1], ALU.mult)
                KK_ps = v3(qa[0:n, 0:2 * n], n)
                QK_ps = v3(qb[0:n, 0:2 * n], n)
                for e in range(2):
                    P.mm(KK_ps[:, e, :], km[e], S2[:, 2 + j, cs], inc=(e == 1))
                for e in range(2):
                    P.mm(QK_ps[:, e, :], km[e], S2[:, j, cs], inc=(e == 1))
                yield
                P.tt("dve", q2(Eu), q2(tmpE), Ub, ALU.mult)
                P.tt("dve", q2(Eus), q2(tmpE), Usb, ALU.mult)
                P.tt("dve", q2(Eus), q2(Eus), Bb_ps, ALU.mult)
                P.tt("dve", q2(Mm[0]), KK_ps, q2(Eus), ALU.mult)
                P.tt("dve", q2(QKd), QK_ps, q2(Eu), ALU.mult)
                P.tt("dve", vb_[0:n], kv_ps[:, 1], hb(bt, 64), ALU.mult)
                P.tt("dve", kbd[0:n], kv_ps[:, 0], hb(bege[0:n, :], 64), ALU.mult)
                P.tt("dve", kdec[0:n], kv_ps[:, 0], Eu[0:n, :, n - 1:n].to_broadcast([n, 2, 64]), ALU.mult)
                At_ps = v3(qd[0:n, 0:2 * n], n)
                for e in range(2):
                    P.tr(At_ps[:, e, :], Mm[0][0:n, e, 0:n], ident[0:n, 0:n], inc=(e == 1))
                yield
                P.copy("act", q2(Ma[0]), At_ps)
                P.tt("dve", q2(Xt), Ib, q2(Mm[0]), ALU.subtract)
                for k in range(1, nlev + 1):
                    cur, prv = k % 2, (k - 1) % 2
                    Pa_ps = v3(qa[0:n, 0:2 * n], n)
                    for e in range(2):
                        P.mm(Pa_ps[:, e, :], Mm[prv][0:n, e, 0:n], Ma[prv][0:n, e, 0:n], inc=(e == 1))
                    if k < nlev:
                        Pm_ps = v3(qb[0:n, 0:2 * n], n)
                        for e in range(2):
                            P.mm(Pm_ps[:, e, :], Ma[prv][0:n, e, 0:n], Mm[prv][0:n, e, 0:n], inc=(e == 1))
                    yield
                    P.copy("act", q2(Ma[cur]), Pa_ps)
                    if k < nlev:
                        P.copy("dve", q2(Mm[cur]), Pm_ps)
                    X_ps = v3(qd[0:n, 0:2 * n], n)
                    for e in range(2):
                        P.mm(X_ps[:, e, :], Ma[cur][0:n, e, 0:n], Xt[0:n, e, 0:n], inc=(e == 1))
                    yield
                    P.tt("dve", q2(Xt), q2(Xt), X_ps, ALU.add)
                wT_ps = qc[:, 0:n]
                for e in range(2):
                    P.mm(wT_ps[e * 64:(e + 1) * 64, :], kbd[0:n, e, :], Xt[0:n, e, 0:n], inc=(e == 1))
                yield
                P.ts("dve", nwT[:, 0:n], wT_ps, -1.0, ALU.mult)
                P.copy("act", Sb[:], Sj[:])
                vn_ps = qd[0:n, 0:128].rearrange("p (e d) -> p e d", e=2)
                for e in range(2):
                    pr = slice(e * 64, (e + 1) * 64)
                    P.mm(vn_ps[:, e, :], Xt[0:n, e, 0:n], vb_[0:n, e, :], start=True, stop=False, inc=False)
                    P.mm(vn_ps[:, e, :], nwT[pr, 0:n], Sb[pr, :], start=False, stop=True, inc=(e == 1))
                yield
                P.copy("act", vnew[0:n], vn_ps)
                o_ps = qc[:, 128:128 + n]
                for e in range(2):
                    pr = slice(e * 64, (e + 1) * 64)
                    P.mm(o_ps[pr, :], Sb[pr, :], qdec[pr, 0:n], start=True, stop=False, inc=False)
                    P.mm(o_ps[pr, :], vnew[0:n, e, :], QKd[0:n, e, 0:n], start=False, stop=True, inc=(e == 1))
                Su_ps = qb[:, 384:448]
                for e in range(2):
                    pr = slice(e * 64, (e + 1) * 64)
                    P.mm(Su_ps[pr, :], kdec[0:n, e, :], vnew[0:n, e, :], inc=(e == 1))
                yield
                P.stt(Sj[:], Sj[:], egl2[:, 0:1], Su_ps, ALU.mult, ALU.add)
                P.act(osq[:, 0:n], o_ps, AF.Square)
                ss_ps = qa[:, 256:256 + n]
                P.mm(ss_ps, bonesb[:], osq[:, 0:n])
                yield
                P.act(orn[:, 0:n], ss_ps, AF.Sqrt, bias=EPS, scale=1.0 / 64)
                P.recip(orn[:, 0:n], orn[:, 0:n])
                P.tt("dve", on_[:, 0:n], o_ps, orn[:, 0:n], ALU.mult)
                P.stt(mixT[:, j, cs], on_[:, 0:n], d["gn"][:], zs[:, j, cs], ALU.mult, ALU.mult)
                if not prompt:
                    for e in range(2):
                        P.dma("sp", O["sd_s"][l, b, 2 * j + e], Sj[e * 64:(e + 1) * 64, :])
            if prompt and last:
                for e in range(2):
                    P.dma("sp", O["sd_p"][l, 2 * j + e], Sj[e * 64:(e + 1) * 64, :])

        gens = [gdn_thread(0), gdn_thread(1)]
        alive = [True, True]
        while any(alive):
            for gi, g in enumerate(gens):
                if alive[gi]:
                    try:
                        next(g)
                    except StopIteration:
                        alive[gi] = False


        stage(3)
        sl3 = wslab(w_in[:, 1288:1800])
        preB = [seg3(S1[:, c * (TP + 48):(c + 1) * (TP + 48)], 2) for c in range(2)]
        if prompt:
            for c in range(2):
                if first:
                    P.memset("dve", preB[c][:, :, 0:2], 0.0)
                else:
                    P.copy("dve", preB[c][:, 0, 0:2], d["cB"][:, c, :])
        else:
            load_rows(I["state_shortconv"][l], 32, 2)
            for c in range(2):
                P.copy("dve", preB[c][:, :, 0:2], stF[:, c, 0:32].rearrange("p (b r) -> p b r", r=2))
        for c in range(2):
            ps = bigps()
            bigmm(ps, sl2, 8 + c * 128, 128, hT, T)
            evac(S2[:, c, 0:T], ps[:, 0:T])
        for c in range(2):
            ps = bigps()
            bigmm(ps, sl3, c * 128, 128, hT, T)
            evac(tA[:, 0:T], ps[:, 0:T])
            ps = bigps()
            bigmm(ps, sl3, 256 + c * 128, 128, hT, T)
            P.tt("dve", preB[c][:, :, 2:2 + seglen], tok3(tA[:, 0:T]), tok3(ps[:, 0:T]), ALU.mult)
        if prompt:
            for c in range(2):
                P.copy("dve", d["cB"][:, c, :], preB[c][:, 0, seglen:seglen + 2])
            if last:
                store_rows(O["bc_p"][l], lambda c: d["cB"][:, c, :], 2, [128] * 2)
        else:
            for c in range(2):
                P.copy("dve", stF[:, c, 0:32].rearrange("p (b r) -> p b r", r=2), preB[c][:, :, seglen:seglen + 2])
            store_rows(O["bc_s"][l], lambda c: stF[:, c, 0:32], 32, [128] * 2)
        for c in range(2):
            conv(tok3(tB[:, 0:T]), preB[c], d["wB"], c, 3, seglen, nseg)
            P.tt("dve", mixT[:, 2 + c, 0:T], S2[:, c, 0:T], tB[:, 0:T], ALU.mult)

        stage(4)
        sl4 = wslab(w_in[:, 1800:2312])
        if prompt:
            P.dma("sp", cosT[:, 0:T], C["cosp"][:, tile_i * TP:(tile_i + 1) * TP])
            P.dma("sp", sinT[:, 0:T], C["sinp"][:, tile_i * TP:(tile_i + 1) * TP])
        else:
            P.dma("sp", cosT[:, 0:T], C["coss"])
            P.dma("sp", sinT[:, 0:T], C["sins"])

        def rope(dst_bf, dst_f, src):
            P.mm(pb[2][:, 0:T], perm[:], src)
            P.tt("dve", tB[:, 0:T], src, cosT[:, 0:T], ALU.mult)
            P.tt("dve", tC[:, 0:T], pb[2][:, 0:T], sinT[:, 0:T], ALU.mult)
            if dst_f is not None:
                P.tt("dve", dst_f, tB[:, 0:T], tC[:, 0:T], ALU.add)
                P.copy("act", dst_bf, dst_f)
            else:
                P.tt("dve", dst_bf, tB[:, 0:T], tC[:, 0:T], ALU.add)

        for j in range(2):
            ps = bigps()
            for half in range(2):
                h = half * 2 + j
                bigmm(ps, sl4, h * 64, 64, hT, T, m0=half * 64)
            evac(tA[:, 0:T], ps[:, 0:T])
            rope(qTb[:, j, 0:T], None, tA[:, 0:T])
        ps = bigps()
        bigmm(ps, sl4, 256, 128, hT, T)
        evac(tA[:, 0:T], ps[:, 0:T])
        rope(kTb[:, 0:T], krf[:, 0:T], tA[:, 0:T])
        ps = bigps()
        bigmm(ps, sl4, 384, 128, hT, T)
        evac(S2[:, 7, 0:T], ps[:, 0:T])
        if not prompt:
            P.tr(pb[6][0:64, 0:128], krf[:, 0:64], ident[:])
            evac(ktok[0:64, :], pb[6][0:64, 0:128])
            P.tr(pb[6][0:64, 128:256], S2[:, 7, 0:64], ident[:])
            evac(vtokf[0:64, :], pb[6][0:64, 128:256])
            for b in range(16):
                P.dma("sp", O["ck_s"][l, b, 124:128, :], ktok[b * 4:(b + 1) * 4, :])
                P.dma("sp", O["cv_s"][l, b, 124:128, :], vtokf[b * 4:(b + 1) * 4, :])
            P.dma("sp", O["ck_s"][l, :, 0:124, :], I["cache_win_k"][l, :, 4:128, :])
            P.dma("sp", O["cv_s"][l, :, 0:124, :], I["cache_win_v"][l, :, 4:128, :])
        for b in range(nblk):
            cs = slice(b * n, (b + 1) * n)
            hasprev = (not prompt) or (not (first and b == 0))
            off = 0 if hasprev else 128
            W = 256 - off if prompt else 128 + n
            if prompt:
                kpv = d["kprev"]
                vpv = d["vprev"]
            else:
                P.dma("sp", kcache[:], I["cache_win_k"][l, b])
                P.dma("sp", vcache[:], I["cache_win_v"][l, b])
                P.tr(pb[6][:, 0:128], kcache[:], ident[:])
                evac(d["kprev"][:], pb[6][:, 0:128])
                P.copy("dve", d["vprev"][:], vcache[:])
                kpv = d["kprev"]
                vpv = d["vprev"]
            P.tr(pb[6][0:n, 256:384], S2[:, 7, cs], ident[:])
            P.copy("dve", vtokb[0:n, :], pb[6][0:n, 256:384])
            if prompt and last and b == nblk - 1:
                P.copy("act", vtokf[0:n, :], pb[6][0:n, 256:384])
                P.dma("sp", O["cv_p"][l], vtokf[:])
                P.tr(pb[6][:, 384:512], krf[:, cs], ident[:])
                evac(ktok[:], pb[6][:, 384:512])
                P.dma("sp", O["ck_p"][l], ktok[:])
            S_ps = [pb[0][0:n, :].rearrange("p (h k) -> p h k", h=2), pb[1][0:n, :].rearrange("p (h k) -> p h k", h=2)]
            for h in range(4):
                pr = slice((h // 2) * 64, (h // 2) * 64 + 64)
                dst = S_ps[h // 2][:, h % 2, :]
                if hasprev:
                    P.mm(dst[:, 0:128], qTb[pr, h % 2, cs], kpv[pr, :], inc=False)
                P.mm(dst[:, 128:128 + n], qTb[pr, h % 2, cs], kTb[pr, cs], inc=(h % 2 == 1))
            mk = amask[0:n, off:off + W].unsqueeze(1).to_broadcast([n, 2, W])
            for hh in range(2):
                P.tt("dve", sc[0:n, 2 * hh:2 * hh + 2, 0:W], S_ps[hh][:, :, off:off + W], mk, ALU.add)
            P.reduce(mx[0:n, :], sc[0:n, :, 0:W], ALU.max)
            P.ts("dve", nmx[0:n, :], mx[0:n, :], -0.125, ALU.mult)
            for h in range(4):
                P.act(Pf[0:n, h, 0:W], sc[0:n, h, 0:W], AF.Exp, bias=nmx[0:n, h:h + 1], scale=0.125,
                      accum_out=sm[0:n, h:h + 1])
            P.tt("dve", es[0:n, :], nmx[0:n, :], d["sink"][0:n, :], ALU.add)
            P.act(es[0:n, :], es[0:n, :], AF.Exp)
            P.tt("dve", es[0:n, :], es[0:n, :], sm[0:n, :], ALU.add)
            P.recip(es[0:n, :], es[0:n, :])
            P.tt("dve", Pn[0:n, :, 0:W], Pf[0:n, :, 0:W], hb(es[0:n, :], W), ALU.mult)
            PT_ps = pT16[:, :].rearrange("p (h s q) -> p h s q", h=4, s=2)
            for h in range(4):
                if hasprev:
                    P.tr(PT_ps[:, h, 0, 0:n], Pn[0:n, h, 0:128], identb[0:n, 0:n], inc=False)
                P.tr(PT_ps[0:n, h, 1, 0:n], Pn[0:n, h, 128 - off:128 - off + n], identb[0:n, 0:n], inc=(h == 3))
            if hasprev:
                P.copy("act", PT[:, :, 0, 0:n], PT_ps[:, :, 0, 0:n])
            P.copy("dve", PT[0:n, :, 1, 0:n], PT_ps[0:n, :, 1, 0:n])
            o_ps = pb[6][:, 0:2 * n].rearrange("p (j t) -> p j t", j=2)
            for h in range(4):
                kvs = slice((h // 2) * 64, (h // 2) * 64 + 64)
                dst = o_ps[(h % 2) * 64:(h % 2) * 64 + 64, h // 2, :]
                if hasprev:
                    P.mm(dst, vpv[:, kvs], PT[:, h, 0, 0:n], start=True, stop=False, inc=False)
                P.mm(dst, vtokb[0:n, kvs], PT[0:n, h, 1, 0:n], start=(not hasprev), stop=True, inc=(h == 3))
            evac(mixT[:, 4:6, cs], o_ps)
            if prompt:
                P.copy("dve", d["kprev"][:], kTb[:, cs])
                P.copy("act", d["vprev"][:], vtokb[:])

        stage(5)
        sl5 = wslab(w_in[:, 2312:2824])

        def gelu(dst, ps):
            P.act(tB[:, 0:T], ps, AF.Square)
            P.ts("dve", tB[:, 0:T], tB[:, 0:T], 0.044715, ALU.mult, 1.0, ALU.add)
            P.tt("dve", tB[:, 0:T], tB[:, 0:T], ps, ALU.mult)
            P.act(tB[:, 0:T], tB[:, 0:T], AF.Sigmoid, scale=1.5957691216057308)
            P.tt("dve", dst, tB[:, 0:T], ps, ALU.mult)

        gsc = [tB, gcv, sgb, tA]
        for g_ in range(4):
            bigmm(pb[g_], sl5, g_ * 128, 128, hT, T)
        for g_ in range(4):
            P.act(gsc[g_][:, 0:T], pb[g_][:, 0:T], AF.Square)
        for g_ in range(4):
            P.ts("dve", gsc[g_][:, 0:T], gsc[g_][:, 0:T], 0.044715, ALU.mult, 1.0, ALU.add)
        for g_ in range(4):
            P.tt("dve", gsc[g_][:, 0:T], gsc[g_][:, 0:T], pb[g_][:, 0:T], ALU.mult)
        for g_ in range(4):
            P.act(gsc[g_][:, 0:T], gsc[g_][:, 0:T], AF.Sigmoid, scale=1.5957691216057308)
        for g_ in range(4):
            P.tt("dve", S2[:, g_, 0:T], gsc[g_][:, 0:T], pb[g_][:, 0:T], ALU.mult)
        P.copy("act", sqb[:, 0:2, 0:T], S2[:, 2:4, 0:T])
        P.act(sqb[:, 2:4, 0:T], S2[:, 2:4, 0:T], AF.Square)
        for c in range(2):
            P.mm(pb[2][:, 0:T], onesb[:], sqb[:, c, 0:T], start=(c == 0), stop=(c == 1), inc=(c == 1))
        for c in range(2):
            P.mm(pb[3][:, 0:T], onesb[:], sqb[:, 2 + c, 0:T], start=(c == 0), stop=(c == 1), inc=(c == 1))
        P.act(tA[:, 0:T], pb[2][:, 0:T], AF.Copy, scale=1.0 / 256)
        P.tt("dve", tB[:, 0:T], tA[:, 0:T], tA[:, 0:T], ALU.mult)
        P.stt(tB[:, 0:T], pb[3][:, 0:T], 1.0 / 256, tB[:, 0:T], ALU.mult, ALU.subtract)
        P.act(tB[:, 0:T], tB[:, 0:T], AF.Sqrt, bias=EPS)
        P.recip(tB[:, 0:T], tB[:, 0:T])
        for c in range(2):
            P.tt("dve", S2[:, 2 + c, 0:T], S2[:, 2 + c, 0:T], tA[:, 0:T], ALU.subtract)
            P.tt("dve", S2[:, 2 + c, 0:T], S2[:, 2 + c, 0:T], tB[:, 0:T], ALU.mult)
            P.ts("dve", S2[:, 2 + c, 0:T], S2[:, 2 + c, 0:T], d["lng"][:, c:c + 1], ALU.mult, d["lnb"][:, c:c + 1], ALU.add)
        if not prompt:
            store_rows(O["dv_s"][l], lambda c: S2[:, 2 + c, 0:T], 64, [128] * 2)
        for b in range(nblk):
            cs = slice(b * n, (b + 1) * n)
            for c in range(2):
                P.tr(pb[2][0:n, c * 128:(c + 1) * 128], S2[:, 2 + c, cs], ident[:], inc=(c == 1))
            evac(dvT[0:n, :], pb[2][0:n, 0:256])
            m_ps = pb[3][:, 0:2 * n].rearrange("p (j t) -> p j t", j=2)
            for g in range(4):
                dst = m_ps[(g % 2) * 64:(g % 2) * 64 + 64, g // 2, :]
                P.mm(dst, dvT[0:n, g * 64:(g + 1) * 64], d["WT"][0:n, g, 0:n], start=True, stop=False, inc=False)
                P.mm(dst, onesrow[0:1, 0:64], d["brow"][0:1, g * 128:g * 128 + n], start=False, stop=True, inc=(g == 3))
            P.tt("dve", mixT[:, 6:8, cs], S2[:, 0:2, cs], m_ps, ALU.mult)

        stage(6)
        for s in range(2):
            sl = wslab(I["w_out"][l][:, s * 512:(s + 1) * 512])
            for c in range(4):
                ps = bigps()
                bigmm(ps, sl, c * 128, 128, mixT, T)
                P.tt("dve", xs_t[:, s * 4 + c, 0:T], xs_t[:, s * 4 + c, 0:T], ps[:, 0:T], ALU.add)

        stage(7)
        rmsnorm(d["g2"], T)
        gp3s = [seg3(gpre[:, :], 2), seg3(gpre2[:, :], 2)]
        gcvs = [gcv, sgb]
        if not prompt:
            load_rows(I["state_ffn_conv"][l], 32, FC)
        slabs = {}

        def fslab(kind, s_):
            if (kind, s_) not in slabs:
                ncol = 512 if s_ < 5 else 256
                w_ = I["ffn_w_gate" if kind == "g" else "ffn_w_up"][l]
                slabs[(kind, s_)] = wslab(w_[:, s_ * 512:s_ * 512 + ncol])
            return slabs[(kind, s_)]

        fpi = {"i": 0}

        def fbank():
            fpi["i"] = (fpi["i"] + 1) % 4
            return pb[fpi["i"]]

        def stageA(j):
            s_, c = divmod(j, 4)
            gp3, gcv_ = gp3s[j % 2], gcvs[j % 2]
            ps = fbank()
            bigmm(ps, fslab("g", s_), c * 128, 128, hT, T)
            if prompt:
                if first:
                    P.memset("dve", gp3[:, :, 0:2], 0.0)
                else:
                    P.copy("dve", gp3[:, 0, 0:2], d["cF"][:, j, :])
            else:
                P.copy("dve", gp3[:, :, 0:2], stF[:, j, 0:32].rearrange("p (b r) -> p b r", r=2))
            P.copy("act", gp3[:, :, 2:2 + seglen], tok3(ps[:, 0:T]))
            if prompt:
                P.copy("dve", d["cF"][:, j, :], gp3[:, 0, seglen:seglen + 2])
            else:
                P.copy("dve", stF[:, j, 0:32].rearrange("p (b r) -> p b r", r=2), gp3[:, :, seglen:seglen + 2])
            conv(tok3(gcv_[:, 0:T]), gp3, d["wF"], j, 3, seglen, nseg)

        def stageB(j):
            s_, c = divmod(j, 4)
            gcv_ = gcvs[j % 2]
            P.act(gcv_[:, 0:T], gcv_[:, 0:T], AF.Silu)
            ps = fbank()
            bigmm(ps, fslab("u", s_), c * 128, 128, hT, T)
            P.tt("dve", actb[:, j, 0:T], gcv_[:, 0:T], ps[:, 0:T], ALU.mult)

        stageA(0)
        for j in range(FC):
            if j + 1 < FC:
                stageA(j + 1)
            stageB(j)
        stage(7.5)
        if prompt and last:
            store_rows(O["fc_p"][l], lambda c: d["cF"][:, c, :], 2, [128] * FC)
        if not prompt:
            store_rows(O["fc_s"][l], lambda c: stF[:, c, 0:32], 32, [128] * FC)
        wd = I["ffn_w_down"][l].rearrange("(j p) n -> p j n", p=128)
        acc = [pb[i][:, :] for i in range(7)] + [pT16[:, :].bitcast(F32)]
        for jg in range((FC + 3) // 4):
            nj = min(4, FC - 4 * jg)
            tw = wring[wpos["w"] % NRING]
            wpos["w"] += 1
            t = tw[:, :, :].rearrange("p k n -> p (k n)").rearrange("p (j n) -> p j n", j=4)
            P.dma("pool", t[:, 0:nj, :], wd[:, 4 * jg:4 * jg + nj, :])
            for jj in range(nj):
                j = 4 * jg + jj
                for oc in range(8):
                    P.mm(acc[oc][:, 0:T], t[:, jj, oc * 128:(oc + 1) * 128], actb[:, j, 0:T],
                         start=(j == 0), stop=(j == FC - 1), inc=(oc == 7 or j == FC - 1))
        for oc in range(8):
            P.tt("dve", xs_t[:, oc, 0:T], xs_t[:, oc, 0:T], acc[oc][:, 0:T], ALU.add)
        stage(8)

    def run_tile(prompt, ti):
        if prompt:
            T, n, nblk, nseg, seglen = TP, 128, 4, 1, TP
            src = I["xp"][ti * TP:(ti + 1) * TP, :]
            dst = O["y_p"][ti * TP:(ti + 1) * TP, :]
        else:
            T, n, nblk, nseg, seglen = TS, 4, 16, 16, 4
            src = I["xs"]
            dst = O["y_s"]
        nrb = (T + 127) // 128
        iobufs = [xin[:, :], S1[:, 0:1024], zs[:, :, :].rearrange("p c t -> p (c t)"),
                  sqb[:, :, :].rearrange("p c t -> p (c t)").bitcast(F32)]
        for rb in range(nrb):
            r = min(128, T - rb * 128)
            P.dma("sp", iobufs[rb % 4][0:r, :], src[rb * 128:rb * 128 + r, :])
        for rb in range(nrb):
            r = min(128, T - rb * 128)
            xb = iobufs[rb % 4]
            for g in range(2):
                bank = pb[3 + (2 * rb + g) % 4]
                for k in range(4):
                    P.tr(bank[:, k * r:(k + 1) * r], xb[0:r, (g * 4 + k) * 128:(g * 4 + k + 1) * 128],
                         ident[0:r, 0:r], inc=(k == 3))
                evac(xs_t[:, g * 4:g * 4 + 4, rb * 128:rb * 128 + r],
                     bank[:, 0:4 * r].rearrange("p (k t) -> p k t", k=4))
        for li, l in enumerate([int(c) for c in os.environ.get("MK_LAYERS", "01")]):
            lpos["i"] = li
            run_layer(l, T, n, nblk, nseg, seglen, prompt, first=(ti == 0), last=(ti == NTP - 1), tile_i=ti)
        rmsnorm(gfin, T, out=S2)
        for rb in range(nrb):
            r = min(128, T - rb * 128)
            yb = iobufs[rb % 4]
            for g in range(2):
                bank = pb[3 + (2 * rb + g) % 4]
                for k in range(4):
                    P.tr(bank[0:r, k * 128:(k + 1) * 128], S2[:, g * 4 + k, rb * 128:rb * 128 + r], ident[:],
                         inc=(k == 3))
                evac(yb[0:r, g * 512:(g + 1) * 512], bank[0:r, :])
            P.dma("sp", dst[rb * 128:rb * 128 + r, :], yb[0:r, :])

    ntiles = int(os.environ.get("MK_NTILES", NTP))
    try:
        for ti in range(ntiles):
            run_tile(True, ti)
        if os.environ.get("MK_NOSAMPLE") is None:
            run_tile(False, 0)
    except _Stop:
        P.dma("sp", O["y_s"][0:64, :], xs_t[0:64, 0, 0:TP].rearrange("p t -> p t") if False else xin[0:64, :])
    P.emit()
    P.close()
    return nc, P


_CACHE = {}


def kernel(**inp):
    f = np.float32
    if "nc" not in _CACHE:
        _CACHE["nc"], _ = build_program()
        _CACHE["consts"] = _consts()
    nc = _CACHE["nc"]
    consts = _CACHE["consts"]
    shared = {}
    for k in IN_SHAPES:
        if k in ("xp", "xs", "state_delta", "state_delta_conv", "state_shortconv", "cache_win_k", "cache_win_v",
                 "state_ffn_conv"):
            continue
        shared[k] = np.ascontiguousarray(np.asarray(inp[k], f))
    for k, v in consts.items():
        shared["c_" + k] = np.ascontiguousarray(v.reshape(CONST_SHAPES[k]).astype(f))
    in_maps = []
    for c in range(8):
        m = dict(shared)
        bs = slice(16 * c, 16 * c + 16)
        m["xp"] = np.ascontiguousarray(np.asarray(inp["x_prompt"][c], f))
        m["xs"] = np.ascontiguousarray(np.asarray(inp["x_sample"][bs], f).reshape(64, D))
        m["state_delta"] = np.ascontiguousarray(np.asarray(inp["state_delta"][:, bs], f))
        m["state_delta_conv"] = np.ascontiguousarray(np.asarray(inp["state_delta_conv"][:, bs], f).reshape(2, 48, 768))
        m["state_shortconv"] = np.ascontiguousarray(np.asarray(inp["state_shortconv"][:, bs], f).reshape(2, 32, 256))
        m["cache_win_k"] = np.ascontiguousarray(np.asarray(inp["cache_win_k"][:, bs], f).reshape(2, 16, 128, 128))
        m["cache_win_v"] = np.ascontiguousarray(np.asarray(inp["cache_win_v"][:, bs], f).reshape(2, 16, 128, 128))
        m["state_ffn_conv"] = np.ascontiguousarray(np.asarray(inp["state_ffn_conv"][:, bs], f).reshape(2, 32, DFF))
        in_maps.append(m)
    res = run_bass_kernel_spmd(nc, in_maps, core_ids=list(range(8)))
    R = res.results

    def cat_p(k, shape):
        return np.stack([np.asarray(R[c][k], f).reshape(shape) for c in range(8)], axis=1)

    def cat_s(k, shape):
        return np.concatenate([np.asarray(R[c][k], f).reshape(shape) for c in range(8)], axis=1)

    y_p = np.stack([np.asarray(R[c]["y_p"], f) for c in range(8)], axis=0)
    y_s = np.concatenate([np.asarray(R[c]["y_s"], f).reshape(16, 4, D) for c in range(8)], axis=0)
    return (
        y_p, y_s,
        cat_p("sd_p", (2, 4, 64, 64)), cat_s("sd_s", (2, 16, 4, 64, 64)),
        cat_p("ac_p", (2, 3, 768)), cat_s("ac_s", (2, 16, 3, 768)),
        cat_p("bc_p", (2, 2, 256)), cat_s("bc_s", (2, 16, 2, 256)),
        cat_p("ck_p", (2, 128, 2, 64)), cat_s("ck_s", (2, 16, 128, 2, 64)),
        cat_p("cv_p", (2, 128, 2, 64)), cat_s("cv_s", (2, 16, 128, 2, 64)),
        cat_p("fc_p", (2, 2, DFF)), cat_s("fc_s", (2, 16, 2, DFF)),
        cat_s("dv_s", (2, 16, 4, 256)),
    )
```

```python
import numpy as np
import concourse.bass as bass
import concourse.mybir as mybir

F32 = mybir.dt.float32
BF16 = mybir.dt.bfloat16
I32 = mybir.dt.int32
ALU = mybir.AluOpType
AF = mybir.ActivationFunctionType
AX = mybir.AxisListType

ENGINES = ("pe", "act", "dve", "pool", "sp")


class _Buf:
    __slots__ = ("last_w", "readers")

    def __init__(self):
        self.last_w = None
        self.readers = []


class Prog:
    def __init__(self, nc, ring=12):
        self.nc = nc
        self.ops = {e: [] for e in ENGINES}
        self.cnt = {e: 0 for e in ENGINES}
        self.known = {e: {} for e in ENGINES}
        self.bufs = {}
        self.stack = []
        self.sems = {}
        self.nsem = 0
        self.ring_n = ring
        self.ring = {}
        self.ring_pos = {}
        self.untracked = set()
        self.nops = 0

    def _sem(self, key):
        if key not in self.sems:
            cm = self.nc.semaphore("s_" + key)
            self.sems[key] = cm.__enter__()
            self.stack.append(cm)
            self.nsem += 1
        return self.sems[key]

    def sb(self, name, shape, dtype=F32):
        cm = self.nc.sbuf_tensor(name, list(shape), dtype)
        t = cm.__enter__()
        self.stack.append(cm)
        return t

    def ps(self, name, shape, dtype=F32):
        cm = self.nc.psum_tensor(name, list(shape), dtype)
        t = cm.__enter__()
        self.stack.append(cm)
        return t

    def dram(self, name, shape, dtype=F32, kind="Internal"):
        return self.nc.dram_tensor(name, list(shape), dtype, kind=kind)

    def _key(self, ap):
        if isinstance(ap, str):
            return ap
        return ap.tensor.name

    def _deps(self, eng, reads, writes):
        waits = []
        known = self.known[eng]

        def need(ev):
            if ev is None:
                return
            sk, val, kn = ev
            if eng == "pe" and sk == "pe":
                return
            if known.get(sk, 0) >= val:
                return
            waits.append((sk, val))
            known[sk] = val
            for k, v in kn.items():
                if known.get(k, 0) < v:
                    known[k] = v

        rk = [self._key(a) for a in reads]
        wk = [self._key(a) for a in writes]
        rk = [k for k in rk if k not in self.untracked]
        wk = [k for k in wk if k not in self.untracked]
        for k in rk:
            b = self.bufs.setdefault(k, _Buf())
            need(b.last_w)
        for k in wk:
            b = self.bufs.setdefault(k, _Buf())
            need(b.last_w)
            for r in b.readers:
                need(r)
        return waits, rk, wk

    def _commit(self, ev, rk, wk):
        for k in rk:
            if k not in wk:
                self.bufs[k].readers.append(ev)
        for k in wk:
            b = self.bufs[k]
            b.last_w = ev
            b.readers = []

    def op(self, eng, fn, reads, writes, inc=True):
        waits, rk, wk = self._deps(eng, reads, writes)
        ev = (eng, self.cnt[eng] + 1, dict(self.known[eng]))
        if inc:
            self.cnt[eng] += 1
        self.ops[eng].append((waits, fn, (eng, 1) if inc else None))
        self._commit(ev, rk, wk)
        self.nops += 1

    def dma(self, eng, out, in_, extra_reads=(), extra_writes=(), **kw):
        if eng not in self.ring:
            self.ring[eng] = [["d%s%d" % (eng, i), 0] for i in range(self.ring_n)]
            self.ring_pos[eng] = 0
        slot = self.ring[eng][self.ring_pos[eng]]
        self.ring_pos[eng] = (self.ring_pos[eng] + 1) % self.ring_n
        waits, rk, wk = self._deps(eng, [in_] + list(extra_reads), [out] + list(extra_writes))
        known = self.known[eng]
        if slot[1] > 0 and known.get(slot[0], 0) < 16 * slot[1]:
            waits.append((slot[0], 16 * slot[1]))
            known[slot[0]] = 16 * slot[1]
        slot[1] += 1
        ev = (slot[0], 16 * slot[1], dict(known))

        def fn(e, out=out, in_=in_, kw=kw):
            return e.dma_start(out=out, in_=in_, **kw)

        self.ops[eng].append((waits, fn, (slot[0], 16)))
        self._commit(ev, rk, wk)
        self.nops += 1

    def mm(self, out, lhsT, rhs, start=True, stop=True, inc=True, **kw):
        self.op("pe", lambda e: e.matmul(out, lhsT, rhs, start=start, stop=stop, **kw),
                [lhsT, rhs] + ([] if start else [out]), [out], inc=inc)

    def tr(self, out, in_, ident, inc=True):
        self.op("pe", lambda e: e.transpose(out, in_, ident), [in_, ident], [out], inc=inc)

    def act(self, out, in_, func, bias=None, scale=None, accum_out=None, eng="act"):
        kw = {}
        reads = [in_]
        writes = [out]
        if bias is not None:
            kw["bias"] = bias
            if not isinstance(bias, (int, float)):
                reads.append(bias)
        if scale is not None:
            kw["scale"] = scale
            if not isinstance(scale, (int, float)):
                reads.append(scale)
        if accum_out is not None:
            kw["accum_out"] = accum_out
            writes.append(accum_out)
        self.op("act", lambda e: e.activation(out, in_, func, **kw), reads, writes)

    def tt(self, eng, out, in0, in1, op):
        self.op(eng, lambda e: e.tensor_tensor(out, in0, in1, op), [in0, in1], [out])

    def ts(self, eng, out, in0, s1, op0, s2=None, op1=None, accum_out=None):
        reads = [in0]
        if not isinstance(s1, (int, float)):
            reads.append(s1)
        if s2 is not None and not isinstance(s2, (int, float)):
            reads.append(s2)
        writes = [out]
        kw = {}
        if op1 is not None:
            kw["op1"] = op1
        if accum_out is not None:
            kw["accum_out"] = accum_out
            writes.append(accum_out)
        self.op(eng, lambda e: e.tensor_scalar(out, in0, s1, s2, op0, **kw), reads, writes)

    def stt(self, out, in0, scalar, in1, op0, op1, eng="dve"):
        reads = [in0, in1]
        if not isinstance(scalar, (int, float)):
            reads.append(scalar)
        self.op(eng, lambda e: e.scalar_tensor_tensor(out, in0, scalar, in1, op0, op1), reads, [out])

    def copy(self, eng, out, in_):
        if eng == "act":
            self.op("act", lambda e: e.copy(out, in_), [in_], [out])
        else:
            self.op(eng, lambda e: e.tensor_copy(out, in_), [in_], [out])

    def memset(self, eng, out, val):
        self.op(eng, lambda e: e.memset(out, val), [], [out])

    def reduce(self, out, in_, op, axis=AX.X, eng="dve"):
        self.op(eng, lambda e: e.tensor_reduce(out, in_, axis, op), [in_], [out])

    def recip(self, out, in_):
        self.op("dve", lambda e: e.reciprocal(out, in_), [in_], [out])

    def emit(self):
        nc = self.nc
        fin = []
        for eng, ring in self.ring.items():
            for key, uses in ring:
                if uses > 0:
                    fin.append((key, 16 * uses))
        for e in ("pe", "act", "dve", "pool"):
            if self.cnt[e] > 0:
                fin.append((e, self.cnt[e]))
        for k in list(self.sems) + [f[0] for f in fin]:
            self._sem(k)
        for eng in ENGINES:
            for waits, fn, inc in self.ops[eng]:
                for sk, v in waits:
                    self._sem(sk)
                if inc is not None:
                    self._sem(inc[0])
        sems = self.sems
        ops = self.ops

        def replay(e, lst, final=False):
            for waits, fn, inc in lst:
                for sk, v in waits:
                    e.wait_ge(sems[sk], v)
                ins = fn(e)
                if inc is not None:
                    ins.then_inc(sems[inc[0]], inc[1])
            if final:
                for sk, v in fin:
                    e.wait_ge(sems[sk], v)

        with nc.Block() as block:
            @block.sync
            def _(e):
                replay(e, ops["sp"], final=True)

            @block.tensor
            def _(e):
                replay(e, ops["pe"])

            @block.scalar
            def _(e):
                replay(e, ops["act"])

            @block.vector
            def _(e):
                replay(e, ops["dve"])

            @block.gpsimd
            def _(e):
                replay(e, ops["pool"])

    def close(self):
        while self.stack:
            self.stack.pop().__exit__(None, None, None)


from concourse.bass_utils import run_bass_kernel_spmd

D = 1024
KC = 8
PTOT = 2824
DFF = 2816
FC = 22
EPS = 1e-6
PAST = 16384
NEG = -30000.0
TP = 512
NTP = 8
TS = 64


def _consts():
    f = np.float32
    c = {}
    c["ident"] = np.eye(128, dtype=f)
    s = np.arange(128)
    c["U"] = (s[:, None] <= s[None, :]).astype(f)
    c["Us"] = (s[:, None] < s[None, :]).astype(f)
    c["ones"] = np.ones((128, 128), f)
    bo = np.zeros((128, 128), f)
    bo[:64, :64] = 1
    bo[64:, 64:] = 1
    c["bones"] = bo
    sel = np.zeros((8, 4, 128), f)
    for h in range(4):
        sel[4 + h, h, :] = 1
    c["selg"] = sel.reshape(8, 512)
    selb = np.zeros((8, 4, 128), f)
    for h in range(4):
        selb[h, h, :] = 1
    c["selb"] = selb.reshape(8, 512)
    sel2 = np.zeros((8, 2, 128), f)
    for j in range(2):
        sel2[4 + 2 * j, j, :64] = 1
        sel2[4 + 2 * j + 1, j, 64:] = 1
    c["sel2"] = sel2.reshape(8, 256)
    perm = np.zeros((128, 128), f)
    for half in range(2):
        for d in range(8):
            r = half * 64 + d
            perm[r + 8, r] = -1.0
            perm[r, r + 8] = 1.0
    c["perm"] = perm
    inv = np.power(np.float32(500000.0), -np.arange(8, dtype=f) * f(2.0 / 16)).astype(f)

    def cs(pos):
        ang = pos.astype(f)[None, :] * inv[:, None]
        C = np.ones((128, pos.shape[0]), f)
        S = np.zeros((128, pos.shape[0]), f)
        for half in range(2):
            for q in range(2):
                r0 = half * 64 + q * 8
                C[r0:r0 + 8] = np.cos(ang)
                S[r0:r0 + 8] = np.sin(ang)
        return C, S

    c["cosp"], c["sinp"] = cs(np.arange(4096))
    c["coss"], c["sins"] = cs(np.tile(PAST + np.arange(4), 16))
    i = np.arange(128)[:, None]
    j = np.arange(256)[None, :]
    c["amask"] = np.where((j > i) & (j <= i + 128), 0.0, NEG).astype(f)
    rp = np.ones((8, TP), f)
    rp[:, ::128] = 0
    c["rstp"] = rp
    rs = np.ones((8, TS), f)
    rs[:, ::4] = 0
    c["rsts"] = rs
    return c


CONST_SHAPES = {"ident": (128, 128), "U": (128, 128), "Us": (128, 128), "ones": (128, 128), "bones": (128, 128),
                "selg": (8, 512), "selb": (8, 512), "sel2": (8, 256), "perm": (128, 128),
                "cosp": (128, 4096), "sinp": (128, 4096), "coss": (128, 64), "sins": (128, 64),
                "amask": (128, 256), "rstp": (8, TP), "rsts": (8, TS)}

IN_SHAPES = {
    "xp": (4096, D), "xs": (64, D),
    "state_delta": (2, 16, 4, 64, 64), "state_delta_conv": (2, 48, 768), "state_shortconv": (2, 32, 256),
    "cache_win_k": (2, 16, 128, 128), "cache_win_v": (2, 16, 128, 128), "state_ffn_conv": (2, 32, DFF),
    "norm1_g": (2, D), "w_in": (2, D, PTOT), "a_conv_w": (2, 4, 768), "a_log": (2, 4), "a_dt_bias": (2, 4),
    "a_norm_g": (2, 64), "b_conv_w": (2, 3, 256), "c_sinks": (2, 4), "d_ln_g": (2, 256), "d_ln_b": (2, 256),
    "d_ws": (2, 4, 128, 128), "d_bias": (2, 4, 128), "w_out": (2, D, D), "norm2_g": (2, D),
    "ffn_w_gate": (2, D, DFF), "ffn_w_up": (2, D, DFF), "ffn_conv_w": (2, 3, DFF), "ffn_w_down": (2, DFF, D),
    "final_norm_g": (D,),
}
OUT_SHAPES = {
    "y_p": (4096, D), "y_s": (64, D), "sd_p": (2, 4, 64, 64), "sd_s": (2, 16, 4, 64, 64),
    "ac_p": (2, 3, 768), "ac_s": (2, 48, 768), "bc_p": (2, 2, 256), "bc_s": (2, 32, 256),
    "ck_p": (2, 128, 128), "ck_s": (2, 16, 128, 128), "cv_p": (2, 128, 128), "cv_s": (2, 16, 128, 128),
    "fc_p": (2, 2, DFF), "fc_s": (2, 32, DFF), "dv_s": (2, 64, 256),
}


def build_program():
    nc = bass.Bass("TRN2", target_bir_lowering=False)
    P = Prog(nc, ring=12)
    I = {k: nc.dram_tensor(k, list(s), F32, kind="ExternalInput").ap() for k, s in IN_SHAPES.items()}
    C = {k: nc.dram_tensor("c_" + k, list(s), F32, kind="ExternalInput").ap() for k, s in CONST_SHAPES.items()}
    O = {k: nc.dram_tensor(k, list(s), F32, kind="ExternalOutput").ap() for k, s in OUT_SHAPES.items()}
    for k in list(IN_SHAPES) + ["c_" + k for k in CONST_SHAPES] + list(OUT_SHAPES):
        P.untracked.add(k)
    NC_ = {"ns": 0}

    def small(out, in_):
        P.dma("sp", out, in_, allow_slow_non_contiguous=True)

    def cload(name, shape, src, bf=False):
        t = P.sb("k_" + name, shape)
        P.dma("sp", t[:], src)
        if bf:
            tb = P.sb("kb_" + name, shape, BF16)
            P.copy("dve", tb[:], t[:])
            return t, tb
        return t

    ident, identb = cload("ident", [128, 128], C["ident"], True)
    U = cload("U", [128, 128], C["U"])
    Us = cload("Us", [128, 128], C["Us"])
    ones, onesb = cload("ones", [128, 128], C["ones"], True)
    bones, bonesb = cload("bones", [128, 128], C["bones"], True)
    selg = cload("selg", [8, 512], C["selg"])
    selb = cload("selb", [8, 512], C["selb"])
    sel2 = cload("sel2", [8, 256], C["sel2"])
    perm = cload("perm", [128, 128], C["perm"])
    amask = cload("amask", [128, 256], C["amask"])

    pb = [P.ps("pb%d" % i, [128, 512]) for i in range(7)]
    pT16 = P.ps("pT16", [128, 1024], BF16)

    S2flat = P.sb("S2", [128, 11 * TP])

    def ptrans(dst3, src_rows, R, Cn):
        P.dma("sp", S2flat[0:R, 0:Cn * 128], src_rows)
        for c in range(Cn):
            P.tr(pb[6][:, c * R:(c + 1) * R], S2flat[0:R, c * 128:(c + 1) * 128], ident[0:R, 0:R], inc=(c == Cn - 1))
        P.copy("dve", dst3, pb[6][:, 0:Cn * R].rearrange("p (c r) -> p c r", c=Cn))

    class _Stop(Exception):
        pass

    import os
    stop_at = float(os.environ.get("MK_STOP", "99"))

    lpos = {"i": 0}

    def stage(k):
        if lpos["i"] * 10 + k >= stop_at:
            raise _Stop()

    prm = []
    wtmp_sh = S2flat[:, 0:512].rearrange("p (g s) -> p g s", g=4)
    brow_sh = S2flat[0:1, 512:1024]
    for l in range(2):
        d = {}
        d["g1"] = P.sb("g1_%d" % l, [128, 8])
        ptrans(d["g1"][:].unsqueeze(1), I["norm1_g"][l].rearrange("(k p) -> k p", p=128), 8, 1)
        d["g2"] = P.sb("g2_%d" % l, [128, 8])
        ptrans(d["g2"][:].unsqueeze(1), I["norm2_g"][l].rearrange("(k p) -> k p", p=128), 8, 1)
        d["wA"] = P.sb("wA_%d" % l, [128, 6, 4])
        ptrans(d["wA"][:], I["a_conv_w"][l], 4, 6)
        d["wB"] = P.sb("wB_%d" % l, [128, 2, 3])
        ptrans(d["wB"][:], I["b_conv_w"][l], 3, 2)
        d["wF"] = P.sb("wF_%d" % l, [128, FC, 3])
        ptrans(d["wF"][:], I["ffn_conv_w"][l], 3, FC)
        d["dtb"] = P.sb("dtb_%d" % l, [8, 1])
        P.memset("dve", d["dtb"][:], 0.0)
        small(d["dtb"][4:8, :], I["a_dt_bias"][l].rearrange("(h o) -> h o", o=1))
        al = P.sb("al_%d" % l, [8, 1])
        P.memset("dve", al[:], 0.0)
        small(al[4:8, :], I["a_log"][l].rearrange("(h o) -> h o", o=1))
        d["negA"] = P.sb("negA_%d" % l, [8, 1])
        P.act(d["negA"][:], al[:], AF.Exp)
        P.ts("dve", d["negA"][:], d["negA"][:], -1.0, ALU.mult)
        d["gn"] = P.sb("gn_%d" % l, [128, 1])
        small(d["gn"][0:64, :], I["a_norm_g"][l].rearrange("(h o) -> h o", o=1))
        small(d["gn"][64:128, :], I["a_norm_g"][l].rearrange("(h o) -> h o", o=1))
        d["sink"] = P.sb("sink_%d" % l, [128, 4])
        small(d["sink"][:], I["c_sinks"][l:l + 1, :].partition_broadcast(128).rearrange("p o h -> p (o h)"))
        d["lng"] = P.sb("lng_%d" % l, [128, 2])
        ptrans(d["lng"][:].unsqueeze(1), I["d_ln_g"][l].rearrange("(c p) -> c p", p=128), 2, 1)
        d["lnb"] = P.sb("lnb_%d" % l, [128, 2])
        ptrans(d["lnb"][:].unsqueeze(1), I["d_ln_b"][l].rearrange("(c p) -> c p", p=128), 2, 1)
        d["WT"] = P.sb("WT_%d" % l, [128, 4, 128], BF16)
        wtmp = wtmp_sh
        P.dma("sp", wtmp[:], I["d_ws"][l].rearrange("g t s -> t g s"))
        for g in range(4):
            P.tr(pb[6][:, g * 128:(g + 1) * 128], wtmp[:, g, :], ident[:], inc=(g == 3))
        P.tt("dve", d["WT"][:], pb[6][:].rearrange("p (g t) -> p g t", g=4),
             U[:].unsqueeze(1).to_broadcast([128, 4, 128]), ALU.mult)
        brow = brow_sh
        P.dma("sp", brow[:], I["d_bias"][l:l + 1].rearrange("o g t -> o (g t)"))
        d["brow"] = P.sb("browb_%d" % l, [1, 512], BF16)
        P.copy("dve", d["brow"][:], brow[:])
        d["S"] = [P.sb("S_%d_%d" % (l, jj), [128, 64]) for jj in range(2)]
        d["cA"] = P.sb("cA_%d" % l, [128, 6, 3])
        d["cB"] = P.sb("cB_%d" % l, [128, 2, 2])
        d["cF"] = P.sb("cF_%d" % l, [128, FC, 2])
        d["kprev"] = P.sb("kprev_%d" % l, [128, 128], BF16)
        d["vprev"] = P.sb("vprev_%d" % l, [128, 128], BF16)
        prm.append(d)
    gfin = P.sb("gfin", [128, 8])
    ptrans(gfin[:].unsqueeze(1), I["final_norm_g"].rearrange("(k p) -> k p", p=128), 8, 1)
    onesrow = onesb

    xs_t = P.sb("xs_t", [128, 8, TP])
    hT = P.sb("hT", [128, 8, TP], BF16)
    mixT = P.sb("mixT", [128, 8, TP], BF16)
    sqb = P.sb("sqb", [128, 4, TP], BF16)
    rstd = P.sb("rstd", [128, TP])
    S1 = P.sb("S1", [128, 6 * (TP + 3 * 16)])
    S2 = S2flat[:, 0:8 * TP].rearrange("p (c t) -> p c t", c=8)
    p8 = P.sb("p8", [8, TP])
    sig8 = P.sb("sig8", [8, TP])
    g8 = P.sb("g8", [8, TP])
    gc8 = P.sb("gc8", [8, TP])
    actb = S2flat[:].bitcast(BF16).rearrange("p (j t) -> p j t", j=FC)
    sgb = P.sb("sgb", [128, TP])
    gpre = P.sb("gpre", [128, TP + 2 * 16])
    gcv = P.sb("gcv", [128, TP])
    gpre2 = P.sb("gpre2", [128, TP + 2 * 16])
    xin = P.sb("xio", [128, D])
    yout = xin
    NRING = 5
    wring = [P.sb("wr%d" % i, [128, 8, 512], BF16) for i in range(NRING)]
    wpos = {"w": 0, "d": 0}
    tA = P.sb("tA", [128, TP])
    tB = P.sb("tB", [128, TP])
    tC = gcv
    cosT = sgb
    sinT = gpre
    qTb = P.sb("qTb", [128, 2, TP], BF16)
    kTb = P.sb("kTb", [128, TP], BF16)
    krf = P.sb("krf", [128, TP])
    def T3(name, dt=F32):
        return P.sb(name, [128, 4, 128], dt)
    TH = []
    for jj in range(2):
        t_ = {}
        for nm in ("Eu", "QKd", "Mm0", "Mm1", "Ma0", "Ma1", "Xt"):
            t_[nm] = P.sb("%s_%d" % (nm, jj), [128, 2, 128])
        for nm in ("vb", "kbd", "kdec", "vnew"):
            t_[nm] = P.sb("%s_%d" % (nm, jj), [128, 2, 64])
        t_["nwT"] = P.sb("nwT_%d" % jj, [128, 128], BF16)
        t_["qdec"] = P.sb("qdec_%d" % jj, [128, 128], BF16)
        t_["eG2"] = P.sb("eG2_%d" % jj, [128, 128])
        t_["egl2"] = P.sb("egl2_%d" % jj, [128, 1])
        t_["x8"] = P.sb("x8_%d" % jj, [128, 16])
        t_["bege"] = P.sb("bege_%d" % jj, [128, 2])
        t_["egc"] = P.sb("egc_%d" % jj, [128, 2])
        t_["osq"] = P.sb("osq_%d" % jj, [128, 128], BF16)
        t_["orn"] = P.sb("orn_%d" % jj, [128, 128])
        t_["on_"] = P.sb("on__%d" % jj, [128, 128])
        t_["Sb"] = P.sb("Sb_%d" % jj, [128, 64], BF16)
        TH.append(t_)
    pbv = [pb[i][:, :] for i in range(7)] + [pT16[:, :].bitcast(F32)]
    x8 = P.sb("x8", [128, 16])
    zs = P.sb("zs", [128, 2, TP])
    sc = P.sb("sc", [128, 4, 256]); Pf = sc; Pn = P.sb("Pn", [128, 4, 256], BF16)
    PT = P.sb("PT", [128, 4, 2, 128], BF16)
    mx = P.sb("mx", [128, 4]); nmx = P.sb("nmx", [128, 4]); sm = P.sb("sm", [128, 4]); es = P.sb("es", [128, 4])
    vtokb = P.sb("vtokb", [128, 128], BF16); vtokf = P.sb("vtokf", [128, 128])
    kcache = P.sb("kcache", [128, 128]); vcache = P.sb("vcache", [128, 128])
    dvT = P.sb("dvT", [128, 256], BF16)
    stT = P.sb("stT", [64, 512])
    stF = P.sb("stF", [128, FC, 48])
    ktok = P.sb("ktok", [128, 128])

    evt = {"i": 0}

    def evac(out, in_, scale=None):
        evt["i"] ^= 1
        if scale is not None:
            if evt["i"]:
                P.act(out, in_, AF.Copy, scale=float(scale))
            else:
                P.ts("dve", out, in_, float(scale), ALU.mult)
        elif evt["i"]:
            P.copy("act", out, in_)
        else:
            P.copy("dve", out, in_)

    def wslab(src):
        t = wring[wpos["w"] % NRING]
        wpos["w"] += 1
        ncol = src.shape[1]
        P.dma("pool", t[:, :, 0:ncol], src.rearrange("(k p) n -> p k n", p=128))
        return t

    def bigmm(ps, slab, c0, ncols, rhs, T, m0=0):
        for k in range(KC):
            P.mm(ps[m0:m0 + ncols, 0:T], slab[:, k, c0:c0 + ncols], rhs[:, k, 0:T],
                 start=(k == 0), stop=(k == KC - 1), inc=(k == KC - 1))

    bigi = {"i": 0}

    def bigps():
        bigi["i"] ^= 1
        return pb[bigi["i"]]

    def rmsnorm(g, T, out_bf=True, out=None):
        ps = pb[2]
        P.act(mixT[:, 0:4, 0:T], xs_t[:, 0:4, 0:T], AF.Square)
        P.tt("dve", sqb[:, 0:4, 0:T], xs_t[:, 4:8, 0:T], xs_t[:, 4:8, 0:T], ALU.mult)
        for k in range(4):
            P.mm(ps[:, 0:T], onesb[:], mixT[:, k, 0:T], start=(k == 0), stop=False, inc=(k == 3))
        for k in range(4):
            P.mm(ps[:, 0:T], onesb[:], sqb[:, k, 0:T], start=False, stop=(k == 3), inc=(k == 3))
        P.act(rstd[:, 0:T], ps[:, 0:T], AF.Sqrt, bias=EPS, scale=1.0 / D)
        P.recip(rstd[:, 0:T], rstd[:, 0:T])
        dst = hT if out is None else out
        for k in range(KC):
            P.stt(dst[:, k, 0:T], xs_t[:, k, 0:T], g[:, k:k + 1], rstd[:, 0:T], ALU.mult, ALU.mult)

    def conv(out3, buf3, w, c, K, seglen, nseg):
        P.ts("dve", out3, buf3[:, :, 0:seglen], w[:, c, 0:1], ALU.mult)
        for j in range(1, K):
            P.stt(out3, buf3[:, :, j:j + seglen], w[:, c, j:j + 1], out3, ALU.mult, ALU.add)

    def store_rows(dst_rows, srcF, nrows, ncols_list):
        nchunk = len(ncols_list)
        for c0 in range(0, nchunk, 4):
            cn = min(4, nchunk - c0)
            for c in range(c0, c0 + cn):
                P.tr(pb[6][0:nrows, (c - c0) * 128:(c - c0 + 1) * 128], srcF(c), ident[:], inc=(c == c0 + cn - 1))
            evac(stT[0:nrows, 0:cn * 128], pb[6][0:nrows, 0:cn * 128])
            P.dma("sp", dst_rows[:, c0 * 128:(c0 + cn) * 128], stT[0:nrows, 0:cn * 128])

    def load_rows(src_rows, nrows, nchunk):
        for c0 in range(0, nchunk, 4):
            cn = min(4, nchunk - c0)
            P.dma("sp", stT[0:nrows, 0:cn * 128], src_rows[:, c0 * 128:(c0 + cn) * 128])
            for c in range(c0, c0 + cn):
                P.tr(pb[6][:, (c - c0) * nrows:(c - c0 + 1) * nrows], stT[0:nrows, (c - c0) * 128:(c - c0 + 1) * 128],
                     ident[0:nrows, 0:nrows], inc=(c == c0 + cn - 1))
            evac(stF[:, c0:c0 + cn, 0:nrows], pb[6][:, 0:cn * nrows].rearrange("p (c r) -> p c r", c=cn))

    def run_layer(l, T, n, nblk, nseg, seglen, prompt, first, last, tile_i):
        d = prm[l]
        hb = lambda ap2, m: ap2.unsqueeze(2).to_broadcast([ap2.shape[0], ap2.shape[1], m])

        def seg3(t2d, K1):
            return t2d[:, 0:nseg * (K1 + seglen)].rearrange("p (s t) -> p s t", s=nseg)

        def tok3(ap2d):
            return ap2d.rearrange("p (s t) -> p s t", s=nseg)

        stage(0)
        rmsnorm(d["g1"], T)
        w_in = I["w_in"][l]
        stage(1)

        pre = [seg3(S1[:, c * (TP + 48):(c + 1) * (TP + 48)], 3) for c in range(6)]
        sl0 = wslab(w_in[:, 0:512])
        sl1 = wslab(w_in[:, 512:1024])
        sl2 = wslab(w_in[:, 1024:1288])
        if prompt:
            for c in range(6):
                if first:
                    P.memset("dve", pre[c][:, :, 0:3], 0.0)
                else:
                    P.copy("dve", pre[c][:, 0, 0:3], d["cA"][:, c, :])
        else:
            load_rows(I["state_delta_conv"][l], 48, 6)
            for c in range(6):
                P.copy("dve", pre[c][:, :, 0:3], stF[:, c, 0:48].rearrange("p (b r) -> p b r", r=3))
        stage(1.1)
        for c in range(4):
            ps = bigps()
            bigmm(ps, sl0, c * 128, 128, hT, T)
            evac(pre[c][:, :, 3:3 + seglen], tok3(ps[:, 0:T]))
        stage(1.2)
        for c in range(2):
            ps = bigps()
            bigmm(ps, sl1, c * 128, 128, hT, T)
            evac(pre[4 + c][:, :, 3:3 + seglen], tok3(ps[:, 0:T]))
        for c in range(2):
            ps = bigps()
            bigmm(ps, sl1, 256 + c * 128, 128, hT, T)
            P.act(zs[:, c, 0:T], ps[:, 0:T], AF.Silu)
        stage(1.3)
        ps = bigps()
        bigmm(ps, sl2, 0, 8, hT, T)
        P.copy("dve", p8[:, 0:T], ps[0:8, 0:T])
        stage(1.4)
        if prompt:
            for c in range(6):
                P.copy("dve", d["cA"][:, c, :], pre[c][:, 0, seglen:seglen + 3])
            if last:
                store_rows(O["ac_p"][l], lambda c: d["cA"][:, c, :], 3, [128] * 6)
        else:
            for c in range(6):
                P.copy("dve", stF[:, c, 0:48].rearrange("p (b r) -> p b r", r=3), pre[c][:, :, seglen:seglen + 3])
            store_rows(O["ac_s"][l], lambda c: stF[:, c, 0:48], 48, [128] * 6)
        stage(1.5)
        for c in range(6):
            conv(tok3(S2[:, c, 0:T]), pre[c], d["wA"], c, 4, seglen, nseg)
        P.act(S2[:, 0:6, 0:T], S2[:, 0:6, 0:T], AF.Silu)
        P.act(sqb[:, 0:4, 0:T], S2[:, 0:4, 0:T], AF.Square)
        sc4 = [tA, tB, gcv, sgb]
        for c in range(4):
            P.mm(pb[c][:, 0:T], bonesb[:], sqb[:, c, 0:T])
        for c in range(4):
            P.act(sc4[c][:, 0:T], pb[c][:, 0:T], AF.Sqrt, bias=EPS)
        for c in range(4):
            P.recip(sc4[c][:, 0:T], sc4[c][:, 0:T])
        for c in range(4):
            if c < 2:
                P.stt(S2[:, c, 0:T], S2[:, c, 0:T], 0.125, sc4[c][:, 0:T], ALU.mult, ALU.mult)
            else:
                P.tt("dve", S2[:, c, 0:T], S2[:, c, 0:T], sc4[c][:, 0:T], ALU.mult)
        stage(1.7)
        P.act(sig8[:, 0:T], p8[:, 0:T], AF.Sigmoid)
        P.act(g8[:, 0:T], p8[:, 0:T], AF.Exp, bias=d["dtb"][:])
        P.act(g8[:, 0:T], g8[:, 0:T], AF.Ln, bias=1.0)
        P.ts("dve", g8[:, 0:T], g8[:, 0:T], d["negA"][:], ALU.mult)
        stage(1.8)
        for b in range(nblk):
            cs = slice(b * n, (b + 1) * n)
            P.mm(pb[5][0:n, 0:8], g8[:, cs], ident[0:8, 0:8])
            P.copy("dve", x8[0:n, 0:8], pb[5][0:n, 0:8])
            P.mm(pb[5][0:8, 16:16 + n], x8[0:n, 0:8], U[0:n, 0:n])
            P.copy("dve", gc8[:, cs], pb[5][0:8, 16:16 + n])
        nlev = int(np.log2(n)) - 1
        stage(2)
        bk = [(pbv[0], pbv[1], pbv[2], pbv[3]), (pbv[4], pbv[5], pbv[6], pbv[7])]

        def gdn_thread(j):
            qa, qb, qc, qd = bk[j]
            t = TH[j]
            Sj = d["S"][j]
            Eu, QKd, Xt = t["Eu"], t["QKd"], t["Xt"]
            Mm = [t["Mm0"], t["Mm1"]]
            Ma = [t["Ma0"], t["Ma1"]]
            tmpE, Eus = Ma[1], Mm[1]
            vb_, kbd, kdec, vnew = t["vb"], t["kbd"], t["kdec"], t["vnew"]
            nwT, qdec, eG2, egl2, x8t = t["nwT"], t["qdec"], t["eG2"], t["egl2"], t["x8"]
            bege, egc, osq, orn, on_, Sb = t["bege"], t["egc"], t["osq"], t["orn"], t["on_"], t["Sb"]
            q2 = lambda t3: t3[0:n, :, 0:n]
            v3 = lambda ap, w: ap.rearrange("p (e t) -> p e t", e=2)
            Ub = U[0:n, 0:n].unsqueeze(1).to_broadcast([n, 2, n])
            Usb = Us[0:n, 0:n].unsqueeze(1).to_broadcast([n, 2, n])
            Ib = ident[0:n, 0:n].unsqueeze(1).to_broadcast([n, 2, n])
            for b in range(nblk):
                cs = slice(b * n, (b + 1) * n)
                if not prompt:
                    for e in range(2):
                        P.dma("sp", Sj[e * 64:(e + 1) * 64, :], I["state_delta"][l, b, 2 * j + e])
                elif first and b == 0:
                    P.memset("dve", Sj[:], 0.0)
                P.tr(qc[0:n, 0:128], S2[:, 2 + j, cs], ident[:], inc=False)
                P.tr(qc[0:n, 128:256], S2[:, 4 + j, cs], ident[:])
                kv_ps = qc[0:n, 0:256].rearrange("p (a e d) -> p a e d", a=2, e=2)
                P.mm(qb[0:n, 256:264], sig8[:, cs], ident[0:8, 0:8], inc=False)
                P.mm(qb[0:n, 264:272], gc8[:, cs], ident[0:8, 0:8])
                G_ps = v3(qd[:, 0:2 * n], n)
                for e in range(2):
                    h = 2 * j + e
                    P.mm(G_ps[:, e, :], selg[:, h * 128:(h + 1) * 128], gc8[:, cs], inc=(e == 1))
                G2_ps = qb[:, 272:272 + n]
                P.mm(G2_ps, sel2[:, j * 128:(j + 1) * 128], gc8[:, cs])
                yield
                P.copy("dve", x8t[0:n, :], qb[0:n, 256:272])
                bt = x8t[0:n, 2 * j:2 * j + 2]
                gct = x8t[0:n, 12 + 2 * j:14 + 2 * j]
                P.act(egc[0:n, :], gct, AF.Exp)
                P.tt("dve", bege[0:n, :], bt, egc[0:n, :], ALU.mult)
                P.act(eG2[:, 0:n], G2_ps, AF.Exp)
                P.copy("dve", egl2[:], eG2[:, n - 1:n])
                P.tt("dve", qdec[:, 0:n], S2[:, j, cs], eG2[:, 0:n], ALU.mult)
                P.tt("dve", q2(tmpE), G_ps[0:n], hb(gct, n), ALU.subtract)
                P.ts("dve", q2(tmpE), q2(tmpE), 0.0, ALU.min)
                P.act(q2(tmpE), q2(tmpE), AF.Exp)
                Bb_ps = v3(qd[0:n, 0:2 * n], n)
                for e in range(2):
                    h = 2 * j + e
                    P.mm(Bb_ps[:, e, :], selb[:, h * 128:h * 128 + n], sig8[:, cs], inc=(e == 1))
                km = [orn[:, 0:n], on_[:, 0:n]]
                for e in range(2):
                    P.ts("dve", km[e], S2[:, 2 + j, cs], bones[:, e * 64:e * 64 + 1], ALU.mult)
                KK_ps = v3(qa[0:n, 0:2 * n], n)
                QK_ps = v3(qb[0:n, 0:2 * n], n)
                for e in range(2):
                    P.mm(KK_ps[:, e, :], km[e], S2[:, 2 + j, cs], inc=(e == 1))
                for e in range(2):
                    P.mm(QK_ps[:, e, :], km[e], S2[:, j, cs], inc=(e == 1))
                yield
                P.tt("dve", q2(Eu), q2(tmpE), Ub, ALU.mult)
                P.tt("dve", q2(Eus), q2(tmpE), Usb, ALU.mult)
                P.tt("dve", q2(Eus), q2(Eus), Bb_ps, ALU.mult)
                P.tt("dve", q2(Mm[0]), KK_ps, q2(Eus), ALU.mult)
                P.tt("dve", q2(QKd), QK_ps, q2(Eu), ALU.mult)
                P.tt("dve", vb_[0:n], kv_ps[:, 1], hb(bt, 64), ALU.mult)
                P.tt("dve", kbd[0:n], kv_ps[:, 0], hb(bege[0:n, :], 64), ALU.mult)
                P.tt("dve", kdec[0:n], kv_ps[:, 0], Eu[0:n, :, n - 1:n].to_broadcast([n, 2, 64]), ALU.mult)
                At_ps = v3(qd[0:n, 0:2 * n], n)
                for e in range(2):
                    P.tr(At_ps[:, e, :], Mm[0][0:n, e, 0:n], ident[0:n, 0:n], inc=(e == 1))
                yield
                P.copy("act", q2(Ma[0]), At_ps)
                P.tt("dve", q2(Xt), Ib, q2(Mm[0]), ALU.subtract)
                for k in range(1, nlev + 1):
                    cur, prv = k % 2, (k - 1) % 2
                    Pa_ps = v3(qa[0:n, 0:2 * n], n)
                    for e in range(2):
                        P.mm(Pa_ps[:, e, :], Mm[prv][0:n, e, 0:n], Ma[prv][0:n, e, 0:n], inc=(e == 1))
                    if k < nlev:
                        Pm_ps = v3(qb[0:n, 0:2 * n], n)
                        for e in range(2):
                            P.mm(Pm_ps[:, e, :], Ma[prv][0:n, e, 0:n], Mm[prv][0:n, e, 0:n], inc=(e == 1))
                    yield
                    P.copy("act", q2(Ma[cur]), Pa_ps)
                    if k < nlev:
                        P.copy("dve", q2(Mm[cur]), Pm_ps)
                    X_ps = v3(qd[0:n, 0:2 * n], n)
                    for e in range(2):
                        P.mm(X_ps[:, e, :], Ma[cur][0:n, e, 0:n], Xt[0:n, e, 0:n], inc=(e == 1))
                    yield
                    P.tt("dve", q2(Xt), q2(Xt), X_ps, ALU.add)
                wT_ps = qc[:, 0:n]
                for e in range(2):
                    P.mm(wT_ps[e * 64:(e + 1) * 64, :], kbd[0:n, e, :], Xt[0:n, e, 0:n], inc=(e == 1))
                yield
                P.ts("dve", nwT[:, 0:n], wT_ps, -1.0, ALU.mult)
                P.copy("act", Sb[:], Sj[:])
                vn_ps = qd[0:n, 0:128].rearrange("p (e d) -> p e d", e=2)
                for e in range(2):
                    pr = slice(e * 64, (e + 1) * 64)
                    P.mm(vn_ps[:, e, :], Xt[0:n, e, 0:n], vb_[0:n, e, :], start=True, stop=False, inc=False)
                    P.mm(vn_ps[:, e, :], nwT[pr, 0:n], Sb[pr, :], start=False, stop=True, inc=(e == 1))
                yield
                P.copy("act", vnew[0:n], vn_ps)
                o_ps = qc[:, 128:128 + n]
                for e in range(2):
                    pr = slice(e * 64, (e + 1) * 64)
                    P.mm(o_ps[pr, :], Sb[pr, :], qdec[pr, 0:n], start=True, stop=False, inc=False)
                    P.mm(o_ps[pr, :], vnew[0:n, e, :], QKd[0:n, e, 0:n], start=False, stop=True, inc=(e == 1))
                Su_ps = qb[:, 384:448]
                for e in range(2):
                    pr = slice(e * 64, (e + 1) * 64)
                    P.mm(Su_ps[pr, :], kdec[0:n, e, :], vnew[0:n, e, :], inc=(e == 1))
                yield
                P.stt(Sj[:], Sj[:], egl2[:, 0:1], Su_ps, ALU.mult, ALU.add)
                P.act(osq[:, 0:n], o_ps, AF.Square)
                ss_ps = qa[:, 256:256 + n]
                P.mm(ss_ps, bonesb[:], osq[:, 0:n])
                yield
                P.act(orn[:, 0:n], ss_ps, AF.Sqrt, bias=EPS, scale=1.0 / 64)
                P.recip(orn[:, 0:n], orn[:, 0:n])
                P.tt("dve", on_[:, 0:n], o_ps, orn[:, 0:n], ALU.mult)
                P.stt(mixT[:, j, cs], on_[:, 0:n], d["gn"][:], zs[:, j, cs], ALU.mult, ALU.mult)
                if not prompt:
                    for e in range(2):
                        P.dma("sp", O["sd_s"][l, b, 2 * j + e], Sj[e * 64:(e + 1) * 64, :])
            if prompt and last:
                for e in range(2):
                    P.dma("sp", O["sd_p"][l, 2 * j + e], Sj[e * 64:(e + 1) * 64, :])

        gens = [gdn_thread(0), gdn_thread(1)]
        alive = [True, True]
        while any(alive):
            for gi, g in enumerate(gens):
                if alive[gi]:
                    try:
                        next(g)
                    except StopIteration:
                        alive[gi] = False


        stage(3)
        sl3 = wslab(w_in[:, 1288:1800])
        preB = [seg3(S1[:, c * (TP + 48):(c + 1) * (TP + 48)], 2) for c in range(2)]
        if prompt:
            for c in range(2):
                if first:
                    P.memset("dve", preB[c][:, :, 0:2], 0.0)
                else:
                    P.copy("dve", preB[c][:, 0, 0:2], d["cB"][:, c, :])
        else:
            load_rows(I["state_shortconv"][l], 32, 2)
            for c in range(2):
                P.copy("dve", preB[c][:, :, 0:2], stF[:, c, 0:32].rearrange("p (b r) -> p b r", r=2))
        for c in range(2):
            ps = bigps()
            bigmm(ps, sl2, 8 + c * 128, 128, hT, T)
            evac(S2[:, c, 0:T], ps[:, 0:T])
        for c in range(2):
            ps = bigps()
            bigmm(ps, sl3, c * 128, 128, hT, T)
            evac(tA[:, 0:T], ps[:, 0:T])
            ps = bigps()
            bigmm(ps, sl3, 256 + c * 128, 128, hT, T)
            P.tt("dve", preB[c][:, :, 2:2 + seglen], tok3(tA[:, 0:T]), tok3(ps[:, 0:T]), ALU.mult)
        if prompt:
            for c in range(2):
                P.copy("dve", d["cB"][:, c, :], preB[c][:, 0, seglen:seglen + 2])
            if last:
                store_rows(O["bc_p"][l], lambda c: d["cB"][:, c, :], 2, [128] * 2)
        else:
            for c in range(2):
                P.copy("dve", stF[:, c, 0:32].rearrange("p (b r) -> p b r", r=2), preB[c][:, :, seglen:seglen + 2])
            store_rows(O["bc_s"][l], lambda c: stF[:, c, 0:32], 32, [128] * 2)
        for c in range(2):
            conv(tok3(tB[:, 0:T]), preB[c], d["wB"], c, 3, seglen, nseg)
            P.tt("dve", mixT[:, 2 + c, 0:T], S2[:, c, 0:T], tB[:, 0:T], ALU.mult)

        stage(4)
        sl4 = wslab(w_in[:, 1800:2312])
        if prompt:
            P.dma("sp", cosT[:, 0:T], C["cosp"][:, tile_i * TP:(tile_i + 1) * TP])
            P.dma("sp", sinT[:, 0:T], C["sinp"][:, tile_i * TP:(tile_i + 1) * TP])
        else:
            P.dma("sp", cosT[:, 0:T], C["coss"])
            P.dma("sp", sinT[:, 0:T], C["sins"])

        def rope(dst_bf, dst_f, src):
            P.mm(pb[2][:, 0:T], perm[:], src)
            P.tt("dve", tB[:, 0:T], src, cosT[:, 0:T], ALU.mult)
            P.tt("dve", tC[:, 0:T], pb[2][:, 0:T], sinT[:, 0:T], ALU.mult)
            if dst_f is not None:
                P.tt("dve", dst_f, tB[:, 0:T], tC[:, 0:T], ALU.add)
                P.copy("act", dst_bf, dst_f)
            else:
                P.tt("dve", dst_bf, tB[:, 0:T], tC[:, 0:T], ALU.add)

        for j in range(2):
            ps = bigps()
            for half in range(2):
                h = half * 2 + j
                bigmm(ps, sl4, h * 64, 64, hT, T, m0=half * 64)
            evac(tA[:, 0:T], ps[:, 0:T])
            rope(qTb[:, j, 0:T], None, tA[:, 0:T])
        ps = bigps()
        bigmm(ps, sl4, 256, 128, hT, T)
        evac(tA[:, 0:T], ps[:, 0:T])
        rope(kTb[:, 0:T], krf[:, 0:T], tA[:, 0:T])
        ps = bigps()
        bigmm(ps, sl4, 384, 128, hT, T)
        evac(S2[:, 7, 0:T], ps[:, 0:T])
        if not prompt:
            P.tr(pb[6][0:64, 0:128], krf[:, 0:64], ident[:])
            evac(ktok[0:64, :], pb[6][0:64, 0:128])
            P.tr(pb[6][0:64, 128:256], S2[:, 7, 0:64], ident[:])
            evac(vtokf[0:64, :], pb[6][0:64, 128:256])
            for b in range(16):
                P.dma("sp", O["ck_s"][l, b, 124:128, :], ktok[b * 4:(b + 1) * 4, :])
                P.dma("sp", O["cv_s"][l, b, 124:128, :], vtokf[b * 4:(b + 1) * 4, :])
            P.dma("sp", O["ck_s"][l, :, 0:124, :], I["cache_win_k"][l, :, 4:128, :])
            P.dma("sp", O["cv_s"][l, :, 0:124, :], I["cache_win_v"][l, :, 4:128, :])
        for b in range(nblk):
            cs = slice(b * n, (b + 1) * n)
            hasprev = (not prompt) or (not (first and b == 0))
            off = 0 if hasprev else 128
            W = 256 - off if prompt else 128 + n
            if prompt:
                kpv = d["kprev"]
                vpv = d["vprev"]
            else:
                P.dma("sp", kcache[:], I["cache_win_k"][l, b])
                P.dma("sp", vcache[:], I["cache_win_v"][l, b])
                P.tr(pb[6][:, 0:128], kcache[:], ident[:])
                evac(d["kprev"][:], pb[6][:, 0:128])
                P.copy("dve", d["vprev"][:], vcache[:])
                kpv = d["kprev"]
                vpv = d["vprev"]
            P.tr(pb[6][0:n, 256:384], S2[:, 7, cs], ident[:])
            P.copy("dve", vtokb[0:n, :], pb[6][0:n, 256:384])
            if prompt and last and b == nblk - 1:
                P.copy("act", vtokf[0:n, :], pb[6][0:n, 256:384])
                P.dma("sp", O["cv_p"][l], vtokf[:])
                P.tr(pb[6][:, 384:512], krf[:, cs], ident[:])
                evac(ktok[:], pb[6][:, 384:512])
                P.dma("sp", O["ck_p"][l], ktok[:])
            S_ps = [pb[0][0:n, :].rearrange("p (h k) -> p h k", h=2), pb[1][0:n, :].rearrange("p (h k) -> p h k", h=2)]
            for h in range(4):
                pr = slice((h // 2) * 64, (h // 2) * 64 + 64)
                dst = S_ps[h // 2][:, h % 2, :]
                if hasprev:
                    P.mm(dst[:, 0:128], qTb[pr, h % 2, cs], kpv[pr, :], inc=False)
                P.mm(dst[:, 128:128 + n], qTb[pr, h % 2, cs], kTb[pr, cs], inc=(h % 2 == 1))
            mk = amask[0:n, off:off + W].unsqueeze(1).to_broadcast([n, 2, W])
            for hh in range(2):
                P.tt("dve", sc[0:n, 2 * hh:2 * hh + 2, 0:W], S_ps[hh][:, :, off:off + W], mk, ALU.add)
            P.reduce(mx[0:n, :], sc[0:n, :, 0:W], ALU.max)
            P.ts("dve", nmx[0:n, :], mx[0:n, :], -0.125, ALU.mult)
            for h in range(4):
                P.act(Pf[0:n, h, 0:W], sc[0:n, h, 0:W], AF.Exp, bias=nmx[0:n, h:h + 1], scale=0.125,
                      accum_out=sm[0:n, h:h + 1])
            P.tt("dve", es[0:n, :], nmx[0:n, :], d["sink"][0:n, :], ALU.add)
            P.act(es[0:n, :], es[0:n, :], AF.Exp)
            P.tt("dve", es[0:n, :], es[0:n, :], sm[0:n, :], ALU.add)
            P.recip(es[0:n, :], es[0:n, :])
            P.tt("dve", Pn[0:n, :, 0:W], Pf[0:n, :, 0:W], hb(es[0:n, :], W), ALU.mult)
            PT_ps = pT16[:, :].rearrange("p (h s q) -> p h s q", h=4, s=2)
            for h in range(4):
                if hasprev:
                    P.tr(PT_ps[:, h, 0, 0:n], Pn[0:n, h, 0:128], identb[0:n, 0:n], inc=False)
                P.tr(PT_ps[0:n, h, 1, 0:n], Pn[0:n, h, 128 - off:128 - off + n], identb[0:n, 0:n], inc=(h == 3))
            if hasprev:
                P.copy("act", PT[:, :, 0, 0:n], PT_ps[:, :, 0, 0:n])
            P.copy("dve", PT[0:n, :, 1, 0:n], PT_ps[0:n, :, 1, 0:n])
            o_ps = pb[6][:, 0:2 * n].rearrange("p (j t) -> p j t", j=2)
            for h in range(4):
                kvs = slice((h // 2) * 64, (h // 2) * 64 + 64)
                dst = o_ps[(h % 2) * 64:(h % 2) * 64 + 64, h // 2, :]
                if hasprev:
                    P.mm(dst, vpv[:, kvs], PT[:, h, 0, 0:n], start=True, stop=False, inc=False)
                P.mm(dst, vtokb[0:n, kvs], PT[0:n, h, 1, 0:n], start=(not hasprev), stop=True, inc=(h == 3))
            evac(mixT[:, 4:6, cs], o_ps)
            if prompt:
                P.copy("dve", d["kprev"][:], kTb[:, cs])
                P.copy("act", d["vprev"][:], vtokb[:])

        stage(5)
        sl5 = wslab(w_in[:, 2312:2824])

        def gelu(dst, ps):
            P.act(tB[:, 0:T], ps, AF.Square)
            P.ts("dve", tB[:, 0:T], tB[:, 0:T], 0.044715, ALU.mult, 1.0, ALU.add)
            P.tt("dve", tB[:, 0:T], tB[:, 0:T], ps, ALU.mult)
            P.act(tB[:, 0:T], tB[:, 0:T], AF.Sigmoid, scale=1.5957691216057308)
            P.tt("dve", dst, tB[:, 0:T], ps, ALU.mult)

        gsc = [tB, gcv, sgb, tA]
        for g_ in range(4):
            bigmm(pb[g_], sl5, g_ * 128, 128, hT, T)
        for g_ in range(4):
            P.act(gsc[g_][:, 0:T], pb[g_][:, 0:T], AF.Square)
        for g_ in range(4):
            P.ts("dve", gsc[g_][:, 0:T], gsc[g_][:, 0:T], 0.044715, ALU.mult, 1.0, ALU.add)
        for g_ in range(4):
            P.tt("dve", gsc[g_][:, 0:T], gsc[g_][:, 0:T], pb[g_][:, 0:T], ALU.mult)
        for g_ in range(4):
            P.act(gsc[g_][:, 0:T], gsc[g_][:, 0:T], AF.Sigmoid, scale=1.5957691216057308)
        for g_ in range(4):
            P.tt("dve", S2[:, g_, 0:T], gsc[g_][:, 0:T], pb[g_][:, 0:T], ALU.mult)
        P.copy("act", sqb[:, 0:2, 0:T], S2[:, 2:4, 0:T])
        P.act(sqb[:, 2:4, 0:T], S2[:, 2:4, 0:T], AF.Square)
        for c in range(2):
            P.mm(pb[2][:, 0:T], onesb[:], sqb[:, c, 0:T], start=(c == 0), stop=(c == 1), inc=(c == 1))
        for c in range(2):
            P.mm(pb[3][:, 0:T], onesb[:], sqb[:, 2 + c, 0:T], start=(c == 0), stop=(c == 1), inc=(c == 1))
        P.act(tA[:, 0:T], pb[2][:, 0:T], AF.Copy, scale=1.0 / 256)
        P.tt("dve", tB[:, 0:T], tA[:, 0:T], tA[:, 0:T], ALU.mult)
        P.stt(tB[:, 0:T], pb[3][:, 0:T], 1.0 / 256, tB[:, 0:T], ALU.mult, ALU.subtract)
        P.act(tB[:, 0:T], tB[:, 0:T], AF.Sqrt, bias=EPS)
        P.recip(tB[:, 0:T], tB[:, 0:T])
        for c in range(2):
            P.tt("dve", S2[:, 2 + c, 0:T], S2[:, 2 + c, 0:T], tA[:, 0:T], ALU.subtract)
            P.tt("dve", S2[:, 2 + c, 0:T], S2[:, 2 + c, 0:T], tB[:, 0:T], ALU.mult)
            P.ts("dve", S2[:, 2 + c, 0:T], S2[:, 2 + c, 0:T], d["lng"][:, c:c + 1], ALU.mult, d["lnb"][:, c:c + 1], ALU.add)
        if not prompt:
            store_rows(O["dv_s"][l], lambda c: S2[:, 2 + c, 0:T], 64, [128] * 2)
        for b in range(nblk):
            cs = slice(b * n, (b + 1) * n)
            for c in range(2):
                P.tr(pb[2][0:n, c * 128:(c + 1) * 128], S2[:, 2 + c, cs], ident[:], inc=(c == 1))
            evac(dvT[0:n, :], pb[2][0:n, 0:256])
            m_ps = pb[3][:, 0:2 * n].rearrange("p (j t) -> p j t", j=2)
            for g in range(4):
                dst = m_ps[(g % 2) * 64:(g % 2) * 64 + 64, g // 2, :]
                P.mm(dst, dvT[0:n, g * 64:(g + 1) * 64], d["WT"][0:n, g, 0:n], start=True, stop=False, inc=False)
                P.mm(dst, onesrow[0:1, 0:64], d["brow"][0:1, g * 128:g * 128 + n], start=False, stop=True, inc=(g == 3))
            P.tt("dve", mixT[:, 6:8, cs], S2[:, 0:2, cs], m_ps, ALU.mult)

        stage(6)
        for s in range(2):
            sl = wslab(I["w_out"][l][:, s * 512:(s + 1) * 512])
            for c in range(4):
                ps = bigps()
                bigmm(ps, sl, c * 128, 128, mixT, T)
                P.tt("dve", xs_t[:, s * 4 + c, 0:T], xs_t[:, s * 4 + c, 0:T], ps[:, 0:T], ALU.add)

        stage(7)
        rmsnorm(d["g2"], T)
        gp3s = [seg3(gpre[:, :], 2), seg3(gpre2[:, :], 2)]
        gcvs = [gcv, sgb]
        if not prompt:
            load_rows(I["state_ffn_conv"][l], 32, FC)
        slabs = {}

        def fslab(kind, s_):
            if (kind, s_) not in slabs:
                ncol = 512 if s_ < 5 else 256
                w_ = I["ffn_w_gate" if kind == "g" else "ffn_w_up"][l]
                slabs[(kind, s_)] = wslab(w_[:, s_ * 512:s_ * 512 + ncol])
            return slabs[(kind, s_)]

        fpi = {"i": 0}

        def fbank():
            fpi["i"] = (fpi["i"] + 1) % 4
            return pb[fpi["i"]]

        def stageA(j):
            s_, c = divmod(j, 4)
            gp3, gcv_ = gp3s[j % 2], gcvs[j % 2]
            ps = fbank()
            bigmm(ps, fslab("g", s_), c * 128, 128, hT, T)
            if prompt:
                if first:
                    P.memset("dve", gp3[:, :, 0:2], 0.0)
                else:
                    P.copy("dve", gp3[:, 0, 0:2], d["cF"][:, j, :])
            else:
                P.copy("dve", gp3[:, :, 0:2], stF[:, j, 0:32].rearrange("p (b r) -> p b r", r=2))
            P.copy("act", gp3[:, :, 2:2 + seglen], tok3(ps[:, 0:T]))
            if prompt:
                P.copy("dve", d["cF"][:, j, :], gp3[:, 0, seglen:seglen + 2])
            else:
                P.copy("dve", stF[:, j, 0:32].rearrange("p (b r) -> p b r", r=2), gp3[:, :, seglen:seglen + 2])
            conv(tok3(gcv_[:, 0:T]), gp3, d["wF"], j, 3, seglen, nseg)

        def stageB(j):
            s_, c = divmod(j, 4)
            gcv_ = gcvs[j % 2]
            P.act(gcv_[:, 0:T], gcv_[:, 0:T], AF.Silu)
            ps = fbank()
            bigmm(ps, fslab("u", s_), c * 128, 128, hT, T)
            P.tt("dve", actb[:, j, 0:T], gcv_[:, 0:T], ps[:, 0:T], ALU.mult)

        stageA(0)
        for j in range(FC):
            if j + 1 < FC:
                stageA(j + 1)
            stageB(j)
        stage(7.5)
        if prompt and last:
            store_rows(O["fc_p"][l], lambda c: d["cF"][:, c, :], 2, [128] * FC)
        if not prompt:
            store_rows(O["fc_s"][l], lambda c: stF[:, c, 0:32], 32, [128] * FC)
        wd = I["ffn_w_down"][l].rearrange("(j p) n -> p j n", p=128)
        acc = [pb[i][:, :] for i in range(7)] + [pT16[:, :].bitcast(F32)]
        for jg in range((FC + 3) // 4):
            nj = min(4, FC - 4 * jg)
            tw = wring[wpos["w"] % NRING]
            wpos["w"] += 1
            t = tw[:, :, :].rearrange("p k n -> p (k n)").rearrange("p (j n) -> p j n", j=4)
            P.dma("pool", t[:, 0:nj, :], wd[:, 4 * jg:4 * jg + nj, :])
            for jj in range(nj):
                j = 4 * jg + jj
                for oc in range(8):
                    P.mm(acc[oc][:, 0:T], t[:, jj, oc * 128:(oc + 1) * 128], actb[:, j, 0:T],
                         start=(j == 0), stop=(j == FC - 1), inc=(oc == 7 or j == FC - 1))
        for oc in range(8):
            P.tt("dve", xs_t[:, oc, 0:T], xs_t[:, oc, 0:T], acc[oc][:, 0:T], ALU.add)
        stage(8)

    def run_tile(prompt, ti):
        if prompt:
            T, n, nblk, nseg, seglen = TP, 128, 4, 1, TP
            src = I["xp"][ti * TP:(ti + 1) * TP, :]
            dst = O["y_p"][ti * TP:(ti + 1) * TP, :]
        else:
            T, n, nblk, nseg, seglen = TS, 4, 16, 16, 4
            src = I["xs"]
            dst = O["y_s"]
        nrb = (T + 127) // 128
        iobufs = [xin[:, :], S1[:, 0:1024], zs[:, :, :].rearrange("p c t -> p (c t)"),
                  sqb[:, :, :].rearrange("p c t -> p (c t)").bitcast(F32)]
        for rb in range(nrb):
            r = min(128, T - rb * 128)
            P.dma("sp", iobufs[rb % 4][0:r, :], src[rb * 128:rb * 128 + r, :])
        for rb in range(nrb):
            r = min(128, T - rb * 128)
            xb = iobufs[rb % 4]
            for g in range(2):
                bank = pb[3 + (2 * rb + g) % 4]
                for k in range(4):
                    P.tr(bank[:, k * r:(k + 1) * r], xb[0:r, (g * 4 + k) * 128:(g * 4 + k + 1) * 128],
                         ident[0:r, 0:r], inc=(k == 3))
                evac(xs_t[:, g * 4:g * 4 + 4, rb * 128:rb * 128 + r],
                     bank[:, 0:4 * r].rearrange("p (k t) -> p k t", k=4))
        for li, l in enumerate([int(c) for c in os.environ.get("MK_LAYERS", "01")]):
            lpos["i"] = li
            run_layer(l, T, n, nblk, nseg, seglen, prompt, first=(ti == 0), last=(ti == NTP - 1), tile_i=ti)
        rmsnorm(gfin, T, out=S2)
        for rb in range(nrb):
            r = min(128, T - rb * 128)
            yb = iobufs[rb % 4]
            for g in range(2):
                bank = pb[3 + (2 * rb + g) % 4]
                for k in range(4):
                    P.tr(bank[0:r, k * 128:(k + 1) * 128], S2[:, g * 4 + k, rb * 128:rb * 128 + r], ident[:],
                         inc=(k == 3))
                evac(yb[0:r, g * 512:(g + 1) * 512], bank[0:r, :])
            P.dma("sp", dst[rb * 128:rb * 128 + r, :], yb[0:r, :])

    ntiles = int(os.environ.get("MK_NTILES", NTP))
    try:
        for ti in range(ntiles):
            run_tile(True, ti)
        if os.environ.get("MK_NOSAMPLE") is None:
            run_tile(False, 0)
    except _Stop:
        P.dma("sp", O["y_s"][0:64, :], xs_t[0:64, 0, 0:TP].rearrange("p t -> p t") if False else xin[0:64, :])
    P.emit()
    P.close()
    return nc, P


_CACHE = {}


def kernel(**inp):
    f = np.float32
    if "nc" not in _CACHE:
        _CACHE["nc"], _ = build_program()
        _CACHE["consts"] = _consts()
    nc = _CACHE["nc"]
    consts = _CACHE["consts"]
    shared = {}
    for k in IN_SHAPES:
        if k in ("xp", "xs", "state_delta", "state_delta_conv", "state_shortconv", "cache_win_k", "cache_win_v",
                 "state_ffn_conv"):
            continue
        shared[k] = np.ascontiguousarray(np.asarray(inp[k], f))
    for k, v in consts.items():
        shared["c_" + k] = np.ascontiguousarray(v.reshape(CONST_SHAPES[k]).astype(f))
    in_maps = []
    for c in range(8):
        m = dict(shared)
        bs = slice(16 * c, 16 * c + 16)
        m["xp"] = np.ascontiguousarray(np.asarray(inp["x_prompt"][c], f))
        m["xs"] = np.ascontiguousarray(np.asarray(inp["x_sample"][bs], f).reshape(64, D))
        m["state_delta"] = np.ascontiguousarray(np.asarray(inp["state_delta"][:, bs], f))
        m["state_delta_conv"] = np.ascontiguousarray(np.asarray(inp["state_delta_conv"][:, bs], f).reshape(2, 48, 768))
        m["state_shortconv"] = np.ascontiguousarray(np.asarray(inp["state_shortconv"][:, bs], f).reshape(2, 32, 256))
        m["cache_win_k"] = np.ascontiguousarray(np.asarray(inp["cache_win_k"][:, bs], f).reshape(2, 16, 128, 128))
        m["cache_win_v"] = np.ascontiguousarray(np.asarray(inp["cache_win_v"][:, bs], f).reshape(2, 16, 128, 128))
        m["state_ffn_conv"] = np.ascontiguousarray(np.asarray(inp["state_ffn_conv"][:, bs], f).reshape(2, 32, DFF))
        in_maps.append(m)
    res = run_bass_kernel_spmd(nc, in_maps, core_ids=list(range(8)))
    R = res.results

    def cat_p(k, shape):
        return np.stack([np.asarray(R[c][k], f).reshape(shape) for c in range(8)], axis=1)

    def cat_s(k, shape):
        return np.concatenate([np.asarray(R[c][k], f).reshape(shape) for c in range(8)], axis=1)

    y_p = np.stack([np.asarray(R[c]["y_p"], f) for c in range(8)], axis=0)
    y_s = np.concatenate([np.asarray(R[c]["y_s"], f).reshape(16, 4, D) for c in range(8)], axis=0)
    return (
        y_p, y_s,
        cat_p("sd_p", (2, 4, 64, 64)), cat_s("sd_s", (2, 16, 4, 64, 64)),
        cat_p("ac_p", (2, 3, 768)), cat_s("ac_s", (2, 16, 3, 768)),
        cat_p("bc_p", (2, 2, 256)), cat_s("bc_s", (2, 16, 2, 256)),
        cat_p("ck_p", (2, 128, 2, 64)), cat_s("ck_s", (2, 16, 128, 2, 64)),
        cat_p("cv_p", (2, 128, 2, 64)), cat_s("cv_s", (2, 16, 128, 2, 64)),
        cat_p("fc_p", (2, 2, DFF)), cat_s("fc_s", (2, 16, 2, DFF)),
        cat_s("dv_s", (2, 16, 4, 256)),
    )
```

```python
import numpy as np
import concourse.bass as bass
import concourse.mybir as mybir

F32 = mybir.dt.float32
BF16 = mybir.dt.bfloat16
I32 = mybir.dt.int32
ALU = mybir.AluOpType
AF = mybir.ActivationFunctionType
AX = mybir.AxisListType

ENGINES = ("pe", "act", "dve", "pool", "sp")


class _Buf:
    __slots__ = ("last_w", "readers")

    def __init__(self):
        self.last_w = None
        self.readers = []


class Prog:
    def __init__(self, nc, ring=12):
        self.nc = nc
        self.ops = {e: [] for e in ENGINES}
        self.cnt = {e: 0 for e in ENGINES}
        self.known = {e: {} for e in ENGINES}
        self.bufs = {}
        self.stack = []
        self.sems = {}
        self.nsem = 0
        self.ring_n = ring
        self.ring = {}
        self.ring_pos = {}
        self.untracked = set()
        self.nops = 0

    def _sem(self, key):
        if key not in self.sems:
            cm = self.nc.semaphore("s_" + key)
            self.sems[key] = cm.__enter__()
            self.stack.append(cm)
            self.nsem += 1
        return self.sems[key]

    def sb(self, name, shape, dtype=F32):
        cm = self.nc.sbuf_tensor(name, list(shape), dtype)
        t = cm.__enter__()
        self.stack.append(cm)
        return t

    def ps(self, name, shape, dtype=F32):
        cm = self.nc.psum_tensor(name, list(shape), dtype)
        t = cm.__enter__()
        self.stack.append(cm)
        return t

    def dram(self, name, shape, dtype=F32, kind="Internal"):
        return self.nc.dram_tensor(name, list(shape), dtype, kind=kind)

    def _key(self, ap):
        if isinstance(ap, str):
            return ap
        return ap.tensor.name

    def _deps(self, eng, reads, writes):
        waits = []
        known = self.known[eng]

        def need(ev):
            if ev is None:
                return
            sk, val, kn = ev
            if eng == "pe" and sk == "pe":
                return
            if known.get(sk, 0) >= val:
                return
            waits.append((sk, val))
            known[sk] = val
            for k, v in kn.items():
                if known.get(k, 0) < v:
                    known[k] = v

        rk = [self._key(a) for a in reads]
        wk = [self._key(a) for a in writes]
        rk = [k for k in rk if k not in self.untracked]
        wk = [k for k in wk if k not in self.untracked]
        for k in rk:
            b = self.bufs.setdefault(k, _Buf())
            need(b.last_w)
        for k in wk:
            b = self.bufs.setdefault(k, _Buf())
            need(b.last_w)
            for r in b.readers:
                need(r)
        return waits, rk, wk

    def _commit(self, ev, rk, wk):
        for k in rk:
            if k not in wk:
                self.bufs[k].readers.append(ev)
        for k in wk:
            b = self.bufs[k]
            b.last_w = ev
            b.readers = []

    def op(self, eng, fn, reads, writes, inc=True):
        waits, rk, wk = self._deps(eng, reads, writes)
        ev = (eng, self.cnt[eng] + 1, dict(self.known[eng]))
        if inc:
            self.cnt[eng] += 1
        self.ops[eng].append((waits, fn, (eng, 1) if inc else None))
        self._commit(ev, rk, wk)
        self.nops += 1

    def dma(self, eng, out, in_, extra_reads=(), extra_writes=(), **kw):
        if eng not in self.ring:
            self.ring[eng] = [["d%s%d" % (eng, i), 0] for i in range(self.ring_n)]
            self.ring_pos[eng] = 0
        slot = self.ring[eng][self.ring_pos[eng]]
        self.ring_pos[eng] = (self.ring_pos[eng] + 1) % self.ring_n
        waits, rk, wk = self._deps(eng, [in_] + list(extra_reads), [out] + list(extra_writes))
        known = self.known[eng]
        if slot[1] > 0 and known.get(slot[0], 0) < 16 * slot[1]:
            waits.append((slot[0], 16 * slot[1]))
            known[slot[0]] = 16 * slot[1]
        slot[1] += 1
        ev = (slot[0], 16 * slot[1], dict(known))

        def fn(e, out=out, in_=in_, kw=kw):
            return e.dma_start(out=out, in_=in_, **kw)

        self.ops[eng].append((waits, fn, (slot[0], 16)))
        self._commit(ev, rk, wk)
        self.nops += 1

    def mm(self, out, lhsT, rhs, start=True, stop=True, inc=True, **kw):
        self.op("pe", lambda e: e.matmul(out, lhsT, rhs, start=start, stop=stop, **kw),
                [lhsT, rhs] + ([] if start else [out]), [out], inc=inc)

    def tr(self, out, in_, ident, inc=True):
        self.op("pe", lambda e: e.transpose(out, in_, ident), [in_, ident], [out], inc=inc)

    def act(self, out, in_, func, bias=None, scale=None, accum_out=None, eng="act"):
        kw = {}
        reads = [in_]
        writes = [out]
        if bias is not None:
            kw["bias"] = bias
            if not isinstance(bias, (int, float)):
                reads.append(bias)
        if scale is not None:
            kw["scale"] = scale
            if not isinstance(scale, (int, float)):
                reads.append(scale)
        if accum_out is not None:
            kw["accum_out"] = accum_out
            writes.append(accum_out)
        self.op("act", lambda e: e.activation(out, in_, func, **kw), reads, writes)

    def tt(self, eng, out, in0, in1, op):
        self.op(eng, lambda e: e.tensor_tensor(out, in0, in1, op), [in0, in1], [out])

    def ts(self, eng, out, in0, s1, op0, s2=None, op1=None, accum_out=None):
        reads = [in0]
        if not isinstance(s1, (int, float)):
            reads.append(s1)
        if s2 is not None and not isinstance(s2, (int, float)):
            reads.append(s2)
        writes = [out]
        kw = {}
        if op1 is not None:
            kw["op1"] = op1
        if accum_out is not None:
            kw["accum_out"] = accum_out
            writes.append(accum_out)
        self.op(eng, lambda e: e.tensor_scalar(out, in0, s1, s2, op0, **kw), reads, writes)

    def stt(self, out, in0, scalar, in1, op0, op1, eng="dve"):
        reads = [in0, in1]
        if not isinstance(scalar, (int, float)):
            reads.append(scalar)
        self.op(eng, lambda e: e.scalar_tensor_tensor(out, in0, scalar, in1, op0, op1), reads, [out])

    def copy(self, eng, out, in_):
        if eng == "act":
            self.op("act", lambda e: e.copy(out, in_), [in_], [out])
        else:
            self.op(eng, lambda e: e.tensor_copy(out, in_), [in_], [out])

    def memset(self, eng, out, val):
        self.op(eng, lambda e: e.memset(out, val), [], [out])

    def reduce(self, out, in_, op, axis=AX.X, eng="dve"):
        self.op(eng, lambda e: e.tensor_reduce(out, in_, axis, op), [in_], [out])

    def recip(self, out, in_):
        self.op("dve", lambda e: e.reciprocal(out, in_), [in_], [out])

    def emit(self):
        nc = self.nc
        fin = []
        for eng, ring in self.ring.items():
            for key, uses in ring:
                if uses > 0:
                    fin.append((key, 16 * uses))
        for e in ("pe", "act", "dve", "pool"):
            if self.cnt[e] > 0:
                fin.append((e, self.cnt[e]))
        for k in list(self.sems) + [f[0] for f in fin]:
            self._sem(k)
        for eng in ENGINES:
            for waits, fn, inc in self.ops[eng]:
                for sk, v in waits:
                    self._sem(sk)
                if inc is not None:
                    self._sem(inc[0])
        sems = self.sems
        ops = self.ops

        def replay(e, lst, final=False):
            for waits, fn, inc in lst:
                for sk, v in waits:
                    e.wait_ge(sems[sk], v)
                ins = fn(e)
                if inc is not None:
                    ins.then_inc(sems[inc[0]], inc[1])
            if final:
                for sk, v in fin:
                    e.wait_ge(sems[sk], v)

        with nc.Block() as block:
            @block.sync
            def _(e):
                replay(e, ops["sp"], final=True)

            @block.tensor
            def _(e):
                replay(e, ops["pe"])

            @block.scalar
            def _(e):
                replay(e, ops["act"])

            @block.vector
            def _(e):
                replay(e, ops["dve"])

            @block.gpsimd
            def _(e):
                replay(e, ops["pool"])

    def close(self):
        while self.stack:
            self.stack.pop().__exit__(None, None, None)


from concourse.bass_utils import run_bass_kernel_spmd

D = 1024
KC = 8
PTOT = 2824
DFF = 2816
FC = 22
EPS = 1e-6
PAST = 16384
NEG = -30000.0
TP = 512
NTP = 8
TS = 64


def _consts():
    f = np.float32
    c = {}
    c["ident"] = np.eye(128, dtype=f)
    s = np.arange(128)
    c["U"] = (s[:, None] <= s[None, :]).astype(f)
    c["Us"] = (s[:, None] < s[None, :]).astype(f)
    c["ones"] = np.ones((128, 128), f)
    bo = np.zeros((128, 128), f)
    bo[:64, :64] = 1
    bo[64:, 64:] = 1
    c["bones"] = bo
    sel = np.zeros((8, 4, 128), f)
    for h in range(4):
        sel[4 + h, h, :] = 1
    c["selg"] = sel.reshape(8, 512)
    selb = np.zeros((8, 4, 128), f)
    for h in range(4):
        selb[h, h, :] = 1
    c["selb"] = selb.reshape(8, 512)
    sel2 = np.zeros((8, 2, 128), f)
    for j in range(2):
        sel2[4 + 2 * j, j, :64] = 1
        sel2[4 + 2 * j + 1, j, 64:] = 1
    c["sel2"] = sel2.reshape(8, 256)
    perm = np.zeros((128, 128), f)
    for half in range(2):
        for d in range(8):
            r = half * 64 + d
            perm[r + 8, r] = -1.0
            perm[r, r + 8] = 1.0
    c["perm"] = perm
    inv = np.power(np.float32(500000.0), -np.arange(8, dtype=f) * f(2.0 / 16)).astype(f)

    def cs(pos):
        ang = pos.astype(f)[None, :] * inv[:, None]
        C = np.ones((128, pos.shape[0]), f)
        S = np.zeros((128, pos.shape[0]), f)
        for half in range(2):
            for q in range(2):
                r0 = half * 64 + q * 8
                C[r0:r0 + 8] = np.cos(ang)
                S[r0:r0 + 8] = np.sin(ang)
        return C, S

    c["cosp"], c["sinp"] = cs(np.arange(4096))
    c["coss"], c["sins"] = cs(np.tile(PAST + np.arange(4), 16))
    i = np.arange(128)[:, None]
    j = np.arange(256)[None, :]
    c["amask"] = np.where((j > i) & (j <= i + 128), 0.0, NEG).astype(f)
    rp = np.ones((8, TP), f)
    rp[:, ::128] = 0
    c["rstp"] = rp
    rs = np.ones((8, TS), f)
    rs[:, ::4] = 0
    c["rsts"] = rs
    return c


CONST_SHAPES = {"ident": (128, 128), "U": (128, 128), "Us": (128, 128), "ones": (128, 128), "bones": (128, 128),
                "selg": (8, 512), "selb": (8, 512), "sel2": (8, 256), "perm": (128, 128),
                "cosp": (128, 4096), "sinp": (128, 4096), "coss": (128, 64), "sins": (128, 64),
                "amask": (128, 256), "rstp": (8, TP), "rsts": (8, TS)}

IN_SHAPES = {
    "xp": (4096, D), "xs": (64, D),
    "state_delta": (2, 16, 4, 64, 64), "state_delta_conv": (2, 48, 768), "state_shortconv": (2, 32, 256),
    "cache_win_k": (2, 16, 128, 128), "cache_win_v": (2, 16, 128, 128), "state_ffn_conv": (2, 32, DFF),
    "norm1_g": (2, D), "w_in": (2, D, PTOT), "a_conv_w": (2, 4, 768), "a_log": (2, 4), "a_dt_bias": (2, 4),
    "a_norm_g": (2, 64), "b_conv_w": (2, 3, 256), "c_sinks": (2, 4), "d_ln_g": (2, 256), "d_ln_b": (2, 256),
    "d_ws": (2, 4, 128, 128), "d_bias": (2, 4, 128), "w_out": (2, D, D), "norm2_g": (2, D),
    "ffn_w_gate": (2, D, DFF), "ffn_w_up": (2, D, DFF), "ffn_conv_w": (2, 3, DFF), "ffn_w_down": (2, DFF, D),
    "final_norm_g": (D,),
}
OUT_SHAPES = {
    "y_p": (4096, D), "y_s": (64, D), "sd_p": (2, 4, 64, 64), "sd_s": (2, 16, 4, 64, 64),
    "ac_p": (2, 3, 768), "ac_s": (2, 48, 768), "bc_p": (2, 2, 256), "bc_s": (2, 32, 256),
    "ck_p": (2, 128, 128), "ck_s": (2, 16, 128, 128), "cv_p": (2, 128, 128), "cv_s": (2, 16, 128, 128),
    "fc_p": (2, 2, DFF), "fc_s": (2, 32, DFF), "dv_s": (2, 64, 256),
}


def build_program():
    nc = bass.Bass("TRN2", target_bir_lowering=False)
    P = Prog(nc, ring=12)
    I = {k: nc.dram_tensor(k, list(s), F32, kind="ExternalInput").ap() for k, s in IN_SHAPES.items()}
    C = {k: nc.dram_tensor("c_" + k, list(s), F32, kind="ExternalInput").ap() for k, s in CONST_SHAPES.items()}
    O = {k: nc.dram_tensor(k, list(s), F32, kind="ExternalOutput").ap() for k, s in OUT_SHAPES.items()}
    for k in list(IN_SHAPES) + ["c_" + k for k in CONST_SHAPES] + list(OUT_SHAPES):
        P.untracked.add(k)
    NC_ = {"ns": 0}

    def small(out, in_):
        P.dma("sp", out, in_, allow_slow_non_contiguous=True)

    def cload(name, shape, src, bf=False):
        t = P.sb("k_" + name, shape)
        P.dma("sp", t[:], src)
        if bf:
            tb = P.sb("kb_" + name, shape, BF16)
            P.copy("dve", tb[:], t[:])
            return t, tb
        return t

    ident, identb = cload("ident", [128, 128], C["ident"], True)
    U = cload("U", [128, 128], C["U"])
    Us = cload("Us", [128, 128], C["Us"])
    ones, onesb = cload("ones", [128, 128], C["ones"], True)
    bones, bonesb = cload("bones", [128, 128], C["bones"], True)
    selg = cload("selg", [8, 512], C["selg"])
    selb = cload("selb", [8, 512], C["selb"])
    sel2 = cload("sel2", [8, 256], C["sel2"])
    perm = cload("perm", [128, 128], C["perm"])
    amask = cload("amask", [128, 256], C["amask"])

    pb = [P.ps("pb%d" % i, [128, 512]) for i in range(7)]
    pT16 = P.ps("pT16", [128, 1024], BF16)

    S2flat = P.sb("S2", [128, 11 * TP])

    def ptrans(dst3, src_rows, R, Cn):
        P.dma("sp", S2flat[0:R, 0:Cn * 128], src_rows)
        for c in range(Cn):
            P.tr(pb[6][:, c * R:(c + 1) * R], S2flat[0:R, c * 128:(c + 1) * 128], ident[0:R, 0:R], inc=(c == Cn - 1))
        P.copy("dve", dst3, pb[6][:, 0:Cn * R].rearrange("p (c r) -> p c r", c=Cn))

    class _Stop(Exception):
        pass

    import os
    stop_at = float(os.environ.get("MK_STOP", "99"))

    lpos = {"i": 0}

    def stage(k):
        if lpos["i"] * 10 + k >= stop_at:
            raise _Stop()

    prm = []
    wtmp_sh = S2flat[:, 0:512].rearrange("p (g s) -> p g s", g=4)
    brow_sh = S2flat[0:1, 512:1024]
    for l in range(2):
        d = {}
        d["g1"] = P.sb("g1_%d" % l, [128, 8])
        ptrans(d["g1"][:].unsqueeze(1), I["norm1_g"][l].rearrange("(k p) -> k p", p=128), 8, 1)
        d["g2"] = P.sb("g2_%d" % l, [128, 8])
        ptrans(d["g2"][:].unsqueeze(1), I["norm2_g"][l].rearrange("(k p) -> k p", p=128), 8, 1)
        d["wA"] = P.sb("wA_%d" % l, [128, 6, 4])
        ptrans(d["wA"][:], I["a_conv_w"][l], 4, 6)
        d["wB"] = P.sb("wB_%d" % l, [128, 2, 3])
        ptrans(d["wB"][:], I["b_conv_w"][l], 3, 2)
        d["wF"] = P.sb("wF_%d" % l, [128, FC, 3])
        ptrans(d["wF"][:], I["ffn_conv_w"][l], 3, FC)
        d["dtb"] = P.sb("dtb_%d" % l, [8, 1])
        P.memset("dve", d["dtb"][:], 0.0)
        small(d["dtb"][4:8, :], I["a_dt_bias"][l].rearrange("(h o) -> h o", o=1))
        al = P.sb("al_%d" % l, [8, 1])
        P.memset("dve", al[:], 0.0)
        small(al[4:8, :], I["a_log"][l].rearrange("(h o) -> h o", o=1))
        d["negA"] = P.sb("negA_%d" % l, [8, 1])
        P.act(d["negA"][:], al[:], AF.Exp)
        P.ts("dve", d["negA"][:], d["negA"][:], -1.0, ALU.mult)
        d["gn"] = P.sb("gn_%d" % l, [128, 1])
        small(d["gn"][0:64, :], I["a_norm_g"][l].rearrange("(h o) -> h o", o=1))
        small(d["gn"][64:128, :], I["a_norm_g"][l].rearrange("(h o) -> h o", o=1))
        d["sink"] = P.sb("sink_%d" % l, [128, 4])
        small(d["sink"][:], I["c_sinks"][l:l + 1, :].partition_broadcast(128).rearrange("p o h -> p (o h)"))
        d["lng"] = P.sb("lng_%d" % l, [128, 2])
        ptrans(d["lng"][:].unsqueeze(1), I["d_ln_g"][l].rearrange("(c p) -> c p", p=128), 2, 1)
        d["lnb"] = P.sb("lnb_%d" % l, [128, 2])
        ptrans(d["lnb"][:].unsqueeze(1), I["d_ln_b"][l].rearrange("(c p) -> c p", p=128), 2, 1)
        d["WT"] = P.sb("WT_%d" % l, [128, 4, 128], BF16)
        wtmp = wtmp_sh
        P.dma("sp", wtmp[:], I["d_ws"][l].rearrange("g t s -> t g s"))
        for g in range(4):
            P.tr(pb[6][:, g * 128:(g + 1) * 128], wtmp[:, g, :], ident[:], inc=(g == 3))
        P.tt("dve", d["WT"][:], pb[6][:].rearrange("p (g t) -> p g t", g=4),
             U[:].unsqueeze(1).to_broadcast([128, 4, 128]), ALU.mult)
        brow = brow_sh
        P.dma("sp", brow[:], I["d_bias"][l:l + 1].rearrange("o g t -> o (g t)"))
        d["brow"] = P.sb("browb_%d" % l, [1, 512], BF16)
        P.copy("dve", d["brow"][:], brow[:])
        d["S"] = [P.sb("S_%d_%d" % (l, jj), [128, 64]) for jj in range(2)]
        d["cA"] = P.sb("cA_%d" % l, [128, 6, 3])
        d["cB"] = P.sb("cB_%d" % l, [128, 2, 2])
        d["cF"] = P.sb("cF_%d" % l, [128, FC, 2])
        d["kprev"] = P.sb("kprev_%d" % l, [128, 128], BF16)
        d["vprev"] = P.sb("vprev_%d" % l, [128, 128], BF16)
        prm.append(d)
    gfin = P.sb("gfin", [128, 8])
    ptrans(gfin[:].unsqueeze(1), I["final_norm_g"].rearrange("(k p) -> k p", p=128), 8, 1)
    onesrow = onesb

    xs_t = P.sb("xs_t", [128, 8, TP])
    hT = P.sb("hT", [128, 8, TP], BF16)
    mixT = P.sb("mixT", [128, 8, TP], BF16)
    sqb = P.sb("sqb", [128, 4, TP], BF16)
    rstd = P.sb("rstd", [128, TP])
    S1 = P.sb("S1", [128, 6 * (TP + 3 * 16)])
    S2 = S2flat[:, 0:8 * TP].rearrange("p (c t) -> p c t", c=8)
    p8 = P.sb("p8", [8, TP])
    sig8 = P.sb("sig8", [8, TP])
    g8 = P.sb("g8", [8, TP])
    gc8 = P.sb("gc8", [8, TP])
    actb = S2flat[:].bitcast(BF16).rearrange("p (j t) -> p j t", j=FC)
    sgb = P.sb("sgb", [128, TP])
    gpre = P.sb("gpre", [128, TP + 2 * 16])
    gcv = P.sb("gcv", [128, TP])
    gpre2 = P.sb("gpre2", [128, TP + 2 * 16])
    xin = P.sb("xio", [128, D])
    yout = xin
    NRING = 5
    wring = [P.sb("wr%d" % i, [128, 8, 512], BF16) for i in range(NRING)]
    wpos = {"w": 0, "d": 0}
    tA = P.sb("tA", [128, TP])
    tB = P.sb("tB", [128, TP])
    tC = gcv
    cosT = sgb
    sinT = gpre
    qTb = P.sb("qTb", [128, 2, TP], BF16)
    kTb = P.sb("kTb", [128, TP], BF16)
    krf = P.sb("krf", [128, TP])
    def T3(name, dt=F32):
        return P.sb(name, [128, 4, 128], dt)
    TH = []
    for jj in range(2):
        t_ = {}
        for nm in ("Eu", "QKd", "Mm0", "Mm1", "Ma0", "Ma1", "Xt"):
            t_[nm] = P.sb("%s_%d" % (nm, jj), [128, 2, 128])
        for nm in ("vb", "kbd", "kdec", "vnew"):
            t_[nm] = P.sb("%s_%d" % (nm, jj), [128, 2, 64])
        t_["nwT"] = P.sb("nwT_%d" % jj, [128, 128], BF16)
        t_["qdec"] = P.sb("qdec_%d" % jj, [128, 128], BF16)
        t_["eG2"] = P.sb("eG2_%d" % jj, [128, 128])
        t_["egl2"] = P.sb("egl2_%d" % jj, [128, 1])
        t_["x8"] = P.sb("x8_%d" % jj, [128, 16])
        t_["bege"] = P.sb("bege_%d" % jj, [128, 2])
        t_["egc"] = P.sb("egc_%d" % jj, [128, 2])
        t_["osq"] = P.sb("osq_%d" % jj, [128, 128], BF16)
        t_["orn"] = P.sb("orn_%d" % jj, [128, 128])
        t_["on_"] = P.sb("on__%d" % jj, [128, 128])
        t_["Sb"] = P.sb("Sb_%d" % jj, [128, 64], BF16)
        TH.append(t_)
    pbv = [pb[i][:, :] for i in range(7)] + [pT16[:, :].bitcast(F32)]
    x8 = P.sb("x8", [128, 16])
    zs = P.sb("zs", [128, 2, TP])
    sc = P.sb("sc", [128, 4, 256]); Pf = sc; Pn = P.sb("Pn", [128, 4, 256], BF16)
    PT = P.sb("PT", [128, 4, 2, 128], BF16)
    mx = P.sb("mx", [128, 4]); nmx = P.sb("nmx", [128, 4]); sm = P.sb("sm", [128, 4]); es = P.sb("es", [128, 4])
    vtokb = P.sb("vtokb", [128, 128], BF16); vtokf = P.sb("vtokf", [128, 128])
    kcache = P.sb("kcache", [128, 128]); vcache = P.sb("vcache", [128, 128])
    dvT = P.sb("dvT", [128, 256], BF16)
    stT = P.sb("stT", [64, 512])
    stF = P.sb("stF", [128, FC, 48])
    ktok = P.sb("ktok", [128, 128])

    evt = {"i": 0}

    def evac(out, in_, scale=None):
        evt["i"] ^= 1
        if scale is not None:
            if evt["i"]:
                P.act(out, in_, AF.Copy, scale=float(scale))
            else:
                P.ts("dve", out, in_, float(scale), ALU.mult)
        elif evt["i"]:
            P.copy("act", out, in_)
        else:
            P.copy("dve", out, in_)

    def wslab(src):
        t = wring[wpos["w"] % NRING]
        wpos["w"] += 1
        ncol = src.shape[1]
        P.dma("pool", t[:, :, 0:ncol], src.rearrange("(k p) n -> p k n", p=128))
        return t

    def bigmm(ps, slab, c0, ncols, rhs, T, m0=0):
        for k in range(KC):
            P.mm(ps[m0:m0 + ncols, 0:T], slab[:, k, c0:c0 + ncols], rhs[:, k, 0:T],
                 start=(k == 0), stop=(k == KC - 1), inc=(k == KC - 1))

    bigi = {"i": 0}

    def bigps():
        bigi["i"] ^= 1
        return pb[bigi["i"]]

    def rmsnorm(g, T, out_bf=True, out=None):
        ps = pb[2]
        P.act(mixT[:, 0:4, 0:T], xs_t[:, 0:4, 0:T], AF.Square)
        P.tt("dve", sqb[:, 0:4, 0:T], xs_t[:, 4:8, 0:T], xs_t[:, 4:8, 0:T], ALU.mult)
        for k in range(4):
            P.mm(ps[:, 0:T], onesb[:], mixT[:, k, 0:T], start=(k == 0), stop=False, inc=(k == 3))
        for k in range(4):
            P.mm(ps[:, 0:T], onesb[:], sqb[:, k, 0:T], start=False, stop=(k == 3), inc=(k == 3))
        P.act(rstd[:, 0:T], ps[:, 0:T], AF.Sqrt, bias=EPS, scale=1.0 / D)
        P.recip(rstd[:, 0:T], rstd[:, 0:T])
        dst = hT if out is None else out
        for k in range(KC):
            P.stt(dst[:, k, 0:T], xs_t[:, k, 0:T], g[:, k:k + 1], rstd[:, 0:T], ALU.mult, ALU.mult)

    def conv(out3, buf3, w, c, K, seglen, nseg):
        P.ts("dve", out3, buf3[:, :, 0:seglen], w[:, c, 0:1], ALU.mult)
        for j in range(1, K):
            P.stt(out3, buf3[:, :, j:j + seglen], w[:, c, j:j + 1], out3, ALU.mult, ALU.add)

    def store_rows(dst_rows, srcF, nrows, ncols_list):
        nchunk = len(ncols_list)
        for c0 in range(0, nchunk, 4):
            cn = min(4, nchunk - c0)
            for c in range(c0, c0 + cn):
                P.tr(pb[6][0:nrows, (c - c0) * 128:(c - c0 + 1) * 128], srcF(c), ident[:], inc=(c == c0 + cn - 1))
            evac(stT[0:nrows, 0:cn * 128], pb[6][0:nrows, 0:cn * 128])
            P.dma("sp", dst_rows[:, c0 * 128:(c0 + cn) * 128], stT[0:nrows, 0:cn * 128])

    def load_rows(src_rows, nrows, nchunk):
        for c0 in range(0, nchunk, 4):
            cn = min(4, nchunk - c0)
            P.dma("sp", stT[0:nrows, 0:cn * 128], src_rows[:, c0 * 128:(c0 + cn) * 128])
            for c in range(c0, c0 + cn):
                P.tr(pb[6][:, (c - c0) * nrows:(c - c0 + 1) * nrows], stT[0:nrows, (c - c0) * 128:(c - c0 + 1) * 128],
                     ident[0:nrows, 0:nrows], inc=(c == c0 + cn - 1))
            evac(stF[:, c0:c0 + cn, 0:nrows], pb[6][:, 0:cn * nrows].rearrange("p (c r) -> p c r", c=cn))

    def run_layer(l, T, n, nblk, nseg, seglen, prompt, first, last, tile_i):
        d = prm[l]
        hb = lambda ap2, m: ap2.unsqueeze(2).to_broadcast([ap2.shape[0], ap2.shape[1], m])

        def seg3(t2d, K1):
            return t2d[:, 0:nseg * (K1 + seglen)].rearrange("p (s t) -> p s t", s=nseg)

        def tok3(ap2d):
            return ap2d.rearrange("p (s t) -> p s t", s=nseg)

        stage(0)
        rmsnorm(d["g1"], T)
        w_in = I["w_in"][l]
        stage(1)

        pre = [seg3(S1[:, c * (TP + 48):(c + 1) * (TP + 48)], 3) for c in range(6)]
        sl0 = wslab(w_in[:, 0:512])
        sl1 = wslab(w_in[:, 512:1024])
        sl2 = wslab(w_in[:, 1024:1288])
        if prompt:
            for c in range(6):
                if first:
                    P.memset("dve", pre[c][:, :, 0:3], 0.0)
                else:
                    P.copy("dve", pre[c][:, 0, 0:3], d["cA"][:, c, :])
        else:
            load_rows(I["state_delta_conv"][l], 48, 6)
            for c in range(6):
                P.copy("dve", pre[c][:, :, 0:3], stF[:, c, 0:48].rearrange("p (b r) -> p b r", r=3))
        stage(1.1)
        for c in range(4):
            ps = bigps()
            bigmm(ps, sl0, c * 128, 128, hT, T)
            P.copy("act", pre[c][:, :, 3:3 + seglen], tok3(ps[:, 0:T]))
        stage(1.2)
        for c in range(2):
            ps = bigps()
            bigmm(ps, sl1, c * 128, 128, hT, T)
            P.copy("act", pre[4 + c][:, :, 3:3 + seglen], tok3(ps[:, 0:T]))
        for c in range(2):
            ps = bigps()
            bigmm(ps, sl1, 256 + c * 128, 128, hT, T)
            P.act(zs[:, c, 0:T], ps[:, 0:T], AF.Silu)
        stage(1.3)
        ps = bigps()
        bigmm(ps, sl2, 0, 8, hT, T)
        P.copy("dve", p8[:, 0:T], ps[0:8, 0:T])
        stage(1.4)
        if prompt:
            for c in range(6):
                P.copy("dve", d["cA"][:, c, :], pre[c][:, 0, seglen:seglen + 3])
            if last:
                store_rows(O["ac_p"][l], lambda c: d["cA"][:, c, :], 3, [128] * 6)
        else:
            for c in range(6):
                P.copy("dve", stF[:, c, 0:48].rearrange("p (b r) -> p b r", r=3), pre[c][:, :, seglen:seglen + 3])
            store_rows(O["ac_s"][l], lambda c: stF[:, c, 0:48], 48, [128] * 6)
        stage(1.5)
        for c in range(6):
            conv(tok3(S2[:, c, 0:T]), pre[c], d["wA"], c, 4, seglen, nseg)
            P.act(S2[:, c, 0:T], S2[:, c, 0:T], AF.Silu)
        P.act(sqb[:, 0:4, 0:T], S2[:, 0:4, 0:T], AF.Square)
        sc4 = [tA, tB, gcv, sgb]
        for c in range(4):
            P.mm(pb[c][:, 0:T], bonesb[:], sqb[:, c, 0:T])
        for c in range(4):
            P.act(sc4[c][:, 0:T], pb[c][:, 0:T], AF.Sqrt, bias=EPS)
        for c in range(4):
            P.recip(sc4[c][:, 0:T], sc4[c][:, 0:T])
        for c in range(4):
            if c < 2:
                P.stt(S2[:, c, 0:T], S2[:, c, 0:T], 0.125, sc4[c][:, 0:T], ALU.mult, ALU.mult)
            else:
                P.tt("dve", S2[:, c, 0:T], S2[:, c, 0:T], sc4[c][:, 0:T], ALU.mult)
        stage(1.7)
        P.act(sig8[:, 0:T], p8[:, 0:T], AF.Sigmoid)
        P.act(g8[:, 0:T], p8[:, 0:T], AF.Exp, bias=d["dtb"][:])
        P.act(g8[:, 0:T], g8[:, 0:T], AF.Ln, bias=1.0)
        P.ts("dve", g8[:, 0:T], g8[:, 0:T], d["negA"][:], ALU.mult)
        stage(1.8)
        for b in range(nblk):
            cs = slice(b * n, (b + 1) * n)
            P.mm(pb[5][0:n, 0:8], g8[:, cs], ident[0:8, 0:8])
            P.copy("dve", x8[0:n, 0:8], pb[5][0:n, 0:8])
            P.mm(pb[5][0:8, 16:16 + n], x8[0:n, 0:8], U[0:n, 0:n])
            P.copy("dve", gc8[:, cs], pb[5][0:8, 16:16 + n])
        nlev = int(np.log2(n)) - 1
        stage(2)
        bk = [(pbv[0], pbv[1], pbv[2], pbv[3]), (pbv[4], pbv[5], pbv[6], pbv[7])]

        def gdn_thread(j):
            qa, qb, qc, qd = bk[j]
            t = TH[j]
            Sj = d["S"][j]
            Eu, QKd, Xt = t["Eu"], t["QKd"], t["Xt"]
            Mm = [t["Mm0"], t["Mm1"]]
            Ma = [t["Ma0"], t["Ma1"]]
            tmpE, Eus = Ma[1], Mm[1]
            vb_, kbd, kdec, vnew = t["vb"], t["kbd"], t["kdec"], t["vnew"]
            nwT, qdec, eG2, egl2, x8t = t["nwT"], t["qdec"], t["eG2"], t["egl2"], t["x8"]
            bege, egc, osq, orn, on_, Sb = t["bege"], t["egc"], t["osq"], t["orn"], t["on_"], t["Sb"]
            q2 = lambda t3: t3[0:n, :, 0:n]
            v3 = lambda ap, w: ap.rearrange("p (e t) -> p e t", e=2)
            Ub = U[0:n, 0:n].unsqueeze(1).to_broadcast([n, 2, n])
            Usb = Us[0:n, 0:n].unsqueeze(1).to_broadcast([n, 2, n])
            Ib = ident[0:n, 0:n].unsqueeze(1).to_broadcast([n, 2, n])
            for b in range(nblk):
                cs = slice(b * n, (b + 1) * n)
                if not prompt:
                    for e in range(2):
                        P.dma("sp", Sj[e * 64:(e + 1) * 64, :], I["state_delta"][l, b, 2 * j + e])
                elif first and b == 0:
                    P.memset("dve", Sj[:], 0.0)
                P.tr(qc[0:n, 0:128], S2[:, 2 + j, cs], ident[:], inc=False)
                P.tr(qc[0:n, 128:256], S2[:, 4 + j, cs], ident[:])
                kv_ps = qc[0:n, 0:256].rearrange("p (a e d) -> p a e d", a=2, e=2)
                P.mm(qb[0:n, 256:264], sig8[:, cs], ident[0:8, 0:8], inc=False)
                P.mm(qb[0:n, 264:272], gc8[:, cs], ident[0:8, 0:8])
                G_ps = v3(qd[:, 0:2 * n], n)
                for e in range(2):
                    h = 2 * j + e
                    P.mm(G_ps[:, e, :], selg[:, h * 128:(h + 1) * 128], gc8[:, cs], inc=(e == 1))
                G2_ps = qb[:, 272:272 + n]
                P.mm(G2_ps, sel2[:, j * 128:(j + 1) * 128], gc8[:, cs])
                yield
                P.copy("dve", x8t[0:n, :], qb[0:n, 256:272])
                bt = x8t[0:n, 2 * j:2 * j + 2]
                gct = x8t[0:n, 12 + 2 * j:14 + 2 * j]
                P.act(egc[0:n, :], gct, AF.Exp)
                P.tt("dve", bege[0:n, :], bt, egc[0:n, :], ALU.mult)
                P.act(eG2[:, 0:n], G2_ps, AF.Exp)
                P.copy("dve", egl2[:], eG2[:, n - 1:n])
                P.tt("dve", qdec[:, 0:n], S2[:, j, cs], eG2[:, 0:n], ALU.mult)
                P.tt("dve", q2(tmpE), G_ps[0:n], hb(gct, n), ALU.subtract)
                P.ts("dve", q2(tmpE), q2(tmpE), 0.0, ALU.min)
                P.act(q2(tmpE), q2(tmpE), AF.Exp)
                Bb_ps = v3(qd[0:n, 0:2 * n], n)
                for e in range(2):
                    h = 2 * j + e
                    P.mm(Bb_ps[:, e, :], selb[:, h * 128:h * 128 + n], sig8[:, cs], inc=(e == 1))
                km = [orn[:, 0:n], on_[:, 0:n]]
                for e in range(2):
                    P.ts("dve", km[e], S2[:, 2 + j, cs], bones[:, e * 64:e * 64 + 1], ALU.mult)
                KK_ps = v3(qa[0:n, 0:2 * n], n)
                QK_ps = v3(qb[0:n, 0:2 * n], n)
                for e in range(2):
                    P.mm(KK_ps[:, e, :], km[e], S2[:, 2 + j, cs], inc=(e == 1))
                for e in range(2):
                    P.mm(QK_ps[:, e, :], km[e], S2[:, j, cs], inc=(e == 1))
                yield
                P.tt("dve", q2(Eu), q2(tmpE), Ub, ALU.mult)
                P.tt("dve", q2(Eus), q2(tmpE), Usb, ALU.mult)
                P.tt("dve", q2(Eus), q2(Eus), Bb_ps, ALU.mult)
                P.tt("dve", q2(Mm[0]), KK_ps, q2(Eus), ALU.mult)
                P.tt("dve", q2(QKd), QK_ps, q2(Eu), ALU.mult)
                P.tt("dve", vb_[0:n], kv_ps[:, 1], hb(bt, 64), ALU.mult)
                P.tt("dve", kbd[0:n], kv_ps[:, 0], hb(bege[0:n, :], 64), ALU.mult)
                P.tt("dve", kdec[0:n], kv_ps[:, 0], Eu[0:n, :, n - 1:n].to_broadcast([n, 2, 64]), ALU.mult)
                At_ps = v3(qd[0:n, 0:2 * n], n)
                for e in range(2):
                    P.tr(At_ps[:, e, :], Mm[0][0:n, e, 0:n], ident[0:n, 0:n], inc=(e == 1))
                yield
                P.copy("act", q2(Ma[0]), At_ps)
                P.tt("dve", q2(Xt), Ib, q2(Mm[0]), ALU.subtract)
                for k in range(1, nlev + 1):
                    cur, prv = k % 2, (k - 1) % 2
                    Pa_ps = v3(qa[0:n, 0:2 * n], n)
                    for e in range(2):
                        P.mm(Pa_ps[:, e, :], Mm[prv][0:n, e, 0:n], Ma[prv][0:n, e, 0:n], inc=(e == 1))
                    if k < nlev:
                        Pm_ps = v3(qb[0:n, 0:2 * n], n)
                        for e in range(2):
                            P.mm(Pm_ps[:, e, :], Ma[prv][0:n, e, 0:n], Mm[prv][0:n, e, 0:n], inc=(e == 1))
                    yield
                    P.copy("act", q2(Ma[cur]), Pa_ps)
                    if k < nlev:
                        P.copy("dve", q2(Mm[cur]), Pm_ps)
                    X_ps = v3(qd[0:n, 0:2 * n], n)
                    for e in range(2):
                        P.mm(X_ps[:, e, :], Ma[cur][0:n, e, 0:n], Xt[0:n, e, 0:n], inc=(e == 1))
                    yield
                    P.tt("dve", q2(Xt), q2(Xt), X_ps, ALU.add)
                wT_ps = qc[:, 0:n]
                for e in range(2):
                    P.mm(wT_ps[e * 64:(e + 1) * 64, :], kbd[0:n, e, :], Xt[0:n, e, 0:n], inc=(e == 1))
                yield
                P.ts("dve", nwT[:, 0:n], wT_ps, -1.0, ALU.mult)
                P.copy("act", Sb[:], Sj[:])
                vn_ps = qd[0:n, 0:128].rearrange("p (e d) -> p e d", e=2)
                for e in range(2):
                    pr = slice(e * 64, (e + 1) * 64)
                    P.mm(vn_ps[:, e, :], Xt[0:n, e, 0:n], vb_[0:n, e, :], start=True, stop=False, inc=False)
                    P.mm(vn_ps[:, e, :], nwT[pr, 0:n], Sb[pr, :], start=False, stop=True, inc=(e == 1))
                yield
                P.copy("act", vnew[0:n], vn_ps)
                o_ps = qc[:, 128:128 + n]
                for e in range(2):
                    pr = slice(e * 64, (e + 1) * 64)
                    P.mm(o_ps[pr, :], Sb[pr, :], qdec[pr, 0:n], start=True, stop=False, inc=False)
                    P.mm(o_ps[pr, :], vnew[0:n, e, :], QKd[0:n, e, 0:n], start=False, stop=True, inc=(e == 1))
                Su_ps = qb[:, 384:448]
                for e in range(2):
                    pr = slice(e * 64, (e + 1) * 64)
                    P.mm(Su_ps[pr, :], kdec[0:n, e, :], vnew[0:n, e, :], inc=(e == 1))
                yield
                P.stt(Sj[:], Sj[:], egl2[:, 0:1], Su_ps, ALU.mult, ALU.add)
                P.act(osq[:, 0:n], o_ps, AF.Square)
                ss_ps = qa[:, 256:256 + n]
                P.mm(ss_ps, bonesb[:], osq[:, 0:n])
                yield
                P.act(orn[:, 0:n], ss_ps, AF.Sqrt, bias=EPS, scale=1.0 / 64)
                P.recip(orn[:, 0:n], orn[:, 0:n])
                P.tt("dve", on_[:, 0:n], o_ps, orn[:, 0:n], ALU.mult)
                P.stt(mixT[:, j, cs], on_[:, 0:n], d["gn"][:], zs[:, j, cs], ALU.mult, ALU.mult)
                if not prompt:
                    for e in range(2):
                        P.dma("sp", O["sd_s"][l, b, 2 * j + e], Sj[e * 64:(e + 1) * 64, :])
            if prompt and last:
                for e in range(2):
                    P.dma("sp", O["sd_p"][l, 2 * j + e], Sj[e * 64:(e + 1) * 64, :])

        gens = [gdn_thread(0), gdn_thread(1)]
        alive = [True, True]
        while any(alive):
            for gi, g in enumerate(gens):
                if alive[gi]:
                    try:
                        next(g)
                    except StopIteration:
                        alive[gi] = False


        stage(3)
        sl3 = wslab(w_in[:, 1288:1800])
        preB = [seg3(S1[:, c * (TP + 48):(c + 1) * (TP + 48)], 2) for c in range(2)]
        if prompt:
            for c in range(2):
                if first:
                    P.memset("dve", preB[c][:, :, 0:2], 0.0)
                else:
                    P.copy("dve", preB[c][:, 0, 0:2], d["cB"][:, c, :])
        else:
            load_rows(I["state_shortconv"][l], 32, 2)
            for c in range(2):
                P.copy("dve", preB[c][:, :, 0:2], stF[:, c, 0:32].rearrange("p (b r) -> p b r", r=2))
        for c in range(2):
            ps = bigps()
            bigmm(ps, sl2, 8 + c * 128, 128, hT, T)
            evac(S2[:, c, 0:T], ps[:, 0:T])
        for c in range(2):
            ps = bigps()
            bigmm(ps, sl3, c * 128, 128, hT, T)
            evac(tA[:, 0:T], ps[:, 0:T])
            ps = bigps()
            bigmm(ps, sl3, 256 + c * 128, 128, hT, T)
            P.tt("dve", preB[c][:, :, 2:2 + seglen], tok3(tA[:, 0:T]), tok3(ps[:, 0:T]), ALU.mult)
        if prompt:
            for c in range(2):
                P.copy("dve", d["cB"][:, c, :], preB[c][:, 0, seglen:seglen + 2])
            if last:
                store_rows(O["bc_p"][l], lambda c: d["cB"][:, c, :], 2, [128] * 2)
        else:
            for c in range(2):
                P.copy("dve", stF[:, c, 0:32].rearrange("p (b r) -> p b r", r=2), preB[c][:, :, seglen:seglen + 2])
            store_rows(O["bc_s"][l], lambda c: stF[:, c, 0:32], 32, [128] * 2)
        for c in range(2):
            conv(tok3(tB[:, 0:T]), preB[c], d["wB"], c, 3, seglen, nseg)
            P.tt("dve", mixT[:, 2 + c, 0:T], S2[:, c, 0:T], tB[:, 0:T], ALU.mult)

        stage(4)
        sl4 = wslab(w_in[:, 1800:2312])
        if prompt:
            P.dma("sp", cosT[:, 0:T], C["cosp"][:, tile_i * TP:(tile_i + 1) * TP])
            P.dma("sp", sinT[:, 0:T], C["sinp"][:, tile_i * TP:(tile_i + 1) * TP])
        else:
            P.dma("sp", cosT[:, 0:T], C["coss"])
            P.dma("sp", sinT[:, 0:T], C["sins"])

        def rope(dst_bf, dst_f, src):
            P.mm(pb[2][:, 0:T], perm[:], src)
            P.tt("dve", tB[:, 0:T], src, cosT[:, 0:T], ALU.mult)
            P.tt("dve", tC[:, 0:T], pb[2][:, 0:T], sinT[:, 0:T], ALU.mult)
            if dst_f is not None:
                P.tt("dve", dst_f, tB[:, 0:T], tC[:, 0:T], ALU.add)
                P.copy("act", dst_bf, dst_f)
            else:
                P.tt("dve", dst_bf, tB[:, 0:T], tC[:, 0:T], ALU.add)

        for j in range(2):
            ps = bigps()
            for half in range(2):
                h = half * 2 + j
                bigmm(ps, sl4, h * 64, 64, hT, T, m0=half * 64)
            evac(tA[:, 0:T], ps[:, 0:T])
            rope(qTb[:, j, 0:T], None, tA[:, 0:T])
        ps = bigps()
        bigmm(ps, sl4, 256, 128, hT, T)
        evac(tA[:, 0:T], ps[:, 0:T])
        rope(kTb[:, 0:T], krf[:, 0:T], tA[:, 0:T])
        ps = bigps()
        bigmm(ps, sl4, 384, 128, hT, T)
        evac(S2[:, 7, 0:T], ps[:, 0:T])
        if not prompt:
            P.tr(pb[6][0:64, 0:128], krf[:, 0:64], ident[:])
            evac(ktok[0:64, :], pb[6][0:64, 0:128])
            P.tr(pb[6][0:64, 128:256], S2[:, 7, 0:64], ident[:])
            evac(vtokf[0:64, :], pb[6][0:64, 128:256])
            for b in range(16):
                P.dma("sp", O["ck_s"][l, b, 124:128, :], ktok[b * 4:(b + 1) * 4, :])
                P.dma("sp", O["cv_s"][l, b, 124:128, :], vtokf[b * 4:(b + 1) * 4, :])
            P.dma("sp", O["ck_s"][l, :, 0:124, :], I["cache_win_k"][l, :, 4:128, :])
            P.dma("sp", O["cv_s"][l, :, 0:124, :], I["cache_win_v"][l, :, 4:128, :])
        for b in range(nblk):
            cs = slice(b * n, (b + 1) * n)
            hasprev = (not prompt) or (not (first and b == 0))
            off = 0 if hasprev else 128
            W = 256 - off if prompt else 128 + n
            if prompt:
                kpv = d["kprev"]
                vpv = d["vprev"]
            else:
                P.dma("sp", kcache[:], I["cache_win_k"][l, b])
                P.dma("sp", vcache[:], I["cache_win_v"][l, b])
                P.tr(pb[6][:, 0:128], kcache[:], ident[:])
                evac(d["kprev"][:], pb[6][:, 0:128])
                P.copy("dve", d["vprev"][:], vcache[:])
                kpv = d["kprev"]
                vpv = d["vprev"]
            P.tr(pb[6][0:n, 256:384], S2[:, 7, cs], ident[:])
            P.copy("dve", vtokb[0:n, :], pb[6][0:n, 256:384])
            if prompt and last and b == nblk - 1:
                P.copy("act", vtokf[0:n, :], pb[6][0:n, 256:384])
                P.dma("sp", O["cv_p"][l], vtokf[:])
                P.tr(pb[6][:, 384:512], krf[:, cs], ident[:])
                evac(ktok[:], pb[6][:, 384:512])
                P.dma("sp", O["ck_p"][l], ktok[:])
            S_ps = [pb[0][0:n, :].rearrange("p (h k) -> p h k", h=2), pb[1][0:n, :].rearrange("p (h k) -> p h k", h=2)]
            for h in range(4):
                pr = slice((h // 2) * 64, (h // 2) * 64 + 64)
                dst = S_ps[h // 2][:, h % 2, :]
                if hasprev:
                    P.mm(dst[:, 0:128], qTb[pr, h % 2, cs], kpv[pr, :], inc=False)
                P.mm(dst[:, 128:128 + n], qTb[pr, h % 2, cs], kTb[pr, cs], inc=(h % 2 == 1))
            mk = amask[0:n, off:off + W].unsqueeze(1).to_broadcast([n, 2, W])
            for hh in range(2):
                P.tt("dve", sc[0:n, 2 * hh:2 * hh + 2, 0:W], S_ps[hh][:, :, off:off + W], mk, ALU.add)
            P.reduce(mx[0:n, :], sc[0:n, :, 0:W], ALU.max)
            P.ts("dve", nmx[0:n, :], mx[0:n, :], -0.125, ALU.mult)
            for h in range(4):
                P.act(Pf[0:n, h, 0:W], sc[0:n, h, 0:W], AF.Exp, bias=nmx[0:n, h:h + 1], scale=0.125,
                      accum_out=sm[0:n, h:h + 1])
            P.tt("dve", es[0:n, :], nmx[0:n, :], d["sink"][0:n, :], ALU.add)
            P.act(es[0:n, :], es[0:n, :], AF.Exp)
            P.tt("dve", es[0:n, :], es[0:n, :], sm[0:n, :], ALU.add)
            P.recip(es[0:n, :], es[0:n, :])
            P.tt("dve", Pn[0:n, :, 0:W], Pf[0:n, :, 0:W], hb(es[0:n, :], W), ALU.mult)
            PT_ps = pT16[:, :].rearrange("p (h s q) -> p h s q", h=4, s=2)
            for h in range(4):
                if hasprev:
                    P.tr(PT_ps[:, h, 0, 0:n], Pn[0:n, h, 0:128], identb[0:n, 0:n], inc=False)
                P.tr(PT_ps[0:n, h, 1, 0:n], Pn[0:n, h, 128 - off:128 - off + n], identb[0:n, 0:n], inc=(h == 3))
            if hasprev:
                P.copy("act", PT[:, :, 0, 0:n], PT_ps[:, :, 0, 0:n])
            P.copy("dve", PT[0:n, :, 1, 0:n], PT_ps[0:n, :, 1, 0:n])
            o_ps = pb[6][:, 0:2 * n].rearrange("p (j t) -> p j t", j=2)
            for h in range(4):
                kvs = slice((h // 2) * 64, (h // 2) * 64 + 64)
                dst = o_ps[(h % 2) * 64:(h % 2) * 64 + 64, h // 2, :]
                if hasprev:
                    P.mm(dst, vpv[:, kvs], PT[:, h, 0, 0:n], start=True, stop=False, inc=False)
                P.mm(dst, vtokb[0:n, kvs], PT[0:n, h, 1, 0:n], start=(not hasprev), stop=True, inc=(h == 3))
            evac(mixT[:, 4:6, cs], o_ps)
            if prompt:
                P.copy("dve", d["kprev"][:], kTb[:, cs])
                P.copy("act", d["vprev"][:], vtokb[:])

        stage(5)
        sl5 = wslab(w_in[:, 2312:2824])

        def gelu(dst, ps):
            P.act(tB[:, 0:T], ps, AF.Square)
            P.ts("dve", tB[:, 0:T], tB[:, 0:T], 0.044715, ALU.mult, 1.0, ALU.add)
            P.tt("dve", tB[:, 0:T], tB[:, 0:T], ps, ALU.mult)
            P.act(tB[:, 0:T], tB[:, 0:T], AF.Sigmoid, scale=1.5957691216057308)
            P.tt("dve", dst, tB[:, 0:T], ps, ALU.mult)

        gsc = [tB, gcv, sgb, tA]
        for g_ in range(4):
            bigmm(pb[g_], sl5, g_ * 128, 128, hT, T)
        for g_ in range(4):
            P.act(gsc[g_][:, 0:T], pb[g_][:, 0:T], AF.Square)
        for g_ in range(4):
            P.ts("dve", gsc[g_][:, 0:T], gsc[g_][:, 0:T], 0.044715, ALU.mult, 1.0, ALU.add)
        for g_ in range(4):
            P.tt("dve", gsc[g_][:, 0:T], gsc[g_][:, 0:T], pb[g_][:, 0:T], ALU.mult)
        for g_ in range(4):
            P.act(gsc[g_][:, 0:T], gsc[g_][:, 0:T], AF.Sigmoid, scale=1.5957691216057308)
        for g_ in range(4):
            P.tt("dve", S2[:, g_, 0:T], gsc[g_][:, 0:T], pb[g_][:, 0:T], ALU.mult)
        P.copy("act", sqb[:, 0:2, 0:T], S2[:, 2:4, 0:T])
        P.act(sqb[:, 2:4, 0:T], S2[:, 2:4, 0:T], AF.Square)
        for c in range(2):
            P.mm(pb[2][:, 0:T], onesb[:], sqb[:, c, 0:T], start=(c == 0), stop=(c == 1), inc=(c == 1))
        for c in range(2):
            P.mm(pb[3][:, 0:T], onesb[:], sqb[:, 2 + c, 0:T], start=(c == 0), stop=(c == 1), inc=(c == 1))
        P.act(tA[:, 0:T], pb[2][:, 0:T], AF.Copy, scale=1.0 / 256)
        P.tt("dve", tB[:, 0:T], tA[:, 0:T], tA[:, 0:T], ALU.mult)
        P.stt(tB[:, 0:T], pb[3][:, 0:T], 1.0 / 256, tB[:, 0:T], ALU.mult, ALU.subtract)
        P.act(tB[:, 0:T], tB[:, 0:T], AF.Sqrt, bias=EPS)
        P.recip(tB[:, 0:T], tB[:, 0:T])
        for c in range(2):
            P.tt("dve", S2[:, 2 + c, 0:T], S2[:, 2 + c, 0:T], tA[:, 0:T], ALU.subtract)
            P.tt("dve", S2[:, 2 + c, 0:T], S2[:, 2 + c, 0:T], tB[:, 0:T], ALU.mult)
            P.ts("dve", S2[:, 2 + c, 0:T], S2[:, 2 + c, 0:T], d["lng"][:, c:c + 1], ALU.mult, d["lnb"][:, c:c + 1], ALU.add)
        if not prompt:
            store_rows(O["dv_s"][l], lambda c: S2[:, 2 + c, 0:T], 64, [128] * 2)
        for b in range(nblk):
            cs = slice(b * n, (b + 1) * n)
            for c in range(2):
                P.tr(pb[2][0:n, c * 128:(c + 1) * 128], S2[:, 2 + c, cs], ident[:], inc=(c == 1))
            evac(dvT[0:n, :], pb[2][0:n, 0:256])
            m_ps = pb[3][:, 0:2 * n].rearrange("p (j t) -> p j t", j=2)
            for g in range(4):
                dst = m_ps[(g % 2) * 64:(g % 2) * 64 + 64, g // 2, :]
                P.mm(dst, dvT[0:n, g * 64:(g + 1) * 64], d["WT"][0:n, g, 0:n], start=True, stop=False, inc=False)
                P.mm(dst, onesrow[0:1, 0:64], d["brow"][0:1, g * 128:g * 128 + n], start=False, stop=True, inc=(g == 3))
            P.tt("dve", mixT[:, 6:8, cs], S2[:, 0:2, cs], m_ps, ALU.mult)

        stage(6)
        for s in range(2):
            sl = wslab(I["w_out"][l][:, s * 512:(s + 1) * 512])
            for c in range(4):
                ps = bigps()
                bigmm(ps, sl, c * 128, 128, mixT, T)
                P.tt("dve", xs_t[:, s * 4 + c, 0:T], xs_t[:, s * 4 + c, 0:T], ps[:, 0:T], ALU.add)

        stage(7)
        rmsnorm(d["g2"], T)
        gp3s = [seg3(gpre[:, :], 2), seg3(gpre2[:, :], 2)]
        gcvs = [gcv, sgb]
        if not prompt:
            load_rows(I["state_ffn_conv"][l], 32, FC)
        slabs = {}

        def fslab(kind, s_):
            if (kind, s_) not in slabs:
                ncol = 512 if s_ < 5 else 256
                w_ = I["ffn_w_gate" if kind == "g" else "ffn_w_up"][l]
                slabs[(kind, s_)] = wslab(w_[:, s_ * 512:s_ * 512 + ncol])
            return slabs[(kind, s_)]

        fpi = {"i": 0}

        def fbank():
            fpi["i"] = (fpi["i"] + 1) % 4
            return pb[fpi["i"]]

        def stageA(j):
            s_, c = divmod(j, 4)
            gp3, gcv_ = gp3s[j % 2], gcvs[j % 2]
            ps = fbank()
            bigmm(ps, fslab("g", s_), c * 128, 128, hT, T)
            if prompt:
                if first:
                    P.memset("dve", gp3[:, :, 0:2], 0.0)
                else:
                    P.copy("dve", gp3[:, 0, 0:2], d["cF"][:, j, :])
            else:
                P.copy("dve", gp3[:, :, 0:2], stF[:, j, 0:32].rearrange("p (b r) -> p b r", r=2))
            P.copy("act", gp3[:, :, 2:2 + seglen], tok3(ps[:, 0:T]))
            if prompt:
                P.copy("dve", d["cF"][:, j, :], gp3[:, 0, seglen:seglen + 2])
            else:
                P.copy("dve", stF[:, j, 0:32].rearrange("p (b r) -> p b r", r=2), gp3[:, :, seglen:seglen + 2])
            conv(tok3(gcv_[:, 0:T]), gp3, d["wF"], j, 3, seglen, nseg)

        def stageB(j):
            s_, c = divmod(j, 4)
            gcv_ = gcvs[j % 2]
            P.act(gcv_[:, 0:T], gcv_[:, 0:T], AF.Silu)
            ps = fbank()
            bigmm(ps, fslab("u", s_), c * 128, 128, hT, T)
            P.tt("dve", actb[:, j, 0:T], gcv_[:, 0:T], ps[:, 0:T], ALU.mult)

        stageA(0)
        for j in range(FC):
            if j + 1 < FC:
                stageA(j + 1)
            stageB(j)
        stage(7.5)
        if prompt and last:
            store_rows(O["fc_p"][l], lambda c: d["cF"][:, c, :], 2, [128] * FC)
        if not prompt:
            store_rows(O["fc_s"][l], lambda c: stF[:, c, 0:32], 32, [128] * FC)
        wd = I["ffn_w_down"][l].rearrange("(j p) n -> p j n", p=128)
        acc = [pb[i][:, :] for i in range(7)] + [pT16[:, :].bitcast(F32)]
        for jg in range((FC + 3) // 4):
            nj = min(4, FC - 4 * jg)
            tw = wring[wpos["w"] % NRING]
            wpos["w"] += 1
            t = tw[:, :, :].rearrange("p k n -> p (k n)").rearrange("p (j n) -> p j n", j=4)
            P.dma("pool", t[:, 0:nj, :], wd[:, 4 * jg:4 * jg + nj, :])
            lastg = (4 * jg + nj == FC)
            if not lastg:
                for jj in range(nj):
                    j = 4 * jg + jj
                    for oc in range(8):
                        P.mm(acc[oc][:, 0:T], t[:, jj, oc * 128:(oc + 1) * 128], actb[:, j, 0:T],
                             start=(j == 0), stop=False, inc=(oc == 7))
            else:
                for oc in range(8):
                    for jj in range(nj):
                        j = 4 * jg + jj
                        P.mm(acc[oc][:, 0:T], t[:, jj, oc * 128:(oc + 1) * 128], actb[:, j, 0:T],
                             start=False, stop=(j == FC - 1), inc=(j == FC - 1))
                    P.tt("dve", xs_t[:, oc, 0:T], xs_t[:, oc, 0:T], acc[oc][:, 0:T], ALU.add)
        stage(8)

    def run_tile(prompt, ti):
        if prompt:
            T, n, nblk, nseg, seglen = TP, 128, 4, 1, TP
            src = I["xp"][ti * TP:(ti + 1) * TP, :]
            dst = O["y_p"][ti * TP:(ti + 1) * TP, :]
        else:
            T, n, nblk, nseg, seglen = TS, 4, 16, 16, 4
            src = I["xs"]
            dst = O["y_s"]
        nrb = (T + 127) // 128
        iobufs = [xin[:, :], S1[:, 0:1024], zs[:, :, :].rearrange("p c t -> p (c t)"),
                  sqb[:, :, :].rearrange("p c t -> p (c t)").bitcast(F32)]
        for rb in range(nrb):
            r = min(128, T - rb * 128)
            P.dma("sp", iobufs[rb % 4][0:r, :], src[rb * 128:rb * 128 + r, :])
        for rb in range(nrb):
            r = min(128, T - rb * 128)
            xb = iobufs[rb % 4]
            for g in range(2):
                bank = pb[3 + (2 * rb + g) % 4]
                for k in range(4):
                    P.tr(bank[:, k * r:(k + 1) * r], xb[0:r, (g * 4 + k) * 128:(g * 4 + k + 1) * 128],
                         ident[0:r, 0:r], inc=(k == 3))
                evac(xs_t[:, g * 4:g * 4 + 4, rb * 128:rb * 128 + r],
                     bank[:, 0:4 * r].rearrange("p (k t) -> p k t", k=4))
        for li, l in enumerate([int(c) for c in os.environ.get("MK_LAYERS", "01")]):
            lpos["i"] = li
            run_layer(l, T, n, nblk, nseg, seglen, prompt, first=(ti == 0), last=(ti == NTP - 1), tile_i=ti)
        rmsnorm(gfin, T, out=S2)
        for rb in range(nrb):
            r = min(128, T - rb * 128)
            yb = iobufs[rb % 4]
            for g in range(2):
                bank = pb[3 + (2 * rb + g) % 4]
                for k in range(4):
                    P.tr(bank[0:r, k * 128:(k + 1) * 128], S2[:, g * 4 + k, rb * 128:rb * 128 + r], ident[:],
                         inc=(k == 3))
                evac(yb[0:r, g * 512:(g + 1) * 512], bank[0:r, :])
            P.dma("sp", dst[rb * 128:rb * 128 + r, :], yb[0:r, :])

    ntiles = int(os.environ.get("MK_NTILES", NTP))
    try:
        for ti in range(ntiles):
            run_tile(True, ti)
        if os.environ.get("MK_NOSAMPLE") is None:
            run_tile(False, 0)
    except _Stop:
        P.dma("sp", O["y_s"][0:64, :], xs_t[0:64, 0, 0:TP].rearrange("p t -> p t") if False else xin[0:64, :])
    P.emit()
    P.close()
    return nc, P


_CACHE = {}


def kernel(**inp):
    f = np.float32
    if "nc" not in _CACHE:
        _CACHE["nc"], _ = build_program()
        _CACHE["consts"] = _consts()
    nc = _CACHE["nc"]
    consts = _CACHE["consts"]
    shared = {}
    for k in IN_SHAPES:
        if k in ("xp", "xs", "state_delta", "state_delta_conv", "state_shortconv", "cache_win_k", "cache_win_v",
                 "state_ffn_conv"):
            continue
        shared[k] = np.ascontiguousarray(np.asarray(inp[k], f))
    for k, v in consts.items():
        shared["c_" + k] = np.ascontiguousarray(v.reshape(CONST_SHAPES[k]).astype(f))
    in_maps = []
    for c in range(8):
        m = dict(shared)
        bs = slice(16 * c, 16 * c + 16)
        m["xp"] = np.ascontiguousarray(np.asarray(inp["x_prompt"][c], f))
        m["xs"] = np.ascontiguousarray(np.asarray(inp["x_sample"][bs], f).reshape(64, D))
        m["state_delta"] = np.ascontiguousarray(np.asarray(inp["state_delta"][:, bs], f))
        m["state_delta_conv"] = np.ascontiguousarray(np.asarray(inp["state_delta_conv"][:, bs], f).reshape(2, 48, 768))
        m["state_shortconv"] = np.ascontiguousarray(np.asarray(inp["state_shortconv"][:, bs], f).reshape(2, 32, 256))
        m["cache_win_k"] = np.ascontiguousarray(np.asarray(inp["cache_win_k"][:, bs], f).reshape(2, 16, 128, 128))
        m["cache_win_v"] = np.ascontiguousarray(np.asarray(inp["cache_win_v"][:, bs], f).reshape(2, 16, 128, 128))
        m["state_ffn_conv"] = np.ascontiguousarray(np.asarray(inp["state_ffn_conv"][:, bs], f).reshape(2, 32, DFF))
        in_maps.append(m)
    res = run_bass_kernel_spmd(nc, in_maps, core_ids=list(range(8)))
    R = res.results

    def cat_p(k, shape):
        return np.stack([np.asarray(R[c][k], f).reshape(shape) for c in range(8)], axis=1)

    def cat_s(k, shape):
        return np.concatenate([np.asarray(R[c][k], f).reshape(shape) for c in range(8)], axis=1)

    y_p = np.stack([np.asarray(R[c]["y_p"], f) for c in range(8)], axis=0)
    y_s = np.concatenate([np.asarray(R[c]["y_s"], f).reshape(16, 4, D) for c in range(8)], axis=0)
    return (
        y_p, y_s,
        cat_p("sd_p", (2, 4, 64, 64)), cat_s("sd_s", (2, 16, 4, 64, 64)),
        cat_p("ac_p", (2, 3, 768)), cat_s("ac_s", (2, 16, 3, 768)),
        cat_p("bc_p", (2, 2, 256)), cat_s("bc_s", (2, 16, 2, 256)),
        cat_p("ck_p", (2, 128, 2, 64)), cat_s("ck_s", (2, 16, 128, 2, 64)),
        cat_p("cv_p", (2, 128, 2, 64)), cat_s("cv_s", (2, 16, 128, 2, 64)),
        cat_p("fc_p", (2, 2, DFF)), cat_s("fc_s", (2, 16, 2, DFF)),
        cat_s("dv_s", (2, 16, 4, 256)),
    )
```
